# Optimizing a Trainium2 kernel written in Bass

```python
import math
import jax, jax.numpy as jnp
from jax import lax
import numpy as np

D_MODEL = 2048
BATCH = 1
SEQ = 8192
DEPTH = 4

N_A_LAYERS = DEPTH // 2
N_B_LAYERS = DEPTH - N_A_LAYERS

HEAD_DIM = 128
D_FF = 4 * D_MODEL
LRU_WIDTH = 3 * D_MODEL // 4
LRU_BLOCK_WIDTH = 128
LRU_BLOCKS = LRU_WIDTH // LRU_BLOCK_WIDTH
CONV_WIDTH = 4
RG_C = 8.0
MEM_TOKENS = 256
MEM_HEADS = 4
MEM_WIDTH = MEM_HEADS * HEAD_DIM
DIL_PATTERNS = ((128, 1), (512, 4), (2048, 16))
DIL_GROUPS = len(DIL_PATTERNS)
DIL_HEADS = (D_MODEL - MEM_WIDTH) // (DIL_GROUPS * HEAD_DIM)
DIL_WIDTH = DIL_GROUPS * DIL_HEADS * HEAD_DIM
DIL_OUT_WIDTH = DIL_HEADS * HEAD_DIM
Q_BLOCK = 128
REL_BUCKETS = 32
REL_MAX_EXACT = REL_BUCKETS // 2
REL_MAX_DISTANCE = 2048
A_IN_WIDTH = 2 * LRU_WIDTH + MEM_WIDTH
A_OUT_WIDTH = LRU_WIDTH + MEM_WIDTH
B_IN_WIDTH = DIL_WIDTH + MEM_WIDTH
B_OUT_WIDTH = DIL_OUT_WIDTH + MEM_WIDTH
NORM_EPS = 1e-6
NEG_INF = -1e30

kernel_name = "hawk_yoco_dilated_hybrid"


def rms_norm(x, g):
    xf = x.astype(jnp.float32)
    y = xf * lax.rsqrt(jnp.mean(xf * xf, axis=-1, keepdims=True) + NORM_EPS)
    return (y * g.astype(jnp.float32)).astype(x.dtype)


def head_norm(x, g):
    xf = x.astype(jnp.float32)
    y = xf * lax.rsqrt(jnp.mean(xf * xf, axis=-1, keepdims=True) + NORM_EPS)
    return y * g.astype(jnp.float32)


def t5_bucket(distance):
    n = jnp.maximum(distance, 0)
    small = n < REL_MAX_EXACT
    nf = jnp.maximum(n, 1).astype(jnp.float32)
    large = REL_MAX_EXACT + (jnp.log(nf / REL_MAX_EXACT)
                             / math.log(REL_MAX_DISTANCE / REL_MAX_EXACT)
                             * (REL_BUCKETS - REL_MAX_EXACT)).astype(jnp.int32)
    large = jnp.minimum(large, REL_BUCKETS - 1)
    return jnp.where(small, n, large)


def rglru_branch(u, conv_w, conv_b, gr_w, gr_b, gi_w, gi_b, lam):
    B, S, W = u.shape
    upad = jnp.pad(u, ((0, 0), (CONV_WIDTH - 1, 0), (0, 0)))
    xc = conv_b + sum(upad[:, k:k + S] * conv_w[k] for k in range(CONV_WIDTH))
    xf = xc.astype(jnp.float32)
    xb = xf.reshape(B, S, LRU_BLOCKS, LRU_BLOCK_WIDTH)
    r = jax.nn.sigmoid(jnp.einsum('bsnc,ncd->bsnd', xb, gr_w.astype(jnp.float32))
                       + gr_b.astype(jnp.float32)).reshape(B, S, W)
    i = jax.nn.sigmoid(jnp.einsum('bsnc,ncd->bsnd', xb, gi_w.astype(jnp.float32))
                       + gi_b.astype(jnp.float32)).reshape(B, S, W)
    log_a = -RG_C * r * jax.nn.softplus(-lam.astype(jnp.float32))
    a = jnp.exp(log_a)
    b = jnp.sqrt(-jnp.expm1(2.0 * log_a)) * (i * xf)

    def combine(left, right):
        a_l, b_l = left
        a_r, b_r = right
        return a_l * a_r, a_r * b_l + b_r

    _, h = lax.associative_scan(combine, (a, b), axis=1)
    return h.astype(u.dtype)


def memory_kv(mem, g, w_kv, k_g):
    B, M, _ = mem.shape
    kv = rms_norm(mem, g) @ w_kv
    k = head_norm(kv[..., :MEM_WIDTH].reshape(B, M, MEM_HEADS, HEAD_DIM), k_g)
    v = kv[..., MEM_WIDTH:].reshape(B, M, MEM_HEADS, HEAD_DIM).astype(jnp.float32)
    return k, v


def memory_attention(q, k, v):
    logits = jnp.einsum('bshd,bmhd->bhsm', q, k)
    p = jax.nn.softmax(logits, axis=-1)
    return jnp.einsum('bhsm,bmhd->bshd', p, v)


def shared_kv(x, g, w_kv, k_g):
    B, S, _ = x.shape
    kv = rms_norm(x, g) @ w_kv
    k = kv[..., :DIL_WIDTH].reshape(B, S, DIL_GROUPS, DIL_HEADS, HEAD_DIM)
    v = kv[..., DIL_WIDTH:].reshape(B, S, DIL_GROUPS, DIL_HEADS, HEAD_DIM)
    k = head_norm(k, k_g[:, None, :])
    return k, v.astype(jnp.float32)


def dilated_window_attention(q, k, v, bias_table, window, dilation):
    B, S, H, hd = q.shape
    L = S // dilation
    n_back = window // dilation
    assert n_back <= Q_BLOCK
    nb = -(-L // Q_BLOCK)
    Lp = nb * Q_BLOCK

    def to_blocks(t):
        t = t.reshape(B, L, dilation, H, hd).transpose(0, 2, 1, 3, 4)
        t = jnp.pad(t, ((0, 0), (0, 0), (0, Lp - L), (0, 0), (0, 0)))
        return t.reshape(B, dilation, nb, Q_BLOCK, H, hd)

    def with_prev(t):
        prev = jnp.pad(t, ((0, 0), (0, 0), (1, 0), (0, 0), (0, 0), (0, 0)))[:, :, :-1]
        return jnp.concatenate([prev, t], axis=3)

    qb = to_blocks(q)
    kk = with_prev(to_blocks(k))
    vv = with_prev(to_blocks(v))
    logits = jnp.einsum('brnqhd,brnkhd->brnhqk', qb, kk)

    qi = jnp.arange(Q_BLOCK)[:, None]
    ki = jnp.arange(2 * Q_BLOCK)[None, :]
    u = qi + Q_BLOCK - ki
    bias = bias_table[t5_bucket(u * dilation)].transpose(2, 0, 1).astype(jnp.float32)
    kpos = jnp.arange(nb)[:, None] * Q_BLOCK + ki - Q_BLOCK
    valid = ((u >= 0) & (u <= n_back))[None] & (kpos >= 0)[:, None, :]
    logits = jnp.where(valid[None, None, :, None], logits + bias, NEG_INF)

    m = jnp.max(logits, axis=-1, keepdims=True)
    p = jnp.exp(logits - m)
    den = jnp.sum(p, axis=-1)
    o = jnp.einsum('brnhqk,brnkhd->brnqhd', p, vv) / den.transpose(0, 1, 2, 4, 3)[..., None]
    lse = (m[..., 0] + jnp.log(den)).transpose(0, 1, 2, 4, 3)

    o = o.reshape(B, dilation, Lp, H, hd)[:, :, :L].transpose(0, 2, 1, 3, 4).reshape(B, S, H, hd)
    lse = lse.reshape(B, dilation, Lp, H)[:, :, :L].transpose(0, 2, 1, 3).reshape(B, S, H)
    return o, lse


def dilated_mixture(q, k, v, rel_bias):
    outs, lses = [], []
    for g, (window, dilation) in enumerate(DIL_PATTERNS):
        o, l = dilated_window_attention(q[:, :, g], k[:, :, g], v[:, :, g],
                                        rel_bias[:, g * DIL_HEADS:(g + 1) * DIL_HEADS],
                                        window, dilation)
        outs.append(o)
        lses.append(l)
    w = jax.nn.softmax(jnp.stack(lses, axis=0), axis=0)
    return jnp.sum(w[..., None] * jnp.stack(outs, axis=0), axis=0)


def setup_inputs(seed: int = 0) -> dict:
    key = jax.random.key(seed)
    ks = jax.random.split(key, 26)
    f32 = jnp.float32

    def normal(k, shape, scale):
        return jax.random.normal(k, shape, f32) * scale

    def gain(k, shape):
        return 1.0 + 0.02 * jax.random.normal(k, shape, f32)

    a_pow = jax.random.uniform(ks[17], (N_A_LAYERS, LRU_WIDTH), f32, 0.9, 0.999)
    a_base = a_pow ** (1.0 / RG_C)
    a_lambda = jnp.log(a_base) - jnp.log1p(-a_base)
    return {
        'x': normal(ks[0], (BATCH, SEQ, D_MODEL), 1.0),
        'mem': normal(ks[1], (BATCH, MEM_TOKENS, D_MODEL), 1.0),
        'norm_mix_g': gain(ks[2], (DEPTH, D_MODEL)),
        'norm_mlp_g': gain(ks[3], (DEPTH, D_MODEL)),
        'mlp_w1': normal(ks[4], (DEPTH, D_MODEL, D_FF), D_MODEL ** -0.5),
        'mlp_w2': normal(ks[5], (DEPTH, D_FF, D_MODEL), 0.5 * D_FF ** -0.5),
        'mem_norm_g': gain(ks[6], (DEPTH, D_MODEL)),
        'mem_w_kv': normal(ks[7], (DEPTH, D_MODEL, 2 * MEM_WIDTH), D_MODEL ** -0.5),
        'mem_q_norm_g': gain(ks[8], (DEPTH, HEAD_DIM)),
        'mem_k_norm_g': gain(ks[9], (DEPTH, HEAD_DIM)),
        'a_w_in': normal(ks[10], (N_A_LAYERS, D_MODEL, A_IN_WIDTH), D_MODEL ** -0.5),
        'a_conv_w': normal(ks[11], (N_A_LAYERS, CONV_WIDTH, LRU_WIDTH), CONV_WIDTH ** -0.5),
        'a_conv_b': normal(ks[12], (N_A_LAYERS, LRU_WIDTH), 0.01),
        'a_gate_r_w': normal(ks[13], (N_A_LAYERS, LRU_BLOCKS, LRU_BLOCK_WIDTH, LRU_BLOCK_WIDTH), LRU_BLOCK_WIDTH ** -0.5),
        'a_gate_r_b': normal(ks[14], (N_A_LAYERS, LRU_BLOCKS, LRU_BLOCK_WIDTH), 0.01),
        'a_gate_i_w': normal(ks[15], (N_A_LAYERS, LRU_BLOCKS, LRU_BLOCK_WIDTH, LRU_BLOCK_WIDTH), LRU_BLOCK_WIDTH ** -0.5),
        'a_gate_i_b': normal(ks[16], (N_A_LAYERS, LRU_BLOCKS, LRU_BLOCK_WIDTH), 0.01),
        'a_lambda': a_lambda,
        'a_w_out': normal(ks[18], (N_A_LAYERS, A_OUT_WIDTH, D_MODEL), A_OUT_WIDTH ** -0.5),
        'kv_norm_g': gain(ks[19], (D_MODEL,)),
        'kv_w': normal(ks[20], (D_MODEL, 2 * DIL_WIDTH), D_MODEL ** -0.5),
        'k_norm_g': gain(ks[21], (DIL_GROUPS, HEAD_DIM)),
        'rel_bias': normal(ks[22], (REL_BUCKETS, DIL_GROUPS * DIL_HEADS), 0.3),
        'b_w_q': normal(ks[23], (N_B_LAYERS, D_MODEL, B_IN_WIDTH), D_MODEL ** -0.5),
        'b_q_norm_g': gain(ks[24], (N_B_LAYERS, DIL_GROUPS, HEAD_DIM)),
        'b_w_out': normal(ks[25], (N_B_LAYERS, B_OUT_WIDTH, D_MODEL), B_OUT_WIDTH ** -0.5),
    }


def reference(x, mem, norm_mix_g, norm_mlp_g, mlp_w1, mlp_w2, mem_norm_g, mem_w_kv,
              mem_q_norm_g, mem_k_norm_g, a_w_in, a_conv_w, a_conv_b, a_gate_r_w, a_gate_r_b,
              a_gate_i_w, a_gate_i_b, a_lambda, a_w_out, kv_norm_g, kv_w, k_norm_g, rel_bias,
              b_w_q, b_q_norm_g, b_w_out):
    B, S, _ = x.shape
    scale = HEAD_DIM ** -0.5
    k_shared = None
    v_shared = None
    for layer in range(DEPTH):
        h = rms_norm(x, norm_mix_g[layer])
        mem_k, mem_v = memory_kv(mem, mem_norm_g[layer], mem_w_kv[layer], mem_k_norm_g[layer])
        if layer < N_A_LAYERS:
            proj = h @ a_w_in[layer]
        else:
            proj = h @ b_w_q[layer - N_A_LAYERS]
        mq = proj[..., -MEM_WIDTH:].reshape(B, S, MEM_HEADS, HEAD_DIM)
        mq = head_norm(mq, mem_q_norm_g[layer]) * scale
        mem_out = memory_attention(mq, mem_k, mem_v).reshape(B, S, MEM_WIDTH).astype(x.dtype)

        if layer < N_A_LAYERS:
            i = layer
            lru = rglru_branch(proj[..., :LRU_WIDTH], a_conv_w[i], a_conv_b[i], a_gate_r_w[i],
                               a_gate_r_b[i], a_gate_i_w[i], a_gate_i_b[i], a_lambda[i])
            gate = jax.nn.gelu(proj[..., LRU_WIDTH:2 * LRU_WIDTH])
            mixed = jnp.concatenate([lru * gate, mem_out], axis=-1) @ a_w_out[i]
        else:
            j = layer - N_A_LAYERS
            if j == 0:
                k_shared, v_shared = shared_kv(x, kv_norm_g, kv_w, k_norm_g)
            dq = proj[..., :DIL_WIDTH].reshape(B, S, DIL_GROUPS, DIL_HEADS, HEAD_DIM)
            dq = head_norm(dq, b_q_norm_g[j][:, None, :]) * scale
            dil = dilated_mixture(dq, k_shared, v_shared, rel_bias)
            dil = dil.reshape(B, S, DIL_OUT_WIDTH).astype(x.dtype)
            mixed = jnp.concatenate([dil, mem_out], axis=-1) @ b_w_out[j]
        x = x + mixed

        hm = rms_norm(x, norm_mlp_g[layer])
        x = x + jnp.square(jax.nn.relu(hm @ mlp_w1[layer])) @ mlp_w2[layer]
    return x
```

```python
import math
import numpy as np
import ml_dtypes
import concourse.bass as bass
import concourse.mybir as mybir
from concourse.bass_utils import run_bass_kernel_spmd

F32 = mybir.dt.float32
BF16 = mybir.dt.bfloat16
AF = mybir.ActivationFunctionType
ALU = mybir.AluOpType

NCORES = 8
D = 2048
S = 8192
T = S // NCORES
KC = D // 128
DFF = 4 * D
EPS = 1e-6
WT_ELEMS = 4096
NPAR = 256
SCALE = 128 ** -0.5
DIL = (1, 4, 16)


class Buf:
    __slots__ = ("name", "w", "r")

    def __init__(self, name=""):
        self.name = name
        self.w = None
        self.r = []


class Eng:
    def __init__(self, name, sem, is_pe=False):
        self.name = name
        self.sem = sem
        self.count = 0
        self.ops = []
        self.waited = {}
        self.is_pe = is_pe


class Prog:
    N_DMA_SEMS = 12

    def __init__(self, nc):
        self.nc = nc
        self.pe = Eng("pe", nc.alloc_semaphore("s_pe"), is_pe=True)
        self.act = Eng("act", nc.alloc_semaphore("s_act"))
        self.dve = Eng("dve", nc.alloc_semaphore("s_dve"))
        self.pool = Eng("pool", nc.alloc_semaphore("s_pool"))
        self.sp = Eng("sp", None)
        self.engs = [self.pe, self.act, self.dve, self.pool, self.sp]
        self.dma_sems = [nc.alloc_semaphore(f"s_dma{i}") for i in range(self.N_DMA_SEMS)]
        self.dma_cnt = [0] * self.N_DMA_SEMS
        self.dma_last = [None] * self.N_DMA_SEMS
        self.dma_rr = 0

    def _deps(self, reads, writes):
        deps = []
        for b in reads:
            if b.w is not None:
                deps.append(b.w)
        for b in writes:
            if b.w is not None:
                deps.append(b.w)
            deps.extend(b.r)
        return deps

    def _filter(self, eng, deps):
        waits = {}
        for (sem, val) in deps:
            if eng.is_pe and sem is eng.sem:
                continue
            k = id(sem)
            if eng.waited.get(k, 0) >= val:
                continue
            if k not in waits or waits[k][1] < val:
                waits[k] = (sem, val)
        for k, (sem, val) in waits.items():
            eng.waited[k] = val
        return list(waits.values())

    def _update(self, tok, reads, writes):
        for b in writes:
            b.w = tok
            b.r = []
        for b in reads:
            if b not in writes:
                b.r.append(tok)

    def op(self, eng, fn, reads=(), writes=()):
        deps = self._deps(reads, writes)
        waits = self._filter(eng, deps)
        eng.count += 1
        tok = (eng.sem, eng.count)
        eng.ops.append((waits, fn, (eng.sem, 1)))
        self._update(tok, reads, writes)
        return tok

    def dma(self, eng, fn, reads=(), writes=(), n=1):
        k = self.dma_rr
        self.dma_rr = (self.dma_rr + 1) % self.N_DMA_SEMS
        sem = self.dma_sems[k]
        deps = self._deps(reads, writes)
        if self.dma_last[k] is not None:
            deps.append(self.dma_last[k])
        waits = self._filter(eng, deps)
        self.dma_cnt[k] += 16 * n
        tok = (sem, self.dma_cnt[k])
        self.dma_last[k] = tok
        eng.ops.append((waits, (lambda e, fn=fn, sem=sem: fn(e, sem)), None))
        self._update(tok, reads, writes)
        return tok

    def wait_all(self, eng, toks):
        waits = self._filter(eng, list(toks))
        eng.ops.append((waits, None, None))

    def emit(self):
        nc = self.nc

        def run(eng, h):
            for (waits, fn, inc) in eng.ops:
                for (sem, val) in waits:
                    h.wait_ge(sem, val)
                if fn is None:
                    continue
                ins = fn(h)
                if inc is not None:
                    ins.then_inc(inc[0], inc[1])

        with nc.Block() as block:
            @block.tensor
            def _(h):
                run(self.pe, h)

            @block.scalar
            def _(h):
                run(self.act, h)

            @block.vector
            def _(h):
                run(self.dve, h)

            @block.gpsimd
            def _(h):
                run(self.pool, h)

            @block.sync
            def _(h):
                run(self.sp, h)


class Bld:
    def __init__(self, wtiles, n_wslots=4, n_ps=8):
        self.discover = wtiles is None
        if self.discover:
            wtiles = []
        self.nc = bass.Bass("TRN2", target_bir_lowering=False)
        self.P = Prog(self.nc)
        self.pools = {}
        self.rr = {}
        self.out_toks = []
        self.pool("ps", n_ps, [128, 512], F32, psum=True)
        if n_ps < 8:
            self.pool("pstat", 8 - n_ps, [128, 512], F32, psum=True)
        self.wtiles = wtiles
        self.NT = 4096 if self.discover else len(wtiles)
        self.n_wslots = n_wslots
        self.wst_d = self.din("wst", [max(self.NT, 1), 128, WT_ELEMS])
        self.par_d = self.din("par", [128, NPAR])
        self.pool("w", n_wslots, [128, WT_ELEMS], BF16)
        self.par = self.sb("par_sb", [128, NPAR], F32)
        self.b_par = Buf("par")
        self.ones = self.sb("ones", [128, 128], BF16)
        self.b_ones = Buf("ones")
        self.w_next_load = 0
        self.w_cur = 0
        self.pool("tf", 4, [128, 512], F32)
        self.pool("tb", 6, [128, 512], BF16)
        self.pool("sq", 4, [128, 512], BF16)
        self.pool("rstd", 4, [128, 512], F32)
        self.dma_in(self.par[:, :], self.par_d, [self.b_par])
        self.P.op(self.P.pool, lambda e: e.memset(self.ones[:, :], 1.0), writes=(self.b_ones,))
        self._ensure(n_wslots - 1)

    def din(self, name, shape, dt=F32):
        return self.nc.dram_tensor(name, list(shape), dt, kind="ExternalInput").ap()

    def dout(self, name, shape, dt=F32):
        return self.nc.dram_tensor(name, list(shape), dt, kind="ExternalOutput").ap()

    def sb(self, name, shape, dt):
        return self.nc.alloc_sbuf_tensor("sb_" + name, list(shape), dt)

    def pool(self, name, n, shape, dt, psum=False):
        if psum:
            lst = [(self.nc.alloc_psum_tensor(f"pp_{name}{i}", list(shape), dt), Buf(f"{name}{i}")) for i in range(n)]
        else:
            lst = [(self.nc.alloc_sbuf_tensor(f"pl_{name}{i}", list(shape), dt), Buf(f"{name}{i}")) for i in range(n)]
        self.pools[name] = lst
        self.rr[name] = 0

    def get(self, name):
        lst = self.pools[name]
        i = self.rr[name]
        self.rr[name] = (i + 1) % len(lst)
        return lst[i]

    def dma_in(self, dst, src, wbufs, eng=None, rbufs=()):
        eng = eng or self.P.sp
        return self.P.dma(eng, lambda e, s: e.dma_start(out=dst, in_=src).then_inc(s, 16), reads=rbufs, writes=wbufs)

    def dma_out(self, dst, src, rbufs):
        tok = self.P.dma(self.P.sp, lambda e, s: e.dma_start(out=dst, in_=src).then_inc(s, 16), reads=rbufs)
        self.out_toks.append(tok)
        return tok

    def finish(self):
        self.P.wait_all(self.P.sp, self.out_toks)
        self.P.emit()
        return self.nc

    def _ensure(self, upto):
        while self.w_next_load <= min(upto, self.NT - 1):
            i = self.w_next_load
            wt, bw = self.pools["w"][i % self.n_wslots]
            self.P.dma(self.P.pool, (lambda e, s, i=i, wt=wt: e.dma_start(out=wt[:, :], in_=self.wst_d[i]).then_inc(s, 16)),
                       writes=(bw,))
            self.w_next_load += 1

    def next_w(self, desc=None):
        i = self.w_cur
        if desc is not None:
            if self.discover:
                self.wtiles.append(desc)
            else:
                assert self.wtiles[i] == desc, (i, self.wtiles[i], desc)
        assert i < self.NT, "weight stream exhausted"
        self.w_cur += 1
        self._ensure(i + self.n_wslots - 1)
        return self.pools["w"][i % self.n_wslots]

    def mm(self, out_ap, bps, pairs, reads):
        n = len(pairs)

        def f(e):
            for i, (l, r) in enumerate(pairs):
                ins = e.matmul(out_ap, l, r, start=(i == 0), stop=(i == n - 1))
            return ins
        return self.P.op(self.P.pe, f, reads=reads, writes=(bps,))

    def act(self, out, in_, func, reads, writes, bias=None, scale=None):
        kw = {}
        if bias is not None:
            kw["bias"] = bias
        if scale is not None:
            kw["scale"] = scale
        return self.P.op(self.P.act, lambda e: e.activation(out=out, in_=in_, func=func, **kw), reads=reads, writes=writes)

    def tt(self, out, a, b_, op, reads, writes, eng=None):
        eng = eng or self.P.dve
        return self.P.op(eng, lambda e: e.tensor_tensor(out=out, in0=a, in1=b_, op=op), reads=reads, writes=writes)

    def ts(self, out, a, s1, s2, op0, op1, reads, writes, eng=None):
        eng = eng or self.P.dve
        if s2 is None:
            return self.P.op(eng, lambda e: e.tensor_scalar(out=out, in0=a, scalar1=s1, scalar2=None, op0=op0), reads=reads, writes=writes)
        return self.P.op(eng, lambda e: e.tensor_scalar(out=out, in0=a, scalar1=s1, scalar2=s2, op0=op0, op1=op1), reads=reads, writes=writes)

    def stt(self, out, a, sc, b_, op0, op1, reads, writes, eng=None):
        eng = eng or self.P.dve
        return self.P.op(eng, lambda e: e.scalar_tensor_tensor(out=out, in0=a, scalar=sc, in1=b_, op0=op0, op1=op1), reads=reads, writes=writes)

    def recip(self, out, in_, reads, writes):
        return self.P.op(self.P.dve, lambda e: e.reciprocal(out=out, in_=in_), reads=reads, writes=writes)

    def copy(self, out, in_, reads, writes, eng=None):
        eng = eng or self.P.act
        if eng is self.P.act:
            return self.P.op(eng, lambda e: e.copy(out=out, in_=in_), reads=reads, writes=writes)
        return self.P.op(eng, lambda e: e.tensor_copy(out=out, in_=in_), reads=reads, writes=writes)

    def pcol(self, c, n=1):
        return self.par[:, c:c + n]

    def rstd_from_ss(self, ss_ap, b_ss, inv_count, out_ap, out_b, ncols=512):
        tf, btf = self.get("tf")
        self.act(tf[:, :ncols], ss_ap, AF.Ln, reads=(b_ss, self.b_par), writes=(btf,), bias=self.pcol(255), scale=inv_count)
        self.act(out_ap, tf[:, :ncols], AF.Exp, reads=(btf,), writes=(out_b,), scale=-0.5)

    def recip_act(self, out_ap, in_ap, reads, writes, nrows=128, ncols=512):
        tf, btf = self.get("tf")
        self.act(tf[:nrows, :ncols], in_ap, AF.Ln, reads=reads, writes=(btf,))
        self.act(out_ap, tf[:nrows, :ncols], AF.Exp, reads=(btf,), writes=writes, scale=-1.0)


def wt_std(key, l, row0, nk, cols):
    return ("std", key, l, row0, nk, tuple(cols))


def pack_weights(tiles, W):
    out = np.zeros((max(len(tiles), 1), 128, WT_ELEMS), np.float32)
    for i, tl in enumerate(tiles):
        if tl[0] == "std":
            _, key, l, row0, nk, cols = tl
            w = W[key][l] if l is not None else W[key]
            blk = np.concatenate([w[row0:row0 + nk * 128, c0:c0 + n] for (c0, n) in cols], axis=1)
            ncol = blk.shape[1]
            out[i, :, :nk * ncol] = blk.reshape(nk, 128, ncol).transpose(1, 0, 2).reshape(128, nk * ncol)
        elif tl[0] == "gates":
            _, l = tl
            g = np.concatenate([W["a_gate_r_w"][l], W["a_gate_i_w"][l]], axis=0)
            out[i, :, :24 * 128] = g.transpose(1, 0, 2).reshape(128, 24 * 128)
    return out


def mlp_tiles(l):
    tl = []
    for q in range(4):
        for c in range(0, 2048, 256):
            tl.append(wt_std("mlp_w1", l, 0, 16, [(q * 2048 + c, 256)]))
        for c in range(0, 2048, 256):
            tl.append(wt_std("mlp_w2", l, q * 2048, 16, [(c, 256)]))
    return tl


def memkv_tiles(l):
    return [wt_std("mem_w_kv", l, 0, 16, [(c, 256)]) for c in range(0, 1024, 256)]


class Stats:
    def __init__(self, b):
        self.b = b
        self.ps = [b.pools["pstat"][t] for t in range(2)]
        self.n = [0, 0]

    def add(self, src_ap, b_src, t):
        b = self.b
        sq, bsq = b.get("sq")
        b.act(sq[:, :], src_ap, AF.Square, reads=(b_src,), writes=(bsq,))
        k = self.n[t]
        self.n[t] += 1
        ps, bps = self.ps[t]

        def f(e, k=k, sq=sq, ps=ps):
            return e.matmul(ps[:, :], b.ones[:, :], sq[:, :], start=(k == 0), stop=(k == KC - 1))
        b.P.op(b.P.pe, f, reads=(bsq, b.b_ones), writes=(bps,))

    def rstd(self):
        b = self.b
        assert self.n == [KC, KC]
        out = []
        for t in range(2):
            rs, brs = b.get("rstdp")
            b.rstd_from_ss(self.ps[t][0][:, :], self.ps[t][1], 1.0 / D, rs[:, :], brs)
            out.append((rs, brs))
        self.n = [0, 0]
        return out


def st_apply_norm(b, xs, b_xs, xn, b_xn, gcol, rstds):
    for t in range(2):
        tsl = slice(t * 512, (t + 1) * 512)
        rs, brs = rstds[t]
        for k in range(KC):
            b.stt(xn[:, k, tsl], xs[:, k, tsl], b.pcol(gcol + k), rs[:, :], ALU.mult, ALU.mult,
                  reads=(b_xs[k][t], brs, b.b_par), writes=(b_xn[k][t],))


def st_norm_resident(b, xs, b_xs, xn, b_xn, gcol):
    for t in range(2):
        tsl = slice(t * 512, (t + 1) * 512)
        ps, bps = b.get("ps")
        for k in range(KC):
            sq, bsq = b.get("sq")
            b.act(sq[:, :], xs[:, k, tsl], AF.Square, reads=(b_xs[k][t],), writes=(bsq,))

            def f(e, k=k, sq=sq, ps=ps):
                return e.matmul(ps[:, :], b.ones[:, :], sq[:, :], start=(k == 0), stop=(k == KC - 1))
            b.P.op(b.P.pe, f, reads=(bsq, b.b_ones), writes=(bps,))
        rs, brs = b.get("rstd")
        b.rstd_from_ss(ps[:, :], bps, 1.0 / D, rs[:, :], brs)
        for k in range(KC):
            b.stt(xn[:, k, tsl], xs[:, k, tsl], b.pcol(gcol + k), rs[:, :], ALU.mult, ALU.mult,
                  reads=(b_xs[k][t], brs, b.b_par), writes=(b_xn[k][t],))


def st_mlp(b, xs, b_xs, xn, b_xn, h1, b_h1, gcol, out_d=None, in_rstd=None, out_stats=None):
    if in_rstd is not None:
        st_apply_norm(b, xs, b_xs, xn, b_xn, gcol, in_rstd)
    else:
        st_norm_resident(b, xs, b_xs, xn, b_xn, gcol)
    for q in range(4):
        for cg in range(8):
            wt, bw = b.next_w()
            wv = wt[:, :].rearrange("p (k c) -> p k c", k=16)
            for half in range(2):
                fc = cg * 2 + half
                for t in range(2):
                    tsl = slice(t * 512, (t + 1) * 512)
                    ps, bps = b.get("ps")
                    b.mm(ps[:, :], bps, [(wv[:, k, half * 128:(half + 1) * 128], xn[:, k, tsl]) for k in range(KC)],
                         reads=[bw] + [b_xn[k][t] for k in range(KC)])
                    tf, btf = b.get("tf")
                    b.act(tf[:, :], ps[:, :], AF.Relu, reads=(bps,), writes=(btf,))
                    b.tt(h1[:, fc, tsl], tf[:, :], tf[:, :], ALU.mult, reads=(btf,), writes=(b_h1[fc][t],), eng=b.P.pool)
        for cg in range(8):
            wt, bw = b.next_w()
            wv = wt[:, :].rearrange("p (k c) -> p k c", k=16)
            for half in range(2):
                dc = cg * 2 + half
                for t in range(2):
                    tsl = slice(t * 512, (t + 1) * 512)
                    ps, bps = b.get("ps")
                    b.mm(ps[:, :], bps, [(wv[:, k, half * 128:(half + 1) * 128], h1[:, k, tsl]) for k in range(16)],
                         reads=[bw] + [b_h1[k][t] for k in range(16)])
                    b.tt(xs[:, dc, tsl], ps[:, :], xs[:, dc, tsl], ALU.add, reads=(bps, b_xs[dc][t]), writes=(b_xs[dc][t],))
                    if q == 3 and out_stats is not None:
                        out_stats.add(xs[:, dc, tsl], b_xs[dc][t], t)
                if q == 3 and out_d is not None:
                    b.dma_out(out_d[:, dc, :], xs[:, dc, :], rbufs=(b_xs[dc][0], b_xs[dc][1]))


def st_mem_rstd(b, memT, b_mem, rstd_m, b_rstdm):
    ps, bps = b.get("ps")
    for k in range(KC):
        sq, bsq = b.get("sq")
        b.act(sq[:, :256], memT[:, k, :], AF.Square, reads=(b_mem,), writes=(bsq,))

        def f(e, k=k, sq=sq, ps=ps):
            return e.matmul(ps[:, :256], b.ones[:, :], sq[:, :256], start=(k == 0), stop=(k == KC - 1))
        b.P.op(b.P.pe, f, reads=(bsq, b.b_ones), writes=(bps,))
    b.rstd_from_ss(ps[:, :256], bps, 1.0 / D, rstd_m[:, :], b_rstdm, ncols=256)


def st_memkv(b, memT, b_mem, rstd_m, b_rstdm, memn, b_memn, memk, b_memk, memv, b_memv, gcol_mem, col_kg):
    for k in range(KC):
        b.stt(memn[:, k, :], memT[:, k, :], b.pcol(gcol_mem + k), rstd_m[:, :], ALU.mult, ALU.mult,
              reads=(b_mem, b_rstdm, b.b_par), writes=(b_memn,))
    for hp in range(2):
        wt, bw = b.next_w()
        wv = wt[:, :].rearrange("p (k c) -> p k c", k=16)
        for hh in range(2):
            h = hp * 2 + hh
            hs = slice(hh * 128, hh * 128 + 128)
            ps, bps = b.get("ps")
            b.mm(ps[:, :256], bps, [(wv[:, k, hs], memn[:, k, :]) for k in range(KC)], reads=(bw, b_memn))
            sq, bsq = b.get("sq")
            b.act(sq[:, :256], ps[:, :256], AF.Square, reads=(bps,), writes=(bsq,))
            ps2, bps2 = b.get("ps")
            b.mm(ps2[:, :256], bps2, [(b.ones[:, :], sq[:, :256])], reads=(bsq, b.b_ones))
            rs, brs = b.get("rstd")
            b.rstd_from_ss(ps2[:, :256], bps2, 1.0 / 128, rs[:, :256], brs, ncols=256)
            b.stt(memk[:, h, :], ps[:, :256], b.pcol(col_kg), rs[:, :256], ALU.mult, ALU.mult,
                  reads=(bps, brs, b.b_par), writes=(b_memk,))
    for vh in range(2):
        wt, bw = b.next_w()
        wv = wt[:, :].rearrange("p (k c) -> p k c", k=16)
        for mc in range(2):
            ps, bps = b.get("ps")
            b.mm(ps[:, :256], bps, [(memn[:, k, mc * 128:(mc + 1) * 128], wv[:, k, :]) for k in range(KC)],
                 reads=(bw, b_memn))
            b.copy(memv[:, mc, vh * 256:(vh + 1) * 256], ps[:, :256], reads=(bps,), writes=(b_memv,))


class Pipe:
    def __init__(self):
        self.items = []

    def _advance(self, skip_new=False):
        for it in reversed(self.items):
            if it:
                it.pop(0)()
        self.items = [it for it in self.items if it]

    def push(self, stages):
        self.items.append(list(stages))
        self._advance()

    def flush(self):
        while self.items:
            self._advance()


def mem_attn_stages(b, mq_ps, b_mq, memk, b_memk, memv, b_memv, h, col_qg, out_ap, b_out, after=None):
    st = {}

    def A():
        sq, bsq = b.get("sq")
        b.act(sq[:, :], mq_ps, AF.Square, reads=(b_mq,), writes=(bsq,))
        ps2, bps2 = b.get("ps")
        b.mm(ps2[:, :], bps2, [(b.ones[:, :], sq[:, :])], reads=(bsq, b.b_ones))
        st.update(ps2=ps2, bps2=bps2)

    def B():
        rs, brs = b.get("rstd")
        b.rstd_from_ss(st["ps2"][:, :], st["bps2"], 1.0 / 128, rs[:, :], brs)
        qn, bqn = b.get("tb")
        b.stt(qn[:, :], mq_ps, b.pcol(col_qg), rs[:, :], ALU.mult, ALU.mult, reads=(b_mq, brs, b.b_par), writes=(bqn,))
        pts = []
        for mc in range(2):
            ps3, bps3 = b.get("ps")
            b.mm(ps3[:, :], bps3, [(memk[:, h, mc * 128:(mc + 1) * 128], qn[:, :])], reads=(b_memk, bqn))
            pt, bpt = b.get("tb")
            b.act(pt[:, :], ps3[:, :], AF.Exp, reads=(bps3,), writes=(bpt,), scale=SCALE)
            pts.append((pt, bpt))
        st.update(pts=pts)

    def C():
        pts = st["pts"]
        pn, bpn = b.get("ps")
        b.mm(pn[:, :], bpn, [(memv[:, mc, h * 128:(h + 1) * 128], pts[mc][0][:, :]) for mc in range(2)],
             reads=(b_memv, pts[0][1], pts[1][1]))
        pd, bpd = b.get("ps")
        b.mm(pd[:, :], bpd, [(b.ones[:, :], pts[mc][0][:, :]) for mc in range(2)], reads=(b.b_ones, pts[0][1], pts[1][1]))
        rd, brd = b.get("tf")
        b.recip_act(rd[:, :], pd[:, :], reads=(bpd,), writes=(brd,))
        b.tt(out_ap, pn[:, :], rd[:, :], ALU.mult, reads=(bpn, brd), writes=(b_out,))
        if after is not None:
            after()
    return [A, B, C]


class MemState:
    def __init__(self, b, alias=None):
        if alias is None:
            self.memT = b.sb("memT", [128, KC, 256], F32)
            self.memn = b.sb("memn", [128, KC, 256], BF16)
        else:
            self.memT = alias[:, 0:8, :].bitcast(F32).rearrange("p a (b c) -> p (a b) c", c=256)
            self.memn = alias[:, 8:12, :].rearrange("p a (b c) -> p (a b) c", c=256)
        self.b_mem = Buf()
        self.b_memn = Buf()
        self.memk = b.sb("memk", [128, 4, 256], BF16)
        self.b_memk = Buf()
        self.memv = b.sb("memv", [128, 2, 512], BF16)
        self.b_memv = Buf()
        self.rstd_m = b.sb("rstd_m", [128, 256], F32)
        self.b_rstdm = Buf()


def tiles_A1(l):
    tl = list(memkv_tiles(l))
    tl += [wt_std("a_w_in", l, 0, 16, [(3072 + c, 256)]) for c in (0, 256)]
    tl.append(("gates", l))
    for n in range(12):
        tl.append(wt_std("a_w_in", l, 0, 16, [(n * 128, 128), (1536 + n * 128, 128)]))
    return tl


def build_A1(l, from_xn=False):
    b = Bld(tiles_A1(l))
    nc, P = b.nc, b.P
    if from_xn:
        xn_d = b.din("xn", [128, KC, T], BF16)
        xnh_d = b.din("xnh", [128, KC, 4], BF16)
    else:
        xT_d = b.din("xT", [128, KC, T])
        xh_d = b.din("xh", [128, KC, 4])
    memT_d = b.din("memT", [128, KC, 256])
    cat_d = b.dout("cat", [128, 16, T], BF16)
    q_d = b.dout("qq", [128, 12, T], BF16)
    car_d = b.dout("carry", [128, 24])

    b.pool("xt", 4, [128, 512], F32)
    b.pool("ub", 3, [128, 4 + T], F32)
    b.pool("gb", 3, [128, T], F32)
    b.pool("xc", 2, [128, T], F32)
    b.pool("rb", 2, [128, T], F32)
    b.pool("ib", 2, [128, T], F32)
    b.pool("s3", 5, [128, T], F32)
    xn = b.sb("xn", [128, KC, T], BF16)
    b_xn = [[Buf() for t in range(2)] for k in range(KC)]
    xnh = b.sb("xnh", [128, KC, 4], BF16)
    b_xnh = Buf()
    xh = b.sb("xh", [128, KC, 4], F32)
    b_xh = Buf()
    b.pool("cb", 6, [128, T], BF16)
    M = MemState(b, alias=xn)
    rstd_x = [b.sb(f"rstd_x{t}", [128, 512], F32) for t in range(2)]
    b_rstdx = [Buf(), Buf()]
    rstd_h = b.sb("rstd_h", [128, 4], F32)
    b_rstdh = Buf()
    carry = b.sb("carry", [128, 24], F32)
    b_carry = Buf()
    gw = b.sb("gw", [128, 24, 128], BF16)
    b_gw = Buf()
    nsp = b.sb("nsp", [128, 12], F32)
    b_nsp = Buf()
    zeros = b.sb("zeros", [128, T], F32)
    b_zeros = Buf()
    sml = [b.sb(f"sml{i}", [128, 12], F32) for i in range(6)]
    b_sml = [Buf() for i in range(6)]

    b.dma_in(M.memT[:, :, :], memT_d, [M.b_mem])
    if not from_xn:
        b.dma_in(xh[:, :, :], xh_d, [b_xh])
    P.op(P.pool, lambda e: e.memset(zeros[:, :], 0.0), writes=(b_zeros,))

    st_mem_rstd(b, M.memT, M.b_mem, M.rstd_m, M.b_rstdm)
    st_memkv(b, M.memT, M.b_mem, M.rstd_m, M.b_rstdm, M.memn, M.b_memn, M.memk, M.b_memk, M.memv, M.b_memv, 16, 33)

    if from_xn:
        b.dma_in(xnh[:, :, :], xnh_d, [b_xnh])
        for k in list(range(12, KC)) + list(range(12)):
            extra = (M.b_mem,) if k < 8 else ((M.b_memn,) if k < 12 else ())
            b.dma_in(xn[:, k, :], xn_d[:, k, :], [b_xn[k][0], b_xn[k][1]] + list(extra))
    else:
        pss = [b.get("ps"), b.get("ps")]
        for k in range(KC):
            for t in range(2):
                xt, bxt = b.get("xt")
                b.dma_in(xt[:, :], xT_d[:, k, t * 512:(t + 1) * 512], [bxt])
                sq, bsq = b.get("sq")
                b.act(sq[:, :], xt[:, :], AF.Square, reads=(bxt,), writes=(bsq,))

                def f(e, k=k, sq=sq, ps=pss[t][0]):
                    return e.matmul(ps[:, :], b.ones[:, :], sq[:, :], start=(k == 0), stop=(k == KC - 1))
                P.op(P.pe, f, reads=(bsq, b.b_ones), writes=(pss[t][1],))
        for t in range(2):
            b.rstd_from_ss(pss[t][0][:, :], pss[t][1], 1.0 / D, rstd_x[t][:, :], b_rstdx[t])
        psh, bpsh = b.get("ps")
        for k in range(KC):
            sq, bsq = b.get("sq")
            b.act(sq[:, :4], xh[:, k, :], AF.Square, reads=(b_xh,), writes=(bsq,))

            def f(e, k=k, sq=sq, psh=psh):
                return e.matmul(psh[:, :4], b.ones[:, :], sq[:, :4], start=(k == 0), stop=(k == KC - 1))
            P.op(P.pe, f, reads=(bsq, b.b_ones), writes=(bpsh,))
        b.rstd_from_ss(psh[:, :4], bpsh, 1.0 / D, rstd_h[:, :], b_rstdh, ncols=4)
        for k in range(KC):
            b.stt(xnh[:, k, :], xh[:, k, :], b.pcol(k), rstd_h[:, :], ALU.mult, ALU.mult,
                  reads=(b_xh, b_rstdh, b.b_par), writes=(b_xnh,))
        for k in range(KC):
            for t in range(2):
                tsl = slice(t * 512, (t + 1) * 512)
                xt, bxt = b.get("xt")
                b.dma_in(xt[:, :], xT_d[:, k, tsl], [bxt])
                extra = (M.b_mem,) if k < 8 else ((M.b_memn,) if k < 12 else ())
                b.stt(xn[:, k, tsl], xt[:, :], b.pcol(k), rstd_x[t][:, :], ALU.mult, ALU.mult,
                      reads=(bxt, b_rstdx[t], b.b_par), writes=(b_xn[k][t],) + extra)

    pipe = Pipe()
    for hp in range(2):
        wt, bw = b.next_w()
        wv = wt[:, :].rearrange("p (k c) -> p k c", k=16)
        for hh in range(2):
            h = hp * 2 + hh
            cbt, bcb = b.get("cb")
            for t in range(2):
                tsl = slice(t * 512, (t + 1) * 512)
                ps, bps = b.get("ps")
                b.mm(ps[:, :], bps, [(wv[:, k, hh * 128:(hh + 1) * 128], xn[:, k, tsl]) for k in range(KC)],
                     reads=[bw] + [b_xn[k][t] for k in range(KC)])
                after = None
                if t == 1:
                    after = (lambda h=h, cbt=cbt, bcb=bcb: b.dma_out(cat_d[:, 12 + h, :], cbt[:, :], rbufs=(bcb,)))
                pipe.push(mem_attn_stages(b, ps[:, :], bps, M.memk, M.b_memk, M.memv, M.b_memv, h, 32, cbt[:, tsl], bcb, after=after))
    pipe.flush()

    wt, bw = b.next_w()
    b.copy(gw[:, :, :], wt[:, :24 * 128].rearrange("p (g d) -> p g d", g=24), reads=(bw,), writes=(b_gw,), eng=P.pool)

    lam = b.par[:, 124:136]
    s0, s1, s2, s3, s4, s5 = sml
    B0, B1, B2, B3, B4, B5 = b_sml
    b.ts(s0[:, :], lam, -1.0, None, ALU.mult, None, reads=(b.b_par,), writes=(B0,))
    b.tt(s0[:, :], s0[:, :], lam, ALU.max, reads=(B0, b.b_par), writes=(B0,))
    b.act(s1[:, :], s0[:, :], AF.Exp, reads=(B0,), writes=(B1,), scale=-1.0)
    b.ts(s2[:, :], s1[:, :], 2.0, None, ALU.add, None, reads=(B1,), writes=(B2,))
    b.recip(s3[:, :], s2[:, :], reads=(B2,), writes=(B3,))
    b.tt(s2[:, :], s1[:, :], s3[:, :], ALU.mult, reads=(B1, B3), writes=(B2,))
    b.tt(s3[:, :], s2[:, :], s2[:, :], ALU.mult, reads=(B2,), writes=(B3,))
    b.ts(s4[:, :], s3[:, :], 1.0 / 11, 1.0 / 9, ALU.mult, ALU.add, reads=(B3,), writes=(B4,))
    for cst in (1.0 / 7, 1.0 / 5, 1.0 / 3, 1.0):
        b.tt(s4[:, :], s4[:, :], s3[:, :], ALU.mult, reads=(B4, B3), writes=(B4,))
        b.ts(s4[:, :], s4[:, :], cst, None, ALU.add, None, reads=(B4,), writes=(B4,))
    b.tt(s4[:, :], s4[:, :], s2[:, :], ALU.mult, reads=(B4, B2), writes=(B4,))
    b.ts(s5[:, :], lam, -1.0, 0.0, ALU.mult, ALU.max, reads=(b.b_par,), writes=(B5,))
    b.stt(s5[:, :], s4[:, :], 2.0, s5[:, :], ALU.mult, ALU.add, reads=(B4, B5), writes=(B5,))
    b.ts(nsp[:, :], s5[:, :], -8.0, None, ALU.mult, None, reads=(B5,), writes=(b_nsp,))

    stt_ = {}

    def S1(n):
        wt, bw = b.next_w()
        wv = wt[:, :].rearrange("p (k c) -> p k c", k=16)
        ub, bub = b.get("ub")
        gb, bgb = b.get("gb")
        psh, bpsh = b.get("ps")
        b.mm(psh[:, :4], bpsh, [(wv[:, k, 0:128], xnh[:, k, :]) for k in range(KC)], reads=(bw, b_xnh))
        b.copy(ub[:, 0:4], psh[:, :4], reads=(bpsh,), writes=(bub,))
        for t in range(2):
            tsl = slice(t * 512, (t + 1) * 512)
            ps, bps = b.get("ps")
            b.mm(ps[:, :], bps, [(wv[:, k, 0:128], xn[:, k, tsl]) for k in range(KC)],
                 reads=[bw] + [b_xn[k][t] for k in range(KC)])
            b.copy(ub[:, 4 + t * 512:4 + (t + 1) * 512], ps[:, :], reads=(bps,), writes=(bub,))
            pg, bpg = b.get("ps")
            b.mm(pg[:, :], bpg, [(wv[:, k, 128:256], xn[:, k, tsl]) for k in range(KC)],
                 reads=[bw] + [b_xn[k][t] for k in range(KC)])
            b.act(gb[:, tsl], pg[:, :], AF.Gelu_apprx_tanh, reads=(bpg,), writes=(bgb,))
        stt_[n] = dict(ub=ub, bub=bub, gb=gb, bgb=bgb)

    def S2(n):
        d_ = stt_[n]
        ub, bub = d_["ub"], d_["bub"]
        xc, bxc = b.get("xc")
        cw = 40 + n * 4
        b.ts(xc[:, :], ub[:, 1:1 + T], b.pcol(cw), b.pcol(88 + n), ALU.mult, ALU.add, reads=(bub, b.b_par), writes=(bxc,), eng=P.pool)
        for j in range(1, 4):
            b.stt(xc[:, :], ub[:, 1 + j:1 + j + T], b.pcol(cw + j), xc[:, :], ALU.mult, ALU.add,
                  reads=(bub, b.b_par, bxc), writes=(bxc,))
        xcb = [b.get("tb"), b.get("tb")]
        for t in range(2):
            b.copy(xcb[t][0][:, :], xc[:, t * 512:(t + 1) * 512], reads=(bxc,), writes=(xcb[t][1],))
        rb, brb = b.get("rb")
        ib, bib = b.get("ib")
        for t in range(2):
            tsl = slice(t * 512, (t + 1) * 512)
            pr, bpr = b.get("ps")
            b.mm(pr[:, :], bpr, [(gw[:, n, :], xcb[t][0][:, :])], reads=(b_gw, xcb[t][1]))
            b.act(rb[:, tsl], pr[:, :], AF.Sigmoid, reads=(bpr, b.b_par), writes=(brb,), bias=b.pcol(100 + n))
            pi_, bpi = b.get("ps")
            b.mm(pi_[:, :], bpi, [(gw[:, 12 + n, :], xcb[t][0][:, :])], reads=(b_gw, xcb[t][1]))
            b.act(ib[:, tsl], pi_[:, :], AF.Sigmoid, reads=(bpi, b.b_par), writes=(bib,), bias=b.pcol(112 + n))
        d_.update(xc=xc, bxc=bxc, rb=rb, brb=brb, ib=ib, bib=bib)

    def S3(n):
        d_ = stt_.pop(n)
        gb, bgb, xc, bxc, ab, bab, ib, bib = d_["gb"], d_["bgb"], d_["xc"], d_["bxc"], d_["rb"], d_["brb"], d_["ib"], d_["bib"]
        b.act(ab[:, :], ab[:, :], AF.Exp, reads=(bab, b_nsp), writes=(bab,), scale=nsp[:, n:n + 1])
        bb_, bbb = b.get("s3")
        b.tt(bb_[:, :], ab[:, :], ab[:, :], ALU.mult, reads=(bab,), writes=(bbb,), eng=P.pool)
        b.ts(bb_[:, :], bb_[:, :], -1.0, 1.0, ALU.mult, ALU.add, reads=(bbb,), writes=(bbb,), eng=P.pool)
        b.act(bb_[:, :], bb_[:, :], AF.Sqrt, reads=(bbb,), writes=(bbb,))
        b.tt(ib[:, :], ib[:, :], xc[:, :], ALU.mult, reads=(bib, bxc), writes=(bib,), eng=P.pool)
        b.tt(bb_[:, :], bb_[:, :], ib[:, :], ALU.mult, reads=(bbb, bib), writes=(bbb,))
        hb, bhb = b.get("s3")
        P.op(P.dve, lambda e, hb=hb, ab=ab, bb_=bb_: e.tensor_tensor_scan(out=hb[:, :], data0=ab[:, :], data1=bb_[:, :], initial=0.0,
                                                                            op0=ALU.mult, op1=ALU.add),
             reads=(bab, bbb), writes=(bhb,))
        Ab, bAb = b.get("s3")
        P.op(P.dve, lambda e, Ab=Ab, ab=ab: e.tensor_tensor_scan(out=Ab[:, :], data0=ab[:, :], data1=zeros[:, :], initial=1.0,
                                                                  op0=ALU.mult, op1=ALU.add),
             reads=(bab, b_zeros), writes=(bAb,))
        b.copy(carry[:, n:n + 1], hb[:, T - 1:T], reads=(bhb,), writes=(b_carry,))
        b.copy(carry[:, 12 + n:13 + n], Ab[:, T - 1:T], reads=(bAb,), writes=(b_carry,))
        cbt, bcb = b.get("cb")
        b.tt(cbt[:, :], hb[:, :], gb[:, :], ALU.mult, reads=(bhb, bgb), writes=(bcb,))
        b.dma_out(cat_d[:, n, :], cbt[:, :], rbufs=(bcb,))
        cbq, bcq = b.get("cb")
        b.tt(cbq[:, :], Ab[:, :], gb[:, :], ALU.mult, reads=(bAb, bgb), writes=(bcq,), eng=P.pool)
        b.dma_out(q_d[:, n, :], cbq[:, :], rbufs=(bcq,))

    for s_ in range(12 + 2):
        if s_ < 12:
            S1(s_)
        if 0 <= s_ - 1 < 12:
            S2(s_ - 1)
        if 0 <= s_ - 2 < 12:
            S3(s_ - 2)

    b.dma_out(car_d, carry[:, :], rbufs=(b_carry,))
    return b.finish(), b.wtiles


def tiles_A2(l, with_kv):
    tl = [wt_std("a_w_out", l, 0, 16, [(c, 256)]) for c in range(0, 2048, 256)]
    tl += mlp_tiles(l)
    if with_kv:
        tl += [wt_std("kv_w", None, 0, 16, [(c, 256)]) for c in range(0, 3072, 256)]
    return tl


def build_A2(l, with_kv, with_next=True):
    b = Bld(tiles_A2(l, with_kv), n_ps=6)
    b.pool("rstdp", 4, [128, 512], F32)
    stats = Stats(b)
    nc, P = b.nc, b.P
    xT_d = b.din("xT", [128, KC, T])
    cat_d = b.din("cat", [128, 16, T], BF16)
    q_d = b.din("qq", [128, 12, T], BF16)
    car_d = b.din("carr", [128, 8, 24])
    sel_d = b.din("sel", [128, 8])
    yT_d = b.dout("yT", [128, KC, T])
    xs = b.sb("xs", [128, KC, T], F32)
    b_xs = [[Buf() for t in range(2)] for k in range(KC)]
    cat = b.sb("cat", [128, 16, T], BF16)
    b_cat = [[Buf() for t in range(2)] for k in range(16)]
    h1 = b.sb("h1", [128, 16, T], BF16)
    b_h1 = [[Buf() for t in range(2)] for k in range(16)]
    carr = b.sb("carr", [128, 8, 24], F32)
    b_carr = Buf()
    sel = b.sb("sel", [128, 8], F32)
    b_sel = Buf()
    cst = [b.sb(f"cst{i}", [128, 12], F32) for i in range(2)]
    b_cst = [Buf(), Buf()]
    hin = b.sb("hin", [128, 12], F32)
    b_hin = Buf()
    tmp12 = b.sb("tmp12", [128, 12], F32)
    b_tmp12 = Buf()

    b.dma_in(carr[:, :, :], car_d, [b_carr])
    b.dma_in(sel[:, :], sel_d, [b_sel])
    for k in range(16):
        if k < 12:
            b.dma_in(h1[:, k, :], q_d[:, k, :], [b_h1[k][0], b_h1[k][1]])
        b.dma_in(cat[:, k, :], cat_d[:, k, :], [b_cat[k][0], b_cat[k][1]])
    for k in range(KC):
        b.dma_in(xs[:, k, :], xT_d[:, k, :], [b_xs[k][0], b_xs[k][1]])

    P.op(P.pool, lambda e: e.memset(cst[0][:, :], 0.0), writes=(b_cst[0],))
    P.op(P.pool, lambda e: e.memset(hin[:, :], 0.0), writes=(b_hin,))
    for r in range(8):
        cur, bcur = cst[r % 2], b_cst[r % 2]
        nx, bnx = cst[(r + 1) % 2], b_cst[(r + 1) % 2]
        b.stt(hin[:, :], cur[:, :], sel[:, r:r + 1], hin[:, :], ALU.mult, ALU.add, reads=(bcur, b_sel, b_hin), writes=(b_hin,))
        if r < 7:
            b.tt(tmp12[:, :], carr[:, r, 12:24], cur[:, :], ALU.mult, reads=(b_carr, bcur), writes=(b_tmp12,))
            b.tt(nx[:, :], tmp12[:, :], carr[:, r, 0:12], ALU.add, reads=(b_tmp12, b_carr), writes=(bnx,))
    for n in range(12):
        for t in range(2):
            tsl = slice(t * 512, (t + 1) * 512)
            b.stt(cat[:, n, tsl], h1[:, n, tsl], hin[:, n:n + 1], cat[:, n, tsl], ALU.mult, ALU.add,
                  reads=(b_h1[n][t], b_hin, b_cat[n][t]), writes=(b_cat[n][t],))
    for cg in range(8):
        wt, bw = b.next_w()
        wv = wt[:, :].rearrange("p (k c) -> p k c", k=16)
        for half in range(2):
            dc = cg * 2 + half
            for t in range(2):
                tsl = slice(t * 512, (t + 1) * 512)
                ps, bps = b.get("ps")
                b.mm(ps[:, :], bps, [(wv[:, k, half * 128:(half + 1) * 128], cat[:, k, tsl]) for k in range(16)],
                     reads=[bw] + [b_cat[k][t] for k in range(16)])
                b.tt(xs[:, dc, tsl], ps[:, :], xs[:, dc, tsl], ALU.add, reads=(bps, b_xs[dc][t]), writes=(b_xs[dc][t],))
                stats.add(xs[:, dc, tsl], b_xs[dc][t], t)
    need_out = with_next or with_kv
    st_mlp(b, xs, b_xs, cat, b_cat, h1, b_h1, 0, out_d=yT_d, in_rstd=stats.rstd(), out_stats=(stats if need_out else None))
    if need_out:
        rs_out = stats.rstd()
    if with_next:
        xnn_d = b.dout("xnn", [128, KC, T], BF16)
        st_apply_norm(b, xs, b_xs, cat, b_cat, 40, rs_out)
        for k in range(KC):
            b.dma_out(xnn_d[:, k, :], cat[:, k, :], rbufs=(b_cat[k][0], b_cat[k][1]))
    if with_kv:
        kT_d = b.dout("kT", [128, 12, T], BF16)
        vT_d = b.dout("vT", [128, 12, T], BF16)
        st_apply_norm(b, xs, b_xs, cat, b_cat, 16, rs_out)
        kpipe = Pipe()
        for cg in range(12):
            wt, bw = b.next_w()
            wv = wt[:, :].rearrange("p (k c) -> p k c", k=16)
            for half in range(2):
                hc = cg * 2 + half
                for t in range(2):
                    tsl = slice(t * 512, (t + 1) * 512)
                    ps, bps = b.get("ps")
                    b.mm(ps[:, :], bps, [(wv[:, k, half * 128:(half + 1) * 128], cat[:, k, tsl]) for k in range(16)],
                         reads=[bw] + [b_cat[k][t] for k in range(16)])
                    if hc < 12:
                        stq = {}

                        def KA(ps=ps, bps=bps, stq=stq):
                            sq, bsq = b.get("sq")
                            b.act(sq[:, :], ps[:, :], AF.Square, reads=(bps,), writes=(bsq,))
                            ps2, bps2 = b.get("ps")
                            b.mm(ps2[:, :], bps2, [(b.ones[:, :], sq[:, :])], reads=(bsq, b.b_ones))
                            stq.update(ps2=ps2, bps2=bps2)

                        def KB(ps=ps, bps=bps, stq=stq, hc=hc, t=t, tsl=tsl):
                            rs, brs = b.get("rstd")
                            b.rstd_from_ss(stq["ps2"][:, :], stq["bps2"], 1.0 / 128, rs[:, :], brs)
                            b.stt(h1[:, hc, tsl], ps[:, :], b.pcol(32 + hc // 4), rs[:, :], ALU.mult, ALU.mult,
                                  reads=(bps, brs, b.b_par), writes=(b_h1[hc][t],))
                        kpipe.push([KA, KB])
                    else:
                        b.copy(h1[:, hc - 12, tsl], ps[:, :], reads=(bps,), writes=(b_h1[hc - 12][t],))
            if cg == 5:
                kpipe.flush()
                for k in range(12):
                    b.dma_out(kT_d[:, k, :], h1[:, k, :], rbufs=(b_h1[k][0], b_h1[k][1]))
        for k in range(12):
            b.dma_out(vT_d[:, k, :], h1[:, k, :], rbufs=(b_h1[k][0], b_h1[k][1]))
    return b.finish(), b.wtiles


def fm(x):
    n, f = x.shape
    return np.ascontiguousarray(x.T.reshape(f // 128, 128, n).transpose(1, 0, 2))


def unfm(xT):
    p, k, n = xT.shape
    return np.ascontiguousarray(xT.transpose(1, 0, 2).reshape(k * p, n).T)


def colk(v):
    return np.ascontiguousarray(np.asarray(v, np.float32).reshape(-1, 128).T)


_CACHE = {}
_TIMES = []


def _run(nc, ins, tag=""):
    import os
    if os.environ.get("KTRACE"):
        res = run_bass_kernel_spmd(nc, ins, core_ids=list(range(NCORES)), trace=True)
        _TIMES.append((tag, res.exec_time_ns))
        print("KTRACE", tag, res.exec_time_ns, flush=True)
    else:
        res = run_bass_kernel_spmd(nc, ins, core_ids=list(range(NCORES)))
    return res.results


def _launch(key, builder, in_maps):
    if key not in _CACHE:
        _CACHE[key] = builder()
    nc, tiles = _CACHE[key]
    res = run_bass_kernel_spmd(nc, in_maps, core_ids=list(range(NCORES)))
    return res.results


def run_A_layer(l, xT, memT, W, with_kv, xn_in=None, next_g=None):
    f32 = np.float32
    par = np.zeros((128, NPAR), f32)
    par[:, 0:16] = colk(W["norm_mix_g"][l])
    par[:, 16:32] = colk(W["mem_norm_g"][l])
    par[:, 32] = W["mem_q_norm_g"][l]
    par[:, 33] = W["mem_k_norm_g"][l]
    cw = W["a_conv_w"][l]
    for n in range(12):
        for j in range(4):
            par[:, 40 + n * 4 + j] = cw[j, n * 128:(n + 1) * 128]
    par[:, 88:100] = colk(W["a_conv_b"][l])
    par[:, 100:112] = np.asarray(W["a_gate_r_b"][l], f32).T
    par[:, 112:124] = np.asarray(W["a_gate_i_b"][l], f32).T
    par[:, 124:136] = colk(W["a_lambda"][l])
    par[:, 255] = EPS
    nc1 = _CACHE.get(("A1", l))
    if nc1 is None:
        nc1 = _CACHE[("A1", l)] = build_A1(l, from_xn=(xn_in is not None))
    wst = pack_weights(nc1[1], W)
    ins = []
    for c in range(NCORES):
        if xn_in is not None:
            xnh = np.zeros((128, KC, 4), ml_dtypes.bfloat16)
            if c > 0:
                xnh[:, :, :] = np.asarray(xn_in[c - 1])[:, :, T - 4:]
            ins.append({"xn": xn_in[c], "xnh": xnh, "memT": memT, "wst": wst, "par": par})
        else:
            xh = np.zeros((128, KC, 4), f32)
            if c > 0:
                xh[:, :, :] = xT[c - 1][:, :, T - 4:]
            ins.append({"xT": xT[c], "xh": xh, "memT": memT, "wst": wst, "par": par})
    r1 = _run(nc1[0], ins, f"A1_{l}")
    par2 = np.zeros((128, NPAR), f32)
    par2[:, 0:16] = colk(W["norm_mlp_g"][l])
    if with_kv:
        par2[:, 16:32] = colk(W["kv_norm_g"])
        par2[:, 32:35] = np.asarray(W["k_norm_g"], f32).T
    par2[:, 255] = EPS
    if next_g is not None:
        par2[:, 40:56] = colk(next_g)
    nc2 = _CACHE.get(("A2", l))
    if nc2 is None:
        nc2 = _CACHE[("A2", l)] = build_A2(l, with_kv, with_next=(next_g is not None))
    wst2 = pack_weights(nc2[1], W)
    carr = np.ascontiguousarray(np.stack([r1[c]["carry"] for c in range(NCORES)], axis=1))
    ins2 = []
    for c in range(NCORES):
        sel = np.zeros((128, 8), f32)
        sel[:, c] = 1.0
        ins2.append({"xT": xT[c], "cat": r1[c]["cat"], "qq": r1[c]["qq"], "carr": carr, "sel": sel, "wst": wst2, "par": par2})
    r2 = _run(nc2[0], ins2, f"A2_{l}")
    out = [r2[c]["yT"] for c in range(NCORES)]
    xnn = [r2[c]["xnn"] for c in range(NCORES)] if next_g is not None else None
    if with_kv:
        return out, [r2[c]["kT"] for c in range(NCORES)], [r2[c]["vT"] for c in range(NCORES)], r1, xnn
    return out, None, None, r1, xnn


LG = [T // d for d in DIL]
NQ = [min(128, lg) for lg in LG]
NKC = [128 + lg for lg in LG]
NBC = [(n + 127) // 128 for n in NKC]
NK = [d * n for d, n in zip(DIL, NKC)]
NB = [d * n for d, n in zip(DIL, NBC)]


def tiles_B1(l):
    tl = list(memkv_tiles(l))
    tl += [wt_std("b_w_q", l - 2, 0, 16, [(c, 256)]) for c in range(0, 2048, 256)]
    return tl


def build_B1(l, from_xn=True):
    b = Bld(tiles_B1(l))
    nc, P = b.nc, b.P
    if from_xn:
        xn_d = b.din("xn", [128, KC, T], BF16)
    else:
        xT_d = b.din("xT", [128, KC, T])
    memT_d = b.din("memT", [128, KC, 256])
    kt_d = [b.din(f"kt{g}", [128, 4, NK[g]], BF16) for g in range(3)]
    vv_d = [b.din(f"vv{g}", [128, 4, NB[g], 128], BF16) for g in range(3)]
    oh_d = b.din("oh", [33, 6, 256])
    jm_d = b.din("jm", [128, 128])
    relb_d = b.din("relb", [33, 12])
    kval_d = b.din("kval", [128, 3])
    cat_d = b.dout("cat", [128, 8, T], BF16)
    vec_d = nc.dram_tensor("vecd", [6, 4, 256], F32).ap()

    b.pool("xt", 4, [128, 512], F32)
    b.pool("cb", 4, [128, T], BF16)
    b.pool("pp", 4, [128, 512], BF16)
    b.pool("kt", 2, [128, max(NK)], BF16)
    b.pool("vv", 2, [128, max(NB), 128], BF16)
    xn = b.sb("xn", [128, KC, T], BF16)
    b_xn = [[Buf() for t in range(2)] for k in range(KC)]
    qn = b.sb("qn", [128, 12, T], BF16)
    b_qn = [Buf() for k in range(12)]
    M = MemState(b, alias=qn)
    rstd_x = [b.sb(f"rstd_x{t}", [128, 512], F32) for t in range(2)]
    b_rstdx = [Buf(), Buf()]
    accN = b.sb("accN", [128, T], F32)
    accD = b.sb("accD", [128, T], F32)
    b_acc = Buf()
    relb = b.sb("relb", [33, 12], F32)
    b_relb = Buf()
    jm = b.sb("jm", [128, 128], F32)
    b_jm = Buf()
    kval = b.sb("kval", [128, 3], F32)
    b_kval = Buf()
    b.pool("vec", 2, [4, 256], F32)
    b.pool("hk", 2, [128, 512], F32)
    masks = [[b.sb(f"mask{g}_{ty}", [128, 4, 128], F32) for ty in range(3)] for g in range(3)]
    b_masks = [[Buf() for ty in range(3)] for g in range(3)]

    b.dma_in(M.memT, memT_d, [M.b_mem])
    b.dma_in(relb[:, :], relb_d, [b_relb])
    b.dma_in(jm[:, :], jm_d, [b_jm])
    b.dma_in(kval[:, :], kval_d, [b_kval])

    mask_state = {}

    def mask_A(g, ty):
        oh, boh = b.get("tf")
        b.dma_in(oh[:33, :256], oh_d[:, g * 2 + ty, :], [boh])
        ps, bps = b.get("ps")
        b.mm(ps[:4, :256], bps, [(relb[:33, g * 4:(g + 1) * 4], oh[:33, :256])], reads=(b_relb, boh))
        vec, b_vec = b.get("vec")
        b.act(vec[:, :], ps[:4, :256], AF.Exp, reads=(bps,), writes=(b_vec,))
        b_vd = Buf()
        P.dma(P.sp, (lambda e, s, g=g, ty=ty, vec=vec: e.dma_start(out=vec_d[g * 2 + ty], in_=vec[:, :]).then_inc(s, 16)),
              reads=(b_vec,), writes=(b_vd,))
        hk, bhk = b.get("hk")
        src = bass.AP(tensor=vec_d.tensor, offset=(g * 2 + ty) * 1024, ap=[[1, 128], [256, 4], [1, 128]])
        b.dma_in(hk[:, :].rearrange("p (h q) -> p h q", h=4), src, [bhk], rbufs=(b_vd,))
        mask_state[(g, ty)] = (hk, bhk)

    def mask_B(g, ty):
        hk, bhk = mask_state[(g, ty)]
        ps2, bps2 = b.get("ps")
        b.mm(ps2[:, :], bps2, [(jm[:, :], hk[:, :])], reads=(b_jm, bhk))
        b.copy(masks[g][ty][:, :, :], ps2[:, :].rearrange("p (h q) -> p h q", h=4), reads=(bps2,), writes=(b_masks[g][ty],))
        if ty == 1:
            b.ts(masks[g][2][:, :, :], masks[g][1][:, :, :], kval[:, g:g + 1], None, ALU.mult, None,
                 reads=(b_masks[g][1], b_kval), writes=(b_masks[g][2],))

    if from_xn:
        for k in range(KC):
            b.dma_in(xn[:, k, :], xn_d[:, k, :], [b_xn[k][0], b_xn[k][1]])
        st_mem_rstd(b, M.memT, M.b_mem, M.rstd_m, M.b_rstdm)
        st_memkv(b, M.memT, M.b_mem, M.rstd_m, M.b_rstdm, M.memn, M.b_memn, M.memk, M.b_memk, M.memv, M.b_memv, 16, 33)
    else:
        pss = [b.get("ps"), b.get("ps")]
        for k in range(KC):
            for t in range(2):
                xt, bxt = b.get("xt")
                b.dma_in(xt[:, :], xT_d[:, k, t * 512:(t + 1) * 512], [bxt])
                sq, bsq = b.get("sq")
                b.act(sq[:, :], xt[:, :], AF.Square, reads=(bxt,), writes=(bsq,))

                def f(e, k=k, sq=sq, ps=pss[t][0]):
                    return e.matmul(ps[:, :], b.ones[:, :], sq[:, :], start=(k == 0), stop=(k == KC - 1))
                P.op(P.pe, f, reads=(bsq, b.b_ones), writes=(pss[t][1],))
        for t in range(2):
            b.rstd_from_ss(pss[t][0][:, :], pss[t][1], 1.0 / D, rstd_x[t][:, :], b_rstdx[t])
        st_mem_rstd(b, M.memT, M.b_mem, M.rstd_m, M.b_rstdm)
        st_memkv(b, M.memT, M.b_mem, M.rstd_m, M.b_rstdm, M.memn, M.b_memn, M.memk, M.b_memk, M.memv, M.b_memv, 16, 33)
        for k in range(KC):
            for t in range(2):
                tsl = slice(t * 512, (t + 1) * 512)
                xt, bxt = b.get("xt")
                b.dma_in(xt[:, :], xT_d[:, k, tsl], [bxt])
                b.stt(xn[:, k, tsl], xt[:, :], b.pcol(k), rstd_x[t][:, :], ALU.mult, ALU.mult,
                      reads=(bxt, b_rstdx[t], b.b_par), writes=(b_xn[k][t],))

    qpipe = Pipe()
    for cg in range(8):
        if cg < 6:
            mask_A(cg // 2, cg % 2)
        if 1 <= cg < 7:
            mask_B((cg - 1) // 2, (cg - 1) % 2)
        wt, bw = b.next_w()
        wv = wt[:, :].rearrange("p (k c) -> p k c", k=16)
        for half in range(2):
            hq = cg * 2 + half
            cbt = None
            if hq >= 12:
                cbt, bcb = b.get("cb")
            for t in range(2):
                tsl = slice(t * 512, (t + 1) * 512)
                ps, bps = b.get("ps")
                b.mm(ps[:, :], bps, [(wv[:, k, half * 128:(half + 1) * 128], xn[:, k, tsl]) for k in range(KC)],
                     reads=[bw] + [b_xn[k][t] for k in range(KC)])
                if hq >= 12:
                    after = None
                    if t == 1:
                        after = (lambda hq=hq, cbt=cbt, bcb=bcb: b.dma_out(cat_d[:, 4 + hq - 12, :], cbt[:, :], rbufs=(bcb,)))
                    qpipe.push(mem_attn_stages(b, ps[:, :], bps, M.memk, M.b_memk, M.memv, M.b_memv, hq - 12, 32, cbt[:, tsl], bcb,
                                               after=after))
                else:
                    stq = {}

                    def QA(ps=ps, bps=bps, stq=stq):
                        sq, bsq = b.get("sq")
                        b.act(sq[:, :], ps[:, :], AF.Square, reads=(bps,), writes=(bsq,))
                        ps2, bps2 = b.get("ps")
                        b.mm(ps2[:, :], bps2, [(b.ones[:, :], sq[:, :])], reads=(bsq, b.b_ones))
                        stq.update(ps2=ps2, bps2=bps2)

                    def QB(ps=ps, bps=bps, stq=stq, hq=hq, t=t):
                        g = hq // 4
                        d = DIL[g]
                        rs, brs = b.get("rstd")
                        b.rstd_from_ss(stq["ps2"][:, :], stq["bps2"], 1.0 / 128, rs[:, :], brs)
                        nl = 512 // d
                        dst = qn[:, hq, :].rearrange("p (r l) -> p l r", r=d)[:, t * nl:(t + 1) * nl, :]
                        extra = (M.b_mem,) if hq < 8 else (M.b_memn,)
                        b.stt(dst, ps[:, :].rearrange("p (l r) -> p l r", r=d), b.pcol(34 + g), rs[:, :].rearrange("p (l r) -> p l r", r=d),
                              ALU.mult, ALU.mult, reads=(bps, brs, b.b_par), writes=(b_qn[hq],) + extra)
                    qpipe.push([QA, QB])
    qpipe.flush()

    batches = []
    for h in range(4):
        for g in range(3):
            d, lg, nq = DIL[g], LG[g], NQ[g]
            upc = lg // nq
            nunits = d * upc
            U = 512 // nq
            for u0 in range(0, nunits, U):
                batches.append(dict(h=h, g=g, u0=u0, first=(u0 == 0), last=(g == 2 and u0 + U >= nunits)))
    cur_kv = {}

    def T1(bt):
        h, g, u0 = bt["h"], bt["g"], bt["u0"]
        d, lg, nq, nkc, nbc = DIL[g], LG[g], NQ[g], NKC[g], NBC[g]
        if bt["first"]:
            kt, bkt = b.get("kt")
            vv, bvv = b.get("vv")
            b.dma_in(kt[:, :NK[g]], kt_d[g][:, h, :], [bkt])
            b.dma_in(vv[:, :NB[g], :], vv_d[g][:, h, :, :], [bvv])
            cur_kv[(h, g)] = (kt, bkt, vv, bvv)
        kt, bkt, vv, bvv = cur_kv[(h, g)]
        hq = g * 4 + h
        upc = lg // nq
        U = 512 // nq
        psP, bpsP = b.get("ps")
        psD, bpsD = b.get("ps")
        units = []
        for ui in range(U):
            u = u0 + ui
            r, j = u // upc, u % upc
            qs = qn[:, hq, r * lg + j * nq: r * lg + (j + 1) * nq]
            kp = kt[:, r * nkc + j * nq: r * nkc + j * nq + 128]
            kd = kt[:, r * nkc + 128 + j * nq: r * nkc + 128 + (j + 1) * nq]
            units.append((r, j, qs, kp, kd))

        def fS(e, units=units, psP=psP, psD=psD, nq=nq):
            for ui, (r, j, qs, kp, kd) in enumerate(units):
                e.matmul(psP[:, ui * nq:(ui + 1) * nq], kp, qs, start=True, stop=True)
                ins = e.matmul(psD[:nq, ui * nq:(ui + 1) * nq], kd, qs, start=True, stop=True)
            return ins
        P.op(P.pe, fS, reads=(bkt, b_qn[hq]), writes=(bpsP, bpsD))
        bt.update(units=units, psP=psP, bpsP=bpsP, psD=psD, bpsD=bpsD, vv=vv, bvv=bvv)

    def T2(bt):
        h, g = bt["h"], bt["g"]
        nq = NQ[g]
        eP, beP = b.get("tf")
        eD, beD = b.get("tf")
        b.act(eP[:, :], bt["psP"][:, :], AF.Exp, reads=(bt["bpsP"],), writes=(beP,), scale=SCALE)
        b.act(eD[:nq, :], bt["psD"][:nq, :], AF.Exp, reads=(bt["bpsD"],), writes=(beD,), scale=SCALE)
        pP, bpP = b.get("pp")
        pD, bpD = b.get("pp")
        for ui, (r, j, qs, kp, kd) in enumerate(bt["units"]):
            csl = slice(ui * nq, (ui + 1) * nq)
            mty = 2 if j == 0 else 1
            b.tt(pP[:, csl], eP[:, csl], masks[g][mty][:, h, :nq], ALU.mult, reads=(beP, b_masks[g][mty]), writes=(bpP,))
            b.tt(pD[:nq, csl], eD[:nq, csl], masks[g][0][:nq, h, :nq], ALU.mult, reads=(beD, b_masks[g][0]), writes=(bpD,),
                 eng=P.pool)
        bt.update(pP=pP, bpP=bpP, pD=pD, bpD=bpD)

    def T3(bt):
        g = bt["g"]
        nq, nbc = NQ[g], NBC[g]
        psN, bpsN = b.get("ps")
        psS, bpsS = b.get("ps")
        pP, pD, vv = bt["pP"], bt["pD"], bt["vv"]

        def fV(e, units=bt["units"], psN=psN, psS=psS, nq=nq, pP=pP, pD=pD, vv=vv, nbc=nbc):
            for ui, (r, j, qs, kp, kd) in enumerate(units):
                csl = slice(ui * nq, (ui + 1) * nq)
                e.matmul(psN[:, csl], vv[:, r * nbc + j, :], pP[:, csl], start=True, stop=False)
                e.matmul(psN[:, csl], vv[:nq, r * nbc + j + 1, :], pD[:nq, csl], start=False, stop=True)
                e.matmul(psS[:, csl], b.ones[:, :], pP[:, csl], start=True, stop=False)
                ins = e.matmul(psS[:, csl], b.ones[:nq, :], pD[:nq, csl], start=False, stop=True)
            return ins
        P.op(P.pe, fV, reads=(bt["bvv"], bt["bpP"], bt["bpD"], b.b_ones), writes=(bpsN, bpsS))
        bt.update(psN=psN, bpsN=bpsN, psS=psS, bpsS=bpsS)

    def T4(bt):
        h, g, u0 = bt["h"], bt["g"], bt["u0"]
        d, lg, nq = DIL[g], LG[g], NQ[g]
        upc = lg // nq
        U = 512 // nq
        psN, bpsN, psS, bpsS = bt["psN"], bt["bpsN"], bt["psS"], bt["bpsS"]
        r0 = u0 // upc
        if d == 1:
            l0 = u0 * nq
            dN = accN[:, l0:l0 + 512]
            dD = accD[:, l0:l0 + 512]
            sN, sS = psN[:, :], psS[:, :]
        else:
            nr = U // upc
            dN = accN[:, :].rearrange("p (l r) -> p r l", r=d)[:, r0:r0 + nr, :]
            dD = accD[:, :].rearrange("p (l r) -> p r l", r=d)[:, r0:r0 + nr, :]
            sN = psN[:, :].rearrange("p (r l) -> p r l", r=nr)
            sS = psS[:, :].rearrange("p (r l) -> p r l", r=nr)
        if g == 0:
            b.copy(dN, sN, reads=(bpsN,), writes=(b_acc,))
            b.copy(dD, sS, reads=(bpsS,), writes=(b_acc,))
        else:
            b.tt(dN, sN, dN, ALU.add, reads=(bpsN, b_acc), writes=(b_acc,))
            b.tt(dD, sS, dD, ALU.add, reads=(bpsS, b_acc), writes=(b_acc,))
        if bt["last"]:
            cbt, bcb = b.get("cb")
            for t in range(2):
                tsl = slice(t * 512, (t + 1) * 512)
                rd, brd = b.get("tf")
                b.recip_act(rd[:, :], accD[:, tsl], reads=(b_acc,), writes=(brd,))
                b.tt(cbt[:, tsl], accN[:, tsl], rd[:, :], ALU.mult, reads=(b_acc, brd), writes=(bcb,))
            b.dma_out(cat_d[:, h, :], cbt[:, :], rbufs=(bcb,))

    nbt = len(batches)
    for s_ in range(nbt + 2):
        if s_ < nbt:
            T1(batches[s_])
        if 0 <= s_ - 1 < nbt:
            T2(batches[s_ - 1])
            T3(batches[s_ - 1])
        if 0 <= s_ - 2 < nbt:
            T4(batches[s_ - 2])
    return b.finish(), b.wtiles


def tiles_B2(l):
    tl = [wt_std("b_w_out", l - 2, 0, 8, [(c, 512)]) for c in range(0, 2048, 512)]
    tl += mlp_tiles(l)
    return tl


def build_B2(l, with_next=True):
    b = Bld(tiles_B2(l), n_ps=6)
    b.pool("rstdp", 4, [128, 512], F32)
    stats = Stats(b)
    nc, P = b.nc, b.P
    xT_d = b.din("xT", [128, KC, T])
    cat_d = b.din("cat", [128, 8, T], BF16)
    yT_d = b.dout("yT", [128, KC, T])
    xs = b.sb("xs", [128, KC, T], F32)
    b_xs = [[Buf() for t in range(2)] for k in range(KC)]
    cat = b.sb("cat", [128, 16, T], BF16)
    b_cat = [[Buf() for t in range(2)] for k in range(16)]
    h1 = b.sb("h1", [128, 16, T], BF16)
    b_h1 = [[Buf() for t in range(2)] for k in range(16)]
    for k in range(8):
        b.dma_in(cat[:, k, :], cat_d[:, k, :], [b_cat[k][0], b_cat[k][1]])
    for k in range(KC):
        b.dma_in(xs[:, k, :], xT_d[:, k, :], [b_xs[k][0], b_xs[k][1]])
    for cg in range(4):
        wt, bw = b.next_w()
        wv = wt[:, :].rearrange("p (k c) -> p k c", k=8)
        for q4 in range(4):
            dc = cg * 4 + q4
            for t in range(2):
                tsl = slice(t * 512, (t + 1) * 512)
                ps, bps = b.get("ps")
                b.mm(ps[:, :], bps, [(wv[:, k, q4 * 128:(q4 + 1) * 128], cat[:, k, tsl]) for k in range(8)],
                     reads=[bw] + [b_cat[k][t] for k in range(8)])
                b.tt(xs[:, dc, tsl], ps[:, :], xs[:, dc, tsl], ALU.add, reads=(bps, b_xs[dc][t]), writes=(b_xs[dc][t],))
                stats.add(xs[:, dc, tsl], b_xs[dc][t], t)
    st_mlp(b, xs, b_xs, cat, b_cat, h1, b_h1, 0, out_d=yT_d, in_rstd=stats.rstd(), out_stats=(stats if with_next else None))
    if with_next:
        xnn_d = b.dout("xnn", [128, KC, T], BF16)
        st_apply_norm(b, xs, b_xs, cat, b_cat, 40, stats.rstd())
        for k in range(KC):
            b.dma_out(xnn_d[:, k, :], cat[:, k, :], rbufs=(b_cat[k][0], b_cat[k][1]))
    return b.finish(), b.wtiles


def _t5_bucket(n):
    n = np.maximum(np.asarray(n, np.int64), 0)
    nf = np.maximum(n, 1).astype(np.float32)
    large = 16 + (np.log(nf / np.float32(16.0)) / np.float32(math.log(2048 / 16)) * np.float32(16.0)).astype(np.int32)
    large = np.minimum(large, 31)
    return np.where(n < 16, n, large)


def _structural():
    oh = np.zeros((33, 6, 256), np.float32)
    for g, d in enumerate(DIL):
        for i in range(256):
            w = i - 127
            if 0 <= w <= 127:
                oh[_t5_bucket(w * d), g * 2 + 0, i] = 1.0
            else:
                oh[32, g * 2 + 0, i] = 1.0
            if -127 <= w <= 0:
                oh[_t5_bucket((w + 128) * d), g * 2 + 1, i] = 1.0
            else:
                oh[32, g * 2 + 1, i] = 1.0
    jm = np.ascontiguousarray(np.eye(128, dtype=np.float32)[::-1])
    return oh, jm


def _kv_layout(kT, vT):
    bf = ml_dtypes.bfloat16
    Kf = np.concatenate([np.asarray(k) for k in kT], axis=2)
    Vf = np.concatenate([np.asarray(v) for v in vT], axis=2)
    outs = [dict() for _ in range(NCORES)]
    for g, d in enumerate(DIL):
        lg, nkc, nbc = LG[g], NKC[g], NBC[g]
        Lf = S // d
        def cm(A):
            A = A[:, g * 4:(g + 1) * 4, :].reshape(128, 4, Lf, d).transpose(0, 1, 3, 2)
            pad = np.zeros((128, 4, d, 128), bf)
            return np.concatenate([pad, A], axis=3)
        Kc, Vc = cm(Kf), cm(Vf)
        for c in range(NCORES):
            ks = Kc[:, :, :, c * lg:c * lg + nkc]
            outs[c][f"kt{g}"] = np.ascontiguousarray(ks.reshape(128, 4, d * nkc))
            vs = Vc[:, :, :, c * lg:c * lg + nkc]
            vp = np.zeros((128, 4, d, nbc * 128), bf)
            vp[:, :, :, :nkc] = vs
            vp = vp.reshape(128, 4, d, nbc, 128).transpose(4, 1, 2, 3, 0)
            outs[c][f"vv{g}"] = np.ascontiguousarray(vp.reshape(128, 4, d * nbc, 128))
            kv = outs[c].setdefault("kval", np.zeros((128, 3), np.float32))
            kv[:, g] = ((c * lg - 128 + np.arange(128)) >= 0).astype(np.float32)
    return outs


def run_B_layer(l, xT, memT, W, kvin, xn_in=None, next_g=None):
    f32 = np.float32
    j = l - 2
    par = np.zeros((128, NPAR), f32)
    par[:, 0:16] = colk(W["norm_mix_g"][l])
    par[:, 16:32] = colk(W["mem_norm_g"][l])
    par[:, 32] = W["mem_q_norm_g"][l]
    par[:, 33] = W["mem_k_norm_g"][l]
    par[:, 34:37] = np.asarray(W["b_q_norm_g"][j], f32).T
    par[:, 255] = EPS
    nb1 = _CACHE.get(("B1", l))
    if nb1 is None:
        nb1 = _CACHE[("B1", l)] = build_B1(l, from_xn=(xn_in is not None))
    wst = pack_weights(nb1[1], W)
    oh, jm = _structural()
    relb = np.concatenate([np.asarray(W["rel_bias"], f32), np.full((1, 12), -30000.0, f32)], axis=0)
    ins = []
    for c in range(NCORES):
        dct = {"memT": memT, "wst": wst, "par": par, "oh": oh, "jm": jm, "relb": relb}
        if xn_in is not None:
            dct["xn"] = xn_in[c]
        else:
            dct["xT"] = xT[c]
        dct.update(kvin[c])
        ins.append(dct)
    r1 = _run(nb1[0], ins, f"B1_{l}")
    par2 = np.zeros((128, NPAR), f32)
    par2[:, 0:16] = colk(W["norm_mlp_g"][l])
    par2[:, 255] = EPS
    if next_g is not None:
        par2[:, 40:56] = colk(next_g)
    nb2 = _CACHE.get(("B2", l))
    if nb2 is None:
        nb2 = _CACHE[("B2", l)] = build_B2(l, with_next=(next_g is not None))
    wst2 = pack_weights(nb2[1], W)
    ins2 = [{"xT": xT[c], "cat": r1[c]["cat"], "wst": wst2, "par": par2} for c in range(NCORES)]
    r2 = _run(nb2[0], ins2, f"B2_{l}")
    xnn = [r2[c]["xnn"] for c in range(NCORES)] if next_g is not None else None
    return [r2[c]["yT"] for c in range(NCORES)], r1, xnn


def kernel(**inputs):
    W = {k: np.asarray(v) for k, v in inputs.items()}
    x = W["x"][0]
    memT = fm(W["mem"][0])
    xT = [fm(x[c * T:(c + 1) * T]) for c in range(NCORES)]
    g = W["norm_mix_g"]
    xT, _, _, _, xnn = run_A_layer(0, xT, memT, W, with_kv=False, next_g=g[1])
    xT, kT, vT, _, xnn = run_A_layer(1, xT, memT, W, with_kv=True, xn_in=xnn, next_g=g[2])
    kvin = _kv_layout(kT, vT)
    xT, _, xnn = run_B_layer(2, xT, memT, W, kvin, xn_in=xnn, next_g=g[3])
    xT, _, xnn = run_B_layer(3, xT, memT, W, kvin, xn_in=xnn, next_g=None)
    out = np.concatenate([unfm(t) for t in xT], axis=0)
    return out.reshape(1, S, D).astype(np.float32)
```

```python
import math
import numpy as np
import ml_dtypes
import concourse.bass as bass
import concourse.mybir as mybir
from concourse.bass_utils import run_bass_kernel_spmd

F32 = mybir.dt.float32
BF16 = mybir.dt.bfloat16
AF = mybir.ActivationFunctionType
ALU = mybir.AluOpType

NCORES = 8
D = 2048
S = 8192
T = S // NCORES
KC = D // 128
DFF = 4 * D
EPS = 1e-6
WT_ELEMS = 4096
NPAR = 256
SCALE = 128 ** -0.5
DIL = (1, 4, 16)


class Buf:
    __slots__ = ("name", "w", "r")

    def __init__(self, name=""):
        self.name = name
        self.w = None
        self.r = []


class Eng:
    def __init__(self, name, sem, is_pe=False):
        self.name = name
        self.sem = sem
        self.count = 0
        self.ops = []
        self.waited = {}
        self.is_pe = is_pe


class Prog:
    N_DMA_SEMS = 12

    def __init__(self, nc):
        self.nc = nc
        self.pe = Eng("pe", nc.alloc_semaphore("s_pe"), is_pe=True)
        self.act = Eng("act", nc.alloc_semaphore("s_act"))
        self.dve = Eng("dve", nc.alloc_semaphore("s_dve"))
        self.pool = Eng("pool", nc.alloc_semaphore("s_pool"))
        self.sp = Eng("sp", None)
        self.engs = [self.pe, self.act, self.dve, self.pool, self.sp]
        self.dma_sems = [nc.alloc_semaphore(f"s_dma{i}") for i in range(self.N_DMA_SEMS)]
        self.dma_cnt = [0] * self.N_DMA_SEMS
        self.dma_last = [None] * self.N_DMA_SEMS
        self.dma_rr = 0

    def _deps(self, reads, writes):
        deps = []
        for b in reads:
            if b.w is not None:
                deps.append(b.w)
        for b in writes:
            if b.w is not None:
                deps.append(b.w)
            deps.extend(b.r)
        return deps

    def _filter(self, eng, deps):
        waits = {}
        for (sem, val) in deps:
            if eng.is_pe and sem is eng.sem:
                continue
            k = id(sem)
            if eng.waited.get(k, 0) >= val:
                continue
            if k not in waits or waits[k][1] < val:
                waits[k] = (sem, val)
        for k, (sem, val) in waits.items():
            eng.waited[k] = val
        return list(waits.values())

    def _update(self, tok, reads, writes):
        for b in writes:
            b.w = tok
            b.r = []
        for b in reads:
            if b not in writes:
                b.r.append(tok)

    def op(self, eng, fn, reads=(), writes=()):
        deps = self._deps(reads, writes)
        waits = self._filter(eng, deps)
        eng.count += 1
        tok = (eng.sem, eng.count)
        eng.ops.append((waits, fn, (eng.sem, 1)))
        self._update(tok, reads, writes)
        return tok

    def dma(self, eng, fn, reads=(), writes=(), n=1):
        k = self.dma_rr
        self.dma_rr = (self.dma_rr + 1) % self.N_DMA_SEMS
        sem = self.dma_sems[k]
        deps = self._deps(reads, writes)
        if self.dma_last[k] is not None:
            deps.append(self.dma_last[k])
        waits = self._filter(eng, deps)
        self.dma_cnt[k] += 16 * n
        tok = (sem, self.dma_cnt[k])
        self.dma_last[k] = tok
        eng.ops.append((waits, (lambda e, fn=fn, sem=sem: fn(e, sem)), None))
        self._update(tok, reads, writes)
        return tok

    def wait_all(self, eng, toks):
        waits = self._filter(eng, list(toks))
        eng.ops.append((waits, None, None))

    def emit(self):
        nc = self.nc

        def run(eng, h):
            for (waits, fn, inc) in eng.ops:
                for (sem, val) in waits:
                    h.wait_ge(sem, val)
                if fn is None:
                    continue
                ins = fn(h)
                if inc is not None:
                    ins.then_inc(inc[0], inc[1])

        with nc.Block() as block:
            @block.tensor
            def _(h):
                run(self.pe, h)

            @block.scalar
            def _(h):
                run(self.act, h)

            @block.vector
            def _(h):
                run(self.dve, h)

            @block.gpsimd
            def _(h):
                run(self.pool, h)

            @block.sync
            def _(h):
                run(self.sp, h)


class Bld:
    def __init__(self, wtiles, n_wslots=4, n_ps=8):
        self.discover = wtiles is None
        if self.discover:
            wtiles = []
        self.nc = bass.Bass("TRN2", target_bir_lowering=False)
        self.P = Prog(self.nc)
        self.pools = {}
        self.rr = {}
        self.out_toks = []
        self.pool("ps", n_ps, [128, 512], F32, psum=True)
        if n_ps < 8:
            self.pool("pstat", 8 - n_ps, [128, 512], F32, psum=True)
        self.wtiles = wtiles
        self.NT = 4096 if self.discover else len(wtiles)
        self.n_wslots = n_wslots
        self.wst_d = self.din("wst", [max(self.NT, 1), 128, WT_ELEMS])
        self.par_d = self.din("par", [128, NPAR])
        self.pool("w", n_wslots, [128, WT_ELEMS], BF16)
        self.par = self.sb("par_sb", [128, NPAR], F32)
        self.b_par = Buf("par")
        self.ones = self.sb("ones", [128, 128], BF16)
        self.b_ones = Buf("ones")
        self.w_next_load = 0
        self.w_cur = 0
        self.pool("tf", 4, [128, 512], F32)
        self.pool("tb", 6, [128, 512], BF16)
        self.pool("sq", 4, [128, 512], BF16)
        self.pool("rstd", 4, [128, 512], F32)
        self.dma_in(self.par[:, :], self.par_d, [self.b_par])
        self.P.op(self.P.pool, lambda e: e.memset(self.ones[:, :], 1.0), writes=(self.b_ones,))
        self._ensure(n_wslots - 1)

    def din(self, name, shape, dt=F32):
        return self.nc.dram_tensor(name, list(shape), dt, kind="ExternalInput").ap()

    def dout(self, name, shape, dt=F32):
        return self.nc.dram_tensor(name, list(shape), dt, kind="ExternalOutput").ap()

    def sb(self, name, shape, dt):
        return self.nc.alloc_sbuf_tensor("sb_" + name, list(shape), dt)

    def pool(self, name, n, shape, dt, psum=False):
        if psum:
            lst = [(self.nc.alloc_psum_tensor(f"pp_{name}{i}", list(shape), dt), Buf(f"{name}{i}")) for i in range(n)]
        else:
            lst = [(self.nc.alloc_sbuf_tensor(f"pl_{name}{i}", list(shape), dt), Buf(f"{name}{i}")) for i in range(n)]
        self.pools[name] = lst
        self.rr[name] = 0

    def get(self, name):
        lst = self.pools[name]
        i = self.rr[name]
        self.rr[name] = (i + 1) % len(lst)
        return lst[i]

    def dma_in(self, dst, src, wbufs, eng=None, rbufs=()):
        eng = eng or self.P.sp
        return self.P.dma(eng, lambda e, s: e.dma_start(out=dst, in_=src).then_inc(s, 16), reads=rbufs, writes=wbufs)

    def dma_out(self, dst, src, rbufs):
        tok = self.P.dma(self.P.sp, lambda e, s: e.dma_start(out=dst, in_=src).then_inc(s, 16), reads=rbufs)
        self.out_toks.append(tok)
        return tok

    def finish(self):
        self.P.wait_all(self.P.sp, self.out_toks)
        self.P.emit()
        return self.nc

    def _ensure(self, upto):
        while self.w_next_load <= min(upto, self.NT - 1):
            i = self.w_next_load
            wt, bw = self.pools["w"][i % self.n_wslots]
            self.P.dma(self.P.pool, (lambda e, s, i=i, wt=wt: e.dma_start(out=wt[:, :], in_=self.wst_d[i]).then_inc(s, 16)),
                       writes=(bw,))
            self.w_next_load += 1

    def next_w(self, desc=None):
        i = self.w_cur
        if desc is not None:
            if self.discover:
                self.wtiles.append(desc)
            else:
                assert self.wtiles[i] == desc, (i, self.wtiles[i], desc)
        assert i < self.NT, "weight stream exhausted"
        self.w_cur += 1
        self._ensure(i + self.n_wslots - 1)
        return self.pools["w"][i % self.n_wslots]

    def mm(self, out_ap, bps, pairs, reads):
        n = len(pairs)

        def f(e):
            for i, (l, r) in enumerate(pairs):
                ins = e.matmul(out_ap, l, r, start=(i == 0), stop=(i == n - 1))
            return ins
        return self.P.op(self.P.pe, f, reads=reads, writes=(bps,))

    def act(self, out, in_, func, reads, writes, bias=None, scale=None):
        kw = {}
        if bias is not None:
            kw["bias"] = bias
        if scale is not None:
            kw["scale"] = scale
        return self.P.op(self.P.act, lambda e: e.activation(out=out, in_=in_, func=func, **kw), reads=reads, writes=writes)

    def tt(self, out, a, b_, op, reads, writes, eng=None):
        eng = eng or self.P.dve
        return self.P.op(eng, lambda e: e.tensor_tensor(out=out, in0=a, in1=b_, op=op), reads=reads, writes=writes)

    def ts(self, out, a, s1, s2, op0, op1, reads, writes, eng=None):
        eng = eng or self.P.dve
        if s2 is None:
            return self.P.op(eng, lambda e: e.tensor_scalar(out=out, in0=a, scalar1=s1, scalar2=None, op0=op0), reads=reads, writes=writes)
        return self.P.op(eng, lambda e: e.tensor_scalar(out=out, in0=a, scalar1=s1, scalar2=s2, op0=op0, op1=op1), reads=reads, writes=writes)

    def stt(self, out, a, sc, b_, op0, op1, reads, writes, eng=None):
        eng = eng or self.P.dve
        return self.P.op(eng, lambda e: e.scalar_tensor_tensor(out=out, in0=a, scalar=sc, in1=b_, op0=op0, op1=op1), reads=reads, writes=writes)

    def recip(self, out, in_, reads, writes):
        return self.P.op(self.P.dve, lambda e: e.reciprocal(out=out, in_=in_), reads=reads, writes=writes)

    def copy(self, out, in_, reads, writes, eng=None):
        eng = eng or self.P.act
        if eng is self.P.act:
            return self.P.op(eng, lambda e: e.copy(out=out, in_=in_), reads=reads, writes=writes)
        return self.P.op(eng, lambda e: e.tensor_copy(out=out, in_=in_), reads=reads, writes=writes)

    def pcol(self, c, n=1):
        return self.par[:, c:c + n]

    def rstd_from_ss(self, ss_ap, b_ss, inv_count, out_ap, out_b, ncols=512):
        tf, btf = self.get("tf")
        self.act(tf[:, :ncols], ss_ap, AF.Ln, reads=(b_ss, self.b_par), writes=(btf,), bias=self.pcol(255), scale=inv_count)
        self.act(out_ap, tf[:, :ncols], AF.Exp, reads=(btf,), writes=(out_b,), scale=-0.5)

    def recip_act(self, out_ap, in_ap, reads, writes, nrows=128, ncols=512):
        tf, btf = self.get("tf")
        self.act(tf[:nrows, :ncols], in_ap, AF.Ln, reads=reads, writes=(btf,))
        self.act(out_ap, tf[:nrows, :ncols], AF.Exp, reads=(btf,), writes=writes, scale=-1.0)


def wt_std(key, l, row0, nk, cols):
    return ("std", key, l, row0, nk, tuple(cols))


def pack_weights(tiles, W):
    out = np.zeros((max(len(tiles), 1), 128, WT_ELEMS), np.float32)
    for i, tl in enumerate(tiles):
        if tl[0] == "std":
            _, key, l, row0, nk, cols = tl
            w = W[key][l] if l is not None else W[key]
            blk = np.concatenate([w[row0:row0 + nk * 128, c0:c0 + n] for (c0, n) in cols], axis=1)
            ncol = blk.shape[1]
            out[i, :, :nk * ncol] = blk.reshape(nk, 128, ncol).transpose(1, 0, 2).reshape(128, nk * ncol)
        elif tl[0] == "gates":
            _, l = tl
            g = np.concatenate([W["a_gate_r_w"][l], W["a_gate_i_w"][l]], axis=0)
            out[i, :, :24 * 128] = g.transpose(1, 0, 2).reshape(128, 24 * 128)
    return out


def mlp_tiles(l):
    tl = []
    for q in range(4):
        for c in range(0, 2048, 256):
            tl.append(wt_std("mlp_w1", l, 0, 16, [(q * 2048 + c, 256)]))
        for c in range(0, 2048, 256):
            tl.append(wt_std("mlp_w2", l, q * 2048, 16, [(c, 256)]))
    return tl


def memkv_tiles(l):
    return [wt_std("mem_w_kv", l, 0, 16, [(c, 256)]) for c in range(0, 1024, 256)]


class Stats:
    def __init__(self, b):
        self.b = b
        self.ps = [b.pools["pstat"][t] for t in range(2)]
        self.n = [0, 0]
        self.pipe = Pipe()
        b.pool("sqs", 8, [128, 512], BF16)

    def add(self, src_ap, b_src, t):
        b = self.b
        k = self.n[t]
        self.n[t] += 1
        ps, bps = self.ps[t]
        st = {}

        def A():
            sq, bsq = b.get("sqs")
            b.act(sq[:, :], src_ap, AF.Square, reads=(b_src,), writes=(bsq,))
            st.update(sq=sq, bsq=bsq)

        def nop():
            pass

        def B():
            sq, bsq = st["sq"], st["bsq"]

            def f(e, k=k, sq=sq, ps=ps):
                return e.matmul(ps[:, :], b.ones[:, :], sq[:, :], start=(k == 0), stop=(k == KC - 1))
            b.P.op(b.P.pe, f, reads=(bsq, b.b_ones), writes=(bps,))
        self.pipe.push([nop, A, nop, nop, B])

    def rstd(self):
        b = self.b
        assert self.n == [KC, KC]
        self.pipe.flush()
        out = []
        for t in range(2):
            rs, brs = b.get("rstdp")
            b.rstd_from_ss(self.ps[t][0][:, :], self.ps[t][1], 1.0 / D, rs[:, :], brs)
            out.append((rs, brs))
        self.n = [0, 0]
        return out


def st_apply_norm(b, xs, b_xs, xn, b_xn, gcol, rstds):
    for t in range(2):
        tsl = slice(t * 512, (t + 1) * 512)
        rs, brs = rstds[t]
        for k in range(KC):
            b.stt(xn[:, k, tsl], xs[:, k, tsl], b.pcol(gcol + k), rs[:, :], ALU.mult, ALU.mult,
                  reads=(b_xs[k][t], brs, b.b_par), writes=(b_xn[k][t],))


def st_norm_resident(b, xs, b_xs, xn, b_xn, gcol):
    for t in range(2):
        tsl = slice(t * 512, (t + 1) * 512)
        ps, bps = b.get("ps")
        for k in range(KC):
            sq, bsq = b.get("sq")
            b.act(sq[:, :], xs[:, k, tsl], AF.Square, reads=(b_xs[k][t],), writes=(bsq,))

            def f(e, k=k, sq=sq, ps=ps):
                return e.matmul(ps[:, :], b.ones[:, :], sq[:, :], start=(k == 0), stop=(k == KC - 1))
            b.P.op(b.P.pe, f, reads=(bsq, b.b_ones), writes=(bps,))
        rs, brs = b.get("rstd")
        b.rstd_from_ss(ps[:, :], bps, 1.0 / D, rs[:, :], brs)
        for k in range(KC):
            b.stt(xn[:, k, tsl], xs[:, k, tsl], b.pcol(gcol + k), rs[:, :], ALU.mult, ALU.mult,
                  reads=(b_xs[k][t], brs, b.b_par), writes=(b_xn[k][t],))


def st_mlp(b, xs, b_xs, xn, b_xn, h1, b_h1, gcol, out_d=None, in_rstd=None, out_stats=None):
    if in_rstd is not None:
        st_apply_norm(b, xs, b_xs, xn, b_xn, gcol, in_rstd)
    else:
        st_norm_resident(b, xs, b_xs, xn, b_xn, gcol)
    for q in range(4):
        for cg in range(8):
            wt, bw = b.next_w()
            wv = wt[:, :].rearrange("p (k c) -> p k c", k=16)
            for half in range(2):
                fc = cg * 2 + half
                for t in range(2):
                    tsl = slice(t * 512, (t + 1) * 512)
                    ps, bps = b.get("ps")
                    b.mm(ps[:, :], bps, [(wv[:, k, half * 128:(half + 1) * 128], xn[:, k, tsl]) for k in range(KC)],
                         reads=[bw] + [b_xn[k][t] for k in range(KC)])
                    tf, btf = b.get("tf")
                    b.act(tf[:, :], ps[:, :], AF.Relu, reads=(bps,), writes=(btf,))
                    b.tt(h1[:, fc, tsl], tf[:, :], tf[:, :], ALU.mult, reads=(btf,), writes=(b_h1[fc][t],), eng=b.P.pool)
        for cg in range(8):
            wt, bw = b.next_w()
            wv = wt[:, :].rearrange("p (k c) -> p k c", k=16)
            for half in range(2):
                dc = cg * 2 + half
                for t in range(2):
                    tsl = slice(t * 512, (t + 1) * 512)
                    ps, bps = b.get("ps")
                    b.mm(ps[:, :], bps, [(wv[:, k, half * 128:(half + 1) * 128], h1[:, k, tsl]) for k in range(16)],
                         reads=[bw] + [b_h1[k][t] for k in range(16)])
                    b.tt(xs[:, dc, tsl], ps[:, :], xs[:, dc, tsl], ALU.add, reads=(bps, b_xs[dc][t]), writes=(b_xs[dc][t],))
                    if q == 3 and out_stats is not None:
                        out_stats.add(xs[:, dc, tsl], b_xs[dc][t], t)
                if q == 3 and out_d is not None:
                    b.dma_out(out_d[:, dc, :], xs[:, dc, :], rbufs=(b_xs[dc][0], b_xs[dc][1]))


def st_mem_rstd(b, memT, b_mem, rstd_m, b_rstdm):
    ps, bps = b.get("ps")
    for k in range(KC):
        sq, bsq = b.get("sq")
        b.act(sq[:, :256], memT[:, k, :], AF.Square, reads=(b_mem,), writes=(bsq,))

        def f(e, k=k, sq=sq, ps=ps):
            return e.matmul(ps[:, :256], b.ones[:, :], sq[:, :256], start=(k == 0), stop=(k == KC - 1))
        b.P.op(b.P.pe, f, reads=(bsq, b.b_ones), writes=(bps,))
    b.rstd_from_ss(ps[:, :256], bps, 1.0 / D, rstd_m[:, :], b_rstdm, ncols=256)


def st_memkv(b, memT, b_mem, rstd_m, b_rstdm, memn, b_memn, memk, b_memk, memv, b_memv, gcol_mem, col_kg):
    for k in range(KC):
        b.stt(memn[:, k, :], memT[:, k, :], b.pcol(gcol_mem + k), rstd_m[:, :], ALU.mult, ALU.mult,
              reads=(b_mem, b_rstdm, b.b_par), writes=(b_memn,))
    for hp in range(2):
        wt, bw = b.next_w()
        wv = wt[:, :].rearrange("p (k c) -> p k c", k=16)
        for hh in range(2):
            h = hp * 2 + hh
            hs = slice(hh * 128, hh * 128 + 128)
            ps, bps = b.get("ps")
            b.mm(ps[:, :256], bps, [(wv[:, k, hs], memn[:, k, :]) for k in range(KC)], reads=(bw, b_memn))
            sq, bsq = b.get("sq")
            b.act(sq[:, :256], ps[:, :256], AF.Square, reads=(bps,), writes=(bsq,))
            ps2, bps2 = b.get("ps")
            b.mm(ps2[:, :256], bps2, [(b.ones[:, :], sq[:, :256])], reads=(bsq, b.b_ones))
            rs, brs = b.get("rstd")
            b.rstd_from_ss(ps2[:, :256], bps2, 1.0 / 128, rs[:, :256], brs, ncols=256)
            b.stt(memk[:, h, :], ps[:, :256], b.pcol(col_kg), rs[:, :256], ALU.mult, ALU.mult,
                  reads=(bps, brs, b.b_par), writes=(b_memk,))
    for vh in range(2):
        wt, bw = b.next_w()
        wv = wt[:, :].rearrange("p (k c) -> p k c", k=16)
        for mc in range(2):
            ps, bps = b.get("ps")
            b.mm(ps[:, :256], bps, [(memn[:, k, mc * 128:(mc + 1) * 128], wv[:, k, :]) for k in range(KC)],
                 reads=(bw, b_memn))
            b.copy(memv[:, mc, vh * 256:(vh + 1) * 256], ps[:, :256], reads=(bps,), writes=(b_memv,))


class Pipe:
    def __init__(self):
        self.items = []

    def _advance(self, skip_new=False):
        for it in reversed(self.items):
            if it:
                it.pop(0)()
        self.items = [it for it in self.items if it]

    def push(self, stages):
        self.items.append(list(stages))
        self._advance()

    def flush(self):
        while self.items:
            self._advance()


def mem_attn_stages(b, mq_ps, b_mq, memk, b_memk, memv, b_memv, h, col_qg, out_ap, b_out, after=None):
    st = {}

    def A():
        sq, bsq = b.get("sq")
        b.act(sq[:, :], mq_ps, AF.Square, reads=(b_mq,), writes=(bsq,))
        ps2, bps2 = b.get("ps")
        b.mm(ps2[:, :], bps2, [(b.ones[:, :], sq[:, :])], reads=(bsq, b.b_ones))
        st.update(ps2=ps2, bps2=bps2)

    def B():
        rs, brs = b.get("rstd")
        b.rstd_from_ss(st["ps2"][:, :], st["bps2"], 1.0 / 128, rs[:, :], brs)
        qn, bqn = b.get("tb")
        b.stt(qn[:, :], mq_ps, b.pcol(col_qg), rs[:, :], ALU.mult, ALU.mult, reads=(b_mq, brs, b.b_par), writes=(bqn,))
        pts = []
        for mc in range(2):
            ps3, bps3 = b.get("ps")
            b.mm(ps3[:, :], bps3, [(memk[:, h, mc * 128:(mc + 1) * 128], qn[:, :])], reads=(b_memk, bqn))
            pt, bpt = b.get("tb")
            b.act(pt[:, :], ps3[:, :], AF.Exp, reads=(bps3,), writes=(bpt,), scale=SCALE)
            pts.append((pt, bpt))
        st.update(pts=pts)

    def C():
        pts = st["pts"]
        pn, bpn = b.get("ps")
        b.mm(pn[:, :], bpn, [(memv[:, mc, h * 128:(h + 1) * 128], pts[mc][0][:, :]) for mc in range(2)],
             reads=(b_memv, pts[0][1], pts[1][1]))
        pd, bpd = b.get("ps")
        b.mm(pd[:, :], bpd, [(b.ones[:, :], pts[mc][0][:, :]) for mc in range(2)], reads=(b.b_ones, pts[0][1], pts[1][1]))
        rd, brd = b.get("tf")
        b.recip_act(rd[:, :], pd[:, :], reads=(bpd,), writes=(brd,))
        b.tt(out_ap, pn[:, :], rd[:, :], ALU.mult, reads=(bpn, brd), writes=(b_out,))
        if after is not None:
            after()
    return [A, B, C]


class MemState:
    def __init__(self, b, alias=None):
        if alias is None:
            self.memT = b.sb("memT", [128, KC, 256], F32)
            self.memn = b.sb("memn", [128, KC, 256], BF16)
        else:
            self.memT = alias[:, 0:8, :].bitcast(F32).rearrange("p a (b c) -> p (a b) c", c=256)
            self.memn = alias[:, 8:12, :].rearrange("p a (b c) -> p (a b) c", c=256)
        self.b_mem = Buf()
        self.b_memn = Buf()
        self.memk = b.sb("memk", [128, 4, 256], BF16)
        self.b_memk = Buf()
        self.memv = b.sb("memv", [128, 2, 512], BF16)
        self.b_memv = Buf()
        self.rstd_m = b.sb("rstd_m", [128, 256], F32)
        self.b_rstdm = Buf()


def tiles_A1(l):
    tl = list(memkv_tiles(l))
    tl += [wt_std("a_w_in", l, 0, 16, [(3072 + c, 256)]) for c in (0, 256)]
    tl.append(("gates", l))
    for n in range(12):
        tl.append(wt_std("a_w_in", l, 0, 16, [(n * 128, 128), (1536 + n * 128, 128)]))
    return tl


def build_A1(l, from_xn=False):
    b = Bld(tiles_A1(l))
    nc, P = b.nc, b.P
    if from_xn:
        xn_d = b.din("xn", [128, KC, T], BF16)
        xnh_d = b.din("xnh", [128, KC, 4], BF16)
    else:
        xT_d = b.din("xT", [128, KC, T])
        xh_d = b.din("xh", [128, KC, 4])
    memT_d = b.din("memT", [128, KC, 256])
    cat_d = b.dout("cat", [128, 16, T], BF16)
    q_d = b.dout("qq", [128, 12, T], BF16)
    car_d = b.dout("carry", [128, 24])

    b.pool("xt", 4, [128, 512], F32)
    b.pool("ub", 3, [128, 4 + T], F32)
    b.pool("gb", 3, [128, T], F32)
    b.pool("xc", 2, [128, T], F32)
    b.pool("rb", 2, [128, T], F32)
    b.pool("ib", 2, [128, T], F32)
    b.pool("s3", 5, [128, T], F32)
    xn = b.sb("xn", [128, KC, T], BF16)
    b_xn = [[Buf() for t in range(2)] for k in range(KC)]
    xnh = b.sb("xnh", [128, KC, 4], BF16)
    b_xnh = Buf()
    xh = b.sb("xh", [128, KC, 4], F32)
    b_xh = Buf()
    b.pool("cb", 6, [128, T], BF16)
    M = MemState(b, alias=xn)
    rstd_x = [b.sb(f"rstd_x{t}", [128, 512], F32) for t in range(2)]
    b_rstdx = [Buf(), Buf()]
    rstd_h = b.sb("rstd_h", [128, 4], F32)
    b_rstdh = Buf()
    carry = b.sb("carry", [128, 24], F32)
    b_carry = Buf()
    gw = b.sb("gw", [128, 24, 128], BF16)
    b_gw = Buf()
    nsp = b.sb("nsp", [128, 12], F32)
    b_nsp = Buf()
    zeros = b.sb("zeros", [128, T], F32)
    b_zeros = Buf()
    sml = [b.sb(f"sml{i}", [128, 12], F32) for i in range(6)]
    b_sml = [Buf() for i in range(6)]

    b.dma_in(M.memT[:, :, :], memT_d, [M.b_mem])
    if not from_xn:
        b.dma_in(xh[:, :, :], xh_d, [b_xh])
    P.op(P.pool, lambda e: e.memset(zeros[:, :], 0.0), writes=(b_zeros,))

    st_mem_rstd(b, M.memT, M.b_mem, M.rstd_m, M.b_rstdm)
    st_memkv(b, M.memT, M.b_mem, M.rstd_m, M.b_rstdm, M.memn, M.b_memn, M.memk, M.b_memk, M.memv, M.b_memv, 16, 33)

    if from_xn:
        b.dma_in(xnh[:, :, :], xnh_d, [b_xnh])
        for k in list(range(12, KC)) + list(range(12)):
            extra = (M.b_mem,) if k < 8 else ((M.b_memn,) if k < 12 else ())
            b.dma_in(xn[:, k, :], xn_d[:, k, :], [b_xn[k][0], b_xn[k][1]] + list(extra))
    else:
        pss = [b.get("ps"), b.get("ps")]
        for k in range(KC):
            for t in range(2):
                xt, bxt = b.get("xt")
                b.dma_in(xt[:, :], xT_d[:, k, t * 512:(t + 1) * 512], [bxt])
                sq, bsq = b.get("sq")
                b.act(sq[:, :], xt[:, :], AF.Square, reads=(bxt,), writes=(bsq,))

                def f(e, k=k, sq=sq, ps=pss[t][0]):
                    return e.matmul(ps[:, :], b.ones[:, :], sq[:, :], start=(k == 0), stop=(k == KC - 1))
                P.op(P.pe, f, reads=(bsq, b.b_ones), writes=(pss[t][1],))
        for t in range(2):
            b.rstd_from_ss(pss[t][0][:, :], pss[t][1], 1.0 / D, rstd_x[t][:, :], b_rstdx[t])
        psh, bpsh = b.get("ps")
        for k in range(KC):
            sq, bsq = b.get("sq")
            b.act(sq[:, :4], xh[:, k, :], AF.Square, reads=(b_xh,), writes=(bsq,))

            def f(e, k=k, sq=sq, psh=psh):
                return e.matmul(psh[:, :4], b.ones[:, :], sq[:, :4], start=(k == 0), stop=(k == KC - 1))
            P.op(P.pe, f, reads=(bsq, b.b_ones), writes=(bpsh,))
        b.rstd_from_ss(psh[:, :4], bpsh, 1.0 / D, rstd_h[:, :], b_rstdh, ncols=4)
        for k in range(KC):
            b.stt(xnh[:, k, :], xh[:, k, :], b.pcol(k), rstd_h[:, :], ALU.mult, ALU.mult,
                  reads=(b_xh, b_rstdh, b.b_par), writes=(b_xnh,))
        for k in range(KC):
            for t in range(2):
                tsl = slice(t * 512, (t + 1) * 512)
                xt, bxt = b.get("xt")
                b.dma_in(xt[:, :], xT_d[:, k, tsl], [bxt])
                extra = (M.b_mem,) if k < 8 else ((M.b_memn,) if k < 12 else ())
                b.stt(xn[:, k, tsl], xt[:, :], b.pcol(k), rstd_x[t][:, :], ALU.mult, ALU.mult,
                      reads=(bxt, b_rstdx[t], b.b_par), writes=(b_xn[k][t],) + extra)

    pipe = Pipe()
    for hp in range(2):
        wt, bw = b.next_w()
        wv = wt[:, :].rearrange("p (k c) -> p k c", k=16)
        for hh in range(2):
            h = hp * 2 + hh
            cbt, bcb = b.get("cb")
            for t in range(2):
                tsl = slice(t * 512, (t + 1) * 512)
                ps, bps = b.get("ps")
                b.mm(ps[:, :], bps, [(wv[:, k, hh * 128:(hh + 1) * 128], xn[:, k, tsl]) for k in range(KC)],
                     reads=[bw] + [b_xn[k][t] for k in range(KC)])
                after = None
                if t == 1:
                    after = (lambda h=h, cbt=cbt, bcb=bcb: b.dma_out(cat_d[:, 12 + h, :], cbt[:, :], rbufs=(bcb,)))
                pipe.push(mem_attn_stages(b, ps[:, :], bps, M.memk, M.b_memk, M.memv, M.b_memv, h, 32, cbt[:, tsl], bcb, after=after))
    pipe.flush()

    wt, bw = b.next_w()
    b.copy(gw[:, :, :], wt[:, :24 * 128].rearrange("p (g d) -> p g d", g=24), reads=(bw,), writes=(b_gw,), eng=P.pool)

    lam = b.par[:, 124:136]
    s0, s1, s2, s3, s4, s5 = sml
    B0, B1, B2, B3, B4, B5 = b_sml
    b.ts(s0[:, :], lam, -1.0, None, ALU.mult, None, reads=(b.b_par,), writes=(B0,))
    b.tt(s0[:, :], s0[:, :], lam, ALU.max, reads=(B0, b.b_par), writes=(B0,))
    b.act(s1[:, :], s0[:, :], AF.Exp, reads=(B0,), writes=(B1,), scale=-1.0)
    b.ts(s2[:, :], s1[:, :], 2.0, None, ALU.add, None, reads=(B1,), writes=(B2,))
    b.recip(s3[:, :], s2[:, :], reads=(B2,), writes=(B3,))
    b.tt(s2[:, :], s1[:, :], s3[:, :], ALU.mult, reads=(B1, B3), writes=(B2,))
    b.tt(s3[:, :], s2[:, :], s2[:, :], ALU.mult, reads=(B2,), writes=(B3,))
    b.ts(s4[:, :], s3[:, :], 1.0 / 11, 1.0 / 9, ALU.mult, ALU.add, reads=(B3,), writes=(B4,))
    for cst in (1.0 / 7, 1.0 / 5, 1.0 / 3, 1.0):
        b.tt(s4[:, :], s4[:, :], s3[:, :], ALU.mult, reads=(B4, B3), writes=(B4,))
        b.ts(s4[:, :], s4[:, :], cst, None, ALU.add, None, reads=(B4,), writes=(B4,))
    b.tt(s4[:, :], s4[:, :], s2[:, :], ALU.mult, reads=(B4, B2), writes=(B4,))
    b.ts(s5[:, :], lam, -1.0, 0.0, ALU.mult, ALU.max, reads=(b.b_par,), writes=(B5,))
    b.stt(s5[:, :], s4[:, :], 2.0, s5[:, :], ALU.mult, ALU.add, reads=(B4, B5), writes=(B5,))
    b.ts(nsp[:, :], s5[:, :], -8.0, None, ALU.mult, None, reads=(B5,), writes=(b_nsp,))

    stt_ = {}

    def S1(n):
        wt, bw = b.next_w()
        wv = wt[:, :].rearrange("p (k c) -> p k c", k=16)
        ub, bub = b.get("ub")
        gb, bgb = b.get("gb")
        psh, bpsh = b.get("ps")
        b.mm(psh[:, :4], bpsh, [(wv[:, k, 0:128], xnh[:, k, :]) for k in range(KC)], reads=(bw, b_xnh))
        b.copy(ub[:, 0:4], psh[:, :4], reads=(bpsh,), writes=(bub,))
        for t in range(2):
            tsl = slice(t * 512, (t + 1) * 512)
            ps, bps = b.get("ps")
            b.mm(ps[:, :], bps, [(wv[:, k, 0:128], xn[:, k, tsl]) for k in range(KC)],
                 reads=[bw] + [b_xn[k][t] for k in range(KC)])
            b.copy(ub[:, 4 + t * 512:4 + (t + 1) * 512], ps[:, :], reads=(bps,), writes=(bub,))
            pg, bpg = b.get("ps")
            b.mm(pg[:, :], bpg, [(wv[:, k, 128:256], xn[:, k, tsl]) for k in range(KC)],
                 reads=[bw] + [b_xn[k][t] for k in range(KC)])
            b.act(gb[:, tsl], pg[:, :], AF.Gelu_apprx_tanh, reads=(bpg,), writes=(bgb,))
        stt_[n] = dict(ub=ub, bub=bub, gb=gb, bgb=bgb)

    def S2(n):
        d_ = stt_[n]
        ub, bub = d_["ub"], d_["bub"]
        xc, bxc = b.get("xc")
        cw = 40 + n * 4
        b.ts(xc[:, :], ub[:, 1:1 + T], b.pcol(cw), b.pcol(88 + n), ALU.mult, ALU.add, reads=(bub, b.b_par), writes=(bxc,), eng=P.pool)
        for j in range(1, 4):
            b.stt(xc[:, :], ub[:, 1 + j:1 + j + T], b.pcol(cw + j), xc[:, :], ALU.mult, ALU.add,
                  reads=(bub, b.b_par, bxc), writes=(bxc,))
        xcb = [b.get("tb"), b.get("tb")]
        for t in range(2):
            b.copy(xcb[t][0][:, :], xc[:, t * 512:(t + 1) * 512], reads=(bxc,), writes=(xcb[t][1],))
        rb, brb = b.get("rb")
        ib, bib = b.get("ib")
        for t in range(2):
            tsl = slice(t * 512, (t + 1) * 512)
            pr, bpr = b.get("ps")
            b.mm(pr[:, :], bpr, [(gw[:, n, :], xcb[t][0][:, :])], reads=(b_gw, xcb[t][1]))
            b.act(rb[:, tsl], pr[:, :], AF.Sigmoid, reads=(bpr, b.b_par), writes=(brb,), bias=b.pcol(100 + n))
            pi_, bpi = b.get("ps")
            b.mm(pi_[:, :], bpi, [(gw[:, 12 + n, :], xcb[t][0][:, :])], reads=(b_gw, xcb[t][1]))
            b.act(ib[:, tsl], pi_[:, :], AF.Sigmoid, reads=(bpi, b.b_par), writes=(bib,), bias=b.pcol(112 + n))
        d_.update(xc=xc, bxc=bxc, rb=rb, brb=brb, ib=ib, bib=bib)

    def S3(n):
        d_ = stt_.pop(n)
        gb, bgb, xc, bxc, ab, bab, ib, bib = d_["gb"], d_["bgb"], d_["xc"], d_["bxc"], d_["rb"], d_["brb"], d_["ib"], d_["bib"]
        b.act(ab[:, :], ab[:, :], AF.Exp, reads=(bab, b_nsp), writes=(bab,), scale=nsp[:, n:n + 1])
        bb_, bbb = b.get("s3")
        b.tt(bb_[:, :], ab[:, :], ab[:, :], ALU.mult, reads=(bab,), writes=(bbb,), eng=P.pool)
        b.ts(bb_[:, :], bb_[:, :], -1.0, 1.0, ALU.mult, ALU.add, reads=(bbb,), writes=(bbb,), eng=P.pool)
        b.act(bb_[:, :], bb_[:, :], AF.Sqrt, reads=(bbb,), writes=(bbb,))
        b.tt(ib[:, :], ib[:, :], xc[:, :], ALU.mult, reads=(bib, bxc), writes=(bib,), eng=P.pool)
        b.tt(bb_[:, :], bb_[:, :], ib[:, :], ALU.mult, reads=(bbb, bib), writes=(bbb,))
        hb, bhb = b.get("s3")
        P.op(P.dve, lambda e, hb=hb, ab=ab, bb_=bb_: e.tensor_tensor_scan(out=hb[:, :], data0=ab[:, :], data1=bb_[:, :], initial=0.0,
                                                                            op0=ALU.mult, op1=ALU.add),
             reads=(bab, bbb), writes=(bhb,))
        Ab, bAb = b.get("s3")
        P.op(P.dve, lambda e, Ab=Ab, ab=ab: e.tensor_tensor_scan(out=Ab[:, :], data0=ab[:, :], data1=zeros[:, :], initial=1.0,
                                                                  op0=ALU.mult, op1=ALU.add),
             reads=(bab, b_zeros), writes=(bAb,))
        b.copy(carry[:, n:n + 1], hb[:, T - 1:T], reads=(bhb,), writes=(b_carry,))
        b.copy(carry[:, 12 + n:13 + n], Ab[:, T - 1:T], reads=(bAb,), writes=(b_carry,))
        cbt, bcb = b.get("cb")
        b.tt(cbt[:, :], hb[:, :], gb[:, :], ALU.mult, reads=(bhb, bgb), writes=(bcb,))
        b.dma_out(cat_d[:, n, :], cbt[:, :], rbufs=(bcb,))
        cbq, bcq = b.get("cb")
        b.tt(cbq[:, :], Ab[:, :], gb[:, :], ALU.mult, reads=(bAb, bgb), writes=(bcq,), eng=P.pool)
        b.dma_out(q_d[:, n, :], cbq[:, :], rbufs=(bcq,))

    for s_ in range(12 + 2):
        if s_ < 12:
            S1(s_)
        if 0 <= s_ - 1 < 12:
            S2(s_ - 1)
        if 0 <= s_ - 2 < 12:
            S3(s_ - 2)

    b.dma_out(car_d, carry[:, :], rbufs=(b_carry,))
    return b.finish(), b.wtiles


def tiles_A2(l, with_kv):
    tl = [wt_std("a_w_out", l, 0, 16, [(c, 256)]) for c in range(0, 2048, 256)]
    tl += mlp_tiles(l)
    if with_kv:
        tl += [wt_std("kv_w", None, 0, 16, [(c, 256)]) for c in range(0, 3072, 256)]
    return tl


def build_A2(l, with_kv, with_next=True):
    b = Bld(tiles_A2(l, with_kv), n_ps=6)
    b.pool("rstdp", 4, [128, 512], F32)
    stats = Stats(b)
    nc, P = b.nc, b.P
    xT_d = b.din("xT", [128, KC, T])
    cat_d = b.din("cat", [128, 16, T], BF16)
    q_d = b.din("qq", [128, 12, T], BF16)
    car_d = b.din("carr", [128, 8, 24])
    sel_d = b.din("sel", [128, 8])
    yT_d = b.dout("yT", [128, KC, T])
    xs = b.sb("xs", [128, KC, T], F32)
    b_xs = [[Buf() for t in range(2)] for k in range(KC)]
    cat = b.sb("cat", [128, 16, T], BF16)
    b_cat = [[Buf() for t in range(2)] for k in range(16)]
    h1 = b.sb("h1", [128, 16, T], BF16)
    b_h1 = [[Buf() for t in range(2)] for k in range(16)]
    carr = b.sb("carr", [128, 8, 24], F32)
    b_carr = Buf()
    sel = b.sb("sel", [128, 8], F32)
    b_sel = Buf()
    cst = [b.sb(f"cst{i}", [128, 12], F32) for i in range(2)]
    b_cst = [Buf(), Buf()]
    hin = b.sb("hin", [128, 12], F32)
    b_hin = Buf()
    tmp12 = b.sb("tmp12", [128, 12], F32)
    b_tmp12 = Buf()

    b.dma_in(carr[:, :, :], car_d, [b_carr])
    b.dma_in(sel[:, :], sel_d, [b_sel])
    for k in range(16):
        if k < 12:
            b.dma_in(h1[:, k, :], q_d[:, k, :], [b_h1[k][0], b_h1[k][1]])
        b.dma_in(cat[:, k, :], cat_d[:, k, :], [b_cat[k][0], b_cat[k][1]])
    for k in range(KC):
        b.dma_in(xs[:, k, :], xT_d[:, k, :], [b_xs[k][0], b_xs[k][1]])

    P.op(P.pool, lambda e: e.memset(cst[0][:, :], 0.0), writes=(b_cst[0],))
    P.op(P.pool, lambda e: e.memset(hin[:, :], 0.0), writes=(b_hin,))
    for r in range(8):
        cur, bcur = cst[r % 2], b_cst[r % 2]
        nx, bnx = cst[(r + 1) % 2], b_cst[(r + 1) % 2]
        b.stt(hin[:, :], cur[:, :], sel[:, r:r + 1], hin[:, :], ALU.mult, ALU.add, reads=(bcur, b_sel, b_hin), writes=(b_hin,))
        if r < 7:
            b.tt(tmp12[:, :], carr[:, r, 12:24], cur[:, :], ALU.mult, reads=(b_carr, bcur), writes=(b_tmp12,))
            b.tt(nx[:, :], tmp12[:, :], carr[:, r, 0:12], ALU.add, reads=(b_tmp12, b_carr), writes=(bnx,))
    for n in range(12):
        for t in range(2):
            tsl = slice(t * 512, (t + 1) * 512)
            b.stt(cat[:, n, tsl], h1[:, n, tsl], hin[:, n:n + 1], cat[:, n, tsl], ALU.mult, ALU.add,
                  reads=(b_h1[n][t], b_hin, b_cat[n][t]), writes=(b_cat[n][t],))
    for cg in range(8):
        wt, bw = b.next_w()
        wv = wt[:, :].rearrange("p (k c) -> p k c", k=16)
        for half in range(2):
            dc = cg * 2 + half
            for t in range(2):
                tsl = slice(t * 512, (t + 1) * 512)
                ps, bps = b.get("ps")
                b.mm(ps[:, :], bps, [(wv[:, k, half * 128:(half + 1) * 128], cat[:, k, tsl]) for k in range(16)],
                     reads=[bw] + [b_cat[k][t] for k in range(16)])
                b.tt(xs[:, dc, tsl], ps[:, :], xs[:, dc, tsl], ALU.add, reads=(bps, b_xs[dc][t]), writes=(b_xs[dc][t],))
                stats.add(xs[:, dc, tsl], b_xs[dc][t], t)
    need_out = with_next or with_kv
    st_mlp(b, xs, b_xs, cat, b_cat, h1, b_h1, 0, out_d=yT_d, in_rstd=stats.rstd(), out_stats=(stats if need_out else None))
    if need_out:
        rs_out = stats.rstd()
    if with_next:
        xnn_d = b.dout("xnn", [128, KC, T], BF16)
        st_apply_norm(b, xs, b_xs, cat, b_cat, 40, rs_out)
        for k in range(KC):
            b.dma_out(xnn_d[:, k, :], cat[:, k, :], rbufs=(b_cat[k][0], b_cat[k][1]))
    if with_kv:
        kT_d = b.dout("kT", [128, 12, T], BF16)
        vT_d = b.dout("vT", [128, 12, T], BF16)
        st_apply_norm(b, xs, b_xs, cat, b_cat, 16, rs_out)
        kpipe = Pipe()
        for cg in range(12):
            wt, bw = b.next_w()
            wv = wt[:, :].rearrange("p (k c) -> p k c", k=16)
            for half in range(2):
                hc = cg * 2 + half
                for t in range(2):
                    tsl = slice(t * 512, (t + 1) * 512)
                    ps, bps = b.get("ps")
                    b.mm(ps[:, :], bps, [(wv[:, k, half * 128:(half + 1) * 128], cat[:, k, tsl]) for k in range(16)],
                         reads=[bw] + [b_cat[k][t] for k in range(16)])
                    if hc < 12:
                        stq = {}

                        def KA(ps=ps, bps=bps, stq=stq):
                            sq, bsq = b.get("sq")
                            b.act(sq[:, :], ps[:, :], AF.Square, reads=(bps,), writes=(bsq,))
                            ps2, bps2 = b.get("ps")
                            b.mm(ps2[:, :], bps2, [(b.ones[:, :], sq[:, :])], reads=(bsq, b.b_ones))
                            stq.update(ps2=ps2, bps2=bps2)

                        def KB(ps=ps, bps=bps, stq=stq, hc=hc, t=t, tsl=tsl):
                            rs, brs = b.get("rstd")
                            b.rstd_from_ss(stq["ps2"][:, :], stq["bps2"], 1.0 / 128, rs[:, :], brs)
                            b.stt(h1[:, hc, tsl], ps[:, :], b.pcol(32 + hc // 4), rs[:, :], ALU.mult, ALU.mult,
                                  reads=(bps, brs, b.b_par), writes=(b_h1[hc][t],))
                        kpipe.push([KA, KB])
                    else:
                        b.copy(h1[:, hc - 12, tsl], ps[:, :], reads=(bps,), writes=(b_h1[hc - 12][t],))
            if cg == 5:
                kpipe.flush()
                for k in range(12):
                    b.dma_out(kT_d[:, k, :], h1[:, k, :], rbufs=(b_h1[k][0], b_h1[k][1]))
        for k in range(12):
            b.dma_out(vT_d[:, k, :], h1[:, k, :], rbufs=(b_h1[k][0], b_h1[k][1]))
    return b.finish(), b.wtiles


def fm(x):
    n, f = x.shape
    return np.ascontiguousarray(x.T.reshape(f // 128, 128, n).transpose(1, 0, 2))


def unfm(xT):
    p, k, n = xT.shape
    return np.ascontiguousarray(xT.transpose(1, 0, 2).reshape(k * p, n).T)


def colk(v):
    return np.ascontiguousarray(np.asarray(v, np.float32).reshape(-1, 128).T)


_CACHE = {}
_TIMES = []


def _run(nc, ins, tag=""):
    import os
    if os.environ.get("KTRACE"):
        res = run_bass_kernel_spmd(nc, ins, core_ids=list(range(NCORES)), trace=True)
        _TIMES.append((tag, res.exec_time_ns))
        print("KTRACE", tag, res.exec_time_ns, flush=True)
    else:
        res = run_bass_kernel_spmd(nc, ins, core_ids=list(range(NCORES)))
    return res.results


def _launch(key, builder, in_maps):
    if key not in _CACHE:
        _CACHE[key] = builder()
    nc, tiles = _CACHE[key]
    res = run_bass_kernel_spmd(nc, in_maps, core_ids=list(range(NCORES)))
    return res.results


def run_A_layer(l, xT, memT, W, with_kv, xn_in=None, next_g=None):
    f32 = np.float32
    par = np.zeros((128, NPAR), f32)
    par[:, 0:16] = colk(W["norm_mix_g"][l])
    par[:, 16:32] = colk(W["mem_norm_g"][l])
    par[:, 32] = W["mem_q_norm_g"][l]
    par[:, 33] = W["mem_k_norm_g"][l]
    cw = W["a_conv_w"][l]
    for n in range(12):
        for j in range(4):
            par[:, 40 + n * 4 + j] = cw[j, n * 128:(n + 1) * 128]
    par[:, 88:100] = colk(W["a_conv_b"][l])
    par[:, 100:112] = np.asarray(W["a_gate_r_b"][l], f32).T
    par[:, 112:124] = np.asarray(W["a_gate_i_b"][l], f32).T
    par[:, 124:136] = colk(W["a_lambda"][l])
    par[:, 255] = EPS
    nc1 = _CACHE.get(("A1", l))
    if nc1 is None:
        nc1 = _CACHE[("A1", l)] = build_A1(l, from_xn=(xn_in is not None))
    wst = pack_weights(nc1[1], W)
    ins = []
    for c in range(NCORES):
        if xn_in is not None:
            xnh = np.zeros((128, KC, 4), ml_dtypes.bfloat16)
            if c > 0:
                xnh[:, :, :] = np.asarray(xn_in[c - 1])[:, :, T - 4:]
            ins.append({"xn": xn_in[c], "xnh": xnh, "memT": memT, "wst": wst, "par": par})
        else:
            xh = np.zeros((128, KC, 4), f32)
            if c > 0:
                xh[:, :, :] = xT[c - 1][:, :, T - 4:]
            ins.append({"xT": xT[c], "xh": xh, "memT": memT, "wst": wst, "par": par})
    r1 = _run(nc1[0], ins, f"A1_{l}")
    par2 = np.zeros((128, NPAR), f32)
    par2[:, 0:16] = colk(W["norm_mlp_g"][l])
    if with_kv:
        par2[:, 16:32] = colk(W["kv_norm_g"])
        par2[:, 32:35] = np.asarray(W["k_norm_g"], f32).T
    par2[:, 255] = EPS
    if next_g is not None:
        par2[:, 40:56] = colk(next_g)
    nc2 = _CACHE.get(("A2", l))
    if nc2 is None:
        nc2 = _CACHE[("A2", l)] = build_A2(l, with_kv, with_next=(next_g is not None))
    wst2 = pack_weights(nc2[1], W)
    carr = np.ascontiguousarray(np.stack([r1[c]["carry"] for c in range(NCORES)], axis=1))
    ins2 = []
    for c in range(NCORES):
        sel = np.zeros((128, 8), f32)
        sel[:, c] = 1.0
        ins2.append({"xT": xT[c], "cat": r1[c]["cat"], "qq": r1[c]["qq"], "carr": carr, "sel": sel, "wst": wst2, "par": par2})
    r2 = _run(nc2[0], ins2, f"A2_{l}")
    out = [r2[c]["yT"] for c in range(NCORES)]
    xnn = [r2[c]["xnn"] for c in range(NCORES)] if next_g is not None else None
    if with_kv:
        return out, [r2[c]["kT"] for c in range(NCORES)], [r2[c]["vT"] for c in range(NCORES)], r1, xnn
    return out, None, None, r1, xnn


LG = [T // d for d in DIL]
NQ = [min(128, lg) for lg in LG]
NKC = [128 + lg for lg in LG]
NBC = [(n + 127) // 128 for n in NKC]
NK = [d * n for d, n in zip(DIL, NKC)]
NB = [d * n for d, n in zip(DIL, NBC)]


def tiles_B1(l):
    tl = list(memkv_tiles(l))
    tl += [wt_std("b_w_q", l - 2, 0, 16, [(c, 256)]) for c in range(0, 2048, 256)]
    return tl


def build_B1(l, from_xn=True):
    b = Bld(tiles_B1(l))
    nc, P = b.nc, b.P
    if from_xn:
        xn_d = b.din("xn", [128, KC, T], BF16)
    else:
        xT_d = b.din("xT", [128, KC, T])
    memT_d = b.din("memT", [128, KC, 256])
    kt_d = [b.din(f"kt{g}", [128, 4, NK[g]], BF16) for g in range(3)]
    vv_d = [b.din(f"vv{g}", [128, 4, NB[g], 128], BF16) for g in range(3)]
    oh_d = b.din("oh", [33, 6, 256])
    jm_d = b.din("jm", [128, 128])
    relb_d = b.din("relb", [33, 12])
    kval_d = b.din("kval", [128, 3])
    cat_d = b.dout("cat", [128, 8, T], BF16)
    vec_d = nc.dram_tensor("vecd", [6, 4, 256], F32).ap()

    b.pool("xt", 4, [128, 512], F32)
    b.pool("cb", 4, [128, T], BF16)
    b.pool("pp", 4, [128, 512], BF16)
    b.pool("kt", 2, [128, max(NK)], BF16)
    b.pool("vv", 2, [128, max(NB), 128], BF16)
    xn = b.sb("xn", [128, KC, T], BF16)
    b_xn = [[Buf() for t in range(2)] for k in range(KC)]
    qn = b.sb("qn", [128, 12, T], BF16)
    b_qn = [Buf() for k in range(12)]
    M = MemState(b, alias=qn)
    rstd_x = [b.sb(f"rstd_x{t}", [128, 512], F32) for t in range(2)]
    b_rstdx = [Buf(), Buf()]
    accN = b.sb("accN", [128, T], F32)
    accD = b.sb("accD", [128, T], F32)
    b_acc = Buf()
    relb = b.sb("relb", [33, 12], F32)
    b_relb = Buf()
    jm = b.sb("jm", [128, 128], F32)
    b_jm = Buf()
    kval = b.sb("kval", [128, 3], F32)
    b_kval = Buf()
    b.pool("vec", 2, [4, 256], F32)
    b.pool("hk", 2, [128, 512], F32)
    masks = [[b.sb(f"mask{g}_{ty}", [128, 4, 128], F32) for ty in range(3)] for g in range(3)]
    b_masks = [[Buf() for ty in range(3)] for g in range(3)]

    b.dma_in(M.memT, memT_d, [M.b_mem])
    b.dma_in(relb[:, :], relb_d, [b_relb])
    b.dma_in(jm[:, :], jm_d, [b_jm])
    b.dma_in(kval[:, :], kval_d, [b_kval])

    mask_state = {}

    def mask_A(g, ty):
        oh, boh = b.get("tf")
        b.dma_in(oh[:33, :256], oh_d[:, g * 2 + ty, :], [boh])
        ps, bps = b.get("ps")
        b.mm(ps[:4, :256], bps, [(relb[:33, g * 4:(g + 1) * 4], oh[:33, :256])], reads=(b_relb, boh))
        vec, b_vec = b.get("vec")
        b.act(vec[:, :], ps[:4, :256], AF.Exp, reads=(bps,), writes=(b_vec,))
        b_vd = Buf()
        P.dma(P.sp, (lambda e, s, g=g, ty=ty, vec=vec: e.dma_start(out=vec_d[g * 2 + ty], in_=vec[:, :]).then_inc(s, 16)),
              reads=(b_vec,), writes=(b_vd,))
        hk, bhk = b.get("hk")
        src = bass.AP(tensor=vec_d.tensor, offset=(g * 2 + ty) * 1024, ap=[[1, 128], [256, 4], [1, 128]])
        b.dma_in(hk[:, :].rearrange("p (h q) -> p h q", h=4), src, [bhk], rbufs=(b_vd,))
        mask_state[(g, ty)] = (hk, bhk)

    def mask_B(g, ty):
        hk, bhk = mask_state[(g, ty)]
        ps2, bps2 = b.get("ps")
        b.mm(ps2[:, :], bps2, [(jm[:, :], hk[:, :])], reads=(b_jm, bhk))
        b.copy(masks[g][ty][:, :, :], ps2[:, :].rearrange("p (h q) -> p h q", h=4), reads=(bps2,), writes=(b_masks[g][ty],))
        if ty == 1:
            b.ts(masks[g][2][:, :, :], masks[g][1][:, :, :], kval[:, g:g + 1], None, ALU.mult, None,
                 reads=(b_masks[g][1], b_kval), writes=(b_masks[g][2],))

    if from_xn:
        for k in range(KC):
            b.dma_in(xn[:, k, :], xn_d[:, k, :], [b_xn[k][0], b_xn[k][1]])
        st_mem_rstd(b, M.memT, M.b_mem, M.rstd_m, M.b_rstdm)
        st_memkv(b, M.memT, M.b_mem, M.rstd_m, M.b_rstdm, M.memn, M.b_memn, M.memk, M.b_memk, M.memv, M.b_memv, 16, 33)
    else:
        pss = [b.get("ps"), b.get("ps")]
        for k in range(KC):
            for t in range(2):
                xt, bxt = b.get("xt")
                b.dma_in(xt[:, :], xT_d[:, k, t * 512:(t + 1) * 512], [bxt])
                sq, bsq = b.get("sq")
                b.act(sq[:, :], xt[:, :], AF.Square, reads=(bxt,), writes=(bsq,))

                def f(e, k=k, sq=sq, ps=pss[t][0]):
                    return e.matmul(ps[:, :], b.ones[:, :], sq[:, :], start=(k == 0), stop=(k == KC - 1))
                P.op(P.pe, f, reads=(bsq, b.b_ones), writes=(pss[t][1],))
        for t in range(2):
            b.rstd_from_ss(pss[t][0][:, :], pss[t][1], 1.0 / D, rstd_x[t][:, :], b_rstdx[t])
        st_mem_rstd(b, M.memT, M.b_mem, M.rstd_m, M.b_rstdm)
        st_memkv(b, M.memT, M.b_mem, M.rstd_m, M.b_rstdm, M.memn, M.b_memn, M.memk, M.b_memk, M.memv, M.b_memv, 16, 33)
        for k in range(KC):
            for t in range(2):
                tsl = slice(t * 512, (t + 1) * 512)
                xt, bxt = b.get("xt")
                b.dma_in(xt[:, :], xT_d[:, k, tsl], [bxt])
                b.stt(xn[:, k, tsl], xt[:, :], b.pcol(k), rstd_x[t][:, :], ALU.mult, ALU.mult,
                      reads=(bxt, b_rstdx[t], b.b_par), writes=(b_xn[k][t],))

    qpipe = Pipe()
    for cg in range(8):
        if cg < 6:
            mask_A(cg // 2, cg % 2)
        if 1 <= cg < 7:
            mask_B((cg - 1) // 2, (cg - 1) % 2)
        wt, bw = b.next_w()
        wv = wt[:, :].rearrange("p (k c) -> p k c", k=16)
        for half in range(2):
            hq = cg * 2 + half
            cbt = None
            if hq >= 12:
                cbt, bcb = b.get("cb")
            for t in range(2):
                tsl = slice(t * 512, (t + 1) * 512)
                ps, bps = b.get("ps")
                b.mm(ps[:, :], bps, [(wv[:, k, half * 128:(half + 1) * 128], xn[:, k, tsl]) for k in range(KC)],
                     reads=[bw] + [b_xn[k][t] for k in range(KC)])
                if hq >= 12:
                    after = None
                    if t == 1:
                        after = (lambda hq=hq, cbt=cbt, bcb=bcb: b.dma_out(cat_d[:, 4 + hq - 12, :], cbt[:, :], rbufs=(bcb,)))
                    qpipe.push(mem_attn_stages(b, ps[:, :], bps, M.memk, M.b_memk, M.memv, M.b_memv, hq - 12, 32, cbt[:, tsl], bcb,
                                               after=after))
                else:
                    stq = {}

                    def QA(ps=ps, bps=bps, stq=stq):
                        sq, bsq = b.get("sq")
                        b.act(sq[:, :], ps[:, :], AF.Square, reads=(bps,), writes=(bsq,))
                        ps2, bps2 = b.get("ps")
                        b.mm(ps2[:, :], bps2, [(b.ones[:, :], sq[:, :])], reads=(bsq, b.b_ones))
                        stq.update(ps2=ps2, bps2=bps2)

                    def QB(ps=ps, bps=bps, stq=stq, hq=hq, t=t):
                        g = hq // 4
                        d = DIL[g]
                        rs, brs = b.get("rstd")
                        b.rstd_from_ss(stq["ps2"][:, :], stq["bps2"], 1.0 / 128, rs[:, :], brs)
                        nl = 512 // d
                        dst = qn[:, hq, :].rearrange("p (r l) -> p l r", r=d)[:, t * nl:(t + 1) * nl, :]
                        extra = (M.b_mem,) if hq < 8 else (M.b_memn,)
                        b.stt(dst, ps[:, :].rearrange("p (l r) -> p l r", r=d), b.pcol(34 + g), rs[:, :].rearrange("p (l r) -> p l r", r=d),
                              ALU.mult, ALU.mult, reads=(bps, brs, b.b_par), writes=(b_qn[hq],) + extra)
                    qpipe.push([QA, QB])
    qpipe.flush()

    batches = []
    for h in range(4):
        for g in range(3):
            d, lg, nq = DIL[g], LG[g], NQ[g]
            upc = lg // nq
            nunits = d * upc
            U = 512 // nq
            for u0 in range(0, nunits, U):
                batches.append(dict(h=h, g=g, u0=u0, first=(u0 == 0), last=(g == 2 and u0 + U >= nunits)))
    cur_kv = {}

    def T1(bt):
        h, g, u0 = bt["h"], bt["g"], bt["u0"]
        d, lg, nq, nkc, nbc = DIL[g], LG[g], NQ[g], NKC[g], NBC[g]
        if bt["first"]:
            kt, bkt = b.get("kt")
            vv, bvv = b.get("vv")
            b.dma_in(kt[:, :NK[g]], kt_d[g][:, h, :], [bkt])
            b.dma_in(vv[:, :NB[g], :], vv_d[g][:, h, :, :], [bvv])
            cur_kv[(h, g)] = (kt, bkt, vv, bvv)
        kt, bkt, vv, bvv = cur_kv[(h, g)]
        hq = g * 4 + h
        upc = lg // nq
        U = 512 // nq
        psP, bpsP = b.get("ps")
        psD, bpsD = b.get("ps")
        units = []
        for ui in range(U):
            u = u0 + ui
            r, j = u // upc, u % upc
            qs = qn[:, hq, r * lg + j * nq: r * lg + (j + 1) * nq]
            kp = kt[:, r * nkc + j * nq: r * nkc + j * nq + 128]
            kd = kt[:, r * nkc + 128 + j * nq: r * nkc + 128 + (j + 1) * nq]
            units.append((r, j, qs, kp, kd))

        def fS(e, units=units, psP=psP, psD=psD, nq=nq):
            for ui, (r, j, qs, kp, kd) in enumerate(units):
                e.matmul(psP[:, ui * nq:(ui + 1) * nq], kp, qs, start=True, stop=True)
                ins = e.matmul(psD[:nq, ui * nq:(ui + 1) * nq], kd, qs, start=True, stop=True)
            return ins
        P.op(P.pe, fS, reads=(bkt, b_qn[hq]), writes=(bpsP, bpsD))
        bt.update(units=units, psP=psP, bpsP=bpsP, psD=psD, bpsD=bpsD, vv=vv, bvv=bvv)

    def T2(bt):
        h, g = bt["h"], bt["g"]
        nq = NQ[g]
        eP, beP = b.get("tf")
        eD, beD = b.get("tf")
        b.act(eP[:, :], bt["psP"][:, :], AF.Exp, reads=(bt["bpsP"],), writes=(beP,), scale=SCALE)
        b.act(eD[:nq, :], bt["psD"][:nq, :], AF.Exp, reads=(bt["bpsD"],), writes=(beD,), scale=SCALE)
        pP, bpP = b.get("pp")
        pD, bpD = b.get("pp")
        for ui, (r, j, qs, kp, kd) in enumerate(bt["units"]):
            csl = slice(ui * nq, (ui + 1) * nq)
            mty = 2 if j == 0 else 1
            b.tt(pP[:, csl], eP[:, csl], masks[g][mty][:, h, :nq], ALU.mult, reads=(beP, b_masks[g][mty]), writes=(bpP,))
            b.tt(pD[:nq, csl], eD[:nq, csl], masks[g][0][:nq, h, :nq], ALU.mult, reads=(beD, b_masks[g][0]), writes=(bpD,),
                 eng=P.pool)
        bt.update(pP=pP, bpP=bpP, pD=pD, bpD=bpD)

    def T3(bt):
        g = bt["g"]
        nq, nbc = NQ[g], NBC[g]
        psN, bpsN = b.get("ps")
        psS, bpsS = b.get("ps")
        pP, pD, vv = bt["pP"], bt["pD"], bt["vv"]

        def fV(e, units=bt["units"], psN=psN, psS=psS, nq=nq, pP=pP, pD=pD, vv=vv, nbc=nbc):
            for ui, (r, j, qs, kp, kd) in enumerate(units):
                csl = slice(ui * nq, (ui + 1) * nq)
                e.matmul(psN[:, csl], vv[:, r * nbc + j, :], pP[:, csl], start=True, stop=False)
                e.matmul(psN[:, csl], vv[:nq, r * nbc + j + 1, :], pD[:nq, csl], start=False, stop=True)
                e.matmul(psS[:, csl], b.ones[:, :], pP[:, csl], start=True, stop=False)
                ins = e.matmul(psS[:, csl], b.ones[:nq, :], pD[:nq, csl], start=False, stop=True)
            return ins
        P.op(P.pe, fV, reads=(bt["bvv"], bt["bpP"], bt["bpD"], b.b_ones), writes=(bpsN, bpsS))
        bt.update(psN=psN, bpsN=bpsN, psS=psS, bpsS=bpsS)

    def T4(bt):
        h, g, u0 = bt["h"], bt["g"], bt["u0"]
        d, lg, nq = DIL[g], LG[g], NQ[g]
        upc = lg // nq
        U = 512 // nq
        psN, bpsN, psS, bpsS = bt["psN"], bt["bpsN"], bt["psS"], bt["bpsS"]
        r0 = u0 // upc
        if d == 1:
            l0 = u0 * nq
            dN = accN[:, l0:l0 + 512]
            dD = accD[:, l0:l0 + 512]
            sN, sS = psN[:, :], psS[:, :]
        else:
            nr = U // upc
            dN = accN[:, :].rearrange("p (l r) -> p r l", r=d)[:, r0:r0 + nr, :]
            dD = accD[:, :].rearrange("p (l r) -> p r l", r=d)[:, r0:r0 + nr, :]
            sN = psN[:, :].rearrange("p (r l) -> p r l", r=nr)
            sS = psS[:, :].rearrange("p (r l) -> p r l", r=nr)
        if g == 0:
            b.copy(dN, sN, reads=(bpsN,), writes=(b_acc,))
            b.copy(dD, sS, reads=(bpsS,), writes=(b_acc,))
        else:
            b.tt(dN, sN, dN, ALU.add, reads=(bpsN, b_acc), writes=(b_acc,))
            b.tt(dD, sS, dD, ALU.add, reads=(bpsS, b_acc), writes=(b_acc,))
        if bt["last"]:
            cbt, bcb = b.get("cb")
            for t in range(2):
                tsl = slice(t * 512, (t + 1) * 512)
                rd, brd = b.get("tf")
                b.recip_act(rd[:, :], accD[:, tsl], reads=(b_acc,), writes=(brd,))
                b.tt(cbt[:, tsl], accN[:, tsl], rd[:, :], ALU.mult, reads=(b_acc, brd), writes=(bcb,))
            b.dma_out(cat_d[:, h, :], cbt[:, :], rbufs=(bcb,))

    nbt = len(batches)
    for s_ in range(nbt + 2):
        if s_ < nbt:
            T1(batches[s_])
        if 0 <= s_ - 1 < nbt:
            T2(batches[s_ - 1])
            T3(batches[s_ - 1])
        if 0 <= s_ - 2 < nbt:
            T4(batches[s_ - 2])
    return b.finish(), b.wtiles


def tiles_B2(l):
    tl = [wt_std("b_w_out", l - 2, 0, 8, [(c, 512)]) for c in range(0, 2048, 512)]
    tl += mlp_tiles(l)
    return tl


def build_B2(l, with_next=True):
    b = Bld(tiles_B2(l), n_ps=6)
    b.pool("rstdp", 4, [128, 512], F32)
    stats = Stats(b)
    nc, P = b.nc, b.P
    xT_d = b.din("xT", [128, KC, T])
    cat_d = b.din("cat", [128, 8, T], BF16)
    yT_d = b.dout("yT", [128, KC, T])
    xs = b.sb("xs", [128, KC, T], F32)
    b_xs = [[Buf() for t in range(2)] for k in range(KC)]
    cat = b.sb("cat", [128, 16, T], BF16)
    b_cat = [[Buf() for t in range(2)] for k in range(16)]
    h1 = b.sb("h1", [128, 16, T], BF16)
    b_h1 = [[Buf() for t in range(2)] for k in range(16)]
    for k in range(8):
        b.dma_in(cat[:, k, :], cat_d[:, k, :], [b_cat[k][0], b_cat[k][1]])
    for k in range(KC):
        b.dma_in(xs[:, k, :], xT_d[:, k, :], [b_xs[k][0], b_xs[k][1]])
    for cg in range(4):
        wt, bw = b.next_w()
        wv = wt[:, :].rearrange("p (k c) -> p k c", k=8)
        for q4 in range(4):
            dc = cg * 4 + q4
            for t in range(2):
                tsl = slice(t * 512, (t + 1) * 512)
                ps, bps = b.get("ps")
                b.mm(ps[:, :], bps, [(wv[:, k, q4 * 128:(q4 + 1) * 128], cat[:, k, tsl]) for k in range(8)],
                     reads=[bw] + [b_cat[k][t] for k in range(8)])
                b.tt(xs[:, dc, tsl], ps[:, :], xs[:, dc, tsl], ALU.add, reads=(bps, b_xs[dc][t]), writes=(b_xs[dc][t],))
                stats.add(xs[:, dc, tsl], b_xs[dc][t], t)
    st_mlp(b, xs, b_xs, cat, b_cat, h1, b_h1, 0, out_d=yT_d, in_rstd=stats.rstd(), out_stats=(stats if with_next else None))
    if with_next:
        xnn_d = b.dout("xnn", [128, KC, T], BF16)
        st_apply_norm(b, xs, b_xs, cat, b_cat, 40, stats.rstd())
        for k in range(KC):
            b.dma_out(xnn_d[:, k, :], cat[:, k, :], rbufs=(b_cat[k][0], b_cat[k][1]))
    return b.finish(), b.wtiles


def _t5_bucket(n):
    n = np.maximum(np.asarray(n, np.int64), 0)
    nf = np.maximum(n, 1).astype(np.float32)
    large = 16 + (np.log(nf / np.float32(16.0)) / np.float32(math.log(2048 / 16)) * np.float32(16.0)).astype(np.int32)
    large = np.minimum(large, 31)
    return np.where(n < 16, n, large)


def _structural():
    oh = np.zeros((33, 6, 256), np.float32)
    for g, d in enumerate(DIL):
        for i in range(256):
            w = i - 127
            if 0 <= w <= 127:
                oh[_t5_bucket(w * d), g * 2 + 0, i] = 1.0
            else:
                oh[32, g * 2 + 0, i] = 1.0
            if -127 <= w <= 0:
                oh[_t5_bucket((w + 128) * d), g * 2 + 1, i] = 1.0
            else:
                oh[32, g * 2 + 1, i] = 1.0
    jm = np.ascontiguousarray(np.eye(128, dtype=np.float32)[::-1])
    return oh, jm


def _kv_layout(kT, vT):
    bf = ml_dtypes.bfloat16
    Kf = np.concatenate([np.asarray(k) for k in kT], axis=2)
    Vf = np.concatenate([np.asarray(v) for v in vT], axis=2)
    outs = [dict() for _ in range(NCORES)]
    for g, d in enumerate(DIL):
        lg, nkc, nbc = LG[g], NKC[g], NBC[g]
        Lf = S // d
        def cm(A):
            A = A[:, g * 4:(g + 1) * 4, :].reshape(128, 4, Lf, d).transpose(0, 1, 3, 2)
            pad = np.zeros((128, 4, d, 128), bf)
            return np.concatenate([pad, A], axis=3)
        Kc, Vc = cm(Kf), cm(Vf)
        for c in range(NCORES):
            ks = Kc[:, :, :, c * lg:c * lg + nkc]
            outs[c][f"kt{g}"] = np.ascontiguousarray(ks.reshape(128, 4, d * nkc))
            vs = Vc[:, :, :, c * lg:c * lg + nkc]
            vp = np.zeros((128, 4, d, nbc * 128), bf)
            vp[:, :, :, :nkc] = vs
            vp = vp.reshape(128, 4, d, nbc, 128).transpose(4, 1, 2, 3, 0)
            outs[c][f"vv{g}"] = np.ascontiguousarray(vp.reshape(128, 4, d * nbc, 128))
            kv = outs[c].setdefault("kval", np.zeros((128, 3), np.float32))
            kv[:, g] = ((c * lg - 128 + np.arange(128)) >= 0).astype(np.float32)
    return outs


def run_B_layer(l, xT, memT, W, kvin, xn_in=None, next_g=None):
    f32 = np.float32
    j = l - 2
    par = np.zeros((128, NPAR), f32)
    par[:, 0:16] = colk(W["norm_mix_g"][l])
    par[:, 16:32] = colk(W["mem_norm_g"][l])
    par[:, 32] = W["mem_q_norm_g"][l]
    par[:, 33] = W["mem_k_norm_g"][l]
    par[:, 34:37] = np.asarray(W["b_q_norm_g"][j], f32).T
    par[:, 255] = EPS
    nb1 = _CACHE.get(("B1", l))
    if nb1 is None:
        nb1 = _CACHE[("B1", l)] = build_B1(l, from_xn=(xn_in is not None))
    wst = pack_weights(nb1[1], W)
    oh, jm = _structural()
    relb = np.concatenate([np.asarray(W["rel_bias"], f32), np.full((1, 12), -30000.0, f32)], axis=0)
    ins = []
    for c in range(NCORES):
        dct = {"memT": memT, "wst": wst, "par": par, "oh": oh, "jm": jm, "relb": relb}
        if xn_in is not None:
            dct["xn"] = xn_in[c]
        else:
            dct["xT"] = xT[c]
        dct.update(kvin[c])
        ins.append(dct)
    r1 = _run(nb1[0], ins, f"B1_{l}")
    par2 = np.zeros((128, NPAR), f32)
    par2[:, 0:16] = colk(W["norm_mlp_g"][l])
    par2[:, 255] = EPS
    if next_g is not None:
        par2[:, 40:56] = colk(next_g)
    nb2 = _CACHE.get(("B2", l))
    if nb2 is None:
        nb2 = _CACHE[("B2", l)] = build_B2(l, with_next=(next_g is not None))
    wst2 = pack_weights(nb2[1], W)
    ins2 = [{"xT": xT[c], "cat": r1[c]["cat"], "wst": wst2, "par": par2} for c in range(NCORES)]
    r2 = _run(nb2[0], ins2, f"B2_{l}")
    xnn = [r2[c]["xnn"] for c in range(NCORES)] if next_g is not None else None
    return [r2[c]["yT"] for c in range(NCORES)], r1, xnn


def kernel(**inputs):
    W = {k: np.asarray(v) for k, v in inputs.items()}
    x = W["x"][0]
    memT = fm(W["mem"][0])
    xT = [fm(x[c * T:(c + 1) * T]) for c in range(NCORES)]
    g = W["norm_mix_g"]
    xT, _, _, _, xnn = run_A_layer(0, xT, memT, W, with_kv=False, next_g=g[1])
    xT, kT, vT, _, xnn = run_A_layer(1, xT, memT, W, with_kv=True, xn_in=xnn, next_g=g[2])
    kvin = _kv_layout(kT, vT)
    xT, _, xnn = run_B_layer(2, xT, memT, W, kvin, xn_in=xnn, next_g=g[3])
    xT, _, xnn = run_B_layer(3, xT, memT, W, kvin, xn_in=xnn, next_g=None)
    out = np.concatenate([unfm(t) for t in xT], axis=0)
    return out.reshape(1, S, D).astype(np.float32)
```

```python
import math
import numpy as np
import ml_dtypes
import concourse.bass as bass
import concourse.mybir as mybir
from concourse.bass_utils import run_bass_kernel_spmd

F32 = mybir.dt.float32
BF16 = mybir.dt.bfloat16
AF = mybir.ActivationFunctionType
ALU = mybir.AluOpType

NCORES = 8
D = 2048
S = 8192
T = S // NCORES
KC = D // 128
DFF = 4 * D
EPS = 1e-6
WT_ELEMS = 4096
NPAR = 256
SCALE = 128 ** -0.5
DIL = (1, 4, 16)


class Buf:
    __slots__ = ("name", "w", "r")

    def __init__(self, name=""):
        self.name = name
        self.w = None
        self.r = []


class Eng:
    def __init__(self, name, sem, is_pe=False):
        self.name = name
        self.sem = sem
        self.count = 0
        self.ops = []
        self.waited = {}
        self.is_pe = is_pe


class Prog:
    N_DMA_SEMS = 12

    def __init__(self, nc):
        self.nc = nc
        self.pe = Eng("pe", nc.alloc_semaphore("s_pe"), is_pe=True)
        self.act = Eng("act", nc.alloc_semaphore("s_act"))
        self.dve = Eng("dve", nc.alloc_semaphore("s_dve"))
        self.pool = Eng("pool", nc.alloc_semaphore("s_pool"))
        self.sp = Eng("sp", None)
        self.engs = [self.pe, self.act, self.dve, self.pool, self.sp]
        self.dma_sems = [nc.alloc_semaphore(f"s_dma{i}") for i in range(self.N_DMA_SEMS)]
        self.dma_cnt = [0] * self.N_DMA_SEMS
        self.dma_last = [None] * self.N_DMA_SEMS
        self.dma_rr = 0

    def _deps(self, reads, writes):
        deps = []
        for b in reads:
            if b.w is not None:
                deps.append(b.w)
        for b in writes:
            if b.w is not None:
                deps.append(b.w)
            deps.extend(b.r)
        return deps

    def _filter(self, eng, deps):
        waits = {}
        for (sem, val) in deps:
            if eng.is_pe and sem is eng.sem:
                continue
            k = id(sem)
            if eng.waited.get(k, 0) >= val:
                continue
            if k not in waits or waits[k][1] < val:
                waits[k] = (sem, val)
        for k, (sem, val) in waits.items():
            eng.waited[k] = val
        return list(waits.values())

    def _update(self, tok, reads, writes):
        for b in writes:
            b.w = tok
            b.r = []
        for b in reads:
            if b not in writes:
                b.r.append(tok)

    def op(self, eng, fn, reads=(), writes=()):
        deps = self._deps(reads, writes)
        waits = self._filter(eng, deps)
        eng.count += 1
        tok = (eng.sem, eng.count)
        eng.ops.append((waits, fn, (eng.sem, 1)))
        self._update(tok, reads, writes)
        return tok

    def dma(self, eng, fn, reads=(), writes=(), n=1):
        k = self.dma_rr
        self.dma_rr = (self.dma_rr + 1) % self.N_DMA_SEMS
        sem = self.dma_sems[k]
        deps = self._deps(reads, writes)
        if self.dma_last[k] is not None:
            deps.append(self.dma_last[k])
        waits = self._filter(eng, deps)
        self.dma_cnt[k] += 16 * n
        tok = (sem, self.dma_cnt[k])
        self.dma_last[k] = tok
        eng.ops.append((waits, (lambda e, fn=fn, sem=sem: fn(e, sem)), None))
        self._update(tok, reads, writes)
        return tok

    def wait_all(self, eng, toks):
        waits = self._filter(eng, list(toks))
        eng.ops.append((waits, None, None))

    def emit(self):
        nc = self.nc

        def run(eng, h):
            for (waits, fn, inc) in eng.ops:
                for (sem, val) in waits:
                    h.wait_ge(sem, val)
                if fn is None:
                    continue
                ins = fn(h)
                if inc is not None:
                    ins.then_inc(inc[0], inc[1])

        with nc.Block() as block:
            @block.tensor
            def _(h):
                run(self.pe, h)

            @block.scalar
            def _(h):
                run(self.act, h)

            @block.vector
            def _(h):
                run(self.dve, h)

            @block.gpsimd
            def _(h):
                run(self.pool, h)

            @block.sync
            def _(h):
                run(self.sp, h)


class Bld:
    def __init__(self, wtiles, n_wslots=4, n_ps=8):
        self.discover = wtiles is None
        if self.discover:
            wtiles = []
        self.nc = bass.Bass("TRN2", target_bir_lowering=False)
        self.P = Prog(self.nc)
        self.pools = {}
        self.rr = {}
        self.out_toks = []
        self.pool("ps", n_ps, [128, 512], F32, psum=True)
        if n_ps < 8:
            self.pool("pstat", 8 - n_ps, [128, 512], F32, psum=True)
        self.wtiles = wtiles
        self.NT = 4096 if self.discover else len(wtiles)
        self.n_wslots = n_wslots
        self.wst_d = self.din("wst", [max(self.NT, 1), 128, WT_ELEMS])
        self.par_d = self.din("par", [128, NPAR])
        self.pool("w", n_wslots, [128, WT_ELEMS], BF16)
        self.par = self.sb("par_sb", [128, NPAR], F32)
        self.b_par = Buf("par")
        self.ones = self.sb("ones", [128, 128], BF16)
        self.b_ones = Buf("ones")
        self.w_next_load = 0
        self.w_cur = 0
        self.pool("tf", 4, [128, 512], F32)
        self.pool("tb", 6, [128, 512], BF16)
        self.pool("sq", 4, [128, 512], BF16)
        self.pool("rstd", 4, [128, 512], F32)
        self.dma_in(self.par[:, :], self.par_d, [self.b_par])
        self.P.op(self.P.pool, lambda e: e.memset(self.ones[:, :], 1.0), writes=(self.b_ones,))
        self._ensure(n_wslots - 1)

    def din(self, name, shape, dt=F32):
        return self.nc.dram_tensor(name, list(shape), dt, kind="ExternalInput").ap()

    def dout(self, name, shape, dt=F32):
        return self.nc.dram_tensor(name, list(shape), dt, kind="ExternalOutput").ap()

    def sb(self, name, shape, dt):
        return self.nc.alloc_sbuf_tensor("sb_" + name, list(shape), dt)

    def pool(self, name, n, shape, dt, psum=False):
        if psum:
            lst = [(self.nc.alloc_psum_tensor(f"pp_{name}{i}", list(shape), dt), Buf(f"{name}{i}")) for i in range(n)]
        else:
            lst = [(self.nc.alloc_sbuf_tensor(f"pl_{name}{i}", list(shape), dt), Buf(f"{name}{i}")) for i in range(n)]
        self.pools[name] = lst
        self.rr[name] = 0

    def get(self, name):
        lst = self.pools[name]
        i = self.rr[name]
        self.rr[name] = (i + 1) % len(lst)
        return lst[i]

    def dma_in(self, dst, src, wbufs, eng=None, rbufs=()):
        eng = eng or self.P.sp
        return self.P.dma(eng, lambda e, s: e.dma_start(out=dst, in_=src).then_inc(s, 16), reads=rbufs, writes=wbufs)

    def dma_out(self, dst, src, rbufs):
        tok = self.P.dma(self.P.sp, lambda e, s: e.dma_start(out=dst, in_=src).then_inc(s, 16), reads=rbufs)
        self.out_toks.append(tok)
        return tok

    def finish(self):
        self.P.wait_all(self.P.sp, self.out_toks)
        self.P.emit()
        return self.nc

    def _ensure(self, upto):
        while self.w_next_load <= min(upto, self.NT - 1):
            i = self.w_next_load
            wt, bw = self.pools["w"][i % self.n_wslots]
            self.P.dma(self.P.pool, (lambda e, s, i=i, wt=wt: e.dma_start(out=wt[:, :], in_=self.wst_d[i]).then_inc(s, 16)),
                       writes=(bw,))
            self.w_next_load += 1

    def next_w(self, desc=None):
        i = self.w_cur
        if desc is not None:
            if self.discover:
                self.wtiles.append(desc)
            else:
                assert self.wtiles[i] == desc, (i, self.wtiles[i], desc)
        assert i < self.NT, "weight stream exhausted"
        self.w_cur += 1
        self._ensure(i + self.n_wslots - 1)
        return self.pools["w"][i % self.n_wslots]

    def mm(self, out_ap, bps, pairs, reads):
        n = len(pairs)

        def f(e):
            for i, (l, r) in enumerate(pairs):
                ins = e.matmul(out_ap, l, r, start=(i == 0), stop=(i == n - 1))
            return ins
        return self.P.op(self.P.pe, f, reads=reads, writes=(bps,))

    def act(self, out, in_, func, reads, writes, bias=None, scale=None):
        kw = {}
        if bias is not None:
            kw["bias"] = bias
        if scale is not None:
            kw["scale"] = scale
        return self.P.op(self.P.act, lambda e: e.activation(out=out, in_=in_, func=func, **kw), reads=reads, writes=writes)

    def tt(self, out, a, b_, op, reads, writes, eng=None):
        eng = eng or self.P.dve
        return self.P.op(eng, lambda e: e.tensor_tensor(out=out, in0=a, in1=b_, op=op), reads=reads, writes=writes)

    def ts(self, out, a, s1, s2, op0, op1, reads, writes, eng=None):
        eng = eng or self.P.dve
        if s2 is None:
            return self.P.op(eng, lambda e: e.tensor_scalar(out=out, in0=a, scalar1=s1, scalar2=None, op0=op0), reads=reads, writes=writes)
        return self.P.op(eng, lambda e: e.tensor_scalar(out=out, in0=a, scalar1=s1, scalar2=s2, op0=op0, op1=op1), reads=reads, writes=writes)

    def stt(self, out, a, sc, b_, op0, op1, reads, writes, eng=None):
        eng = eng or self.P.dve
        return self.P.op(eng, lambda e: e.scalar_tensor_tensor(out=out, in0=a, scalar=sc, in1=b_, op0=op0, op1=op1), reads=reads, writes=writes)

    def recip(self, out, in_, reads, writes):
        return self.P.op(self.P.dve, lambda e: e.reciprocal(out=out, in_=in_), reads=reads, writes=writes)

    def copy(self, out, in_, reads, writes, eng=None):
        eng = eng or self.P.act
        if eng is self.P.act:
            return self.P.op(eng, lambda e: e.copy(out=out, in_=in_), reads=reads, writes=writes)
        return self.P.op(eng, lambda e: e.tensor_copy(out=out, in_=in_), reads=reads, writes=writes)

    def pcol(self, c, n=1):
        return self.par[:, c:c + n]

    def rstd_from_ss(self, ss_ap, b_ss, inv_count, out_ap, out_b, ncols=512):
        tf, btf = self.get("tf")
        self.act(tf[:, :ncols], ss_ap, AF.Ln, reads=(b_ss, self.b_par), writes=(btf,), bias=self.pcol(255), scale=inv_count)
        self.act(out_ap, tf[:, :ncols], AF.Exp, reads=(btf,), writes=(out_b,), scale=-0.5)

    def recip_act(self, out_ap, in_ap, reads, writes, nrows=128, ncols=512):
        tf, btf = self.get("tf")
        self.act(tf[:nrows, :ncols], in_ap, AF.Ln, reads=reads, writes=(btf,))
        self.act(out_ap, tf[:nrows, :ncols], AF.Exp, reads=(btf,), writes=writes, scale=-1.0)


def wt_std(key, l, row0, nk, cols):
    return ("std", key, l, row0, nk, tuple(cols))


def pack_weights(tiles, W):
    out = np.zeros((max(len(tiles), 1), 128, WT_ELEMS), np.float32)
    for i, tl in enumerate(tiles):
        if tl[0] == "std":
            _, key, l, row0, nk, cols = tl
            w = W[key][l] if l is not None else W[key]
            blk = np.concatenate([w[row0:row0 + nk * 128, c0:c0 + n] for (c0, n) in cols], axis=1)
            ncol = blk.shape[1]
            out[i, :, :nk * ncol] = blk.reshape(nk, 128, ncol).transpose(1, 0, 2).reshape(128, nk * ncol)
        elif tl[0] == "gates":
            _, l = tl
            g = np.concatenate([W["a_gate_r_w"][l], W["a_gate_i_w"][l]], axis=0)
            out[i, :, :24 * 128] = g.transpose(1, 0, 2).reshape(128, 24 * 128)
    return out


def mlp_tiles(l):
    tl = []
    for q in range(4):
        for c in range(0, 2048, 256):
            tl.append(wt_std("mlp_w1", l, 0, 16, [(q * 2048 + c, 256)]))
        for c in range(0, 2048, 256):
            tl.append(wt_std("mlp_w2", l, q * 2048, 16, [(c, 256)]))
    return tl


def memkv_tiles(l):
    return [wt_std("mem_w_kv", l, 0, 16, [(c, 256)]) for c in range(0, 1024, 256)]


class Stats:
    def __init__(self, b):
        self.b = b
        self.ps = [b.pools["pstat"][t] for t in range(2)]
        self.n = [0, 0]
        self.pipe = Pipe()
        b.pool("sqs", 8, [128, 512], BF16)

    def add(self, src_ap, b_src, t):
        b = self.b
        k = self.n[t]
        self.n[t] += 1
        ps, bps = self.ps[t]
        st = {}

        def A():
            sq, bsq = b.get("sqs")
            b.act(sq[:, :], src_ap, AF.Square, reads=(b_src,), writes=(bsq,))
            st.update(sq=sq, bsq=bsq)

        def nop():
            pass

        def B():
            sq, bsq = st["sq"], st["bsq"]

            def f(e, k=k, sq=sq, ps=ps):
                return e.matmul(ps[:, :], b.ones[:, :], sq[:, :], start=(k == 0), stop=(k == KC - 1))
            b.P.op(b.P.pe, f, reads=(bsq, b.b_ones), writes=(bps,))
        self.pipe.push([nop, A, nop, nop, B])

    def rstd(self):
        b = self.b
        assert self.n == [KC, KC]
        self.pipe.flush()
        out = []
        for t in range(2):
            rs, brs = b.get("rstdp")
            b.rstd_from_ss(self.ps[t][0][:, :], self.ps[t][1], 1.0 / D, rs[:, :], brs)
            out.append((rs, brs))
        self.n = [0, 0]
        return out


def st_apply_norm(b, xs, b_xs, xn, b_xn, gcol, rstds):
    for t in range(2):
        tsl = slice(t * 512, (t + 1) * 512)
        rs, brs = rstds[t]
        for k in range(KC):
            b.stt(xn[:, k, tsl], xs[:, k, tsl], b.pcol(gcol + k), rs[:, :], ALU.mult, ALU.mult,
                  reads=(b_xs[k][t], brs, b.b_par), writes=(b_xn[k][t],))


def st_norm_resident(b, xs, b_xs, xn, b_xn, gcol):
    for t in range(2):
        tsl = slice(t * 512, (t + 1) * 512)
        ps, bps = b.get("ps")
        for k in range(KC):
            sq, bsq = b.get("sq")
            b.act(sq[:, :], xs[:, k, tsl], AF.Square, reads=(b_xs[k][t],), writes=(bsq,))

            def f(e, k=k, sq=sq, ps=ps):
                return e.matmul(ps[:, :], b.ones[:, :], sq[:, :], start=(k == 0), stop=(k == KC - 1))
            b.P.op(b.P.pe, f, reads=(bsq, b.b_ones), writes=(bps,))
        rs, brs = b.get("rstd")
        b.rstd_from_ss(ps[:, :], bps, 1.0 / D, rs[:, :], brs)
        for k in range(KC):
            b.stt(xn[:, k, tsl], xs[:, k, tsl], b.pcol(gcol + k), rs[:, :], ALU.mult, ALU.mult,
                  reads=(b_xs[k][t], brs, b.b_par), writes=(b_xn[k][t],))


def st_mlp(b, xs, b_xs, xn, b_xn, h1, b_h1, gcol, out_d=None, in_rstd=None, out_stats=None):
    if in_rstd is not None:
        st_apply_norm(b, xs, b_xs, xn, b_xn, gcol, in_rstd)
    else:
        st_norm_resident(b, xs, b_xs, xn, b_xn, gcol)
    for q in range(4):
        for cg in range(8):
            wt, bw = b.next_w()
            wv = wt[:, :].rearrange("p (k c) -> p k c", k=16)
            for half in range(2):
                fc = cg * 2 + half
                for t in range(2):
                    tsl = slice(t * 512, (t + 1) * 512)
                    ps, bps = b.get("ps")
                    b.mm(ps[:, :], bps, [(wv[:, k, half * 128:(half + 1) * 128], xn[:, k, tsl]) for k in range(KC)],
                         reads=[bw] + [b_xn[k][t] for k in range(KC)])
                    tf, btf = b.get("tf")
                    b.act(tf[:, :], ps[:, :], AF.Relu, reads=(bps,), writes=(btf,))
                    b.tt(h1[:, fc, tsl], tf[:, :], tf[:, :], ALU.mult, reads=(btf,), writes=(b_h1[fc][t],), eng=b.P.pool)
        for cg in range(8):
            wt, bw = b.next_w()
            wv = wt[:, :].rearrange("p (k c) -> p k c", k=16)
            for half in range(2):
                dc = cg * 2 + half
                for t in range(2):
                    tsl = slice(t * 512, (t + 1) * 512)
                    ps, bps = b.get("ps")
                    b.mm(ps[:, :], bps, [(wv[:, k, half * 128:(half + 1) * 128], h1[:, k, tsl]) for k in range(16)],
                         reads=[bw] + [b_h1[k][t] for k in range(16)])
                    b.tt(xs[:, dc, tsl], ps[:, :], xs[:, dc, tsl], ALU.add, reads=(bps, b_xs[dc][t]), writes=(b_xs[dc][t],))
                    if q == 3 and out_stats is not None:
                        out_stats.add(xs[:, dc, tsl], b_xs[dc][t], t)
                if q == 3 and out_d is not None:
                    b.dma_out(out_d[:, dc, :], xs[:, dc, :], rbufs=(b_xs[dc][0], b_xs[dc][1]))


def st_mem_rstd(b, memT, b_mem, rstd_m, b_rstdm):
    ps, bps = b.get("ps")
    for k in range(KC):
        sq, bsq = b.get("sq")
        b.act(sq[:, :256], memT[:, k, :], AF.Square, reads=(b_mem,), writes=(bsq,))

        def f(e, k=k, sq=sq, ps=ps):
            return e.matmul(ps[:, :256], b.ones[:, :], sq[:, :256], start=(k == 0), stop=(k == KC - 1))
        b.P.op(b.P.pe, f, reads=(bsq, b.b_ones), writes=(bps,))
    b.rstd_from_ss(ps[:, :256], bps, 1.0 / D, rstd_m[:, :], b_rstdm, ncols=256)


def st_memkv(b, memT, b_mem, rstd_m, b_rstdm, memn, b_memn, memk, b_memk, memv, b_memv, gcol_mem, col_kg):
    for k in range(KC):
        b.stt(memn[:, k, :], memT[:, k, :], b.pcol(gcol_mem + k), rstd_m[:, :], ALU.mult, ALU.mult,
              reads=(b_mem, b_rstdm, b.b_par), writes=(b_memn,))
    for hp in range(2):
        wt, bw = b.next_w()
        wv = wt[:, :].rearrange("p (k c) -> p k c", k=16)
        for hh in range(2):
            h = hp * 2 + hh
            hs = slice(hh * 128, hh * 128 + 128)
            ps, bps = b.get("ps")
            b.mm(ps[:, :256], bps, [(wv[:, k, hs], memn[:, k, :]) for k in range(KC)], reads=(bw, b_memn))
            sq, bsq = b.get("sq")
            b.act(sq[:, :256], ps[:, :256], AF.Square, reads=(bps,), writes=(bsq,))
            ps2, bps2 = b.get("ps")
            b.mm(ps2[:, :256], bps2, [(b.ones[:, :], sq[:, :256])], reads=(bsq, b.b_ones))
            rs, brs = b.get("rstd")
            b.rstd_from_ss(ps2[:, :256], bps2, 1.0 / 128, rs[:, :256], brs, ncols=256)
            b.stt(memk[:, h, :], ps[:, :256], b.pcol(col_kg), rs[:, :256], ALU.mult, ALU.mult,
                  reads=(bps, brs, b.b_par), writes=(b_memk,))
    for vh in range(2):
        wt, bw = b.next_w()
        wv = wt[:, :].rearrange("p (k c) -> p k c", k=16)
        for mc in range(2):
            ps, bps = b.get("ps")
            b.mm(ps[:, :256], bps, [(memn[:, k, mc * 128:(mc + 1) * 128], wv[:, k, :]) for k in range(KC)],
                 reads=(bw, b_memn))
            b.copy(memv[:, mc, vh * 256:(vh + 1) * 256], ps[:, :256], reads=(bps,), writes=(b_memv,))


class Pipe:
    def __init__(self):
        self.items = []

    def _advance(self, skip_new=False):
        for it in reversed(self.items):
            if it:
                it.pop(0)()
        self.items = [it for it in self.items if it]

    def push(self, stages):
        self.items.append(list(stages))
        self._advance()

    def flush(self):
        while self.items:
            self._advance()


def mem_attn_stages(b, mq_ps, b_mq, memk, b_memk, memv, b_memv, h, col_qg, out_ap, b_out, after=None):
    st = {}

    def A():
        sq, bsq = b.get("sq")
        b.act(sq[:, :], mq_ps, AF.Square, reads=(b_mq,), writes=(bsq,))
        ps2, bps2 = b.get("ps")
        b.mm(ps2[:, :], bps2, [(b.ones[:, :], sq[:, :])], reads=(bsq, b.b_ones))
        st.update(ps2=ps2, bps2=bps2)

    def B():
        rs, brs = b.get("rstd")
        b.rstd_from_ss(st["ps2"][:, :], st["bps2"], 1.0 / 128, rs[:, :], brs)
        qn, bqn = b.get("tb")
        b.stt(qn[:, :], mq_ps, b.pcol(col_qg), rs[:, :], ALU.mult, ALU.mult, reads=(b_mq, brs, b.b_par), writes=(bqn,))
        pts = []
        for mc in range(2):
            ps3, bps3 = b.get("ps")
            b.mm(ps3[:, :], bps3, [(memk[:, h, mc * 128:(mc + 1) * 128], qn[:, :])], reads=(b_memk, bqn))
            pt, bpt = b.get("tb")
            b.act(pt[:, :], ps3[:, :], AF.Exp, reads=(bps3,), writes=(bpt,), scale=SCALE)
            pts.append((pt, bpt))
        st.update(pts=pts)

    def C():
        pts = st["pts"]
        pn, bpn = b.get("ps")
        b.mm(pn[:, :], bpn, [(memv[:, mc, h * 128:(h + 1) * 128], pts[mc][0][:, :]) for mc in range(2)],
             reads=(b_memv, pts[0][1], pts[1][1]))
        pd, bpd = b.get("ps")
        b.mm(pd[:, :], bpd, [(b.ones[:, :], pts[mc][0][:, :]) for mc in range(2)], reads=(b.b_ones, pts[0][1], pts[1][1]))
        rd, brd = b.get("tf")
        b.recip_act(rd[:, :], pd[:, :], reads=(bpd,), writes=(brd,))
        b.tt(out_ap, pn[:, :], rd[:, :], ALU.mult, reads=(bpn, brd), writes=(b_out,))
        if after is not None:
            after()
    return [A, B, C]


class MemState:
    def __init__(self, b, alias=None):
        if alias is None:
            self.memT = b.sb("memT", [128, KC, 256], F32)
            self.memn = b.sb("memn", [128, KC, 256], BF16)
        else:
            self.memT = alias[:, 0:8, :].bitcast(F32).rearrange("p a (b c) -> p (a b) c", c=256)
            self.memn = alias[:, 8:12, :].rearrange("p a (b c) -> p (a b) c", c=256)
        self.b_mem = Buf()
        self.b_memn = Buf()
        self.memk = b.sb("memk", [128, 4, 256], BF16)
        self.b_memk = Buf()
        self.memv = b.sb("memv", [128, 2, 512], BF16)
        self.b_memv = Buf()
        self.rstd_m = b.sb("rstd_m", [128, 256], F32)
        self.b_rstdm = Buf()


def tiles_A1(l):
    tl = list(memkv_tiles(l))
    tl += [wt_std("a_w_in", l, 0, 16, [(3072 + c, 256)]) for c in (0, 256)]
    tl.append(("gates", l))
    for n in range(12):
        tl.append(wt_std("a_w_in", l, 0, 16, [(n * 128, 128), (1536 + n * 128, 128)]))
    return tl


def build_A1(l, from_xn=False):
    b = Bld(tiles_A1(l))
    nc, P = b.nc, b.P
    if from_xn:
        xn_d = b.din("xn", [128, KC, T], BF16)
        xnh_d = b.din("xnh", [128, KC, 4], BF16)
    else:
        xT_d = b.din("xT", [128, KC, T])
        xh_d = b.din("xh", [128, KC, 4])
    memT_d = b.din("memT", [128, KC, 256])
    cat_d = b.dout("cat", [128, 16, T], BF16)
    q_d = b.dout("qq", [128, 12, T], BF16)
    car_d = b.dout("carry", [128, 24])

    b.pool("xt", 4, [128, 512], F32)
    b.pool("ub", 3, [128, 4 + T], F32)
    b.pool("gb", 3, [128, T], F32)
    b.pool("xc", 2, [128, T], F32)
    b.pool("rb", 2, [128, T], F32)
    b.pool("ib", 2, [128, T], F32)
    b.pool("s3", 5, [128, T], F32)
    xn = b.sb("xn", [128, KC, T], BF16)
    b_xn = [[Buf() for t in range(2)] for k in range(KC)]
    xnh = b.sb("xnh", [128, KC, 4], BF16)
    b_xnh = Buf()
    xh = b.sb("xh", [128, KC, 4], F32)
    b_xh = Buf()
    b.pool("cb", 6, [128, T], BF16)
    M = MemState(b, alias=xn)
    rstd_x = [b.sb(f"rstd_x{t}", [128, 512], F32) for t in range(2)]
    b_rstdx = [Buf(), Buf()]
    rstd_h = b.sb("rstd_h", [128, 4], F32)
    b_rstdh = Buf()
    carry = b.sb("carry", [128, 24], F32)
    b_carry = Buf()
    gw = b.sb("gw", [128, 24, 128], BF16)
    b_gw = Buf()
    nsp = b.sb("nsp", [128, 12], F32)
    b_nsp = Buf()
    zeros = b.sb("zeros", [128, T], F32)
    b_zeros = Buf()
    sml = [b.sb(f"sml{i}", [128, 12], F32) for i in range(6)]
    b_sml = [Buf() for i in range(6)]

    b.dma_in(M.memT[:, :, :], memT_d, [M.b_mem])
    if not from_xn:
        b.dma_in(xh[:, :, :], xh_d, [b_xh])
    P.op(P.pool, lambda e: e.memset(zeros[:, :], 0.0), writes=(b_zeros,))

    st_mem_rstd(b, M.memT, M.b_mem, M.rstd_m, M.b_rstdm)
    st_memkv(b, M.memT, M.b_mem, M.rstd_m, M.b_rstdm, M.memn, M.b_memn, M.memk, M.b_memk, M.memv, M.b_memv, 16, 33)

    if from_xn:
        b.dma_in(xnh[:, :, :], xnh_d, [b_xnh])
        for k in list(range(12, KC)) + list(range(12)):
            extra = (M.b_mem,) if k < 8 else ((M.b_memn,) if k < 12 else ())
            b.dma_in(xn[:, k, :], xn_d[:, k, :], [b_xn[k][0], b_xn[k][1]] + list(extra))
    else:
        pss = [b.get("ps"), b.get("ps")]
        for k in range(KC):
            for t in range(2):
                xt, bxt = b.get("xt")
                b.dma_in(xt[:, :], xT_d[:, k, t * 512:(t + 1) * 512], [bxt])
                sq, bsq = b.get("sq")
                b.act(sq[:, :], xt[:, :], AF.Square, reads=(bxt,), writes=(bsq,))

                def f(e, k=k, sq=sq, ps=pss[t][0]):
                    return e.matmul(ps[:, :], b.ones[:, :], sq[:, :], start=(k == 0), stop=(k == KC - 1))
                P.op(P.pe, f, reads=(bsq, b.b_ones), writes=(pss[t][1],))
        for t in range(2):
            b.rstd_from_ss(pss[t][0][:, :], pss[t][1], 1.0 / D, rstd_x[t][:, :], b_rstdx[t])
        psh, bpsh = b.get("ps")
        for k in range(KC):
            sq, bsq = b.get("sq")
            b.act(sq[:, :4], xh[:, k, :], AF.Square, reads=(b_xh,), writes=(bsq,))

            def f(e, k=k, sq=sq, psh=psh):
                return e.matmul(psh[:, :4], b.ones[:, :], sq[:, :4], start=(k == 0), stop=(k == KC - 1))
            P.op(P.pe, f, reads=(bsq, b.b_ones), writes=(bpsh,))
        b.rstd_from_ss(psh[:, :4], bpsh, 1.0 / D, rstd_h[:, :], b_rstdh, ncols=4)
        for k in range(KC):
            b.stt(xnh[:, k, :], xh[:, k, :], b.pcol(k), rstd_h[:, :], ALU.mult, ALU.mult,
                  reads=(b_xh, b_rstdh, b.b_par), writes=(b_xnh,))
        for k in range(KC):
            for t in range(2):
                tsl = slice(t * 512, (t + 1) * 512)
                xt, bxt = b.get("xt")
                b.dma_in(xt[:, :], xT_d[:, k, tsl], [bxt])
                extra = (M.b_mem,) if k < 8 else ((M.b_memn,) if k < 12 else ())
                b.stt(xn[:, k, tsl], xt[:, :], b.pcol(k), rstd_x[t][:, :], ALU.mult, ALU.mult,
                      reads=(bxt, b_rstdx[t], b.b_par), writes=(b_xn[k][t],) + extra)

    pipe = Pipe()
    for hp in range(2):
        wt, bw = b.next_w()
        wv = wt[:, :].rearrange("p (k c) -> p k c", k=16)
        for hh in range(2):
            h = hp * 2 + hh
            cbt, bcb = b.get("cb")
            for t in range(2):
                tsl = slice(t * 512, (t + 1) * 512)
                ps, bps = b.get("ps")
                b.mm(ps[:, :], bps, [(wv[:, k, hh * 128:(hh + 1) * 128], xn[:, k, tsl]) for k in range(KC)],
                     reads=[bw] + [b_xn[k][t] for k in range(KC)])
                after = None
                if t == 1:
                    after = (lambda h=h, cbt=cbt, bcb=bcb: b.dma_out(cat_d[:, 12 + h, :], cbt[:, :], rbufs=(bcb,)))
                pipe.push(mem_attn_stages(b, ps[:, :], bps, M.memk, M.b_memk, M.memv, M.b_memv, h, 32, cbt[:, tsl], bcb, after=after))
    pipe.flush()

    wt, bw = b.next_w()
    b.copy(gw[:, :, :], wt[:, :24 * 128].rearrange("p (g d) -> p g d", g=24), reads=(bw,), writes=(b_gw,), eng=P.pool)

    lam = b.par[:, 124:136]
    s0, s1, s2, s3, s4, s5 = sml
    B0, B1, B2, B3, B4, B5 = b_sml
    b.ts(s0[:, :], lam, -1.0, None, ALU.mult, None, reads=(b.b_par,), writes=(B0,))
    b.tt(s0[:, :], s0[:, :], lam, ALU.max, reads=(B0, b.b_par), writes=(B0,))
    b.act(s1[:, :], s0[:, :], AF.Exp, reads=(B0,), writes=(B1,), scale=-1.0)
    b.ts(s2[:, :], s1[:, :], 2.0, None, ALU.add, None, reads=(B1,), writes=(B2,))
    b.recip(s3[:, :], s2[:, :], reads=(B2,), writes=(B3,))
    b.tt(s2[:, :], s1[:, :], s3[:, :], ALU.mult, reads=(B1, B3), writes=(B2,))
    b.tt(s3[:, :], s2[:, :], s2[:, :], ALU.mult, reads=(B2,), writes=(B3,))
    b.ts(s4[:, :], s3[:, :], 1.0 / 11, 1.0 / 9, ALU.mult, ALU.add, reads=(B3,), writes=(B4,))
    for cst in (1.0 / 7, 1.0 / 5, 1.0 / 3, 1.0):
        b.tt(s4[:, :], s4[:, :], s3[:, :], ALU.mult, reads=(B4, B3), writes=(B4,))
        b.ts(s4[:, :], s4[:, :], cst, None, ALU.add, None, reads=(B4,), writes=(B4,))
    b.tt(s4[:, :], s4[:, :], s2[:, :], ALU.mult, reads=(B4, B2), writes=(B4,))
    b.ts(s5[:, :], lam, -1.0, 0.0, ALU.mult, ALU.max, reads=(b.b_par,), writes=(B5,))
    b.stt(s5[:, :], s4[:, :], 2.0, s5[:, :], ALU.mult, ALU.add, reads=(B4, B5), writes=(B5,))
    b.ts(nsp[:, :], s5[:, :], -8.0, None, ALU.mult, None, reads=(B5,), writes=(b_nsp,))

    stt_ = {}

    def S1(n):
        wt, bw = b.next_w()
        wv = wt[:, :].rearrange("p (k c) -> p k c", k=16)
        ub, bub = b.get("ub")
        gb, bgb = b.get("gb")
        psh, bpsh = b.get("ps")
        b.mm(psh[:, :4], bpsh, [(wv[:, k, 0:128], xnh[:, k, :]) for k in range(KC)], reads=(bw, b_xnh))
        b.copy(ub[:, 0:4], psh[:, :4], reads=(bpsh,), writes=(bub,))
        for t in range(2):
            tsl = slice(t * 512, (t + 1) * 512)
            ps, bps = b.get("ps")
            b.mm(ps[:, :], bps, [(wv[:, k, 0:128], xn[:, k, tsl]) for k in range(KC)],
                 reads=[bw] + [b_xn[k][t] for k in range(KC)])
            b.copy(ub[:, 4 + t * 512:4 + (t + 1) * 512], ps[:, :], reads=(bps,), writes=(bub,))
            pg, bpg = b.get("ps")
            b.mm(pg[:, :], bpg, [(wv[:, k, 128:256], xn[:, k, tsl]) for k in range(KC)],
                 reads=[bw] + [b_xn[k][t] for k in range(KC)])
            b.act(gb[:, tsl], pg[:, :], AF.Gelu_apprx_tanh, reads=(bpg,), writes=(bgb,))
        stt_[n] = dict(ub=ub, bub=bub, gb=gb, bgb=bgb)

    def S2(n):
        d_ = stt_[n]
        ub, bub = d_["ub"], d_["bub"]
        xc, bxc = b.get("xc")
        cw = 40 + n * 4
        b.ts(xc[:, :], ub[:, 1:1 + T], b.pcol(cw), b.pcol(88 + n), ALU.mult, ALU.add, reads=(bub, b.b_par), writes=(bxc,), eng=P.pool)
        for j in range(1, 4):
            b.stt(xc[:, :], ub[:, 1 + j:1 + j + T], b.pcol(cw + j), xc[:, :], ALU.mult, ALU.add,
                  reads=(bub, b.b_par, bxc), writes=(bxc,))
        xcb = [b.get("tb"), b.get("tb")]
        for t in range(2):
            b.copy(xcb[t][0][:, :], xc[:, t * 512:(t + 1) * 512], reads=(bxc,), writes=(xcb[t][1],))
        rb, brb = b.get("rb")
        ib, bib = b.get("ib")
        for t in range(2):
            tsl = slice(t * 512, (t + 1) * 512)
            pr, bpr = b.get("ps")
            b.mm(pr[:, :], bpr, [(gw[:, n, :], xcb[t][0][:, :])], reads=(b_gw, xcb[t][1]))
            b.act(rb[:, tsl], pr[:, :], AF.Sigmoid, reads=(bpr, b.b_par), writes=(brb,), bias=b.pcol(100 + n))
            pi_, bpi = b.get("ps")
            b.mm(pi_[:, :], bpi, [(gw[:, 12 + n, :], xcb[t][0][:, :])], reads=(b_gw, xcb[t][1]))
            b.act(ib[:, tsl], pi_[:, :], AF.Sigmoid, reads=(bpi, b.b_par), writes=(bib,), bias=b.pcol(112 + n))
        d_.update(xc=xc, bxc=bxc, rb=rb, brb=brb, ib=ib, bib=bib)

    def S3(n):
        d_ = stt_.pop(n)
        gb, bgb, xc, bxc, ab, bab, ib, bib = d_["gb"], d_["bgb"], d_["xc"], d_["bxc"], d_["rb"], d_["brb"], d_["ib"], d_["bib"]
        b.act(ab[:, :], ab[:, :], AF.Exp, reads=(bab, b_nsp), writes=(bab,), scale=nsp[:, n:n + 1])
        bb_, bbb = b.get("s3")
        b.act(bb_[:, :], ab[:, :], AF.Square, reads=(bab,), writes=(bbb,))
        b.act(bb_[:, :], bb_[:, :], AF.Sqrt, reads=(bbb, b.b_par), writes=(bbb,), scale=-1.0, bias=b.pcol(254))
        b.tt(ib[:, :], ib[:, :], xc[:, :], ALU.mult, reads=(bib, bxc), writes=(bib,), eng=P.pool)
        b.tt(bb_[:, :], bb_[:, :], ib[:, :], ALU.mult, reads=(bbb, bib), writes=(bbb,))
        hb, bhb = b.get("s3")
        P.op(P.dve, lambda e, hb=hb, ab=ab, bb_=bb_: e.tensor_tensor_scan(out=hb[:, :], data0=ab[:, :], data1=bb_[:, :], initial=0.0,
                                                                            op0=ALU.mult, op1=ALU.add),
             reads=(bab, bbb), writes=(bhb,))
        Ab, bAb = b.get("s3")
        P.op(P.dve, lambda e, Ab=Ab, ab=ab: e.tensor_tensor_scan(out=Ab[:, :], data0=ab[:, :], data1=zeros[:, :], initial=1.0,
                                                                  op0=ALU.mult, op1=ALU.add),
             reads=(bab, b_zeros), writes=(bAb,))
        b.copy(carry[:, n:n + 1], hb[:, T - 1:T], reads=(bhb,), writes=(b_carry,), eng=P.dve)
        b.copy(carry[:, 12 + n:13 + n], Ab[:, T - 1:T], reads=(bAb,), writes=(b_carry,), eng=P.dve)
        cbt, bcb = b.get("cb")
        b.tt(cbt[:, :], hb[:, :], gb[:, :], ALU.mult, reads=(bhb, bgb), writes=(bcb,), eng=P.pool)
        b.dma_out(cat_d[:, n, :], cbt[:, :], rbufs=(bcb,))
        cbq, bcq = b.get("cb")
        b.tt(cbq[:, :], Ab[:, :], gb[:, :], ALU.mult, reads=(bAb, bgb), writes=(bcq,), eng=P.pool)
        b.dma_out(q_d[:, n, :], cbq[:, :], rbufs=(bcq,))

    for s_ in range(12 + 2):
        if s_ < 12:
            S1(s_)
        if 0 <= s_ - 1 < 12:
            S2(s_ - 1)
        if 0 <= s_ - 2 < 12:
            S3(s_ - 2)

    b.dma_out(car_d, carry[:, :], rbufs=(b_carry,))
    return b.finish(), b.wtiles


def tiles_A2(l, with_kv):
    tl = [wt_std("a_w_out", l, 0, 16, [(c, 256)]) for c in range(0, 2048, 256)]
    tl += mlp_tiles(l)
    if with_kv:
        tl += [wt_std("kv_w", None, 0, 16, [(c, 256)]) for c in range(0, 3072, 256)]
    return tl


def build_A2(l, with_kv, with_next=True):
    b = Bld(tiles_A2(l, with_kv), n_ps=6)
    b.pool("rstdp", 4, [128, 512], F32)
    stats = Stats(b)
    nc, P = b.nc, b.P
    xT_d = b.din("xT", [128, KC, T])
    cat_d = b.din("cat", [128, 16, T], BF16)
    q_d = b.din("qq", [128, 12, T], BF16)
    car_d = b.din("carr", [128, 8, 24])
    sel_d = b.din("sel", [128, 8])
    yT_d = b.dout("yT", [128, KC, T])
    xs = b.sb("xs", [128, KC, T], F32)
    b_xs = [[Buf() for t in range(2)] for k in range(KC)]
    cat = b.sb("cat", [128, 16, T], BF16)
    b_cat = [[Buf() for t in range(2)] for k in range(16)]
    h1 = b.sb("h1", [128, 16, T], BF16)
    b_h1 = [[Buf() for t in range(2)] for k in range(16)]
    carr = b.sb("carr", [128, 8, 24], F32)
    b_carr = Buf()
    sel = b.sb("sel", [128, 8], F32)
    b_sel = Buf()
    cst = [b.sb(f"cst{i}", [128, 12], F32) for i in range(2)]
    b_cst = [Buf(), Buf()]
    hin = b.sb("hin", [128, 12], F32)
    b_hin = Buf()
    tmp12 = b.sb("tmp12", [128, 12], F32)
    b_tmp12 = Buf()

    b.dma_in(carr[:, :, :], car_d, [b_carr])
    b.dma_in(sel[:, :], sel_d, [b_sel])
    for k in range(16):
        if k < 12:
            b.dma_in(h1[:, k, :], q_d[:, k, :], [b_h1[k][0], b_h1[k][1]])
        b.dma_in(cat[:, k, :], cat_d[:, k, :], [b_cat[k][0], b_cat[k][1]])
    for k in range(KC):
        b.dma_in(xs[:, k, :], xT_d[:, k, :], [b_xs[k][0], b_xs[k][1]])

    P.op(P.pool, lambda e: e.memset(cst[0][:, :], 0.0), writes=(b_cst[0],))
    P.op(P.pool, lambda e: e.memset(hin[:, :], 0.0), writes=(b_hin,))
    for r in range(8):
        cur, bcur = cst[r % 2], b_cst[r % 2]
        nx, bnx = cst[(r + 1) % 2], b_cst[(r + 1) % 2]
        b.stt(hin[:, :], cur[:, :], sel[:, r:r + 1], hin[:, :], ALU.mult, ALU.add, reads=(bcur, b_sel, b_hin), writes=(b_hin,))
        if r < 7:
            b.tt(tmp12[:, :], carr[:, r, 12:24], cur[:, :], ALU.mult, reads=(b_carr, bcur), writes=(b_tmp12,))
            b.tt(nx[:, :], tmp12[:, :], carr[:, r, 0:12], ALU.add, reads=(b_tmp12, b_carr), writes=(bnx,))
    for n in range(12):
        for t in range(2):
            tsl = slice(t * 512, (t + 1) * 512)
            b.stt(cat[:, n, tsl], h1[:, n, tsl], hin[:, n:n + 1], cat[:, n, tsl], ALU.mult, ALU.add,
                  reads=(b_h1[n][t], b_hin, b_cat[n][t]), writes=(b_cat[n][t],))
    for cg in range(8):
        wt, bw = b.next_w()
        wv = wt[:, :].rearrange("p (k c) -> p k c", k=16)
        for half in range(2):
            dc = cg * 2 + half
            for t in range(2):
                tsl = slice(t * 512, (t + 1) * 512)
                ps, bps = b.get("ps")
                b.mm(ps[:, :], bps, [(wv[:, k, half * 128:(half + 1) * 128], cat[:, k, tsl]) for k in range(16)],
                     reads=[bw] + [b_cat[k][t] for k in range(16)])
                b.tt(xs[:, dc, tsl], ps[:, :], xs[:, dc, tsl], ALU.add, reads=(bps, b_xs[dc][t]), writes=(b_xs[dc][t],))
                stats.add(xs[:, dc, tsl], b_xs[dc][t], t)
    need_out = with_next or with_kv
    st_mlp(b, xs, b_xs, cat, b_cat, h1, b_h1, 0, out_d=yT_d, in_rstd=stats.rstd(), out_stats=(stats if need_out else None))
    if need_out:
        rs_out = stats.rstd()
    if with_next:
        xnn_d = b.dout("xnn", [128, KC, T], BF16)
        st_apply_norm(b, xs, b_xs, cat, b_cat, 40, rs_out)
        for k in range(KC):
            b.dma_out(xnn_d[:, k, :], cat[:, k, :], rbufs=(b_cat[k][0], b_cat[k][1]))
    if with_kv:
        kT_d = b.dout("kT", [128, 12, T], BF16)
        vT_d = b.dout("vT", [128, 12, T], BF16)
        st_apply_norm(b, xs, b_xs, cat, b_cat, 16, rs_out)
        kpipe = Pipe()
        for cg in range(12):
            wt, bw = b.next_w()
            wv = wt[:, :].rearrange("p (k c) -> p k c", k=16)
            for half in range(2):
                hc = cg * 2 + half
                for t in range(2):
                    tsl = slice(t * 512, (t + 1) * 512)
                    ps, bps = b.get("ps")
                    b.mm(ps[:, :], bps, [(wv[:, k, half * 128:(half + 1) * 128], cat[:, k, tsl]) for k in range(16)],
                         reads=[bw] + [b_cat[k][t] for k in range(16)])
                    if hc < 12:
                        stq = {}

                        def KA(ps=ps, bps=bps, stq=stq):
                            sq, bsq = b.get("sq")
                            b.act(sq[:, :], ps[:, :], AF.Square, reads=(bps,), writes=(bsq,))
                            ps2, bps2 = b.get("ps")
                            b.mm(ps2[:, :], bps2, [(b.ones[:, :], sq[:, :])], reads=(bsq, b.b_ones))
                            stq.update(ps2=ps2, bps2=bps2)

                        def KB(ps=ps, bps=bps, stq=stq, hc=hc, t=t, tsl=tsl):
                            rs, brs = b.get("rstd")
                            b.rstd_from_ss(stq["ps2"][:, :], stq["bps2"], 1.0 / 128, rs[:, :], brs)
                            b.stt(h1[:, hc, tsl], ps[:, :], b.pcol(32 + hc // 4), rs[:, :], ALU.mult, ALU.mult,
                                  reads=(bps, brs, b.b_par), writes=(b_h1[hc][t],))
                        kpipe.push([KA, KB])
                    else:
                        b.copy(h1[:, hc - 12, tsl], ps[:, :], reads=(bps,), writes=(b_h1[hc - 12][t],))
            if cg == 5:
                kpipe.flush()
                for k in range(12):
                    b.dma_out(kT_d[:, k, :], h1[:, k, :], rbufs=(b_h1[k][0], b_h1[k][1]))
        for k in range(12):
            b.dma_out(vT_d[:, k, :], h1[:, k, :], rbufs=(b_h1[k][0], b_h1[k][1]))
    return b.finish(), b.wtiles


def fm(x):
    n, f = x.shape
    return np.ascontiguousarray(x.T.reshape(f // 128, 128, n).transpose(1, 0, 2))


def unfm(xT):
    p, k, n = xT.shape
    return np.ascontiguousarray(xT.transpose(1, 0, 2).reshape(k * p, n).T)


def colk(v):
    return np.ascontiguousarray(np.asarray(v, np.float32).reshape(-1, 128).T)


_CACHE = {}
_TIMES = []


def _run(nc, ins, tag=""):
    import os
    if os.environ.get("KTRACE"):
        res = run_bass_kernel_spmd(nc, ins, core_ids=list(range(NCORES)), trace=True)
        _TIMES.append((tag, res.exec_time_ns))
        print("KTRACE", tag, res.exec_time_ns, flush=True)
    else:
        res = run_bass_kernel_spmd(nc, ins, core_ids=list(range(NCORES)))
    return res.results


def _launch(key, builder, in_maps):
    if key not in _CACHE:
        _CACHE[key] = builder()
    nc, tiles = _CACHE[key]
    res = run_bass_kernel_spmd(nc, in_maps, core_ids=list(range(NCORES)))
    return res.results


def run_A_layer(l, xT, memT, W, with_kv, xn_in=None, next_g=None):
    f32 = np.float32
    par = np.zeros((128, NPAR), f32)
    par[:, 0:16] = colk(W["norm_mix_g"][l])
    par[:, 16:32] = colk(W["mem_norm_g"][l])
    par[:, 32] = W["mem_q_norm_g"][l]
    par[:, 33] = W["mem_k_norm_g"][l]
    cw = W["a_conv_w"][l]
    for n in range(12):
        for j in range(4):
            par[:, 40 + n * 4 + j] = cw[j, n * 128:(n + 1) * 128]
    par[:, 88:100] = colk(W["a_conv_b"][l])
    par[:, 100:112] = np.asarray(W["a_gate_r_b"][l], f32).T
    par[:, 112:124] = np.asarray(W["a_gate_i_b"][l], f32).T
    par[:, 124:136] = colk(W["a_lambda"][l])
    par[:, 254] = 1.0
    par[:, 255] = EPS
    nc1 = _CACHE.get(("A1", l))
    if nc1 is None:
        nc1 = _CACHE[("A1", l)] = build_A1(l, from_xn=(xn_in is not None))
    wst = pack_weights(nc1[1], W)
    ins = []
    for c in range(NCORES):
        if xn_in is not None:
            xnh = np.zeros((128, KC, 4), ml_dtypes.bfloat16)
            if c > 0:
                xnh[:, :, :] = np.asarray(xn_in[c - 1])[:, :, T - 4:]
            ins.append({"xn": xn_in[c], "xnh": xnh, "memT": memT, "wst": wst, "par": par})
        else:
            xh = np.zeros((128, KC, 4), f32)
            if c > 0:
                xh[:, :, :] = xT[c - 1][:, :, T - 4:]
            ins.append({"xT": xT[c], "xh": xh, "memT": memT, "wst": wst, "par": par})
    r1 = _run(nc1[0], ins, f"A1_{l}")
    par2 = np.zeros((128, NPAR), f32)
    par2[:, 0:16] = colk(W["norm_mlp_g"][l])
    if with_kv:
        par2[:, 16:32] = colk(W["kv_norm_g"])
        par2[:, 32:35] = np.asarray(W["k_norm_g"], f32).T
    par2[:, 255] = EPS
    if next_g is not None:
        par2[:, 40:56] = colk(next_g)
    nc2 = _CACHE.get(("A2", l))
    if nc2 is None:
        nc2 = _CACHE[("A2", l)] = build_A2(l, with_kv, with_next=(next_g is not None))
    wst2 = pack_weights(nc2[1], W)
    carr = np.ascontiguousarray(np.stack([r1[c]["carry"] for c in range(NCORES)], axis=1))
    ins2 = []
    for c in range(NCORES):
        sel = np.zeros((128, 8), f32)
        sel[:, c] = 1.0
        ins2.append({"xT": xT[c], "cat": r1[c]["cat"], "qq": r1[c]["qq"], "carr": carr, "sel": sel, "wst": wst2, "par": par2})
    r2 = _run(nc2[0], ins2, f"A2_{l}")
    out = [r2[c]["yT"] for c in range(NCORES)]
    xnn = [r2[c]["xnn"] for c in range(NCORES)] if next_g is not None else None
    if with_kv:
        return out, [r2[c]["kT"] for c in range(NCORES)], [r2[c]["vT"] for c in range(NCORES)], r1, xnn
    return out, None, None, r1, xnn


LG = [T // d for d in DIL]
NQ = [min(128, lg) for lg in LG]
NKC = [128 + lg for lg in LG]
NBC = [(n + 127) // 128 for n in NKC]
NK = [d * n for d, n in zip(DIL, NKC)]
NB = [d * n for d, n in zip(DIL, NBC)]


def tiles_B1(l):
    tl = list(memkv_tiles(l))
    tl += [wt_std("b_w_q", l - 2, 0, 16, [(c, 256)]) for c in range(0, 2048, 256)]
    return tl


def build_B1(l, from_xn=True):
    b = Bld(tiles_B1(l))
    nc, P = b.nc, b.P
    if from_xn:
        xn_d = b.din("xn", [128, KC, T], BF16)
    else:
        xT_d = b.din("xT", [128, KC, T])
    memT_d = b.din("memT", [128, KC, 256])
    kt_d = [b.din(f"kt{g}", [128, 4, NK[g]], BF16) for g in range(3)]
    vv_d = [b.din(f"vv{g}", [128, 4, NB[g], 128], BF16) for g in range(3)]
    oh_d = b.din("oh", [33, 6, 256])
    jm_d = b.din("jm", [128, 128])
    relb_d = b.din("relb", [33, 12])
    kval_d = b.din("kval", [128, 3])
    cat_d = b.dout("cat", [128, 8, T], BF16)
    vec_d = nc.dram_tensor("vecd", [6, 4, 256], F32).ap()

    b.pool("xt", 4, [128, 512], F32)
    b.pool("cb", 4, [128, T], BF16)
    b.pool("pp", 4, [128, 512], BF16)
    b.pool("kt", 2, [128, max(NK)], BF16)
    b.pool("vv", 2, [128, max(NB), 128], BF16)
    xn = b.sb("xn", [128, KC, T], BF16)
    b_xn = [[Buf() for t in range(2)] for k in range(KC)]
    qn = b.sb("qn", [128, 12, T], BF16)
    b_qn = [Buf() for k in range(12)]
    M = MemState(b, alias=qn)
    rstd_x = [b.sb(f"rstd_x{t}", [128, 512], F32) for t in range(2)]
    b_rstdx = [Buf(), Buf()]
    accN = b.sb("accN", [128, T], F32)
    accD = b.sb("accD", [128, T], F32)
    b_acc = Buf()
    relb = b.sb("relb", [33, 12], F32)
    b_relb = Buf()
    jm = b.sb("jm", [128, 128], F32)
    b_jm = Buf()
    kval = b.sb("kval", [128, 3], F32)
    b_kval = Buf()
    b.pool("vec", 2, [4, 256], F32)
    b.pool("hk", 2, [128, 512], F32)
    masks = [[b.sb(f"mask{g}_{ty}", [128, 4, 128], F32) for ty in range(3)] for g in range(3)]
    b_masks = [[Buf() for ty in range(3)] for g in range(3)]

    b.dma_in(M.memT, memT_d, [M.b_mem])
    b.dma_in(relb[:, :], relb_d, [b_relb])
    b.dma_in(jm[:, :], jm_d, [b_jm])
    b.dma_in(kval[:, :], kval_d, [b_kval])

    mask_state = {}

    def mask_A(g, ty):
        oh, boh = b.get("tf")
        b.dma_in(oh[:33, :256], oh_d[:, g * 2 + ty, :], [boh])
        ps, bps = b.get("ps")
        b.mm(ps[:4, :256], bps, [(relb[:33, g * 4:(g + 1) * 4], oh[:33, :256])], reads=(b_relb, boh))
        vec, b_vec = b.get("vec")
        b.act(vec[:, :], ps[:4, :256], AF.Exp, reads=(bps,), writes=(b_vec,))
        b_vd = Buf()
        P.dma(P.sp, (lambda e, s, g=g, ty=ty, vec=vec: e.dma_start(out=vec_d[g * 2 + ty], in_=vec[:, :]).then_inc(s, 16)),
              reads=(b_vec,), writes=(b_vd,))
        hk, bhk = b.get("hk")
        src = bass.AP(tensor=vec_d.tensor, offset=(g * 2 + ty) * 1024, ap=[[1, 128], [256, 4], [1, 128]])
        b.dma_in(hk[:, :].rearrange("p (h q) -> p h q", h=4), src, [bhk], rbufs=(b_vd,))
        mask_state[(g, ty)] = (hk, bhk)

    def mask_B(g, ty):
        hk, bhk = mask_state[(g, ty)]
        ps2, bps2 = b.get("ps")
        b.mm(ps2[:, :], bps2, [(jm[:, :], hk[:, :])], reads=(b_jm, bhk))
        b.copy(masks[g][ty][:, :, :], ps2[:, :].rearrange("p (h q) -> p h q", h=4), reads=(bps2,), writes=(b_masks[g][ty],))
        if ty == 1:
            b.ts(masks[g][2][:, :, :], masks[g][1][:, :, :], kval[:, g:g + 1], None, ALU.mult, None,
                 reads=(b_masks[g][1], b_kval), writes=(b_masks[g][2],))

    if from_xn:
        for k in range(KC):
            b.dma_in(xn[:, k, :], xn_d[:, k, :], [b_xn[k][0], b_xn[k][1]])
        st_mem_rstd(b, M.memT, M.b_mem, M.rstd_m, M.b_rstdm)
        st_memkv(b, M.memT, M.b_mem, M.rstd_m, M.b_rstdm, M.memn, M.b_memn, M.memk, M.b_memk, M.memv, M.b_memv, 16, 33)
    else:
        pss = [b.get("ps"), b.get("ps")]
        for k in range(KC):
            for t in range(2):
                xt, bxt = b.get("xt")
                b.dma_in(xt[:, :], xT_d[:, k, t * 512:(t + 1) * 512], [bxt])
                sq, bsq = b.get("sq")
                b.act(sq[:, :], xt[:, :], AF.Square, reads=(bxt,), writes=(bsq,))

                def f(e, k=k, sq=sq, ps=pss[t][0]):
                    return e.matmul(ps[:, :], b.ones[:, :], sq[:, :], start=(k == 0), stop=(k == KC - 1))
                P.op(P.pe, f, reads=(bsq, b.b_ones), writes=(pss[t][1],))
        for t in range(2):
            b.rstd_from_ss(pss[t][0][:, :], pss[t][1], 1.0 / D, rstd_x[t][:, :], b_rstdx[t])
        st_mem_rstd(b, M.memT, M.b_mem, M.rstd_m, M.b_rstdm)
        st_memkv(b, M.memT, M.b_mem, M.rstd_m, M.b_rstdm, M.memn, M.b_memn, M.memk, M.b_memk, M.memv, M.b_memv, 16, 33)
        for k in range(KC):
            for t in range(2):
                tsl = slice(t * 512, (t + 1) * 512)
                xt, bxt = b.get("xt")
                b.dma_in(xt[:, :], xT_d[:, k, tsl], [bxt])
                b.stt(xn[:, k, tsl], xt[:, :], b.pcol(k), rstd_x[t][:, :], ALU.mult, ALU.mult,
                      reads=(bxt, b_rstdx[t], b.b_par), writes=(b_xn[k][t],))

    qpipe = Pipe()
    for cg in range(8):
        if cg < 6:
            mask_A(cg // 2, cg % 2)
        if 1 <= cg < 7:
            mask_B((cg - 1) // 2, (cg - 1) % 2)
        wt, bw = b.next_w()
        wv = wt[:, :].rearrange("p (k c) -> p k c", k=16)
        for half in range(2):
            hq = cg * 2 + half
            cbt = None
            if hq >= 12:
                cbt, bcb = b.get("cb")
            for t in range(2):
                tsl = slice(t * 512, (t + 1) * 512)
                ps, bps = b.get("ps")
                b.mm(ps[:, :], bps, [(wv[:, k, half * 128:(half + 1) * 128], xn[:, k, tsl]) for k in range(KC)],
                     reads=[bw] + [b_xn[k][t] for k in range(KC)])
                if hq >= 12:
                    after = None
                    if t == 1:
                        after = (lambda hq=hq, cbt=cbt, bcb=bcb: b.dma_out(cat_d[:, 4 + hq - 12, :], cbt[:, :], rbufs=(bcb,)))
                    qpipe.push(mem_attn_stages(b, ps[:, :], bps, M.memk, M.b_memk, M.memv, M.b_memv, hq - 12, 32, cbt[:, tsl], bcb,
                                               after=after))
                else:
                    stq = {}

                    def QA(ps=ps, bps=bps, stq=stq):
                        sq, bsq = b.get("sq")
                        b.act(sq[:, :], ps[:, :], AF.Square, reads=(bps,), writes=(bsq,))
                        ps2, bps2 = b.get("ps")
                        b.mm(ps2[:, :], bps2, [(b.ones[:, :], sq[:, :])], reads=(bsq, b.b_ones))
                        stq.update(ps2=ps2, bps2=bps2)

                    def QB(ps=ps, bps=bps, stq=stq, hq=hq, t=t):
                        g = hq // 4
                        d = DIL[g]
                        rs, brs = b.get("rstd")
                        b.rstd_from_ss(stq["ps2"][:, :], stq["bps2"], 1.0 / 128, rs[:, :], brs)
                        nl = 512 // d
                        dst = qn[:, hq, :].rearrange("p (r l) -> p l r", r=d)[:, t * nl:(t + 1) * nl, :]
                        extra = (M.b_mem,) if hq < 8 else (M.b_memn,)
                        b.stt(dst, ps[:, :].rearrange("p (l r) -> p l r", r=d), b.pcol(34 + g), rs[:, :].rearrange("p (l r) -> p l r", r=d),
                              ALU.mult, ALU.mult, reads=(bps, brs, b.b_par), writes=(b_qn[hq],) + extra)
                    qpipe.push([QA, QB])
    qpipe.flush()

    batches = []
    for h in range(4):
        for g in range(3):
            d, lg, nq = DIL[g], LG[g], NQ[g]
            upc = lg // nq
            nunits = d * upc
            U = 512 // nq
            for u0 in range(0, nunits, U):
                batches.append(dict(h=h, g=g, u0=u0, first=(u0 == 0), last=(g == 2 and u0 + U >= nunits)))
    cur_kv = {}

    def T1(bt):
        h, g, u0 = bt["h"], bt["g"], bt["u0"]
        d, lg, nq, nkc, nbc = DIL[g], LG[g], NQ[g], NKC[g], NBC[g]
        if bt["first"]:
            kt, bkt = b.get("kt")
            vv, bvv = b.get("vv")
            b.dma_in(kt[:, :NK[g]], kt_d[g][:, h, :], [bkt])
            b.dma_in(vv[:, :NB[g], :], vv_d[g][:, h, :, :], [bvv])
            cur_kv[(h, g)] = (kt, bkt, vv, bvv)
        kt, bkt, vv, bvv = cur_kv[(h, g)]
        hq = g * 4 + h
        upc = lg // nq
        U = 512 // nq
        psP, bpsP = b.get("ps")
        psD, bpsD = b.get("ps")
        units = []
        for ui in range(U):
            u = u0 + ui
            r, j = u // upc, u % upc
            qs = qn[:, hq, r * lg + j * nq: r * lg + (j + 1) * nq]
            kp = kt[:, r * nkc + j * nq: r * nkc + j * nq + 128]
            kd = kt[:, r * nkc + 128 + j * nq: r * nkc + 128 + (j + 1) * nq]
            units.append((r, j, qs, kp, kd))

        def fS(e, units=units, psP=psP, psD=psD, nq=nq):
            for ui, (r, j, qs, kp, kd) in enumerate(units):
                e.matmul(psP[:, ui * nq:(ui + 1) * nq], kp, qs, start=True, stop=True)
                ins = e.matmul(psD[:nq, ui * nq:(ui + 1) * nq], kd, qs, start=True, stop=True)
            return ins
        P.op(P.pe, fS, reads=(bkt, b_qn[hq]), writes=(bpsP, bpsD))
        bt.update(units=units, psP=psP, bpsP=bpsP, psD=psD, bpsD=bpsD, vv=vv, bvv=bvv)

    def T2(bt):
        h, g = bt["h"], bt["g"]
        nq = NQ[g]
        eP, beP = b.get("tf")
        eD, beD = b.get("tf")
        b.act(eP[:, :], bt["psP"][:, :], AF.Exp, reads=(bt["bpsP"],), writes=(beP,), scale=SCALE)
        b.act(eD[:nq, :], bt["psD"][:nq, :], AF.Exp, reads=(bt["bpsD"],), writes=(beD,), scale=SCALE)
        pP, bpP = b.get("pp")
        pD, bpD = b.get("pp")
        for ui, (r, j, qs, kp, kd) in enumerate(bt["units"]):
            csl = slice(ui * nq, (ui + 1) * nq)
            mty = 2 if j == 0 else 1
            b.tt(pP[:, csl], eP[:, csl], masks[g][mty][:, h, :nq], ALU.mult, reads=(beP, b_masks[g][mty]), writes=(bpP,))
            b.tt(pD[:nq, csl], eD[:nq, csl], masks[g][0][:nq, h, :nq], ALU.mult, reads=(beD, b_masks[g][0]), writes=(bpD,),
                 eng=P.pool)
        bt.update(pP=pP, bpP=bpP, pD=pD, bpD=bpD)

    def T3(bt):
        g = bt["g"]
        nq, nbc = NQ[g], NBC[g]
        psN, bpsN = b.get("ps")
        psS, bpsS = b.get("ps")
        pP, pD, vv = bt["pP"], bt["pD"], bt["vv"]

        def fV(e, units=bt["units"], psN=psN, psS=psS, nq=nq, pP=pP, pD=pD, vv=vv, nbc=nbc):
            for ui, (r, j, qs, kp, kd) in enumerate(units):
                csl = slice(ui * nq, (ui + 1) * nq)
                e.matmul(psN[:, csl], vv[:, r * nbc + j, :], pP[:, csl], start=True, stop=False)
                e.matmul(psN[:, csl], vv[:nq, r * nbc + j + 1, :], pD[:nq, csl], start=False, stop=True)
            e.matmul(psS[:, :], b.ones[:, :], pP[:, :], start=True, stop=False)
            ins = e.matmul(psS[:, :], b.ones[:nq, :], pD[:nq, :], start=False, stop=True)
            return ins
        P.op(P.pe, fV, reads=(bt["bvv"], bt["bpP"], bt["bpD"], b.b_ones), writes=(bpsN, bpsS))
        bt.update(psN=psN, bpsN=bpsN, psS=psS, bpsS=bpsS)

    def T4(bt):
        h, g, u0 = bt["h"], bt["g"], bt["u0"]
        d, lg, nq = DIL[g], LG[g], NQ[g]
        upc = lg // nq
        U = 512 // nq
        psN, bpsN, psS, bpsS = bt["psN"], bt["bpsN"], bt["psS"], bt["bpsS"]
        r0 = u0 // upc
        if d == 1:
            l0 = u0 * nq
            dN = accN[:, l0:l0 + 512]
            dD = accD[:, l0:l0 + 512]
            sN, sS = psN[:, :], psS[:, :]
        else:
            nr = U // upc
            dN = accN[:, :].rearrange("p (l r) -> p r l", r=d)[:, r0:r0 + nr, :]
            dD = accD[:, :].rearrange("p (l r) -> p r l", r=d)[:, r0:r0 + nr, :]
            sN = psN[:, :].rearrange("p (r l) -> p r l", r=nr)
            sS = psS[:, :].rearrange("p (r l) -> p r l", r=nr)
        if g == 0:
            b.copy(dN, sN, reads=(bpsN,), writes=(b_acc,))
            b.copy(dD, sS, reads=(bpsS,), writes=(b_acc,))
        else:
            b.tt(dN, sN, dN, ALU.add, reads=(bpsN, b_acc), writes=(b_acc,))
            b.tt(dD, sS, dD, ALU.add, reads=(bpsS, b_acc), writes=(b_acc,))
        if bt["last"]:
            cbt, bcb = b.get("cb")
            for t in range(2):
                tsl = slice(t * 512, (t + 1) * 512)
                rd, brd = b.get("tf")
                b.recip_act(rd[:, :], accD[:, tsl], reads=(b_acc,), writes=(brd,))
                b.tt(cbt[:, tsl], accN[:, tsl], rd[:, :], ALU.mult, reads=(b_acc, brd), writes=(bcb,))
            b.dma_out(cat_d[:, h, :], cbt[:, :], rbufs=(bcb,))

    nbt = len(batches)
    for s_ in range(nbt + 2):
        if s_ < nbt:
            T1(batches[s_])
        if 0 <= s_ - 1 < nbt:
            T2(batches[s_ - 1])
            T3(batches[s_ - 1])
        if 0 <= s_ - 2 < nbt:
            T4(batches[s_ - 2])
    return b.finish(), b.wtiles


def tiles_B2(l):
    tl = [wt_std("b_w_out", l - 2, 0, 8, [(c, 512)]) for c in range(0, 2048, 512)]
    tl += mlp_tiles(l)
    return tl


def build_B2(l, with_next=True):
    b = Bld(tiles_B2(l), n_ps=6)
    b.pool("rstdp", 4, [128, 512], F32)
    stats = Stats(b)
    nc, P = b.nc, b.P
    xT_d = b.din("xT", [128, KC, T])
    cat_d = b.din("cat", [128, 8, T], BF16)
    yT_d = b.dout("yT", [128, KC, T])
    xs = b.sb("xs", [128, KC, T], F32)
    b_xs = [[Buf() for t in range(2)] for k in range(KC)]
    cat = b.sb("cat", [128, 16, T], BF16)
    b_cat = [[Buf() for t in range(2)] for k in range(16)]
    h1 = b.sb("h1", [128, 16, T], BF16)
    b_h1 = [[Buf() for t in range(2)] for k in range(16)]
    for k in range(8):
        b.dma_in(cat[:, k, :], cat_d[:, k, :], [b_cat[k][0], b_cat[k][1]])
    for k in range(KC):
        b.dma_in(xs[:, k, :], xT_d[:, k, :], [b_xs[k][0], b_xs[k][1]])
    for cg in range(4):
        wt, bw = b.next_w()
        wv = wt[:, :].rearrange("p (k c) -> p k c", k=8)
        for q4 in range(4):
            dc = cg * 4 + q4
            for t in range(2):
                tsl = slice(t * 512, (t + 1) * 512)
                ps, bps = b.get("ps")
                b.mm(ps[:, :], bps, [(wv[:, k, q4 * 128:(q4 + 1) * 128], cat[:, k, tsl]) for k in range(8)],
                     reads=[bw] + [b_cat[k][t] for k in range(8)])
                b.tt(xs[:, dc, tsl], ps[:, :], xs[:, dc, tsl], ALU.add, reads=(bps, b_xs[dc][t]), writes=(b_xs[dc][t],))
                stats.add(xs[:, dc, tsl], b_xs[dc][t], t)
    st_mlp(b, xs, b_xs, cat, b_cat, h1, b_h1, 0, out_d=yT_d, in_rstd=stats.rstd(), out_stats=(stats if with_next else None))
    if with_next:
        xnn_d = b.dout("xnn", [128, KC, T], BF16)
        st_apply_norm(b, xs, b_xs, cat, b_cat, 40, stats.rstd())
        for k in range(KC):
            b.dma_out(xnn_d[:, k, :], cat[:, k, :], rbufs=(b_cat[k][0], b_cat[k][1]))
    return b.finish(), b.wtiles


def _t5_bucket(n):
    n = np.maximum(np.asarray(n, np.int64), 0)
    nf = np.maximum(n, 1).astype(np.float32)
    large = 16 + (np.log(nf / np.float32(16.0)) / np.float32(math.log(2048 / 16)) * np.float32(16.0)).astype(np.int32)
    large = np.minimum(large, 31)
    return np.where(n < 16, n, large)


def _structural():
    oh = np.zeros((33, 6, 256), np.float32)
    for g, d in enumerate(DIL):
        for i in range(256):
            w = i - 127
            if 0 <= w <= 127:
                oh[_t5_bucket(w * d), g * 2 + 0, i] = 1.0
            else:
                oh[32, g * 2 + 0, i] = 1.0
            if -127 <= w <= 0:
                oh[_t5_bucket((w + 128) * d), g * 2 + 1, i] = 1.0
            else:
                oh[32, g * 2 + 1, i] = 1.0
    jm = np.ascontiguousarray(np.eye(128, dtype=np.float32)[::-1])
    return oh, jm


def _kv_layout(kT, vT):
    bf = ml_dtypes.bfloat16
    Kf = np.concatenate([np.asarray(k) for k in kT], axis=2)
    Vf = np.concatenate([np.asarray(v) for v in vT], axis=2)
    outs = [dict() for _ in range(NCORES)]
    for g, d in enumerate(DIL):
        lg, nkc, nbc = LG[g], NKC[g], NBC[g]
        Lf = S // d
        def cm(A):
            A = A[:, g * 4:(g + 1) * 4, :].reshape(128, 4, Lf, d).transpose(0, 1, 3, 2)
            pad = np.zeros((128, 4, d, 128), bf)
            return np.concatenate([pad, A], axis=3)
        Kc, Vc = cm(Kf), cm(Vf)
        for c in range(NCORES):
            ks = Kc[:, :, :, c * lg:c * lg + nkc]
            outs[c][f"kt{g}"] = np.ascontiguousarray(ks.reshape(128, 4, d * nkc))
            vs = Vc[:, :, :, c * lg:c * lg + nkc]
            vp = np.zeros((128, 4, d, nbc * 128), bf)
            vp[:, :, :, :nkc] = vs
            vp = vp.reshape(128, 4, d, nbc, 128).transpose(4, 1, 2, 3, 0)
            outs[c][f"vv{g}"] = np.ascontiguousarray(vp.reshape(128, 4, d * nbc, 128))
            kv = outs[c].setdefault("kval", np.zeros((128, 3), np.float32))
            kv[:, g] = ((c * lg - 128 + np.arange(128)) >= 0).astype(np.float32)
    return outs


def run_B_layer(l, xT, memT, W, kvin, xn_in=None, next_g=None):
    f32 = np.float32
    j = l - 2
    par = np.zeros((128, NPAR), f32)
    par[:, 0:16] = colk(W["norm_mix_g"][l])
    par[:, 16:32] = colk(W["mem_norm_g"][l])
    par[:, 32] = W["mem_q_norm_g"][l]
    par[:, 33] = W["mem_k_norm_g"][l]
    par[:, 34:37] = np.asarray(W["b_q_norm_g"][j], f32).T
    par[:, 255] = EPS
    nb1 = _CACHE.get(("B1", l))
    if nb1 is None:
        nb1 = _CACHE[("B1", l)] = build_B1(l, from_xn=(xn_in is not None))
    wst = pack_weights(nb1[1], W)
    oh, jm = _structural()
    relb = np.concatenate([np.asarray(W["rel_bias"], f32), np.full((1, 12), -30000.0, f32)], axis=0)
    ins = []
    for c in range(NCORES):
        dct = {"memT": memT, "wst": wst, "par": par, "oh": oh, "jm": jm, "relb": relb}
        if xn_in is not None:
            dct["xn"] = xn_in[c]
        else:
            dct["xT"] = xT[c]
        dct.update(kvin[c])
        ins.append(dct)
    r1 = _run(nb1[0], ins, f"B1_{l}")
    par2 = np.zeros((128, NPAR), f32)
    par2[:, 0:16] = colk(W["norm_mlp_g"][l])
    par2[:, 255] = EPS
    if next_g is not None:
        par2[:, 40:56] = colk(next_g)
    nb2 = _CACHE.get(("B2", l))
    if nb2 is None:
        nb2 = _CACHE[("B2", l)] = build_B2(l, with_next=(next_g is not None))
    wst2 = pack_weights(nb2[1], W)
    ins2 = [{"xT": xT[c], "cat": r1[c]["cat"], "wst": wst2, "par": par2} for c in range(NCORES)]
    r2 = _run(nb2[0], ins2, f"B2_{l}")
    xnn = [r2[c]["xnn"] for c in range(NCORES)] if next_g is not None else None
    return [r2[c]["yT"] for c in range(NCORES)], r1, xnn


def kernel(**inputs):
    W = {k: np.asarray(v) for k, v in inputs.items()}
    x = W["x"][0]
    memT = fm(W["mem"][0])
    xT = [fm(x[c * T:(c + 1) * T]) for c in range(NCORES)]
    g = W["norm_mix_g"]
    xT, _, _, _, xnn = run_A_layer(0, xT, memT, W, with_kv=False, next_g=g[1])
    xT, kT, vT, _, xnn = run_A_layer(1, xT, memT, W, with_kv=True, xn_in=xnn, next_g=g[2])
    kvin = _kv_layout(kT, vT)
    xT, _, xnn = run_B_layer(2, xT, memT, W, kvin, xn_in=xnn, next_g=g[3])
    xT, _, xnn = run_B_layer(3, xT, memT, W, kvin, xn_in=xnn, next_g=None)
    out = np.concatenate([unfm(t) for t in xT], axis=0)
    return out.reshape(1, S, D).astype(np.float32)
```

```python
import math
import numpy as np
import ml_dtypes
import concourse.bass as bass
import concourse.mybir as mybir
from concourse.bass_utils import run_bass_kernel_spmd

F32 = mybir.dt.float32
BF16 = mybir.dt.bfloat16
AF = mybir.ActivationFunctionType
ALU = mybir.AluOpType

NCORES = 8
D = 2048
S = 8192
T = S // NCORES
KC = D // 128
DFF = 4 * D
EPS = 1e-6
WT_ELEMS = 4096
NPAR = 256
SCALE = 128 ** -0.5
DIL = (1, 4, 16)


class Buf:
    __slots__ = ("name", "w", "r")

    def __init__(self, name=""):
        self.name = name
        self.w = None
        self.r = []


class Eng:
    def __init__(self, name, sem, is_pe=False):
        self.name = name
        self.sem = sem
        self.count = 0
        self.ops = []
        self.waited = {}
        self.is_pe = is_pe


class Prog:
    N_DMA_SEMS = 12

    def __init__(self, nc):
        self.nc = nc
        self.pe = Eng("pe", nc.alloc_semaphore("s_pe"), is_pe=True)
        self.act = Eng("act", nc.alloc_semaphore("s_act"))
        self.dve = Eng("dve", nc.alloc_semaphore("s_dve"))
        self.pool = Eng("pool", nc.alloc_semaphore("s_pool"))
        self.sp = Eng("sp", None)
        self.engs = [self.pe, self.act, self.dve, self.pool, self.sp]
        self.dma_sems = [nc.alloc_semaphore(f"s_dma{i}") for i in range(self.N_DMA_SEMS)]
        self.dma_cnt = [0] * self.N_DMA_SEMS
        self.dma_last = [None] * self.N_DMA_SEMS
        self.dma_rr = 0

    def _deps(self, reads, writes):
        deps = []
        for b in reads:
            if b.w is not None:
                deps.append(b.w)
        for b in writes:
            if b.w is not None:
                deps.append(b.w)
            deps.extend(b.r)
        return deps

    def _filter(self, eng, deps):
        waits = {}
        for (sem, val) in deps:
            if eng.is_pe and sem is eng.sem:
                continue
            k = id(sem)
            if eng.waited.get(k, 0) >= val:
                continue
            if k not in waits or waits[k][1] < val:
                waits[k] = (sem, val)
        for k, (sem, val) in waits.items():
            eng.waited[k] = val
        return list(waits.values())

    def _update(self, tok, reads, writes):
        for b in writes:
            b.w = tok
            b.r = []
        for b in reads:
            if b not in writes:
                b.r.append(tok)

    def op(self, eng, fn, reads=(), writes=()):
        deps = self._deps(reads, writes)
        waits = self._filter(eng, deps)
        eng.count += 1
        tok = (eng.sem, eng.count)
        eng.ops.append((waits, fn, (eng.sem, 1)))
        self._update(tok, reads, writes)
        return tok

    def dma(self, eng, fn, reads=(), writes=(), n=1):
        k = self.dma_rr
        self.dma_rr = (self.dma_rr + 1) % self.N_DMA_SEMS
        sem = self.dma_sems[k]
        deps = self._deps(reads, writes)
        if self.dma_last[k] is not None:
            deps.append(self.dma_last[k])
        waits = self._filter(eng, deps)
        self.dma_cnt[k] += 16 * n
        tok = (sem, self.dma_cnt[k])
        self.dma_last[k] = tok
        eng.ops.append((waits, (lambda e, fn=fn, sem=sem: fn(e, sem)), None))
        self._update(tok, reads, writes)
        return tok

    def wait_all(self, eng, toks):
        waits = self._filter(eng, list(toks))
        eng.ops.append((waits, None, None))

    def emit(self):
        nc = self.nc

        def run(eng, h):
            for (waits, fn, inc) in eng.ops:
                for (sem, val) in waits:
                    h.wait_ge(sem, val)
                if fn is None:
                    continue
                ins = fn(h)
                if inc is not None:
                    ins.then_inc(inc[0], inc[1])

        with nc.Block() as block:
            @block.tensor
            def _(h):
                run(self.pe, h)

            @block.scalar
            def _(h):
                run(self.act, h)

            @block.vector
            def _(h):
                run(self.dve, h)

            @block.gpsimd
            def _(h):
                run(self.pool, h)

            @block.sync
            def _(h):
                run(self.sp, h)


class Bld:
    def __init__(self, wtiles, n_wslots=4, n_ps=8):
        self.discover = wtiles is None
        if self.discover:
            wtiles = []
        self.nc = bass.Bass("TRN2", target_bir_lowering=False)
        self.P = Prog(self.nc)
        self.pools = {}
        self.rr = {}
        self.out_toks = []
        self.pool("ps", n_ps, [128, 512], F32, psum=True)
        if n_ps < 8:
            self.pool("pstat", 8 - n_ps, [128, 512], F32, psum=True)
        self.wtiles = wtiles
        self.NT = 4096 if self.discover else len(wtiles)
        self.n_wslots = n_wslots
        self.wst_d = self.din("wst", [max(self.NT, 1), 128, WT_ELEMS])
        self.par_d = self.din("par", [128, NPAR])
        self.pool("w", n_wslots, [128, WT_ELEMS], BF16)
        self.par = self.sb("par_sb", [128, NPAR], F32)
        self.b_par = Buf("par")
        self.ones = self.sb("ones", [128, 128], BF16)
        self.b_ones = Buf("ones")
        self.w_next_load = 0
        self.w_cur = 0
        self.pool("tf", 4, [128, 512], F32)
        self.pool("tb", 6, [128, 512], BF16)
        self.pool("sq", 4, [128, 512], BF16)
        self.pool("rstd", 4, [128, 512], F32)
        self.dma_in(self.par[:, :], self.par_d, [self.b_par])
        self.P.op(self.P.pool, lambda e: e.memset(self.ones[:, :], 1.0), writes=(self.b_ones,))
        self._ensure(n_wslots - 1)

    def din(self, name, shape, dt=F32):
        return self.nc.dram_tensor(name, list(shape), dt, kind="ExternalInput").ap()

    def dout(self, name, shape, dt=F32):
        return self.nc.dram_tensor(name, list(shape), dt, kind="ExternalOutput").ap()

    def sb(self, name, shape, dt):
        return self.nc.alloc_sbuf_tensor("sb_" + name, list(shape), dt)

    def pool(self, name, n, shape, dt, psum=False):
        if psum:
            lst = [(self.nc.alloc_psum_tensor(f"pp_{name}{i}", list(shape), dt), Buf(f"{name}{i}")) for i in range(n)]
        else:
            lst = [(self.nc.alloc_sbuf_tensor(f"pl_{name}{i}", list(shape), dt), Buf(f"{name}{i}")) for i in range(n)]
        self.pools[name] = lst
        self.rr[name] = 0

    def get(self, name):
        lst = self.pools[name]
        i = self.rr[name]
        self.rr[name] = (i + 1) % len(lst)
        return lst[i]

    def dma_in(self, dst, src, wbufs, eng=None, rbufs=()):
        eng = eng or self.P.sp
        return self.P.dma(eng, lambda e, s: e.dma_start(out=dst, in_=src).then_inc(s, 16), reads=rbufs, writes=wbufs)

    def dma_out(self, dst, src, rbufs):
        tok = self.P.dma(self.P.sp, lambda e, s: e.dma_start(out=dst, in_=src).then_inc(s, 16), reads=rbufs)
        self.out_toks.append(tok)
        return tok

    def finish(self):
        self.P.wait_all(self.P.sp, self.out_toks)
        self.P.emit()
        return self.nc

    def _ensure(self, upto):
        while self.w_next_load <= min(upto, self.NT - 1):
            i = self.w_next_load
            wt, bw = self.pools["w"][i % self.n_wslots]
            self.P.dma(self.P.pool, (lambda e, s, i=i, wt=wt: e.dma_start(out=wt[:, :], in_=self.wst_d[i]).then_inc(s, 16)),
                       writes=(bw,))
            self.w_next_load += 1

    def next_w(self, desc=None):
        i = self.w_cur
        if desc is not None:
            if self.discover:
                self.wtiles.append(desc)
            else:
                assert self.wtiles[i] == desc, (i, self.wtiles[i], desc)
        assert i < self.NT, "weight stream exhausted"
        self.w_cur += 1
        self._ensure(i + self.n_wslots - 1)
        return self.pools["w"][i % self.n_wslots]

    def mm(self, out_ap, bps, pairs, reads):
        n = len(pairs)

        def f(e):
            for i, (l, r) in enumerate(pairs):
                ins = e.matmul(out_ap, l, r, start=(i == 0), stop=(i == n - 1))
            return ins
        return self.P.op(self.P.pe, f, reads=reads, writes=(bps,))

    def act(self, out, in_, func, reads, writes, bias=None, scale=None):
        kw = {}
        if bias is not None:
            kw["bias"] = bias
        if scale is not None:
            kw["scale"] = scale
        return self.P.op(self.P.act, lambda e: e.activation(out=out, in_=in_, func=func, **kw), reads=reads, writes=writes)

    def tt(self, out, a, b_, op, reads, writes, eng=None):
        eng = eng or self.P.dve
        return self.P.op(eng, lambda e: e.tensor_tensor(out=out, in0=a, in1=b_, op=op), reads=reads, writes=writes)

    def ts(self, out, a, s1, s2, op0, op1, reads, writes, eng=None):
        eng = eng or self.P.dve
        if s2 is None:
            return self.P.op(eng, lambda e: e.tensor_scalar(out=out, in0=a, scalar1=s1, scalar2=None, op0=op0), reads=reads, writes=writes)
        return self.P.op(eng, lambda e: e.tensor_scalar(out=out, in0=a, scalar1=s1, scalar2=s2, op0=op0, op1=op1), reads=reads, writes=writes)

    def stt(self, out, a, sc, b_, op0, op1, reads, writes, eng=None):
        eng = eng or self.P.dve
        return self.P.op(eng, lambda e: e.scalar_tensor_tensor(out=out, in0=a, scalar=sc, in1=b_, op0=op0, op1=op1), reads=reads, writes=writes)

    def recip(self, out, in_, reads, writes):
        return self.P.op(self.P.dve, lambda e: e.reciprocal(out=out, in_=in_), reads=reads, writes=writes)

    def copy(self, out, in_, reads, writes, eng=None):
        eng = eng or self.P.act
        if eng is self.P.act:
            return self.P.op(eng, lambda e: e.copy(out=out, in_=in_), reads=reads, writes=writes)
        return self.P.op(eng, lambda e: e.tensor_copy(out=out, in_=in_), reads=reads, writes=writes)

    def pcol(self, c, n=1):
        return self.par[:, c:c + n]

    def rstd_from_ss(self, ss_ap, b_ss, inv_count, out_ap, out_b, ncols=512):
        tf, btf = self.get("tf")
        self.act(tf[:, :ncols], ss_ap, AF.Ln, reads=(b_ss, self.b_par), writes=(btf,), bias=self.pcol(255), scale=inv_count)
        self.act(out_ap, tf[:, :ncols], AF.Exp, reads=(btf,), writes=(out_b,), scale=-0.5)

    def recip_act(self, out_ap, in_ap, reads, writes, nrows=128, ncols=512):
        tf, btf = self.get("tf")
        self.act(tf[:nrows, :ncols], in_ap, AF.Ln, reads=reads, writes=(btf,))
        self.act(out_ap, tf[:nrows, :ncols], AF.Exp, reads=(btf,), writes=writes, scale=-1.0)


def wt_std(key, l, row0, nk, cols):
    return ("std", key, l, row0, nk, tuple(cols))


def pack_weights(tiles, W):
    out = np.zeros((max(len(tiles), 1), 128, WT_ELEMS), np.float32)
    for i, tl in enumerate(tiles):
        if tl[0] == "std":
            _, key, l, row0, nk, cols = tl
            w = W[key][l] if l is not None else W[key]
            blk = np.concatenate([w[row0:row0 + nk * 128, c0:c0 + n] for (c0, n) in cols], axis=1)
            ncol = blk.shape[1]
            out[i, :, :nk * ncol] = blk.reshape(nk, 128, ncol).transpose(1, 0, 2).reshape(128, nk * ncol)
        elif tl[0] == "gates":
            _, l = tl
            g = np.concatenate([W["a_gate_r_w"][l], W["a_gate_i_w"][l]], axis=0)
            out[i, :, :24 * 128] = g.transpose(1, 0, 2).reshape(128, 24 * 128)
    return out


def mlp_tiles(l):
    tl = []
    for q in range(4):
        for c in range(0, 2048, 256):
            tl.append(wt_std("mlp_w1", l, 0, 16, [(q * 2048 + c, 256)]))
        for c in range(0, 2048, 256):
            tl.append(wt_std("mlp_w2", l, q * 2048, 16, [(c, 256)]))
    return tl


def memkv_tiles(l):
    return [wt_std("mem_w_kv", l, 0, 16, [(c, 256)]) for c in range(0, 1024, 256)]


class Stats:
    def __init__(self, b):
        self.b = b
        self.ps = [b.pools["pstat"][t] for t in range(2)]
        self.n = [0, 0]
        self.pipe = Pipe()
        b.pool("sqs", 8, [128, 512], BF16)

    def add(self, src_ap, b_src, t):
        b = self.b
        k = self.n[t]
        self.n[t] += 1
        ps, bps = self.ps[t]
        st = {}

        def A():
            sq, bsq = b.get("sqs")
            b.act(sq[:, :], src_ap, AF.Square, reads=(b_src,), writes=(bsq,))
            st.update(sq=sq, bsq=bsq)

        def nop():
            pass

        def B():
            sq, bsq = st["sq"], st["bsq"]

            def f(e, k=k, sq=sq, ps=ps):
                return e.matmul(ps[:, :], b.ones[:, :], sq[:, :], start=(k == 0), stop=(k == KC - 1))
            b.P.op(b.P.pe, f, reads=(bsq, b.b_ones), writes=(bps,))
        self.pipe.push([nop, A, nop, nop, B])

    def rstd(self):
        b = self.b
        assert self.n == [KC, KC]
        self.pipe.flush()
        out = []
        for t in range(2):
            rs, brs = b.get("rstdp")
            b.rstd_from_ss(self.ps[t][0][:, :], self.ps[t][1], 1.0 / D, rs[:, :], brs)
            out.append((rs, brs))
        self.n = [0, 0]
        return out


def st_apply_norm(b, xs, b_xs, xn, b_xn, gcol, rstds):
    for t in range(2):
        tsl = slice(t * 512, (t + 1) * 512)
        rs, brs = rstds[t]
        for k in range(KC):
            b.stt(xn[:, k, tsl], xs[:, k, tsl], b.pcol(gcol + k), rs[:, :], ALU.mult, ALU.mult,
                  reads=(b_xs[k][t], brs, b.b_par), writes=(b_xn[k][t],))


def st_norm_resident(b, xs, b_xs, xn, b_xn, gcol):
    for t in range(2):
        tsl = slice(t * 512, (t + 1) * 512)
        ps, bps = b.get("ps")
        for k in range(KC):
            sq, bsq = b.get("sq")
            b.act(sq[:, :], xs[:, k, tsl], AF.Square, reads=(b_xs[k][t],), writes=(bsq,))

            def f(e, k=k, sq=sq, ps=ps):
                return e.matmul(ps[:, :], b.ones[:, :], sq[:, :], start=(k == 0), stop=(k == KC - 1))
            b.P.op(b.P.pe, f, reads=(bsq, b.b_ones), writes=(bps,))
        rs, brs = b.get("rstd")
        b.rstd_from_ss(ps[:, :], bps, 1.0 / D, rs[:, :], brs)
        for k in range(KC):
            b.stt(xn[:, k, tsl], xs[:, k, tsl], b.pcol(gcol + k), rs[:, :], ALU.mult, ALU.mult,
                  reads=(b_xs[k][t], brs, b.b_par), writes=(b_xn[k][t],))


def st_mlp(b, xs, b_xs, xn, b_xn, h1, b_h1, gcol, out_d=None, in_rstd=None, out_stats=None):
    if in_rstd is not None:
        st_apply_norm(b, xs, b_xs, xn, b_xn, gcol, in_rstd)
    else:
        st_norm_resident(b, xs, b_xs, xn, b_xn, gcol)
    for q in range(4):
        for cg in range(8):
            wt, bw = b.next_w()
            wv = wt[:, :].rearrange("p (k c) -> p k c", k=16)
            for half in range(2):
                fc = cg * 2 + half
                for t in range(2):
                    tsl = slice(t * 512, (t + 1) * 512)
                    ps, bps = b.get("ps")
                    b.mm(ps[:, :], bps, [(wv[:, k, half * 128:(half + 1) * 128], xn[:, k, tsl]) for k in range(KC)],
                         reads=[bw] + [b_xn[k][t] for k in range(KC)])
                    tf, btf = b.get("tf")
                    b.act(tf[:, :], ps[:, :], AF.Relu, reads=(bps,), writes=(btf,))
                    b.tt(h1[:, fc, tsl], tf[:, :], tf[:, :], ALU.mult, reads=(btf,), writes=(b_h1[fc][t],), eng=b.P.pool)
        for cg in range(8):
            wt, bw = b.next_w()
            wv = wt[:, :].rearrange("p (k c) -> p k c", k=16)
            for half in range(2):
                dc = cg * 2 + half
                for t in range(2):
                    tsl = slice(t * 512, (t + 1) * 512)
                    ps, bps = b.get("ps")
                    b.mm(ps[:, :], bps, [(wv[:, k, half * 128:(half + 1) * 128], h1[:, k, tsl]) for k in range(16)],
                         reads=[bw] + [b_h1[k][t] for k in range(16)])
                    b.tt(xs[:, dc, tsl], ps[:, :], xs[:, dc, tsl], ALU.add, reads=(bps, b_xs[dc][t]), writes=(b_xs[dc][t],))
                    if q == 3 and out_stats is not None:
                        out_stats.add(xs[:, dc, tsl], b_xs[dc][t], t)
                if q == 3 and out_d is not None:
                    b.dma_out(out_d[:, dc, :], xs[:, dc, :], rbufs=(b_xs[dc][0], b_xs[dc][1]))


def st_mem_rstd(b, memT, b_mem, rstd_m, b_rstdm):
    ps, bps = b.get("ps")
    for k in range(KC):
        sq, bsq = b.get("sq")
        b.act(sq[:, :256], memT[:, k, :], AF.Square, reads=(b_mem,), writes=(bsq,))

        def f(e, k=k, sq=sq, ps=ps):
            return e.matmul(ps[:, :256], b.ones[:, :], sq[:, :256], start=(k == 0), stop=(k == KC - 1))
        b.P.op(b.P.pe, f, reads=(bsq, b.b_ones), writes=(bps,))
    b.rstd_from_ss(ps[:, :256], bps, 1.0 / D, rstd_m[:, :], b_rstdm, ncols=256)


def st_memkv(b, memT, b_mem, rstd_m, b_rstdm, memn, b_memn, memk, b_memk, memv, b_memv, gcol_mem, col_kg):
    for k in range(KC):
        b.stt(memn[:, k, :], memT[:, k, :], b.pcol(gcol_mem + k), rstd_m[:, :], ALU.mult, ALU.mult,
              reads=(b_mem, b_rstdm, b.b_par), writes=(b_memn,))
    for hp in range(2):
        wt, bw = b.next_w()
        wv = wt[:, :].rearrange("p (k c) -> p k c", k=16)
        for hh in range(2):
            h = hp * 2 + hh
            hs = slice(hh * 128, hh * 128 + 128)
            ps, bps = b.get("ps")
            b.mm(ps[:, :256], bps, [(wv[:, k, hs], memn[:, k, :]) for k in range(KC)], reads=(bw, b_memn))
            sq, bsq = b.get("sq")
            b.act(sq[:, :256], ps[:, :256], AF.Square, reads=(bps,), writes=(bsq,))
            ps2, bps2 = b.get("ps")
            b.mm(ps2[:, :256], bps2, [(b.ones[:, :], sq[:, :256])], reads=(bsq, b.b_ones))
            rs, brs = b.get("rstd")
            b.rstd_from_ss(ps2[:, :256], bps2, 1.0 / 128, rs[:, :256], brs, ncols=256)
            b.stt(memk[:, h, :], ps[:, :256], b.pcol(col_kg), rs[:, :256], ALU.mult, ALU.mult,
                  reads=(bps, brs, b.b_par), writes=(b_memk,))
    for vh in range(2):
        wt, bw = b.next_w()
        wv = wt[:, :].rearrange("p (k c) -> p k c", k=16)
        for mc in range(2):
            ps, bps = b.get("ps")
            b.mm(ps[:, :256], bps, [(memn[:, k, mc * 128:(mc + 1) * 128], wv[:, k, :]) for k in range(KC)],
                 reads=(bw, b_memn))
            b.copy(memv[:, mc, vh * 256:(vh + 1) * 256], ps[:, :256], reads=(bps,), writes=(b_memv,))


class Pipe:
    def __init__(self):
        self.items = []

    def _advance(self, skip_new=False):
        for it in reversed(self.items):
            if it:
                it.pop(0)()
        self.items = [it for it in self.items if it]

    def push(self, stages):
        self.items.append(list(stages))
        self._advance()

    def flush(self):
        while self.items:
            self._advance()


def mem_attn_stages(b, mq_ps, b_mq, memk, b_memk, memv, b_memv, h, col_qg, out_ap, b_out, after=None):
    st = {}

    def A():
        sq, bsq = b.get("sq")
        b.act(sq[:, :], mq_ps, AF.Square, reads=(b_mq,), writes=(bsq,))
        ps2, bps2 = b.get("ps")
        b.mm(ps2[:, :], bps2, [(b.ones[:, :], sq[:, :])], reads=(bsq, b.b_ones))
        st.update(ps2=ps2, bps2=bps2)

    def B():
        rs, brs = b.get("rstd")
        b.rstd_from_ss(st["ps2"][:, :], st["bps2"], 1.0 / 128, rs[:, :], brs)
        qn, bqn = b.get("tb")
        b.stt(qn[:, :], mq_ps, b.pcol(col_qg), rs[:, :], ALU.mult, ALU.mult, reads=(b_mq, brs, b.b_par), writes=(bqn,))
        pts = []
        for mc in range(2):
            ps3, bps3 = b.get("ps")
            b.mm(ps3[:, :], bps3, [(memk[:, h, mc * 128:(mc + 1) * 128], qn[:, :])], reads=(b_memk, bqn))
            pt, bpt = b.get("tb")
            b.act(pt[:, :], ps3[:, :], AF.Exp, reads=(bps3,), writes=(bpt,), scale=SCALE)
            pts.append((pt, bpt))
        st.update(pts=pts)

    def C():
        pts = st["pts"]
        pn, bpn = b.get("ps")
        b.mm(pn[:, :], bpn, [(memv[:, mc, h * 128:(h + 1) * 128], pts[mc][0][:, :]) for mc in range(2)],
             reads=(b_memv, pts[0][1], pts[1][1]))
        pd, bpd = b.get("ps")
        b.mm(pd[:, :], bpd, [(b.ones[:, :], pts[mc][0][:, :]) for mc in range(2)], reads=(b.b_ones, pts[0][1], pts[1][1]))
        rd, brd = b.get("tf")
        b.recip_act(rd[:, :], pd[:, :], reads=(bpd,), writes=(brd,))
        b.tt(out_ap, pn[:, :], rd[:, :], ALU.mult, reads=(bpn, brd), writes=(b_out,))
        if after is not None:
            after()
    return [A, B, C]


class MemState:
    def __init__(self, b, alias=None):
        if alias is None:
            self.memT = b.sb("memT", [128, KC, 256], F32)
            self.memn = b.sb("memn", [128, KC, 256], BF16)
        else:
            self.memT = alias[:, 0:8, :].bitcast(F32).rearrange("p a (b c) -> p (a b) c", c=256)
            self.memn = alias[:, 8:12, :].rearrange("p a (b c) -> p (a b) c", c=256)
        self.b_mem = Buf()
        self.b_memn = Buf()
        self.memk = b.sb("memk", [128, 4, 256], BF16)
        self.b_memk = Buf()
        self.memv = b.sb("memv", [128, 2, 512], BF16)
        self.b_memv = Buf()
        self.rstd_m = b.sb("rstd_m", [128, 256], F32)
        self.b_rstdm = Buf()


def tiles_A1(l):
    tl = list(memkv_tiles(l))
    tl += [wt_std("a_w_in", l, 0, 16, [(3072 + c, 256)]) for c in (0, 256)]
    tl.append(("gates", l))
    for n in range(12):
        tl.append(wt_std("a_w_in", l, 0, 16, [(n * 128, 128), (1536 + n * 128, 128)]))
    return tl


def build_A1(l, from_xn=False):
    b = Bld(tiles_A1(l))
    nc, P = b.nc, b.P
    if from_xn:
        xn_d = b.din("xn", [128, KC, T], BF16)
        xnh_d = b.din("xnh", [128, KC, 4], BF16)
    else:
        xT_d = b.din("xT", [128, KC, T])
        xh_d = b.din("xh", [128, KC, 4])
    memT_d = b.din("memT", [128, KC, 256])
    cat_d = b.dout("cat", [128, 16, T], BF16)
    q_d = b.dout("qq", [128, 12, T], BF16)
    car_d = b.dout("carry", [128, 24])

    b.pool("xt", 4, [128, 512], F32)
    b.pool("ub", 3, [128, 4 + T], F32)
    b.pool("gb", 3, [128, T], F32)
    b.pool("xc", 2, [128, T], F32)
    b.pool("rb", 2, [128, T], F32)
    b.pool("ib", 2, [128, T], F32)
    b.pool("s3", 5, [128, T], F32)
    xn = b.sb("xn", [128, KC, T], BF16)
    b_xn = [[Buf() for t in range(2)] for k in range(KC)]
    xnh = b.sb("xnh", [128, KC, 4], BF16)
    b_xnh = Buf()
    xh = b.sb("xh", [128, KC, 4], F32)
    b_xh = Buf()
    b.pool("cb", 6, [128, T], BF16)
    M = MemState(b, alias=xn)
    rstd_x = [b.sb(f"rstd_x{t}", [128, 512], F32) for t in range(2)]
    b_rstdx = [Buf(), Buf()]
    rstd_h = b.sb("rstd_h", [128, 4], F32)
    b_rstdh = Buf()
    carry = b.sb("carry", [128, 24], F32)
    b_carry = Buf()
    gw = b.sb("gw", [128, 24, 128], BF16)
    b_gw = Buf()
    nsp = b.sb("nsp", [128, 12], F32)
    b_nsp = Buf()
    zeros = b.sb("zeros", [128, T], F32)
    b_zeros = Buf()
    sml = [b.sb(f"sml{i}", [128, 12], F32) for i in range(6)]
    b_sml = [Buf() for i in range(6)]

    b.dma_in(M.memT[:, :, :], memT_d, [M.b_mem])
    if not from_xn:
        b.dma_in(xh[:, :, :], xh_d, [b_xh])
    P.op(P.pool, lambda e: e.memset(zeros[:, :], 0.0), writes=(b_zeros,))

    st_mem_rstd(b, M.memT, M.b_mem, M.rstd_m, M.b_rstdm)
    st_memkv(b, M.memT, M.b_mem, M.rstd_m, M.b_rstdm, M.memn, M.b_memn, M.memk, M.b_memk, M.memv, M.b_memv, 16, 33)

    if from_xn:
        b.dma_in(xnh[:, :, :], xnh_d, [b_xnh])
        for k in list(range(12, KC)) + list(range(12)):
            extra = (M.b_mem,) if k < 8 else ((M.b_memn,) if k < 12 else ())
            b.dma_in(xn[:, k, :], xn_d[:, k, :], [b_xn[k][0], b_xn[k][1]] + list(extra))
    else:
        pss = [b.get("ps"), b.get("ps")]
        for k in range(KC):
            for t in range(2):
                xt, bxt = b.get("xt")
                b.dma_in(xt[:, :], xT_d[:, k, t * 512:(t + 1) * 512], [bxt])
                sq, bsq = b.get("sq")
                b.act(sq[:, :], xt[:, :], AF.Square, reads=(bxt,), writes=(bsq,))

                def f(e, k=k, sq=sq, ps=pss[t][0]):
                    return e.matmul(ps[:, :], b.ones[:, :], sq[:, :], start=(k == 0), stop=(k == KC - 1))
                P.op(P.pe, f, reads=(bsq, b.b_ones), writes=(pss[t][1],))
        for t in range(2):
            b.rstd_from_ss(pss[t][0][:, :], pss[t][1], 1.0 / D, rstd_x[t][:, :], b_rstdx[t])
        psh, bpsh = b.get("ps")
        for k in range(KC):
            sq, bsq = b.get("sq")
            b.act(sq[:, :4], xh[:, k, :], AF.Square, reads=(b_xh,), writes=(bsq,))

            def f(e, k=k, sq=sq, psh=psh):
                return e.matmul(psh[:, :4], b.ones[:, :], sq[:, :4], start=(k == 0), stop=(k == KC - 1))
            P.op(P.pe, f, reads=(bsq, b.b_ones), writes=(bpsh,))
        b.rstd_from_ss(psh[:, :4], bpsh, 1.0 / D, rstd_h[:, :], b_rstdh, ncols=4)
        for k in range(KC):
            b.stt(xnh[:, k, :], xh[:, k, :], b.pcol(k), rstd_h[:, :], ALU.mult, ALU.mult,
                  reads=(b_xh, b_rstdh, b.b_par), writes=(b_xnh,))
        for k in range(KC):
            for t in range(2):
                tsl = slice(t * 512, (t + 1) * 512)
                xt, bxt = b.get("xt")
                b.dma_in(xt[:, :], xT_d[:, k, tsl], [bxt])
                extra = (M.b_mem,) if k < 8 else ((M.b_memn,) if k < 12 else ())
                b.stt(xn[:, k, tsl], xt[:, :], b.pcol(k), rstd_x[t][:, :], ALU.mult, ALU.mult,
                      reads=(bxt, b_rstdx[t], b.b_par), writes=(b_xn[k][t],) + extra)

    pipe = Pipe()
    for hp in range(2):
        wt, bw = b.next_w()
        wv = wt[:, :].rearrange("p (k c) -> p k c", k=16)
        for hh in range(2):
            h = hp * 2 + hh
            cbt, bcb = b.get("cb")
            for t in range(2):
                tsl = slice(t * 512, (t + 1) * 512)
                ps, bps = b.get("ps")
                b.mm(ps[:, :], bps, [(wv[:, k, hh * 128:(hh + 1) * 128], xn[:, k, tsl]) for k in range(KC)],
                     reads=[bw] + [b_xn[k][t] for k in range(KC)])
                after = None
                if t == 1:
                    after = (lambda h=h, cbt=cbt, bcb=bcb: b.dma_out(cat_d[:, 12 + h, :], cbt[:, :], rbufs=(bcb,)))
                pipe.push(mem_attn_stages(b, ps[:, :], bps, M.memk, M.b_memk, M.memv, M.b_memv, h, 32, cbt[:, tsl], bcb, after=after))
    pipe.flush()

    wt, bw = b.next_w()
    b.copy(gw[:, :, :], wt[:, :24 * 128].rearrange("p (g d) -> p g d", g=24), reads=(bw,), writes=(b_gw,), eng=P.pool)

    lam = b.par[:, 124:136]
    s0, s1, s2, s3, s4, s5 = sml
    B0, B1, B2, B3, B4, B5 = b_sml
    b.ts(s0[:, :], lam, -1.0, None, ALU.mult, None, reads=(b.b_par,), writes=(B0,))
    b.tt(s0[:, :], s0[:, :], lam, ALU.max, reads=(B0, b.b_par), writes=(B0,))
    b.act(s1[:, :], s0[:, :], AF.Exp, reads=(B0,), writes=(B1,), scale=-1.0)
    b.ts(s2[:, :], s1[:, :], 2.0, None, ALU.add, None, reads=(B1,), writes=(B2,))
    b.recip(s3[:, :], s2[:, :], reads=(B2,), writes=(B3,))
    b.tt(s2[:, :], s1[:, :], s3[:, :], ALU.mult, reads=(B1, B3), writes=(B2,))
    b.tt(s3[:, :], s2[:, :], s2[:, :], ALU.mult, reads=(B2,), writes=(B3,))
    b.ts(s4[:, :], s3[:, :], 1.0 / 11, 1.0 / 9, ALU.mult, ALU.add, reads=(B3,), writes=(B4,))
    for cst in (1.0 / 7, 1.0 / 5, 1.0 / 3, 1.0):
        b.tt(s4[:, :], s4[:, :], s3[:, :], ALU.mult, reads=(B4, B3), writes=(B4,))
        b.ts(s4[:, :], s4[:, :], cst, None, ALU.add, None, reads=(B4,), writes=(B4,))
    b.tt(s4[:, :], s4[:, :], s2[:, :], ALU.mult, reads=(B4, B2), writes=(B4,))
    b.ts(s5[:, :], lam, -1.0, 0.0, ALU.mult, ALU.max, reads=(b.b_par,), writes=(B5,))
    b.stt(s5[:, :], s4[:, :], 2.0, s5[:, :], ALU.mult, ALU.add, reads=(B4, B5), writes=(B5,))
    b.ts(nsp[:, :], s5[:, :], -8.0, None, ALU.mult, None, reads=(B5,), writes=(b_nsp,))

    stt_ = {}

    def S1a(n):
        wt, bw = b.next_w()
        wv = wt[:, :].rearrange("p (k c) -> p k c", k=16)
        ub, bub = b.get("ub")
        gb, bgb = b.get("gb")
        psh, bpsh = b.get("ps")
        b.mm(psh[:, :4], bpsh, [(wv[:, k, 0:128], xnh[:, k, :]) for k in range(KC)], reads=(bw, b_xnh))
        b.copy(ub[:, 0:4], psh[:, :4], reads=(bpsh,), writes=(bub,))
        stt_[n] = dict(ub=ub, bub=bub, gb=gb, bgb=bgb, wv=wv, bw=bw)
        S1t(n, 0)

    def S1t(n, t):
        d_ = stt_[n]
        wv, bw, ub, bub, gb, bgb = d_["wv"], d_["bw"], d_["ub"], d_["bub"], d_["gb"], d_["bgb"]
        tsl = slice(t * 512, (t + 1) * 512)
        ps, bps = b.get("ps")
        b.mm(ps[:, :], bps, [(wv[:, k, 0:128], xn[:, k, tsl]) for k in range(KC)],
             reads=[bw] + [b_xn[k][t] for k in range(KC)])
        b.copy(ub[:, 4 + t * 512:4 + (t + 1) * 512], ps[:, :], reads=(bps,), writes=(bub,))
        pg, bpg = b.get("ps")
        b.mm(pg[:, :], bpg, [(wv[:, k, 128:256], xn[:, k, tsl]) for k in range(KC)],
             reads=[bw] + [b_xn[k][t] for k in range(KC)])
        b.act(gb[:, tsl], pg[:, :], AF.Gelu_apprx_tanh, reads=(bpg,), writes=(bgb,))

    def S1b(n):
        S1t(n, 1)

    def S2a(n):
        d_ = stt_[n]
        ub, bub = d_["ub"], d_["bub"]
        xc, bxc = b.get("xc")
        cw = 40 + n * 4
        b.ts(xc[:, :], ub[:, 1:1 + T], b.pcol(cw), b.pcol(88 + n), ALU.mult, ALU.add, reads=(bub, b.b_par), writes=(bxc,), eng=P.pool)
        for j in range(1, 4):
            b.stt(xc[:, :], ub[:, 1 + j:1 + j + T], b.pcol(cw + j), xc[:, :], ALU.mult, ALU.add,
                  reads=(bub, b.b_par, bxc), writes=(bxc,))
        d_.update(xc=xc, bxc=bxc)

    def S2b(n):
        d_ = stt_[n]
        xc, bxc = d_["xc"], d_["bxc"]
        xcb = [b.get("tb"), b.get("tb")]
        for t in range(2):
            b.copy(xcb[t][0][:, :], xc[:, t * 512:(t + 1) * 512], reads=(bxc,), writes=(xcb[t][1],))
        rb, brb = b.get("rb")
        ib, bib = b.get("ib")
        for t in range(2):
            tsl = slice(t * 512, (t + 1) * 512)
            pr, bpr = b.get("ps")
            b.mm(pr[:, :], bpr, [(gw[:, n, :], xcb[t][0][:, :])], reads=(b_gw, xcb[t][1]))
            b.act(rb[:, tsl], pr[:, :], AF.Sigmoid, reads=(bpr, b.b_par), writes=(brb,), bias=b.pcol(100 + n))
            pi_, bpi = b.get("ps")
            b.mm(pi_[:, :], bpi, [(gw[:, 12 + n, :], xcb[t][0][:, :])], reads=(b_gw, xcb[t][1]))
            b.act(ib[:, tsl], pi_[:, :], AF.Sigmoid, reads=(bpi, b.b_par), writes=(bib,), bias=b.pcol(112 + n))
        d_.update(rb=rb, brb=brb, ib=ib, bib=bib)

    def S3a(n):
        d_ = stt_[n]
        xc, bxc, ab, bab, ib, bib = d_["xc"], d_["bxc"], d_["rb"], d_["brb"], d_["ib"], d_["bib"]
        b.act(ab[:, :], ab[:, :], AF.Exp, reads=(bab, b_nsp), writes=(bab,), scale=nsp[:, n:n + 1])
        bb_, bbb = b.get("s3")
        b.act(bb_[:, :], ab[:, :], AF.Square, reads=(bab,), writes=(bbb,))
        b.act(bb_[:, :], bb_[:, :], AF.Sqrt, reads=(bbb, b.b_par), writes=(bbb,), scale=-1.0, bias=b.pcol(254))
        b.tt(ib[:, :], ib[:, :], xc[:, :], ALU.mult, reads=(bib, bxc), writes=(bib,), eng=P.pool)
        d_.update(bb_=bb_, bbb=bbb)

    def S3b(n):
        d_ = stt_.pop(n)
        gb, bgb, ab, bab, ib, bib, bb_, bbb = d_["gb"], d_["bgb"], d_["rb"], d_["brb"], d_["ib"], d_["bib"], d_["bb_"], d_["bbb"]
        b.tt(bb_[:, :], bb_[:, :], ib[:, :], ALU.mult, reads=(bbb, bib), writes=(bbb,))
        hb, bhb = b.get("s3")
        P.op(P.dve, lambda e, hb=hb, ab=ab, bb_=bb_: e.tensor_tensor_scan(out=hb[:, :], data0=ab[:, :], data1=bb_[:, :], initial=0.0,
                                                                            op0=ALU.mult, op1=ALU.add),
             reads=(bab, bbb), writes=(bhb,))
        Ab, bAb = b.get("s3")
        P.op(P.dve, lambda e, Ab=Ab, ab=ab: e.tensor_tensor_scan(out=Ab[:, :], data0=ab[:, :], data1=zeros[:, :], initial=1.0,
                                                                  op0=ALU.mult, op1=ALU.add),
             reads=(bab, b_zeros), writes=(bAb,))
        b.copy(carry[:, n:n + 1], hb[:, T - 1:T], reads=(bhb,), writes=(b_carry,), eng=P.dve)
        b.copy(carry[:, 12 + n:13 + n], Ab[:, T - 1:T], reads=(bAb,), writes=(b_carry,), eng=P.dve)
        cbt, bcb = b.get("cb")
        b.tt(cbt[:, :], hb[:, :], gb[:, :], ALU.mult, reads=(bhb, bgb), writes=(bcb,), eng=P.pool)
        b.dma_out(cat_d[:, n, :], cbt[:, :], rbufs=(bcb,))
        cbq, bcq = b.get("cb")
        b.tt(cbq[:, :], Ab[:, :], gb[:, :], ALU.mult, reads=(bAb, bgb), writes=(bcq,), eng=P.pool)
        b.dma_out(q_d[:, n, :], cbq[:, :], rbufs=(bcq,))

    for s_ in range(12 + 2):
        if 0 <= s_ - 1 < 12:
            S2a(s_ - 1)
        if 0 <= s_ - 2 < 12:
            S3a(s_ - 2)
        if s_ < 12:
            S1a(s_)
        if 0 <= s_ - 1 < 12:
            S2b(s_ - 1)
        if 0 <= s_ - 2 < 12:
            S3b(s_ - 2)
        if s_ < 12:
            S1b(s_)

    b.dma_out(car_d, carry[:, :], rbufs=(b_carry,))
    return b.finish(), b.wtiles


def tiles_A2(l, with_kv):
    tl = [wt_std("a_w_out", l, 0, 16, [(c, 256)]) for c in range(0, 2048, 256)]
    tl += mlp_tiles(l)
    if with_kv:
        tl += [wt_std("kv_w", None, 0, 16, [(c, 256)]) for c in range(0, 3072, 256)]
    return tl


def build_A2(l, with_kv, with_next=True):
    b = Bld(tiles_A2(l, with_kv), n_ps=6)
    b.pool("rstdp", 4, [128, 512], F32)
    stats = Stats(b)
    nc, P = b.nc, b.P
    xT_d = b.din("xT", [128, KC, T])
    cat_d = b.din("cat", [128, 16, T], BF16)
    q_d = b.din("qq", [128, 12, T], BF16)
    car_d = b.din("carr", [128, 8, 24])
    sel_d = b.din("sel", [128, 8])
    yT_d = b.dout("yT", [128, KC, T])
    xs = b.sb("xs", [128, KC, T], F32)
    b_xs = [[Buf() for t in range(2)] for k in range(KC)]
    cat = b.sb("cat", [128, 16, T], BF16)
    b_cat = [[Buf() for t in range(2)] for k in range(16)]
    h1 = b.sb("h1", [128, 16, T], BF16)
    b_h1 = [[Buf() for t in range(2)] for k in range(16)]
    carr = b.sb("carr", [128, 8, 24], F32)
    b_carr = Buf()
    sel = b.sb("sel", [128, 8], F32)
    b_sel = Buf()
    cst = [b.sb(f"cst{i}", [128, 12], F32) for i in range(2)]
    b_cst = [Buf(), Buf()]
    hin = b.sb("hin", [128, 12], F32)
    b_hin = Buf()
    tmp12 = b.sb("tmp12", [128, 12], F32)
    b_tmp12 = Buf()

    b.dma_in(carr[:, :, :], car_d, [b_carr])
    b.dma_in(sel[:, :], sel_d, [b_sel])
    for k in range(16):
        if k < 12:
            b.dma_in(h1[:, k, :], q_d[:, k, :], [b_h1[k][0], b_h1[k][1]])
        b.dma_in(cat[:, k, :], cat_d[:, k, :], [b_cat[k][0], b_cat[k][1]])
    for k in range(KC):
        b.dma_in(xs[:, k, :], xT_d[:, k, :], [b_xs[k][0], b_xs[k][1]])

    P.op(P.pool, lambda e: e.memset(cst[0][:, :], 0.0), writes=(b_cst[0],))
    P.op(P.pool, lambda e: e.memset(hin[:, :], 0.0), writes=(b_hin,))
    for r in range(8):
        cur, bcur = cst[r % 2], b_cst[r % 2]
        nx, bnx = cst[(r + 1) % 2], b_cst[(r + 1) % 2]
        b.stt(hin[:, :], cur[:, :], sel[:, r:r + 1], hin[:, :], ALU.mult, ALU.add, reads=(bcur, b_sel, b_hin), writes=(b_hin,))
        if r < 7:
            b.tt(tmp12[:, :], carr[:, r, 12:24], cur[:, :], ALU.mult, reads=(b_carr, bcur), writes=(b_tmp12,))
            b.tt(nx[:, :], tmp12[:, :], carr[:, r, 0:12], ALU.add, reads=(b_tmp12, b_carr), writes=(bnx,))
    for n in range(12):
        for t in range(2):
            tsl = slice(t * 512, (t + 1) * 512)
            b.stt(cat[:, n, tsl], h1[:, n, tsl], hin[:, n:n + 1], cat[:, n, tsl], ALU.mult, ALU.add,
                  reads=(b_h1[n][t], b_hin, b_cat[n][t]), writes=(b_cat[n][t],))
    for cg in range(8):
        wt, bw = b.next_w()
        wv = wt[:, :].rearrange("p (k c) -> p k c", k=16)
        for half in range(2):
            dc = cg * 2 + half
            for t in range(2):
                tsl = slice(t * 512, (t + 1) * 512)
                ps, bps = b.get("ps")
                b.mm(ps[:, :], bps, [(wv[:, k, half * 128:(half + 1) * 128], cat[:, k, tsl]) for k in range(16)],
                     reads=[bw] + [b_cat[k][t] for k in range(16)])
                b.tt(xs[:, dc, tsl], ps[:, :], xs[:, dc, tsl], ALU.add, reads=(bps, b_xs[dc][t]), writes=(b_xs[dc][t],))
                stats.add(xs[:, dc, tsl], b_xs[dc][t], t)
    need_out = with_next or with_kv
    st_mlp(b, xs, b_xs, cat, b_cat, h1, b_h1, 0, out_d=yT_d, in_rstd=stats.rstd(), out_stats=(stats if need_out else None))
    if need_out:
        rs_out = stats.rstd()
    if with_next:
        xnn_d = b.dout("xnn", [128, KC, T], BF16)
        st_apply_norm(b, xs, b_xs, cat, b_cat, 40, rs_out)
        for k in range(KC):
            b.dma_out(xnn_d[:, k, :], cat[:, k, :], rbufs=(b_cat[k][0], b_cat[k][1]))
    if with_kv:
        kT_d = b.dout("kT", [128, 12, T], BF16)
        vT_d = b.dout("vT", [128, 12, T], BF16)
        st_apply_norm(b, xs, b_xs, cat, b_cat, 16, rs_out)
        kpipe = Pipe()
        for cg in range(12):
            wt, bw = b.next_w()
            wv = wt[:, :].rearrange("p (k c) -> p k c", k=16)
            for half in range(2):
                hc = cg * 2 + half
                for t in range(2):
                    tsl = slice(t * 512, (t + 1) * 512)
                    ps, bps = b.get("ps")
                    b.mm(ps[:, :], bps, [(wv[:, k, half * 128:(half + 1) * 128], cat[:, k, tsl]) for k in range(16)],
                         reads=[bw] + [b_cat[k][t] for k in range(16)])
                    if hc < 12:
                        stq = {}

                        def KA(ps=ps, bps=bps, stq=stq):
                            sq, bsq = b.get("sq")
                            b.act(sq[:, :], ps[:, :], AF.Square, reads=(bps,), writes=(bsq,))
                            ps2, bps2 = b.get("ps")
                            b.mm(ps2[:, :], bps2, [(b.ones[:, :], sq[:, :])], reads=(bsq, b.b_ones))
                            stq.update(ps2=ps2, bps2=bps2)

                        def KB(ps=ps, bps=bps, stq=stq, hc=hc, t=t, tsl=tsl):
                            rs, brs = b.get("rstd")
                            b.rstd_from_ss(stq["ps2"][:, :], stq["bps2"], 1.0 / 128, rs[:, :], brs)
                            b.stt(h1[:, hc, tsl], ps[:, :], b.pcol(32 + hc // 4), rs[:, :], ALU.mult, ALU.mult,
                                  reads=(bps, brs, b.b_par), writes=(b_h1[hc][t],))
                        kpipe.push([KA, KB])
                    else:
                        b.copy(h1[:, hc - 12, tsl], ps[:, :], reads=(bps,), writes=(b_h1[hc - 12][t],))
            if cg == 5:
                kpipe.flush()
                for k in range(12):
                    b.dma_out(kT_d[:, k, :], h1[:, k, :], rbufs=(b_h1[k][0], b_h1[k][1]))
        for k in range(12):
            b.dma_out(vT_d[:, k, :], h1[:, k, :], rbufs=(b_h1[k][0], b_h1[k][1]))
    return b.finish(), b.wtiles


def fm(x):
    n, f = x.shape
    return np.ascontiguousarray(x.T.reshape(f // 128, 128, n).transpose(1, 0, 2))


def unfm(xT):
    p, k, n = xT.shape
    return np.ascontiguousarray(xT.transpose(1, 0, 2).reshape(k * p, n).T)


def colk(v):
    return np.ascontiguousarray(np.asarray(v, np.float32).reshape(-1, 128).T)


_CACHE = {}
_TIMES = []


def _run(nc, ins, tag=""):
    import os
    if os.environ.get("KTRACE"):
        res = run_bass_kernel_spmd(nc, ins, core_ids=list(range(NCORES)), trace=True)
        _TIMES.append((tag, res.exec_time_ns))
        print("KTRACE", tag, res.exec_time_ns, flush=True)
    else:
        res = run_bass_kernel_spmd(nc, ins, core_ids=list(range(NCORES)))
    return res.results


def _launch(key, builder, in_maps):
    if key not in _CACHE:
        _CACHE[key] = builder()
    nc, tiles = _CACHE[key]
    res = run_bass_kernel_spmd(nc, in_maps, core_ids=list(range(NCORES)))
    return res.results


def run_A_layer(l, xT, memT, W, with_kv, xn_in=None, next_g=None):
    f32 = np.float32
    par = np.zeros((128, NPAR), f32)
    par[:, 0:16] = colk(W["norm_mix_g"][l])
    par[:, 16:32] = colk(W["mem_norm_g"][l])
    par[:, 32] = W["mem_q_norm_g"][l]
    par[:, 33] = W["mem_k_norm_g"][l]
    cw = W["a_conv_w"][l]
    for n in range(12):
        for j in range(4):
            par[:, 40 + n * 4 + j] = cw[j, n * 128:(n + 1) * 128]
    par[:, 88:100] = colk(W["a_conv_b"][l])
    par[:, 100:112] = np.asarray(W["a_gate_r_b"][l], f32).T
    par[:, 112:124] = np.asarray(W["a_gate_i_b"][l], f32).T
    par[:, 124:136] = colk(W["a_lambda"][l])
    par[:, 254] = 1.0
    par[:, 255] = EPS
    nc1 = _CACHE.get(("A1", l))
    if nc1 is None:
        nc1 = _CACHE[("A1", l)] = build_A1(l, from_xn=(xn_in is not None))
    wst = pack_weights(nc1[1], W)
    ins = []
    for c in range(NCORES):
        if xn_in is not None:
            xnh = np.zeros((128, KC, 4), ml_dtypes.bfloat16)
            if c > 0:
                xnh[:, :, :] = np.asarray(xn_in[c - 1])[:, :, T - 4:]
            ins.append({"xn": xn_in[c], "xnh": xnh, "memT": memT, "wst": wst, "par": par})
        else:
            xh = np.zeros((128, KC, 4), f32)
            if c > 0:
                xh[:, :, :] = xT[c - 1][:, :, T - 4:]
            ins.append({"xT": xT[c], "xh": xh, "memT": memT, "wst": wst, "par": par})
    r1 = _run(nc1[0], ins, f"A1_{l}")
    par2 = np.zeros((128, NPAR), f32)
    par2[:, 0:16] = colk(W["norm_mlp_g"][l])
    if with_kv:
        par2[:, 16:32] = colk(W["kv_norm_g"])
        par2[:, 32:35] = np.asarray(W["k_norm_g"], f32).T
    par2[:, 255] = EPS
    if next_g is not None:
        par2[:, 40:56] = colk(next_g)
    nc2 = _CACHE.get(("A2", l))
    if nc2 is None:
        nc2 = _CACHE[("A2", l)] = build_A2(l, with_kv, with_next=(next_g is not None))
    wst2 = pack_weights(nc2[1], W)
    carr = np.ascontiguousarray(np.stack([r1[c]["carry"] for c in range(NCORES)], axis=1))
    ins2 = []
    for c in range(NCORES):
        sel = np.zeros((128, 8), f32)
        sel[:, c] = 1.0
        ins2.append({"xT": xT[c], "cat": r1[c]["cat"], "qq": r1[c]["qq"], "carr": carr, "sel": sel, "wst": wst2, "par": par2})
    r2 = _run(nc2[0], ins2, f"A2_{l}")
    out = [r2[c]["yT"] for c in range(NCORES)]
    xnn = [r2[c]["xnn"] for c in range(NCORES)] if next_g is not None else None
    if with_kv:
        return out, [r2[c]["kT"] for c in range(NCORES)], [r2[c]["vT"] for c in range(NCORES)], r1, xnn
    return out, None, None, r1, xnn


LG = [T // d for d in DIL]
NQ = [min(128, lg) for lg in LG]
NKC = [128 + lg for lg in LG]
NBC = [(n + 127) // 128 for n in NKC]
NK = [d * n for d, n in zip(DIL, NKC)]
NB = [d * n for d, n in zip(DIL, NBC)]


def tiles_B1(l):
    tl = list(memkv_tiles(l))
    tl += [wt_std("b_w_q", l - 2, 0, 16, [(c, 256)]) for c in range(0, 2048, 256)]
    return tl


def build_B1(l, from_xn=True):
    b = Bld(tiles_B1(l))
    nc, P = b.nc, b.P
    if from_xn:
        xn_d = b.din("xn", [128, KC, T], BF16)
    else:
        xT_d = b.din("xT", [128, KC, T])
    memT_d = b.din("memT", [128, KC, 256])
    kt_d = [b.din(f"kt{g}", [128, 4, NK[g]], BF16) for g in range(3)]
    vv_d = [b.din(f"vv{g}", [128, 4, NB[g], 128], BF16) for g in range(3)]
    oh_d = b.din("oh", [33, 6, 256])
    jm_d = b.din("jm", [128, 128])
    relb_d = b.din("relb", [33, 12])
    kval_d = b.din("kval", [128, 3])
    cat_d = b.dout("cat", [128, 8, T], BF16)
    vec_d = nc.dram_tensor("vecd", [6, 4, 256], F32).ap()

    b.pool("xt", 4, [128, 512], F32)
    b.pool("cb", 4, [128, T], BF16)
    b.pool("pp", 4, [128, 512], BF16)
    b.pool("kt", 2, [128, max(NK)], BF16)
    b.pool("vv", 2, [128, max(NB), 128], BF16)
    xn = b.sb("xn", [128, KC, T], BF16)
    b_xn = [[Buf() for t in range(2)] for k in range(KC)]
    qn = b.sb("qn", [128, 12, T], BF16)
    b_qn = [Buf() for k in range(12)]
    M = MemState(b, alias=qn)
    rstd_x = [b.sb(f"rstd_x{t}", [128, 512], F32) for t in range(2)]
    b_rstdx = [Buf(), Buf()]
    accN = b.sb("accN", [128, T], F32)
    accD = b.sb("accD", [128, T], F32)
    b_acc = Buf()
    relb = b.sb("relb", [33, 12], F32)
    b_relb = Buf()
    jm = b.sb("jm", [128, 128], F32)
    b_jm = Buf()
    kval = b.sb("kval", [128, 3], F32)
    b_kval = Buf()
    b.pool("vec", 2, [4, 256], F32)
    b.pool("hk", 2, [128, 512], F32)
    masks = [[b.sb(f"mask{g}_{ty}", [128, 4, 128], F32) for ty in range(3)] for g in range(3)]
    b_masks = [[Buf() for ty in range(3)] for g in range(3)]

    b.dma_in(M.memT, memT_d, [M.b_mem])
    b.dma_in(relb[:, :], relb_d, [b_relb])
    b.dma_in(jm[:, :], jm_d, [b_jm])
    b.dma_in(kval[:, :], kval_d, [b_kval])

    mask_state = {}

    def mask_A(g, ty):
        oh, boh = b.get("tf")
        b.dma_in(oh[:33, :256], oh_d[:, g * 2 + ty, :], [boh])
        ps, bps = b.get("ps")
        b.mm(ps[:4, :256], bps, [(relb[:33, g * 4:(g + 1) * 4], oh[:33, :256])], reads=(b_relb, boh))
        vec, b_vec = b.get("vec")
        b.act(vec[:, :], ps[:4, :256], AF.Exp, reads=(bps,), writes=(b_vec,))
        b_vd = Buf()
        P.dma(P.sp, (lambda e, s, g=g, ty=ty, vec=vec: e.dma_start(out=vec_d[g * 2 + ty], in_=vec[:, :]).then_inc(s, 16)),
              reads=(b_vec,), writes=(b_vd,))
        hk, bhk = b.get("hk")
        src = bass.AP(tensor=vec_d.tensor, offset=(g * 2 + ty) * 1024, ap=[[1, 128], [256, 4], [1, 128]])
        b.dma_in(hk[:, :].rearrange("p (h q) -> p h q", h=4), src, [bhk], rbufs=(b_vd,))
        mask_state[(g, ty)] = (hk, bhk)

    def mask_B(g, ty):
        hk, bhk = mask_state[(g, ty)]
        ps2, bps2 = b.get("ps")
        b.mm(ps2[:, :], bps2, [(jm[:, :], hk[:, :])], reads=(b_jm, bhk))
        b.copy(masks[g][ty][:, :, :], ps2[:, :].rearrange("p (h q) -> p h q", h=4), reads=(bps2,), writes=(b_masks[g][ty],))
        if ty == 1:
            b.ts(masks[g][2][:, :, :], masks[g][1][:, :, :], kval[:, g:g + 1], None, ALU.mult, None,
                 reads=(b_masks[g][1], b_kval), writes=(b_masks[g][2],))

    if from_xn:
        for k in range(KC):
            b.dma_in(xn[:, k, :], xn_d[:, k, :], [b_xn[k][0], b_xn[k][1]])
        st_mem_rstd(b, M.memT, M.b_mem, M.rstd_m, M.b_rstdm)
        st_memkv(b, M.memT, M.b_mem, M.rstd_m, M.b_rstdm, M.memn, M.b_memn, M.memk, M.b_memk, M.memv, M.b_memv, 16, 33)
    else:
        pss = [b.get("ps"), b.get("ps")]
        for k in range(KC):
            for t in range(2):
                xt, bxt = b.get("xt")
                b.dma_in(xt[:, :], xT_d[:, k, t * 512:(t + 1) * 512], [bxt])
                sq, bsq = b.get("sq")
                b.act(sq[:, :], xt[:, :], AF.Square, reads=(bxt,), writes=(bsq,))

                def f(e, k=k, sq=sq, ps=pss[t][0]):
                    return e.matmul(ps[:, :], b.ones[:, :], sq[:, :], start=(k == 0), stop=(k == KC - 1))
                P.op(P.pe, f, reads=(bsq, b.b_ones), writes=(pss[t][1],))
        for t in range(2):
            b.rstd_from_ss(pss[t][0][:, :], pss[t][1], 1.0 / D, rstd_x[t][:, :], b_rstdx[t])
        st_mem_rstd(b, M.memT, M.b_mem, M.rstd_m, M.b_rstdm)
        st_memkv(b, M.memT, M.b_mem, M.rstd_m, M.b_rstdm, M.memn, M.b_memn, M.memk, M.b_memk, M.memv, M.b_memv, 16, 33)
        for k in range(KC):
            for t in range(2):
                tsl = slice(t * 512, (t + 1) * 512)
                xt, bxt = b.get("xt")
                b.dma_in(xt[:, :], xT_d[:, k, tsl], [bxt])
                b.stt(xn[:, k, tsl], xt[:, :], b.pcol(k), rstd_x[t][:, :], ALU.mult, ALU.mult,
                      reads=(bxt, b_rstdx[t], b.b_par), writes=(b_xn[k][t],))

    qpipe = Pipe()
    for cg in range(8):
        if cg < 6:
            mask_A(cg // 2, cg % 2)
        if 1 <= cg < 7:
            mask_B((cg - 1) // 2, (cg - 1) % 2)
        wt, bw = b.next_w()
        wv = wt[:, :].rearrange("p (k c) -> p k c", k=16)
        for half in range(2):
            hq = cg * 2 + half
            cbt = None
            if hq >= 12:
                cbt, bcb = b.get("cb")
            for t in range(2):
                tsl = slice(t * 512, (t + 1) * 512)
                ps, bps = b.get("ps")
                b.mm(ps[:, :], bps, [(wv[:, k, half * 128:(half + 1) * 128], xn[:, k, tsl]) for k in range(KC)],
                     reads=[bw] + [b_xn[k][t] for k in range(KC)])
                if hq >= 12:
                    after = None
                    if t == 1:
                        after = (lambda hq=hq, cbt=cbt, bcb=bcb: b.dma_out(cat_d[:, 4 + hq - 12, :], cbt[:, :], rbufs=(bcb,)))
                    qpipe.push(mem_attn_stages(b, ps[:, :], bps, M.memk, M.b_memk, M.memv, M.b_memv, hq - 12, 32, cbt[:, tsl], bcb,
                                               after=after))
                else:
                    stq = {}

                    def QA(ps=ps, bps=bps, stq=stq):
                        sq, bsq = b.get("sq")
                        b.act(sq[:, :], ps[:, :], AF.Square, reads=(bps,), writes=(bsq,))
                        ps2, bps2 = b.get("ps")
                        b.mm(ps2[:, :], bps2, [(b.ones[:, :], sq[:, :])], reads=(bsq, b.b_ones))
                        stq.update(ps2=ps2, bps2=bps2)

                    def QB(ps=ps, bps=bps, stq=stq, hq=hq, t=t):
                        g = hq // 4
                        d = DIL[g]
                        rs, brs = b.get("rstd")
                        b.rstd_from_ss(stq["ps2"][:, :], stq["bps2"], 1.0 / 128, rs[:, :], brs)
                        nl = 512 // d
                        dst = qn[:, hq, :].rearrange("p (r l) -> p l r", r=d)[:, t * nl:(t + 1) * nl, :]
                        extra = (M.b_mem,) if hq < 8 else (M.b_memn,)
                        b.stt(dst, ps[:, :].rearrange("p (l r) -> p l r", r=d), b.pcol(34 + g), rs[:, :].rearrange("p (l r) -> p l r", r=d),
                              ALU.mult, ALU.mult, reads=(bps, brs, b.b_par), writes=(b_qn[hq],) + extra)
                    qpipe.push([QA, QB])
    qpipe.flush()

    batches = []
    for h in range(4):
        for g in range(3):
            d, lg, nq = DIL[g], LG[g], NQ[g]
            upc = lg // nq
            nunits = d * upc
            U = 512 // nq
            for u0 in range(0, nunits, U):
                batches.append(dict(h=h, g=g, u0=u0, first=(u0 == 0), last=(g == 2 and u0 + U >= nunits)))
    cur_kv = {}

    def T1(bt):
        h, g, u0 = bt["h"], bt["g"], bt["u0"]
        d, lg, nq, nkc, nbc = DIL[g], LG[g], NQ[g], NKC[g], NBC[g]
        if bt["first"]:
            kt, bkt = b.get("kt")
            vv, bvv = b.get("vv")
            b.dma_in(kt[:, :NK[g]], kt_d[g][:, h, :], [bkt])
            b.dma_in(vv[:, :NB[g], :], vv_d[g][:, h, :, :], [bvv])
            cur_kv[(h, g)] = (kt, bkt, vv, bvv)
        kt, bkt, vv, bvv = cur_kv[(h, g)]
        hq = g * 4 + h
        upc = lg // nq
        U = 512 // nq
        psP, bpsP = b.get("ps")
        psD, bpsD = b.get("ps")
        units = []
        for ui in range(U):
            u = u0 + ui
            r, j = u // upc, u % upc
            qs = qn[:, hq, r * lg + j * nq: r * lg + (j + 1) * nq]
            kp = kt[:, r * nkc + j * nq: r * nkc + j * nq + 128]
            kd = kt[:, r * nkc + 128 + j * nq: r * nkc + 128 + (j + 1) * nq]
            units.append((r, j, qs, kp, kd))

        def fS(e, units=units, psP=psP, psD=psD, nq=nq):
            for ui, (r, j, qs, kp, kd) in enumerate(units):
                e.matmul(psP[:, ui * nq:(ui + 1) * nq], kp, qs, start=True, stop=True)
                ins = e.matmul(psD[:nq, ui * nq:(ui + 1) * nq], kd, qs, start=True, stop=True)
            return ins
        P.op(P.pe, fS, reads=(bkt, b_qn[hq]), writes=(bpsP, bpsD))
        bt.update(units=units, psP=psP, bpsP=bpsP, psD=psD, bpsD=bpsD, vv=vv, bvv=bvv)

    def T2(bt):
        h, g = bt["h"], bt["g"]
        nq = NQ[g]
        eP, beP = b.get("tf")
        eD, beD = b.get("tf")
        b.act(eP[:, :], bt["psP"][:, :], AF.Exp, reads=(bt["bpsP"],), writes=(beP,), scale=SCALE)
        b.act(eD[:nq, :], bt["psD"][:nq, :], AF.Exp, reads=(bt["bpsD"],), writes=(beD,), scale=SCALE)
        pP, bpP = b.get("pp")
        pD, bpD = b.get("pp")
        for ui, (r, j, qs, kp, kd) in enumerate(bt["units"]):
            csl = slice(ui * nq, (ui + 1) * nq)
            mty = 2 if j == 0 else 1
            b.tt(pP[:, csl], eP[:, csl], masks[g][mty][:, h, :nq], ALU.mult, reads=(beP, b_masks[g][mty]), writes=(bpP,))
            b.tt(pD[:nq, csl], eD[:nq, csl], masks[g][0][:nq, h, :nq], ALU.mult, reads=(beD, b_masks[g][0]), writes=(bpD,),
                 eng=P.pool)
        bt.update(pP=pP, bpP=bpP, pD=pD, bpD=bpD)

    def T3(bt):
        g = bt["g"]
        nq, nbc = NQ[g], NBC[g]
        psN, bpsN = b.get("ps")
        psS, bpsS = b.get("ps")
        pP, pD, vv = bt["pP"], bt["pD"], bt["vv"]

        def fV(e, units=bt["units"], psN=psN, psS=psS, nq=nq, pP=pP, pD=pD, vv=vv, nbc=nbc):
            for ui, (r, j, qs, kp, kd) in enumerate(units):
                csl = slice(ui * nq, (ui + 1) * nq)
                e.matmul(psN[:, csl], vv[:, r * nbc + j, :], pP[:, csl], start=True, stop=False)
                e.matmul(psN[:, csl], vv[:nq, r * nbc + j + 1, :], pD[:nq, csl], start=False, stop=True)
            e.matmul(psS[:, :], b.ones[:, :], pP[:, :], start=True, stop=False)
            ins = e.matmul(psS[:, :], b.ones[:nq, :], pD[:nq, :], start=False, stop=True)
            return ins
        P.op(P.pe, fV, reads=(bt["bvv"], bt["bpP"], bt["bpD"], b.b_ones), writes=(bpsN, bpsS))
        bt.update(psN=psN, bpsN=bpsN, psS=psS, bpsS=bpsS)

    def T4(bt):
        h, g, u0 = bt["h"], bt["g"], bt["u0"]
        d, lg, nq = DIL[g], LG[g], NQ[g]
        upc = lg // nq
        U = 512 // nq
        psN, bpsN, psS, bpsS = bt["psN"], bt["bpsN"], bt["psS"], bt["bpsS"]
        r0 = u0 // upc
        if d == 1:
            l0 = u0 * nq
            dN = accN[:, l0:l0 + 512]
            dD = accD[:, l0:l0 + 512]
            sN, sS = psN[:, :], psS[:, :]
        else:
            nr = U // upc
            dN = accN[:, :].rearrange("p (l r) -> p r l", r=d)[:, r0:r0 + nr, :]
            dD = accD[:, :].rearrange("p (l r) -> p r l", r=d)[:, r0:r0 + nr, :]
            sN = psN[:, :].rearrange("p (r l) -> p r l", r=nr)
            sS = psS[:, :].rearrange("p (r l) -> p r l", r=nr)
        if g == 0:
            b.copy(dN, sN, reads=(bpsN,), writes=(b_acc,))
            b.copy(dD, sS, reads=(bpsS,), writes=(b_acc,))
        else:
            b.tt(dN, sN, dN, ALU.add, reads=(bpsN, b_acc), writes=(b_acc,))
            b.tt(dD, sS, dD, ALU.add, reads=(bpsS, b_acc), writes=(b_acc,))
        if bt["last"]:
            cbt, bcb = b.get("cb")
            for t in range(2):
                tsl = slice(t * 512, (t + 1) * 512)
                rd, brd = b.get("tf")
                b.recip_act(rd[:, :], accD[:, tsl], reads=(b_acc,), writes=(brd,))
                b.tt(cbt[:, tsl], accN[:, tsl], rd[:, :], ALU.mult, reads=(b_acc, brd), writes=(bcb,))
            b.dma_out(cat_d[:, h, :], cbt[:, :], rbufs=(bcb,))

    nbt = len(batches)
    for s_ in range(nbt + 2):
        if s_ < nbt:
            T1(batches[s_])
        if 0 <= s_ - 1 < nbt:
            T2(batches[s_ - 1])
            T3(batches[s_ - 1])
        if 0 <= s_ - 2 < nbt:
            T4(batches[s_ - 2])
    return b.finish(), b.wtiles


def tiles_B2(l):
    tl = [wt_std("b_w_out", l - 2, 0, 8, [(c, 512)]) for c in range(0, 2048, 512)]
    tl += mlp_tiles(l)
    return tl


def build_B2(l, with_next=True):
    b = Bld(tiles_B2(l), n_ps=6)
    b.pool("rstdp", 4, [128, 512], F32)
    stats = Stats(b)
    nc, P = b.nc, b.P
    xT_d = b.din("xT", [128, KC, T])
    cat_d = b.din("cat", [128, 8, T], BF16)
    yT_d = b.dout("yT", [128, KC, T])
    xs = b.sb("xs", [128, KC, T], F32)
    b_xs = [[Buf() for t in range(2)] for k in range(KC)]
    cat = b.sb("cat", [128, 16, T], BF16)
    b_cat = [[Buf() for t in range(2)] for k in range(16)]
    h1 = b.sb("h1", [128, 16, T], BF16)
    b_h1 = [[Buf() for t in range(2)] for k in range(16)]
    for k in range(8):
        b.dma_in(cat[:, k, :], cat_d[:, k, :], [b_cat[k][0], b_cat[k][1]])
    for k in range(KC):
        b.dma_in(xs[:, k, :], xT_d[:, k, :], [b_xs[k][0], b_xs[k][1]])
    for cg in range(4):
        wt, bw = b.next_w()
        wv = wt[:, :].rearrange("p (k c) -> p k c", k=8)
        for q4 in range(4):
            dc = cg * 4 + q4
            for t in range(2):
                tsl = slice(t * 512, (t + 1) * 512)
                ps, bps = b.get("ps")
                b.mm(ps[:, :], bps, [(wv[:, k, q4 * 128:(q4 + 1) * 128], cat[:, k, tsl]) for k in range(8)],
                     reads=[bw] + [b_cat[k][t] for k in range(8)])
                b.tt(xs[:, dc, tsl], ps[:, :], xs[:, dc, tsl], ALU.add, reads=(bps, b_xs[dc][t]), writes=(b_xs[dc][t],))
                stats.add(xs[:, dc, tsl], b_xs[dc][t], t)
    st_mlp(b, xs, b_xs, cat, b_cat, h1, b_h1, 0, out_d=yT_d, in_rstd=stats.rstd(), out_stats=(stats if with_next else None))
    if with_next:
        xnn_d = b.dout("xnn", [128, KC, T], BF16)
        st_apply_norm(b, xs, b_xs, cat, b_cat, 40, stats.rstd())
        for k in range(KC):
            b.dma_out(xnn_d[:, k, :], cat[:, k, :], rbufs=(b_cat[k][0], b_cat[k][1]))
    return b.finish(), b.wtiles


def _t5_bucket(n):
    n = np.maximum(np.asarray(n, np.int64), 0)
    nf = np.maximum(n, 1).astype(np.float32)
    large = 16 + (np.log(nf / np.float32(16.0)) / np.float32(math.log(2048 / 16)) * np.float32(16.0)).astype(np.int32)
    large = np.minimum(large, 31)
    return np.where(n < 16, n, large)


def _structural():
    oh = np.zeros((33, 6, 256), np.float32)
    for g, d in enumerate(DIL):
        for i in range(256):
            w = i - 127
            if 0 <= w <= 127:
                oh[_t5_bucket(w * d), g * 2 + 0, i] = 1.0
            else:
                oh[32, g * 2 + 0, i] = 1.0
            if -127 <= w <= 0:
                oh[_t5_bucket((w + 128) * d), g * 2 + 1, i] = 1.0
            else:
                oh[32, g * 2 + 1, i] = 1.0
    jm = np.ascontiguousarray(np.eye(128, dtype=np.float32)[::-1])
    return oh, jm


def _kv_layout(kT, vT):
    bf = ml_dtypes.bfloat16
    Kf = np.concatenate([np.asarray(k) for k in kT], axis=2)
    Vf = np.concatenate([np.asarray(v) for v in vT], axis=2)
    outs = [dict() for _ in range(NCORES)]
    for g, d in enumerate(DIL):
        lg, nkc, nbc = LG[g], NKC[g], NBC[g]
        Lf = S // d
        def cm(A):
            A = A[:, g * 4:(g + 1) * 4, :].reshape(128, 4, Lf, d).transpose(0, 1, 3, 2)
            pad = np.zeros((128, 4, d, 128), bf)
            return np.concatenate([pad, A], axis=3)
        Kc, Vc = cm(Kf), cm(Vf)
        for c in range(NCORES):
            ks = Kc[:, :, :, c * lg:c * lg + nkc]
            outs[c][f"kt{g}"] = np.ascontiguousarray(ks.reshape(128, 4, d * nkc))
            vs = Vc[:, :, :, c * lg:c * lg + nkc]
            vp = np.zeros((128, 4, d, nbc * 128), bf)
            vp[:, :, :, :nkc] = vs
            vp = vp.reshape(128, 4, d, nbc, 128).transpose(4, 1, 2, 3, 0)
            outs[c][f"vv{g}"] = np.ascontiguousarray(vp.reshape(128, 4, d * nbc, 128))
            kv = outs[c].setdefault("kval", np.zeros((128, 3), np.float32))
            kv[:, g] = ((c * lg - 128 + np.arange(128)) >= 0).astype(np.float32)
    return outs


def run_B_layer(l, xT, memT, W, kvin, xn_in=None, next_g=None):
    f32 = np.float32
    j = l - 2
    par = np.zeros((128, NPAR), f32)
    par[:, 0:16] = colk(W["norm_mix_g"][l])
    par[:, 16:32] = colk(W["mem_norm_g"][l])
    par[:, 32] = W["mem_q_norm_g"][l]
    par[:, 33] = W["mem_k_norm_g"][l]
    par[:, 34:37] = np.asarray(W["b_q_norm_g"][j], f32).T
    par[:, 255] = EPS
    nb1 = _CACHE.get(("B1", l))
    if nb1 is None:
        nb1 = _CACHE[("B1", l)] = build_B1(l, from_xn=(xn_in is not None))
    wst = pack_weights(nb1[1], W)
    oh, jm = _structural()
    relb = np.concatenate([np.asarray(W["rel_bias"], f32), np.full((1, 12), -30000.0, f32)], axis=0)
    ins = []
    for c in range(NCORES):
        dct = {"memT": memT, "wst": wst, "par": par, "oh": oh, "jm": jm, "relb": relb}
        if xn_in is not None:
            dct["xn"] = xn_in[c]
        else:
            dct["xT"] = xT[c]
        dct.update(kvin[c])
        ins.append(dct)
    r1 = _run(nb1[0], ins, f"B1_{l}")
    par2 = np.zeros((128, NPAR), f32)
    par2[:, 0:16] = colk(W["norm_mlp_g"][l])
    par2[:, 255] = EPS
    if next_g is not None:
        par2[:, 40:56] = colk(next_g)
    nb2 = _CACHE.get(("B2", l))
    if nb2 is None:
        nb2 = _CACHE[("B2", l)] = build_B2(l, with_next=(next_g is not None))
    wst2 = pack_weights(nb2[1], W)
    ins2 = [{"xT": xT[c], "cat": r1[c]["cat"], "wst": wst2, "par": par2} for c in range(NCORES)]
    r2 = _run(nb2[0], ins2, f"B2_{l}")
    xnn = [r2[c]["xnn"] for c in range(NCORES)] if next_g is not None else None
    return [r2[c]["yT"] for c in range(NCORES)], r1, xnn


def kernel(**inputs):
    W = {k: np.asarray(v) for k, v in inputs.items()}
    x = W["x"][0]
    memT = fm(W["mem"][0])
    xT = [fm(x[c * T:(c + 1) * T]) for c in range(NCORES)]
    g = W["norm_mix_g"]
    xT, _, _, _, xnn = run_A_layer(0, xT, memT, W, with_kv=False, next_g=g[1])
    xT, kT, vT, _, xnn = run_A_layer(1, xT, memT, W, with_kv=True, xn_in=xnn, next_g=g[2])
    kvin = _kv_layout(kT, vT)
    xT, _, xnn = run_B_layer(2, xT, memT, W, kvin, xn_in=xnn, next_g=g[3])
    xT, _, xnn = run_B_layer(3, xT, memT, W, kvin, xn_in=xnn, next_g=None)
    out = np.concatenate([unfm(t) for t in xT], axis=0)
    return out.reshape(1, S, D).astype(np.float32)
```

```python
import math
import numpy as np
import ml_dtypes
import concourse.bass as bass
import concourse.mybir as mybir
from concourse.bass_utils import run_bass_kernel_spmd

F32 = mybir.dt.float32
BF16 = mybir.dt.bfloat16
AF = mybir.ActivationFunctionType
ALU = mybir.AluOpType

NCORES = 8
D = 2048
S = 8192
T = S // NCORES
KC = D // 128
DFF = 4 * D
EPS = 1e-6
WT_ELEMS = 4096
NPAR = 256
SCALE = 128 ** -0.5
DIL = (1, 4, 16)


class Buf:
    __slots__ = ("name", "w", "r", "slot")

    def __init__(self, name=""):
        self.name = name
        self.w = None
        self.r = []
        self.slot = None


class Eng:
    def __init__(self, name, sem, is_pe=False):
        self.name = name
        self.sem = sem
        self.count = 0
        self.ops = []
        self.waited = {}
        self.is_pe = is_pe


class Prog:
    N_DMA_SEMS = 12

    def __init__(self, nc):
        self.nc = nc
        self.pe = Eng("pe", nc.alloc_semaphore("s_pe"), is_pe=True)
        self.act = Eng("act", nc.alloc_semaphore("s_act"))
        self.dve = Eng("dve", nc.alloc_semaphore("s_dve"))
        self.pool = Eng("pool", nc.alloc_semaphore("s_pool"))
        self.sp = Eng("sp", None)
        self.engs = [self.pe, self.act, self.dve, self.pool, self.sp]
        self.dma_sems = [nc.alloc_semaphore(f"s_dma{i}") for i in range(self.N_DMA_SEMS)]
        self.dma_cnt = [0] * self.N_DMA_SEMS
        self.dma_last = [None] * self.N_DMA_SEMS
        self.dma_rr = 0

    def _deps(self, reads, writes):
        for b in list(reads) + list(writes):
            if b.slot is not None and b.slot[0] is not b:
                raise RuntimeError(f"stale pooled buffer {b.name}: re-allocated before this access was recorded")
        deps = []
        for b in reads:
            if b.w is not None:
                deps.append(b.w)
        for b in writes:
            if b.w is not None:
                deps.append(b.w)
            deps.extend(b.r)
        return deps

    def _filter(self, eng, deps):
        waits = {}
        for (sem, val) in deps:
            if eng.is_pe and sem is eng.sem:
                continue
            k = id(sem)
            if eng.waited.get(k, 0) >= val:
                continue
            if k not in waits or waits[k][1] < val:
                waits[k] = (sem, val)
        for k, (sem, val) in waits.items():
            eng.waited[k] = val
        return list(waits.values())

    def _update(self, tok, reads, writes):
        for b in writes:
            b.w = tok
            b.r = []
        for b in reads:
            if b not in writes:
                b.r.append(tok)

    def op(self, eng, fn, reads=(), writes=()):
        deps = self._deps(reads, writes)
        waits = self._filter(eng, deps)
        eng.count += 1
        tok = (eng.sem, eng.count)
        eng.ops.append((waits, fn, (eng.sem, 1)))
        self._update(tok, reads, writes)
        return tok

    def dma(self, eng, fn, reads=(), writes=(), n=1):
        k = self.dma_rr
        self.dma_rr = (self.dma_rr + 1) % self.N_DMA_SEMS
        sem = self.dma_sems[k]
        deps = self._deps(reads, writes)
        if self.dma_last[k] is not None:
            deps.append(self.dma_last[k])
        waits = self._filter(eng, deps)
        self.dma_cnt[k] += 16 * n
        tok = (sem, self.dma_cnt[k])
        self.dma_last[k] = tok
        eng.ops.append((waits, (lambda e, fn=fn, sem=sem: fn(e, sem)), None))
        self._update(tok, reads, writes)
        return tok

    def wait_all(self, eng, toks):
        waits = self._filter(eng, list(toks))
        eng.ops.append((waits, None, None))

    def emit(self):
        nc = self.nc

        def run(eng, h):
            for (waits, fn, inc) in eng.ops:
                for (sem, val) in waits:
                    h.wait_ge(sem, val)
                if fn is None:
                    continue
                ins = fn(h)
                if inc is not None:
                    ins.then_inc(inc[0], inc[1])

        with nc.Block() as block:
            @block.tensor
            def _(h):
                run(self.pe, h)

            @block.scalar
            def _(h):
                run(self.act, h)

            @block.vector
            def _(h):
                run(self.dve, h)

            @block.gpsimd
            def _(h):
                run(self.pool, h)

            @block.sync
            def _(h):
                run(self.sp, h)


class Bld:
    def __init__(self, wtiles, n_wslots=4, n_ps=8):
        self.discover = wtiles is None
        if self.discover:
            wtiles = []
        self.nc = bass.Bass("TRN2", target_bir_lowering=False)
        self.P = Prog(self.nc)
        self.pools = {}
        self.rr = {}
        self.out_toks = []
        self.pool("ps", n_ps, [128, 512], F32, psum=True)
        if n_ps < 8:
            self.pool("pstat", 8 - n_ps, [128, 512], F32, psum=True)
        self.wtiles = wtiles
        self.NT = 4096 if self.discover else len(wtiles)
        self.n_wslots = n_wslots
        self.wst_d = self.din("wst", [max(self.NT, 1), 128, WT_ELEMS])
        self.par_d = self.din("par", [128, NPAR])
        self.pool("w", n_wslots, [128, WT_ELEMS], BF16)
        self.par = self.sb("par_sb", [128, NPAR], F32)
        self.b_par = Buf("par")
        self.ones = self.sb("ones", [128, 128], BF16)
        self.b_ones = Buf("ones")
        self.w_next_load = 0
        self.w_cur = 0
        self.pool("tf", 4, [128, 512], F32)
        self.pool("tb", 6, [128, 512], BF16)
        self.pool("sq", 4, [128, 512], BF16)
        self.pool("rstd", 4, [128, 512], F32)
        self.dma_in(self.par[:, :], self.par_d, [self.b_par])
        self.P.op(self.P.pool, lambda e: e.memset(self.ones[:, :], 1.0), writes=(self.b_ones,))
        self._ensure(n_wslots - 1)

    def din(self, name, shape, dt=F32):
        return self.nc.dram_tensor(name, list(shape), dt, kind="ExternalInput").ap()

    def dout(self, name, shape, dt=F32):
        return self.nc.dram_tensor(name, list(shape), dt, kind="ExternalOutput").ap()

    def sb(self, name, shape, dt):
        return self.nc.alloc_sbuf_tensor("sb_" + name, list(shape), dt)

    def pool(self, name, n, shape, dt, psum=False):
        if psum:
            lst = [(self.nc.alloc_psum_tensor(f"pp_{name}{i}", list(shape), dt), Buf(f"{name}{i}")) for i in range(n)]
        else:
            lst = [(self.nc.alloc_sbuf_tensor(f"pl_{name}{i}", list(shape), dt), Buf(f"{name}{i}")) for i in range(n)]
        self.pools[name] = lst
        self.rr[name] = 0

    def get(self, name):
        lst = self.pools[name]
        i = self.rr[name]
        self.rr[name] = (i + 1) % len(lst)
        t, old = lst[i]
        nb = Buf(old.name)
        nb.w, nb.r = old.w, list(old.r)
        cell = old.slot if old.slot is not None else [None]
        nb.slot = cell
        cell[0] = nb
        lst[i] = (t, nb)
        return t, nb

    def dma_in(self, dst, src, wbufs, eng=None, rbufs=()):
        eng = eng or self.P.sp
        return self.P.dma(eng, lambda e, s: e.dma_start(out=dst, in_=src).then_inc(s, 16), reads=rbufs, writes=wbufs)

    def dma_out(self, dst, src, rbufs):
        tok = self.P.dma(self.P.sp, lambda e, s: e.dma_start(out=dst, in_=src).then_inc(s, 16), reads=rbufs)
        self.out_toks.append(tok)
        return tok

    def finish(self):
        if self.discover:
            return None
        self.P.wait_all(self.P.sp, self.out_toks)
        self.P.emit()
        return self.nc

    def _ensure(self, upto):
        while self.w_next_load <= min(upto, self.NT - 1):
            i = self.w_next_load
            wt, bw = self.pools["w"][i % self.n_wslots]
            self.P.dma(self.P.pool, (lambda e, s, i=i, wt=wt: e.dma_start(out=wt[:, :], in_=self.wst_d[i]).then_inc(s, 16)),
                       writes=(bw,))
            self.w_next_load += 1

    def next_w(self, desc=None):
        i = self.w_cur
        if desc is not None:
            if self.discover:
                self.wtiles.append(desc)
            else:
                assert self.wtiles[i] == desc, (i, self.wtiles[i], desc)
        assert i < self.NT, "weight stream exhausted"
        self.w_cur += 1
        self._ensure(i + self.n_wslots - 1)
        return self.pools["w"][i % self.n_wslots]

    def mm(self, out_ap, bps, pairs, reads):
        n = len(pairs)

        def f(e):
            for i, (l, r) in enumerate(pairs):
                ins = e.matmul(out_ap, l, r, start=(i == 0), stop=(i == n - 1))
            return ins
        return self.P.op(self.P.pe, f, reads=reads, writes=(bps,))

    def act(self, out, in_, func, reads, writes, bias=None, scale=None):
        kw = {}
        if bias is not None:
            kw["bias"] = bias
        if scale is not None:
            kw["scale"] = scale
        return self.P.op(self.P.act, lambda e: e.activation(out=out, in_=in_, func=func, **kw), reads=reads, writes=writes)

    def tt(self, out, a, b_, op, reads, writes, eng=None):
        eng = eng or self.P.dve
        return self.P.op(eng, lambda e: e.tensor_tensor(out=out, in0=a, in1=b_, op=op), reads=reads, writes=writes)

    def ts(self, out, a, s1, s2, op0, op1, reads, writes, eng=None):
        eng = eng or self.P.dve
        if s2 is None:
            return self.P.op(eng, lambda e: e.tensor_scalar(out=out, in0=a, scalar1=s1, scalar2=None, op0=op0), reads=reads, writes=writes)
        return self.P.op(eng, lambda e: e.tensor_scalar(out=out, in0=a, scalar1=s1, scalar2=s2, op0=op0, op1=op1), reads=reads, writes=writes)

    def stt(self, out, a, sc, b_, op0, op1, reads, writes, eng=None):
        eng = eng or self.P.dve
        return self.P.op(eng, lambda e: e.scalar_tensor_tensor(out=out, in0=a, scalar=sc, in1=b_, op0=op0, op1=op1), reads=reads, writes=writes)

    def recip(self, out, in_, reads, writes):
        return self.P.op(self.P.dve, lambda e: e.reciprocal(out=out, in_=in_), reads=reads, writes=writes)

    def copy(self, out, in_, reads, writes, eng=None):
        eng = eng or self.P.act
        if eng is self.P.act:
            return self.P.op(eng, lambda e: e.copy(out=out, in_=in_), reads=reads, writes=writes)
        return self.P.op(eng, lambda e: e.tensor_copy(out=out, in_=in_), reads=reads, writes=writes)

    def pcol(self, c, n=1):
        return self.par[:, c:c + n]

    def rstd_from_ss(self, ss_ap, b_ss, inv_count, out_ap, out_b, ncols=512):
        tf, btf = self.get("tf")
        self.act(tf[:, :ncols], ss_ap, AF.Ln, reads=(b_ss, self.b_par), writes=(btf,), bias=self.pcol(255), scale=inv_count)
        self.act(out_ap, tf[:, :ncols], AF.Exp, reads=(btf,), writes=(out_b,), scale=-0.5)

    def recip_act(self, out_ap, in_ap, reads, writes, nrows=128, ncols=512):
        tf, btf = self.get("tf")
        self.act(tf[:nrows, :ncols], in_ap, AF.Ln, reads=reads, writes=(btf,))
        self.act(out_ap, tf[:nrows, :ncols], AF.Exp, reads=(btf,), writes=writes, scale=-1.0)


def wt_std(key, l, row0, nk, cols):
    return ("std", key, l, row0, nk, tuple(cols))


def pack_weights(tiles, W):
    out = np.zeros((max(len(tiles), 1), 128, WT_ELEMS), np.float32)
    for i, tl in enumerate(tiles):
        if tl[0] == "std":
            _, key, l, row0, nk, cols = tl
            w = W[key][l] if l is not None else W[key]
            blk = np.concatenate([w[row0:row0 + nk * 128, c0:c0 + n] for (c0, n) in cols], axis=1)
            ncol = blk.shape[1]
            out[i, :, :nk * ncol] = blk.reshape(nk, 128, ncol).transpose(1, 0, 2).reshape(128, nk * ncol)
        elif tl[0] == "gates":
            _, l = tl
            g = np.concatenate([W["a_gate_r_w"][l], W["a_gate_i_w"][l]], axis=0)
            out[i, :, :24 * 128] = g.transpose(1, 0, 2).reshape(128, 24 * 128)
    return out


def mlp_tiles(l):
    tl = []
    for q in range(4):
        for c in range(0, 2048, 256):
            tl.append(wt_std("mlp_w1", l, 0, 16, [(q * 2048 + c, 256)]))
        for c in range(0, 2048, 256):
            tl.append(wt_std("mlp_w2", l, q * 2048, 16, [(c, 256)]))
    return tl


def memkv_tiles(l):
    return [wt_std("mem_w_kv", l, 0, 16, [(c, 256)]) for c in range(0, 1024, 256)]


class Stats:
    def __init__(self, b):
        self.b = b
        self.ps = [b.pools["pstat"][t] for t in range(2)]
        self.n = [0, 0]
        self.pipe = Pipe()
        b.pool("sqs", 8, [128, 512], BF16)

    def add(self, src_ap, b_src, t):
        b = self.b
        k = self.n[t]
        self.n[t] += 1
        ps, bps = self.ps[t]
        st = {}

        def A():
            sq, bsq = b.get("sqs")
            b.act(sq[:, :], src_ap, AF.Square, reads=(b_src,), writes=(bsq,))
            st.update(sq=sq, bsq=bsq)

        def nop():
            pass

        def B():
            sq, bsq = st["sq"], st["bsq"]

            def f(e, k=k, sq=sq, ps=ps):
                return e.matmul(ps[:, :], b.ones[:, :], sq[:, :], start=(k == 0), stop=(k == KC - 1))
            b.P.op(b.P.pe, f, reads=(bsq, b.b_ones), writes=(bps,))
        self.pipe.push([nop, A, nop, nop, B])

    def rstd(self):
        b = self.b
        assert self.n == [KC, KC]
        self.pipe.flush()
        out = []
        for t in range(2):
            rs, brs = b.get("rstdp")
            b.rstd_from_ss(self.ps[t][0][:, :], self.ps[t][1], 1.0 / D, rs[:, :], brs)
            out.append((rs, brs))
        self.n = [0, 0]
        return out


def st_apply_norm(b, xs, b_xs, xn, b_xn, gcol, rstds):
    for t in range(2):
        tsl = slice(t * 512, (t + 1) * 512)
        rs, brs = rstds[t]
        for k in range(KC):
            b.stt(xn[:, k, tsl], xs[:, k, tsl], b.pcol(gcol + k), rs[:, :], ALU.mult, ALU.mult,
                  reads=(b_xs[k][t], brs, b.b_par), writes=(b_xn[k][t],))


def st_norm_resident(b, xs, b_xs, xn, b_xn, gcol):
    for t in range(2):
        tsl = slice(t * 512, (t + 1) * 512)
        ps, bps = b.get("ps")
        for k in range(KC):
            sq, bsq = b.get("sq")
            b.act(sq[:, :], xs[:, k, tsl], AF.Square, reads=(b_xs[k][t],), writes=(bsq,))

            def f(e, k=k, sq=sq, ps=ps):
                return e.matmul(ps[:, :], b.ones[:, :], sq[:, :], start=(k == 0), stop=(k == KC - 1))
            b.P.op(b.P.pe, f, reads=(bsq, b.b_ones), writes=(bps,))
        rs, brs = b.get("rstd")
        b.rstd_from_ss(ps[:, :], bps, 1.0 / D, rs[:, :], brs)
        for k in range(KC):
            b.stt(xn[:, k, tsl], xs[:, k, tsl], b.pcol(gcol + k), rs[:, :], ALU.mult, ALU.mult,
                  reads=(b_xs[k][t], brs, b.b_par), writes=(b_xn[k][t],))


def st_mlp(b, xs, b_xs, xn, b_xn, h1, b_h1, gcol, out_d=None, in_rstd=None, out_stats=None):
    if in_rstd is not None:
        st_apply_norm(b, xs, b_xs, xn, b_xn, gcol, in_rstd)
    else:
        st_norm_resident(b, xs, b_xs, xn, b_xn, gcol)
    for q in range(4):
        for cg in range(8):
            wt, bw = b.next_w()
            wv = wt[:, :].rearrange("p (k c) -> p k c", k=16)
            for half in range(2):
                fc = cg * 2 + half
                for t in range(2):
                    tsl = slice(t * 512, (t + 1) * 512)
                    ps, bps = b.get("ps")
                    b.mm(ps[:, :], bps, [(wv[:, k, half * 128:(half + 1) * 128], xn[:, k, tsl]) for k in range(KC)],
                         reads=[bw] + [b_xn[k][t] for k in range(KC)])
                    tf, btf = b.get("tf")
                    b.act(tf[:, :], ps[:, :], AF.Relu, reads=(bps,), writes=(btf,))
                    b.tt(h1[:, fc, tsl], tf[:, :], tf[:, :], ALU.mult, reads=(btf,), writes=(b_h1[fc][t],), eng=b.P.pool)
        for cg in range(8):
            wt, bw = b.next_w()
            wv = wt[:, :].rearrange("p (k c) -> p k c", k=16)
            for half in range(2):
                dc = cg * 2 + half
                for t in range(2):
                    tsl = slice(t * 512, (t + 1) * 512)
                    ps, bps = b.get("ps")
                    b.mm(ps[:, :], bps, [(wv[:, k, half * 128:(half + 1) * 128], h1[:, k, tsl]) for k in range(16)],
                         reads=[bw] + [b_h1[k][t] for k in range(16)])
                    b.tt(xs[:, dc, tsl], ps[:, :], xs[:, dc, tsl], ALU.add, reads=(bps, b_xs[dc][t]), writes=(b_xs[dc][t],))
                    if q == 3 and out_stats is not None:
                        out_stats.add(xs[:, dc, tsl], b_xs[dc][t], t)
                if q == 3 and out_d is not None:
                    b.dma_out(out_d[:, dc, :], xs[:, dc, :], rbufs=(b_xs[dc][0], b_xs[dc][1]))


def st_mem_rstd(b, memT, b_mem, rstd_m, b_rstdm):
    ps, bps = b.get("ps")
    for k in range(KC):
        sq, bsq = b.get("sq")
        b.act(sq[:, :256], memT[:, k, :], AF.Square, reads=(b_mem,), writes=(bsq,))

        def f(e, k=k, sq=sq, ps=ps):
            return e.matmul(ps[:, :256], b.ones[:, :], sq[:, :256], start=(k == 0), stop=(k == KC - 1))
        b.P.op(b.P.pe, f, reads=(bsq, b.b_ones), writes=(bps,))
    b.rstd_from_ss(ps[:, :256], bps, 1.0 / D, rstd_m[:, :], b_rstdm, ncols=256)


def st_memkv(b, memT, b_mem, rstd_m, b_rstdm, memn, b_memn, memk, b_memk, memv, b_memv, gcol_mem, col_kg, descs=None):
    for k in range(KC):
        b.stt(memn[:, k, :], memT[:, k, :], b.pcol(gcol_mem + k), rstd_m[:, :], ALU.mult, ALU.mult,
              reads=(b_mem, b_rstdm, b.b_par), writes=(b_memn,))
    for hp in range(2):
        wt, bw = b.next_w(descs[hp] if descs else None)
        wv = wt[:, :].rearrange("p (k c) -> p k c", k=16)
        for hh in range(2):
            h = hp * 2 + hh
            hs = slice(hh * 128, hh * 128 + 128)
            ps, bps = b.get("ps")
            b.mm(ps[:, :256], bps, [(wv[:, k, hs], memn[:, k, :]) for k in range(KC)], reads=(bw, b_memn))
            sq, bsq = b.get("sq")
            b.act(sq[:, :256], ps[:, :256], AF.Square, reads=(bps,), writes=(bsq,))
            ps2, bps2 = b.get("ps")
            b.mm(ps2[:, :256], bps2, [(b.ones[:, :], sq[:, :256])], reads=(bsq, b.b_ones))
            rs, brs = b.get("rstd")
            b.rstd_from_ss(ps2[:, :256], bps2, 1.0 / 128, rs[:, :256], brs, ncols=256)
            b.stt(memk[:, h, :], ps[:, :256], b.pcol(col_kg), rs[:, :256], ALU.mult, ALU.mult,
                  reads=(bps, brs, b.b_par), writes=(b_memk,))
    for vh in range(2):
        wt, bw = b.next_w(descs[2 + vh] if descs else None)
        wv = wt[:, :].rearrange("p (k c) -> p k c", k=16)
        for mc in range(2):
            ps, bps = b.get("ps")
            b.mm(ps[:, :256], bps, [(memn[:, k, mc * 128:(mc + 1) * 128], wv[:, k, :]) for k in range(KC)],
                 reads=(bw, b_memn))
            b.copy(memv[:, mc, vh * 256:(vh + 1) * 256], ps[:, :256], reads=(bps,), writes=(b_memv,))


class Pipe:
    def __init__(self):
        self.items = []

    def _advance(self, skip_new=False):
        for it in reversed(self.items):
            if it:
                it.pop(0)()
        self.items = [it for it in self.items if it]

    def push(self, stages):
        self.items.append(list(stages))
        self._advance()

    def tick(self):
        self._advance()

    def flush(self):
        while self.items:
            self._advance()


def mem_attn_stages(b, mq_ps, b_mq, memk, b_memk, memv, b_memv, h, col_qg, out_ap, b_out, after=None):
    st = {}

    def A():
        sq, bsq = b.get("sq")
        b.act(sq[:, :], mq_ps, AF.Square, reads=(b_mq,), writes=(bsq,))
        ps2, bps2 = b.get("ps")
        b.mm(ps2[:, :], bps2, [(b.ones[:, :], sq[:, :])], reads=(bsq, b.b_ones))
        st.update(ps2=ps2, bps2=bps2)

    def B():
        rs, brs = b.get("rstd")
        b.rstd_from_ss(st["ps2"][:, :], st["bps2"], 1.0 / 128, rs[:, :], brs)
        qn, bqn = b.get("tb")
        b.stt(qn[:, :], mq_ps, b.pcol(col_qg), rs[:, :], ALU.mult, ALU.mult, reads=(b_mq, brs, b.b_par), writes=(bqn,))
        pts = []
        for mc in range(2):
            ps3, bps3 = b.get("ps")
            b.mm(ps3[:, :], bps3, [(memk[:, h, mc * 128:(mc + 1) * 128], qn[:, :])], reads=(b_memk, bqn))
            pt, bpt = b.get("tb")
            b.act(pt[:, :], ps3[:, :], AF.Exp, reads=(bps3,), writes=(bpt,), scale=SCALE)
            pts.append((pt, bpt))
        st.update(pts=pts)

    def C():
        pts = st["pts"]
        pn, bpn = b.get("ps")
        b.mm(pn[:, :], bpn, [(memv[:, mc, h * 128:(h + 1) * 128], pts[mc][0][:, :]) for mc in range(2)],
             reads=(b_memv, pts[0][1], pts[1][1]))
        pd, bpd = b.get("ps")
        b.mm(pd[:, :], bpd, [(b.ones[:, :], pts[mc][0][:, :]) for mc in range(2)], reads=(b.b_ones, pts[0][1], pts[1][1]))
        rd, brd = b.get("tf")
        b.recip_act(rd[:, :], pd[:, :], reads=(bpd,), writes=(brd,))
        b.tt(out_ap, pn[:, :], rd[:, :], ALU.mult, reads=(bpn, brd), writes=(b_out,))
        if after is not None:
            after()
    return [A, B, C]


def mem_attn_stages2(b, mq_ps, b_mq, memk, b_memk, memv, b_memv, h, col_qg, out_ap, b_out, after=None):
    st = {}

    def A():
        mqs, bmqs = b.get("mqs")
        b.copy(mqs[:, :], mq_ps, reads=(b_mq,), writes=(bmqs,))
        sq, bsq = b.get("sq")
        b.act(sq[:, :], mq_ps, AF.Square, reads=(b_mq,), writes=(bsq,))
        ps2, bps2 = b.get("ps")
        b.mm(ps2[:, :], bps2, [(b.ones[:, :], sq[:, :])], reads=(bsq, b.b_ones))
        rs, brs = b.get("rstd")
        b.rstd_from_ss(ps2[:, :], bps2, 1.0 / 128, rs[:, :], brs)
        st.update(mqs=mqs, bmqs=bmqs, rs=rs, brs=brs)

    def B():
        qn, bqn = b.get("tb")
        b.stt(qn[:, :], st["mqs"][:, :], b.pcol(col_qg), st["rs"][:, :], ALU.mult, ALU.mult,
              reads=(st["bmqs"], st["brs"], b.b_par), writes=(bqn,))
        pts = []
        for mc in range(2):
            ps3, bps3 = b.get("ps")
            b.mm(ps3[:, :], bps3, [(memk[:, h, mc * 128:(mc + 1) * 128], qn[:, :])], reads=(b_memk, bqn))
            pt, bpt = b.get("ptm")
            b.act(pt[:, :], ps3[:, :], AF.Exp, reads=(bps3,), writes=(bpt,), scale=SCALE)
            pts.append((pt, bpt))
        st.update(pts=pts)

    def C():
        pts = st["pts"]
        pn, bpn = b.get("ps")
        b.mm(pn[:, :], bpn, [(memv[:, mc, h * 128:(h + 1) * 128], pts[mc][0][:, :]) for mc in range(2)],
             reads=(b_memv, pts[0][1], pts[1][1]))
        pd, bpd = b.get("ps")
        b.mm(pd[:, :], bpd, [(b.ones[:, :], pts[mc][0][:, :]) for mc in range(2)], reads=(b.b_ones, pts[0][1], pts[1][1]))
        rd, brd = b.get("tf")
        b.recip_act(rd[:, :], pd[:, :], reads=(bpd,), writes=(brd,))
        b.tt(out_ap, pn[:, :], rd[:, :], ALU.mult, reads=(bpn, brd), writes=(b_out,))
        if after is not None:
            after()
    return [A, B, C]


class MemState:
    def __init__(self, b, alias=None):
        if alias is None:
            self.memT = b.sb("memT", [128, KC, 256], F32)
            self.memn = b.sb("memn", [128, KC, 256], BF16)
        else:
            self.memT = alias[:, 0:8, :].bitcast(F32).rearrange("p a (b c) -> p (a b) c", c=256)
            self.memn = alias[:, 8:12, :].rearrange("p a (b c) -> p (a b) c", c=256)
        self.b_mem = Buf()
        self.b_memn = Buf()
        self.memk = b.sb("memk", [128, 4, 256], BF16)
        self.b_memk = Buf()
        self.memv = b.sb("memv", [128, 2, 512], BF16)
        self.b_memv = Buf()
        self.rstd_m = b.sb("rstd_m", [128, 256], F32)
        self.b_rstdm = Buf()


def tiles_A1(l):
    tl = list(memkv_tiles(l))
    tl += [wt_std("a_w_in", l, 0, 16, [(3072 + c, 256)]) for c in (0, 256)]
    tl.append(("gates", l))
    for n in range(12):
        tl.append(wt_std("a_w_in", l, 0, 16, [(n * 128, 128), (1536 + n * 128, 128)]))
    return tl


def build_A1(l, from_xn=False):
    _, tiles = _build_A1(l, from_xn, None)
    return _build_A1(l, from_xn, tiles)


def _build_A1(l, from_xn, wtiles):
    b = Bld(wtiles)
    nc, P = b.nc, b.P
    if from_xn:
        xn_d = b.din("xn", [128, KC, T], BF16)
        xnh_d = b.din("xnh", [128, KC, 4], BF16)
    else:
        xT_d = b.din("xT", [128, KC, T])
        xh_d = b.din("xh", [128, KC, 4])
    memT_d = b.din("memT", [128, KC, 256])
    cat_d = b.dout("cat", [128, 16, T], BF16)
    q_d = b.dout("qq", [128, 12, T], BF16)
    car_d = b.dout("carry", [128, 24])

    b.pool("xt", 4, [128, 512], F32)
    b.pool("ub", 3, [128, 4 + T], F32)
    b.pool("gb", 3, [128, T], F32)
    b.pool("xc", 2, [128, T], F32)
    b.pool("rb", 2, [128, T], F32)
    b.pool("ib", 2, [128, T], F32)
    b.pool("s3", 5, [128, T], F32)
    xn = b.sb("xn", [128, KC, T], BF16)
    b_xn = [[Buf() for t in range(2)] for k in range(KC)]
    xnh = b.sb("xnh", [128, KC, 4], BF16)
    b_xnh = Buf()
    xh = b.sb("xh", [128, KC, 4], F32)
    b_xh = Buf()
    b.pool("cb", 4, [128, T], BF16)
    b.pool("cbm", 2, [128, T], BF16)
    b.pool("mqs", 2, [128, 512], F32)
    b.pool("ptm", 4, [128, 512], BF16)
    M = MemState(b, alias=xn)
    rstd_x = [b.sb(f"rstd_x{t}", [128, 512], F32) for t in range(2)]
    b_rstdx = [Buf(), Buf()]
    rstd_h = b.sb("rstd_h", [128, 4], F32)
    b_rstdh = Buf()
    carry = b.sb("carry", [128, 24], F32)
    b_carry = Buf()
    gw = b.sb("gw", [128, 24, 128], BF16)
    b_gw = Buf()
    nsp = b.sb("nsp", [128, 12], F32)
    b_nsp = Buf()
    zeros = b.sb("zeros", [128, T], F32)
    b_zeros = Buf()
    sml = [b.sb(f"sml{i}", [128, 12], F32) for i in range(6)]
    b_sml = [Buf() for i in range(6)]

    b.dma_in(M.memT[:, :, :], memT_d, [M.b_mem])
    if not from_xn:
        b.dma_in(xh[:, :, :], xh_d, [b_xh])
    P.op(P.pool, lambda e: e.memset(zeros[:, :], 0.0), writes=(b_zeros,))

    st_mem_rstd(b, M.memT, M.b_mem, M.rstd_m, M.b_rstdm)
    st_memkv(b, M.memT, M.b_mem, M.rstd_m, M.b_rstdm, M.memn, M.b_memn, M.memk, M.b_memk, M.memv, M.b_memv, 16, 33, descs=memkv_tiles(l))

    if from_xn:
        b.dma_in(xnh[:, :, :], xnh_d, [b_xnh])
        for k in list(range(12, KC)) + list(range(12)):
            extra = (M.b_mem,) if k < 8 else ((M.b_memn,) if k < 12 else ())
            b.dma_in(xn[:, k, :], xn_d[:, k, :], [b_xn[k][0], b_xn[k][1]] + list(extra))
    else:
        pss = [b.get("ps"), b.get("ps")]
        for k in range(KC):
            for t in range(2):
                xt, bxt = b.get("xt")
                b.dma_in(xt[:, :], xT_d[:, k, t * 512:(t + 1) * 512], [bxt])
                sq, bsq = b.get("sq")
                b.act(sq[:, :], xt[:, :], AF.Square, reads=(bxt,), writes=(bsq,))

                def f(e, k=k, sq=sq, ps=pss[t][0]):
                    return e.matmul(ps[:, :], b.ones[:, :], sq[:, :], start=(k == 0), stop=(k == KC - 1))
                P.op(P.pe, f, reads=(bsq, b.b_ones), writes=(pss[t][1],))
        for t in range(2):
            b.rstd_from_ss(pss[t][0][:, :], pss[t][1], 1.0 / D, rstd_x[t][:, :], b_rstdx[t])
        psh, bpsh = b.get("ps")
        for k in range(KC):
            sq, bsq = b.get("sq")
            b.act(sq[:, :4], xh[:, k, :], AF.Square, reads=(b_xh,), writes=(bsq,))

            def f(e, k=k, sq=sq, psh=psh):
                return e.matmul(psh[:, :4], b.ones[:, :], sq[:, :4], start=(k == 0), stop=(k == KC - 1))
            P.op(P.pe, f, reads=(bsq, b.b_ones), writes=(bpsh,))
        b.rstd_from_ss(psh[:, :4], bpsh, 1.0 / D, rstd_h[:, :], b_rstdh, ncols=4)
        for k in range(KC):
            b.stt(xnh[:, k, :], xh[:, k, :], b.pcol(k), rstd_h[:, :], ALU.mult, ALU.mult,
                  reads=(b_xh, b_rstdh, b.b_par), writes=(b_xnh,))
        for k in range(KC):
            for t in range(2):
                tsl = slice(t * 512, (t + 1) * 512)
                xt, bxt = b.get("xt")
                b.dma_in(xt[:, :], xT_d[:, k, tsl], [bxt])
                extra = (M.b_mem,) if k < 8 else ((M.b_memn,) if k < 12 else ())
                b.stt(xn[:, k, tsl], xt[:, :], b.pcol(k), rstd_x[t][:, :], ALU.mult, ALU.mult,
                      reads=(bxt, b_rstdx[t], b.b_par), writes=(b_xn[k][t],) + extra)

    pipe = Pipe()
    mem_cb = {}

    def mem_iter(j):
        h, t = j // 2, j % 2
        wt, bw = b.next_w(wt_std("a_w_in", l, 0, 16, [(3072 + h * 128, 128)]))
        wv = wt[:, :16 * 128].rearrange("p (k c) -> p k c", k=16)
        if t == 0:
            mem_cb[h] = b.get("cbm")
        cbt, bcb = mem_cb[h]
        tsl = slice(t * 512, (t + 1) * 512)
        ps, bps = b.get("ps")
        b.mm(ps[:, :], bps, [(wv[:, k, 0:128], xn[:, k, tsl]) for k in range(KC)],
             reads=[bw] + [b_xn[k][t] for k in range(KC)])
        after = None
        if t == 1:
            after = (lambda h=h, cbt=cbt, bcb=bcb: b.dma_out(cat_d[:, 12 + h, :], cbt[:, :], rbufs=(bcb,)))
        pipe.push(mem_attn_stages2(b, ps[:, :], bps, M.memk, M.b_memk, M.memv, M.b_memv, h, 32, cbt[:, tsl], bcb, after=after))

    wt, bw = b.next_w(("gates", l))
    b.copy(gw[:, :, :], wt[:, :24 * 128].rearrange("p (g d) -> p g d", g=24), reads=(bw,), writes=(b_gw,), eng=P.pool)

    lam = b.par[:, 124:136]
    s0, s1, s2, s3, s4, s5 = sml
    B0, B1, B2, B3, B4, B5 = b_sml
    b.ts(s0[:, :], lam, -1.0, None, ALU.mult, None, reads=(b.b_par,), writes=(B0,))
    b.tt(s0[:, :], s0[:, :], lam, ALU.max, reads=(B0, b.b_par), writes=(B0,))
    b.act(s1[:, :], s0[:, :], AF.Exp, reads=(B0,), writes=(B1,), scale=-1.0)
    b.ts(s2[:, :], s1[:, :], 2.0, None, ALU.add, None, reads=(B1,), writes=(B2,))
    b.recip(s3[:, :], s2[:, :], reads=(B2,), writes=(B3,))
    b.tt(s2[:, :], s1[:, :], s3[:, :], ALU.mult, reads=(B1, B3), writes=(B2,))
    b.tt(s3[:, :], s2[:, :], s2[:, :], ALU.mult, reads=(B2,), writes=(B3,))
    b.ts(s4[:, :], s3[:, :], 1.0 / 11, 1.0 / 9, ALU.mult, ALU.add, reads=(B3,), writes=(B4,))
    for cst in (1.0 / 7, 1.0 / 5, 1.0 / 3, 1.0):
        b.tt(s4[:, :], s4[:, :], s3[:, :], ALU.mult, reads=(B4, B3), writes=(B4,))
        b.ts(s4[:, :], s4[:, :], cst, None, ALU.add, None, reads=(B4,), writes=(B4,))
    b.tt(s4[:, :], s4[:, :], s2[:, :], ALU.mult, reads=(B4, B2), writes=(B4,))
    b.ts(s5[:, :], lam, -1.0, 0.0, ALU.mult, ALU.max, reads=(b.b_par,), writes=(B5,))
    b.stt(s5[:, :], s4[:, :], 2.0, s5[:, :], ALU.mult, ALU.add, reads=(B4, B5), writes=(B5,))
    b.ts(nsp[:, :], s5[:, :], -8.0, None, ALU.mult, None, reads=(B5,), writes=(b_nsp,))

    stt_ = {}

    def S1a(n):
        wt, bw = b.next_w(wt_std("a_w_in", l, 0, 16, [(n * 128, 128), (1536 + n * 128, 128)]))
        wv = wt[:, :].rearrange("p (k c) -> p k c", k=16)
        ub, bub = b.get("ub")
        gb, bgb = b.get("gb")
        psh, bpsh = b.get("ps")
        b.mm(psh[:, :4], bpsh, [(wv[:, k, 0:128], xnh[:, k, :]) for k in range(KC)], reads=(bw, b_xnh))
        b.copy(ub[:, 0:4], psh[:, :4], reads=(bpsh,), writes=(bub,))
        stt_[n] = dict(ub=ub, bub=bub, gb=gb, bgb=bgb, wv=wv, bw=bw)
        S1t(n, 0)

    def S1t(n, t):
        d_ = stt_[n]
        wv, bw, ub, bub, gb, bgb = d_["wv"], d_["bw"], d_["ub"], d_["bub"], d_["gb"], d_["bgb"]
        tsl = slice(t * 512, (t + 1) * 512)
        ps, bps = b.get("ps")
        b.mm(ps[:, :], bps, [(wv[:, k, 0:128], xn[:, k, tsl]) for k in range(KC)],
             reads=[bw] + [b_xn[k][t] for k in range(KC)])
        b.copy(ub[:, 4 + t * 512:4 + (t + 1) * 512], ps[:, :], reads=(bps,), writes=(bub,))
        pg, bpg = b.get("ps")
        b.mm(pg[:, :], bpg, [(wv[:, k, 128:256], xn[:, k, tsl]) for k in range(KC)],
             reads=[bw] + [b_xn[k][t] for k in range(KC)])
        b.act(gb[:, tsl], pg[:, :], AF.Gelu_apprx_tanh, reads=(bpg,), writes=(bgb,))

    def S1b(n):
        S1t(n, 1)

    def S2a(n):
        d_ = stt_[n]
        ub, bub = d_["ub"], d_["bub"]
        xc, bxc = b.get("xc")
        cw = 40 + n * 4
        b.ts(xc[:, :], ub[:, 1:1 + T], b.pcol(cw), b.pcol(88 + n), ALU.mult, ALU.add, reads=(bub, b.b_par), writes=(bxc,), eng=P.pool)
        for j in range(1, 4):
            b.stt(xc[:, :], ub[:, 1 + j:1 + j + T], b.pcol(cw + j), xc[:, :], ALU.mult, ALU.add,
                  reads=(bub, b.b_par, bxc), writes=(bxc,))
        d_.update(xc=xc, bxc=bxc)

    def S2b(n):
        d_ = stt_[n]
        xc, bxc = d_["xc"], d_["bxc"]
        xcb = [b.get("tb"), b.get("tb")]
        for t in range(2):
            b.copy(xcb[t][0][:, :], xc[:, t * 512:(t + 1) * 512], reads=(bxc,), writes=(xcb[t][1],))
        rb, brb = b.get("rb")
        ib, bib = b.get("ib")
        for t in range(2):
            tsl = slice(t * 512, (t + 1) * 512)
            pr, bpr = b.get("ps")
            b.mm(pr[:, :], bpr, [(gw[:, n, :], xcb[t][0][:, :])], reads=(b_gw, xcb[t][1]))
            b.act(rb[:, tsl], pr[:, :], AF.Sigmoid, reads=(bpr, b.b_par), writes=(brb,), bias=b.pcol(100 + n))
            pi_, bpi = b.get("ps")
            b.mm(pi_[:, :], bpi, [(gw[:, 12 + n, :], xcb[t][0][:, :])], reads=(b_gw, xcb[t][1]))
            b.act(ib[:, tsl], pi_[:, :], AF.Sigmoid, reads=(bpi, b.b_par), writes=(bib,), bias=b.pcol(112 + n))
        d_.update(rb=rb, brb=brb, ib=ib, bib=bib)

    def S3a(n):
        d_ = stt_[n]
        xc, bxc, ab, bab, ib, bib = d_["xc"], d_["bxc"], d_["rb"], d_["brb"], d_["ib"], d_["bib"]
        b.act(ab[:, :], ab[:, :], AF.Exp, reads=(bab, b_nsp), writes=(bab,), scale=nsp[:, n:n + 1])
        bb_, bbb = b.get("s3")
        b.act(bb_[:, :], ab[:, :], AF.Square, reads=(bab,), writes=(bbb,))
        b.act(bb_[:, :], bb_[:, :], AF.Sqrt, reads=(bbb, b.b_par), writes=(bbb,), scale=-1.0, bias=b.pcol(254))
        b.tt(ib[:, :], ib[:, :], xc[:, :], ALU.mult, reads=(bib, bxc), writes=(bib,), eng=P.pool)
        d_.update(bb_=bb_, bbb=bbb)

    def S3b(n):
        d_ = stt_.pop(n)
        gb, bgb, ab, bab, ib, bib, bb_, bbb = d_["gb"], d_["bgb"], d_["rb"], d_["brb"], d_["ib"], d_["bib"], d_["bb_"], d_["bbb"]
        b.tt(bb_[:, :], bb_[:, :], ib[:, :], ALU.mult, reads=(bbb, bib), writes=(bbb,))
        hb, bhb = b.get("s3")
        P.op(P.dve, lambda e, hb=hb, ab=ab, bb_=bb_: e.tensor_tensor_scan(out=hb[:, :], data0=ab[:, :], data1=bb_[:, :], initial=0.0,
                                                                            op0=ALU.mult, op1=ALU.add),
             reads=(bab, bbb), writes=(bhb,))
        Ab, bAb = b.get("s3")
        P.op(P.dve, lambda e, Ab=Ab, ab=ab: e.tensor_tensor_scan(out=Ab[:, :], data0=ab[:, :], data1=zeros[:, :], initial=1.0,
                                                                  op0=ALU.mult, op1=ALU.add),
             reads=(bab, b_zeros), writes=(bAb,))
        b.copy(carry[:, n:n + 1], hb[:, T - 1:T], reads=(bhb,), writes=(b_carry,), eng=P.dve)
        b.copy(carry[:, 12 + n:13 + n], Ab[:, T - 1:T], reads=(bAb,), writes=(b_carry,), eng=P.dve)
        cbt, bcb = b.get("cb")
        b.tt(cbt[:, :], hb[:, :], gb[:, :], ALU.mult, reads=(bhb, bgb), writes=(bcb,), eng=P.pool)
        b.dma_out(cat_d[:, n, :], cbt[:, :], rbufs=(bcb,))
        cbq, bcq = b.get("cb")
        b.tt(cbq[:, :], Ab[:, :], gb[:, :], ALU.mult, reads=(bAb, bgb), writes=(bcq,), eng=P.pool)
        b.dma_out(q_d[:, n, :], cbq[:, :], rbufs=(bcq,))

    for s_ in range(12 + 2):
        if s_ < 8:
            mem_iter(s_)
        else:
            pipe.tick()
        if 0 <= s_ - 1 < 12:
            S2a(s_ - 1)
        if 0 <= s_ - 2 < 12:
            S3a(s_ - 2)
        if s_ < 12:
            S1a(s_)
        if 0 <= s_ - 1 < 12:
            S2b(s_ - 1)
        if 0 <= s_ - 2 < 12:
            S3b(s_ - 2)
        if s_ < 12:
            S1b(s_)
    pipe.flush()

    b.dma_out(car_d, carry[:, :], rbufs=(b_carry,))
    return b.finish(), b.wtiles


def tiles_A2(l, with_kv):
    tl = [wt_std("a_w_out", l, 0, 16, [(c, 256)]) for c in range(0, 2048, 256)]
    tl += mlp_tiles(l)
    if with_kv:
        tl += [wt_std("kv_w", None, 0, 16, [(c, 256)]) for c in range(0, 3072, 256)]
    return tl


def build_A2(l, with_kv, with_next=True):
    b = Bld(tiles_A2(l, with_kv), n_ps=6)
    b.pool("rstdp", 4, [128, 512], F32)
    stats = Stats(b)
    nc, P = b.nc, b.P
    xT_d = b.din("xT", [128, KC, T])
    cat_d = b.din("cat", [128, 16, T], BF16)
    q_d = b.din("qq", [128, 12, T], BF16)
    car_d = b.din("carr", [128, 8, 24])
    sel_d = b.din("sel", [128, 8])
    yT_d = b.dout("yT", [128, KC, T])
    xs = b.sb("xs", [128, KC, T], F32)
    b_xs = [[Buf() for t in range(2)] for k in range(KC)]
    cat = b.sb("cat", [128, 16, T], BF16)
    b_cat = [[Buf() for t in range(2)] for k in range(16)]
    h1 = b.sb("h1", [128, 16, T], BF16)
    b_h1 = [[Buf() for t in range(2)] for k in range(16)]
    carr = b.sb("carr", [128, 8, 24], F32)
    b_carr = Buf()
    sel = b.sb("sel", [128, 8], F32)
    b_sel = Buf()
    cst = [b.sb(f"cst{i}", [128, 12], F32) for i in range(2)]
    b_cst = [Buf(), Buf()]
    hin = b.sb("hin", [128, 12], F32)
    b_hin = Buf()
    tmp12 = b.sb("tmp12", [128, 12], F32)
    b_tmp12 = Buf()

    b.dma_in(carr[:, :, :], car_d, [b_carr])
    b.dma_in(sel[:, :], sel_d, [b_sel])
    for k in range(16):
        if k < 12:
            b.dma_in(h1[:, k, :], q_d[:, k, :], [b_h1[k][0], b_h1[k][1]])
        b.dma_in(cat[:, k, :], cat_d[:, k, :], [b_cat[k][0], b_cat[k][1]])
    for k in range(KC):
        b.dma_in(xs[:, k, :], xT_d[:, k, :], [b_xs[k][0], b_xs[k][1]])

    P.op(P.pool, lambda e: e.memset(cst[0][:, :], 0.0), writes=(b_cst[0],))
    P.op(P.pool, lambda e: e.memset(hin[:, :], 0.0), writes=(b_hin,))
    for r in range(8):
        cur, bcur = cst[r % 2], b_cst[r % 2]
        nx, bnx = cst[(r + 1) % 2], b_cst[(r + 1) % 2]
        b.stt(hin[:, :], cur[:, :], sel[:, r:r + 1], hin[:, :], ALU.mult, ALU.add, reads=(bcur, b_sel, b_hin), writes=(b_hin,))
        if r < 7:
            b.tt(tmp12[:, :], carr[:, r, 12:24], cur[:, :], ALU.mult, reads=(b_carr, bcur), writes=(b_tmp12,))
            b.tt(nx[:, :], tmp12[:, :], carr[:, r, 0:12], ALU.add, reads=(b_tmp12, b_carr), writes=(bnx,))
    for n in range(12):
        for t in range(2):
            tsl = slice(t * 512, (t + 1) * 512)
            b.stt(cat[:, n, tsl], h1[:, n, tsl], hin[:, n:n + 1], cat[:, n, tsl], ALU.mult, ALU.add,
                  reads=(b_h1[n][t], b_hin, b_cat[n][t]), writes=(b_cat[n][t],))
    for cg in range(8):
        wt, bw = b.next_w()
        wv = wt[:, :].rearrange("p (k c) -> p k c", k=16)
        for half in range(2):
            dc = cg * 2 + half
            for t in range(2):
                tsl = slice(t * 512, (t + 1) * 512)
                ps, bps = b.get("ps")
                b.mm(ps[:, :], bps, [(wv[:, k, half * 128:(half + 1) * 128], cat[:, k, tsl]) for k in range(16)],
                     reads=[bw] + [b_cat[k][t] for k in range(16)])
                b.tt(xs[:, dc, tsl], ps[:, :], xs[:, dc, tsl], ALU.add, reads=(bps, b_xs[dc][t]), writes=(b_xs[dc][t],))
                stats.add(xs[:, dc, tsl], b_xs[dc][t], t)
    need_out = with_next or with_kv
    st_mlp(b, xs, b_xs, cat, b_cat, h1, b_h1, 0, out_d=yT_d, in_rstd=stats.rstd(), out_stats=(stats if need_out else None))
    if need_out:
        rs_out = stats.rstd()
    if with_next:
        xnn_d = b.dout("xnn", [128, KC, T], BF16)
        st_apply_norm(b, xs, b_xs, cat, b_cat, 40, rs_out)
        for k in range(KC):
            b.dma_out(xnn_d[:, k, :], cat[:, k, :], rbufs=(b_cat[k][0], b_cat[k][1]))
    if with_kv:
        kT_d = b.dout("kT", [128, 12, T], BF16)
        vT_d = b.dout("vT", [128, 12, T], BF16)
        st_apply_norm(b, xs, b_xs, cat, b_cat, 16, rs_out)
        kpipe = Pipe()
        for cg in range(12):
            wt, bw = b.next_w()
            wv = wt[:, :].rearrange("p (k c) -> p k c", k=16)
            for half in range(2):
                hc = cg * 2 + half
                for t in range(2):
                    tsl = slice(t * 512, (t + 1) * 512)
                    ps, bps = b.get("ps")
                    b.mm(ps[:, :], bps, [(wv[:, k, half * 128:(half + 1) * 128], cat[:, k, tsl]) for k in range(16)],
                         reads=[bw] + [b_cat[k][t] for k in range(16)])
                    if hc < 12:
                        stq = {}

                        def KA(ps=ps, bps=bps, stq=stq):
                            sq, bsq = b.get("sq")
                            b.act(sq[:, :], ps[:, :], AF.Square, reads=(bps,), writes=(bsq,))
                            stq.update(sq=sq, bsq=bsq)

                        def KA2(stq=stq):
                            ps2, bps2 = b.get("ps")
                            b.mm(ps2[:, :], bps2, [(b.ones[:, :], stq["sq"][:, :])], reads=(stq["bsq"], b.b_ones))
                            stq.update(ps2=ps2, bps2=bps2)

                        def KB(ps=ps, bps=bps, stq=stq, hc=hc, t=t, tsl=tsl):
                            rs, brs = b.get("rstd")
                            b.rstd_from_ss(stq["ps2"][:, :], stq["bps2"], 1.0 / 128, rs[:, :], brs)
                            b.stt(h1[:, hc, tsl], ps[:, :], b.pcol(32 + hc // 4), rs[:, :], ALU.mult, ALU.mult,
                                  reads=(bps, brs, b.b_par), writes=(b_h1[hc][t],))
                        kpipe.push([KA, KA2, KB])
                    else:
                        b.copy(h1[:, hc - 12, tsl], ps[:, :], reads=(bps,), writes=(b_h1[hc - 12][t],))
            if cg == 5:
                kpipe.flush()
                for k in range(12):
                    b.dma_out(kT_d[:, k, :], h1[:, k, :], rbufs=(b_h1[k][0], b_h1[k][1]))
        for k in range(12):
            b.dma_out(vT_d[:, k, :], h1[:, k, :], rbufs=(b_h1[k][0], b_h1[k][1]))
    return b.finish(), b.wtiles


def fm(x):
    n, f = x.shape
    return np.ascontiguousarray(x.T.reshape(f // 128, 128, n).transpose(1, 0, 2))


def unfm(xT):
    p, k, n = xT.shape
    return np.ascontiguousarray(xT.transpose(1, 0, 2).reshape(k * p, n).T)


def colk(v):
    return np.ascontiguousarray(np.asarray(v, np.float32).reshape(-1, 128).T)


_CACHE = {}
_TIMES = []


def _run(nc, ins, tag=""):
    import os
    if os.environ.get("KTRACE"):
        res = run_bass_kernel_spmd(nc, ins, core_ids=list(range(NCORES)), trace=True)
        _TIMES.append((tag, res.exec_time_ns))
        print("KTRACE", tag, res.exec_time_ns, flush=True)
    else:
        res = run_bass_kernel_spmd(nc, ins, core_ids=list(range(NCORES)))
    return res.results


def _launch(key, builder, in_maps):
    if key not in _CACHE:
        _CACHE[key] = builder()
    nc, tiles = _CACHE[key]
    res = run_bass_kernel_spmd(nc, in_maps, core_ids=list(range(NCORES)))
    return res.results


def run_A_layer(l, xT, memT, W, with_kv, xn_in=None, next_g=None):
    f32 = np.float32
    par = np.zeros((128, NPAR), f32)
    par[:, 0:16] = colk(W["norm_mix_g"][l])
    par[:, 16:32] = colk(W["mem_norm_g"][l])
    par[:, 32] = W["mem_q_norm_g"][l]
    par[:, 33] = W["mem_k_norm_g"][l]
    cw = W["a_conv_w"][l]
    for n in range(12):
        for j in range(4):
            par[:, 40 + n * 4 + j] = cw[j, n * 128:(n + 1) * 128]
    par[:, 88:100] = colk(W["a_conv_b"][l])
    par[:, 100:112] = np.asarray(W["a_gate_r_b"][l], f32).T
    par[:, 112:124] = np.asarray(W["a_gate_i_b"][l], f32).T
    par[:, 124:136] = colk(W["a_lambda"][l])
    par[:, 254] = 1.0
    par[:, 255] = EPS
    nc1 = _CACHE.get(("A1", l))
    if nc1 is None:
        nc1 = _CACHE[("A1", l)] = build_A1(l, from_xn=(xn_in is not None))
    wst = pack_weights(nc1[1], W)
    ins = []
    for c in range(NCORES):
        if xn_in is not None:
            xnh = np.zeros((128, KC, 4), ml_dtypes.bfloat16)
            if c > 0:
                xnh[:, :, :] = np.asarray(xn_in[c - 1])[:, :, T - 4:]
            ins.append({"xn": xn_in[c], "xnh": xnh, "memT": memT, "wst": wst, "par": par})
        else:
            xh = np.zeros((128, KC, 4), f32)
            if c > 0:
                xh[:, :, :] = xT[c - 1][:, :, T - 4:]
            ins.append({"xT": xT[c], "xh": xh, "memT": memT, "wst": wst, "par": par})
    r1 = _run(nc1[0], ins, f"A1_{l}")
    par2 = np.zeros((128, NPAR), f32)
    par2[:, 0:16] = colk(W["norm_mlp_g"][l])
    if with_kv:
        par2[:, 16:32] = colk(W["kv_norm_g"])
        par2[:, 32:35] = np.asarray(W["k_norm_g"], f32).T
    par2[:, 255] = EPS
    if next_g is not None:
        par2[:, 40:56] = colk(next_g)
    nc2 = _CACHE.get(("A2", l))
    if nc2 is None:
        nc2 = _CACHE[("A2", l)] = build_A2(l, with_kv, with_next=(next_g is not None))
    wst2 = pack_weights(nc2[1], W)
    carr = np.ascontiguousarray(np.stack([r1[c]["carry"] for c in range(NCORES)], axis=1))
    ins2 = []
    for c in range(NCORES):
        sel = np.zeros((128, 8), f32)
        sel[:, c] = 1.0
        ins2.append({"xT": xT[c], "cat": r1[c]["cat"], "qq": r1[c]["qq"], "carr": carr, "sel": sel, "wst": wst2, "par": par2})
    r2 = _run(nc2[0], ins2, f"A2_{l}")
    out = [r2[c]["yT"] for c in range(NCORES)]
    xnn = [r2[c]["xnn"] for c in range(NCORES)] if next_g is not None else None
    if with_kv:
        return out, [r2[c]["kT"] for c in range(NCORES)], [r2[c]["vT"] for c in range(NCORES)], r1, xnn
    return out, None, None, r1, xnn


LG = [T // d for d in DIL]
NQ = [min(128, lg) for lg in LG]
NKC = [128 + lg for lg in LG]
NBC = [(n + 127) // 128 for n in NKC]
NK = [d * n for d, n in zip(DIL, NKC)]
NB = [d * n for d, n in zip(DIL, NBC)]


def tiles_B1(l):
    tl = list(memkv_tiles(l))
    tl += [wt_std("b_w_q", l - 2, 0, 16, [(c, 256)]) for c in range(0, 2048, 256)]
    return tl


def build_B1(l, from_xn=True):
    b = Bld(tiles_B1(l))
    nc, P = b.nc, b.P
    if from_xn:
        xn_d = b.din("xn", [128, KC, T], BF16)
    else:
        xT_d = b.din("xT", [128, KC, T])
    memT_d = b.din("memT", [128, KC, 256])
    kt_d = [b.din(f"kt{g}", [128, 4, NK[g]], BF16) for g in range(3)]
    vv_d = [b.din(f"vv{g}", [128, 4, NB[g], 128], BF16) for g in range(3)]
    oh_d = b.din("oh", [33, 6, 256])
    jm_d = b.din("jm", [128, 128])
    relb_d = b.din("relb", [33, 12])
    kval_d = b.din("kval", [128, 3])
    cat_d = b.dout("cat", [128, 8, T], BF16)
    vec_d = nc.dram_tensor("vecd", [6, 4, 256], F32).ap()

    if not from_xn:
        b.pool("xt", 4, [128, 512], F32)
    b.pool("cb", 4, [128, T], BF16)
    b.pool("pp", 6, [128, 512], BF16)
    b.pool("kt", 2, [128, max(NK)], BF16)
    b.pool("vv", 2, [128, max(NB), 128], BF16)
    xn = b.sb("xn", [128, KC, T], BF16)
    b_xn = [[Buf() for t in range(2)] for k in range(KC)]
    qn = b.sb("qn", [128, 12, T], BF16)
    b_qn = [Buf() for k in range(12)]
    M = MemState(b, alias=qn)
    rstd_x = [b.sb(f"rstd_x{t}", [128, 512], F32) for t in range(2)] if not from_xn else None
    b_rstdx = [Buf(), Buf()]
    accN = b.sb("accN", [128, T], F32)
    accD = b.sb("accD", [128, T], F32)
    b_acc = Buf()
    relb = b.sb("relb", [33, 12], F32)
    b_relb = Buf()
    jm = b.sb("jm", [128, 128], F32)
    b_jm = Buf()
    kval = b.sb("kval", [128, 3], F32)
    b_kval = Buf()
    b.pool("vec", 2, [4, 256], F32)
    b.pool("hk", 2, [128, 512], F32)
    masks = [[b.sb(f"mask{g}_{ty}", [128, 4, 128], F32) for ty in range(3)] for g in range(3)]
    b_masks = [[Buf() for ty in range(3)] for g in range(3)]

    b.dma_in(M.memT, memT_d, [M.b_mem])
    b.dma_in(relb[:, :], relb_d, [b_relb])
    b.dma_in(jm[:, :], jm_d, [b_jm])
    b.dma_in(kval[:, :], kval_d, [b_kval])

    mask_state = {}

    def mask_A(g, ty):
        oh, boh = b.get("tf")
        b.dma_in(oh[:33, :256], oh_d[:, g * 2 + ty, :], [boh])
        ps, bps = b.get("ps")
        b.mm(ps[:4, :256], bps, [(relb[:33, g * 4:(g + 1) * 4], oh[:33, :256])], reads=(b_relb, boh))
        vec, b_vec = b.get("vec")
        b.act(vec[:, :], ps[:4, :256], AF.Exp, reads=(bps,), writes=(b_vec,))
        b_vd = Buf()
        P.dma(P.sp, (lambda e, s, g=g, ty=ty, vec=vec: e.dma_start(out=vec_d[g * 2 + ty], in_=vec[:, :]).then_inc(s, 16)),
              reads=(b_vec,), writes=(b_vd,))
        hk, bhk = b.get("hk")
        src = bass.AP(tensor=vec_d.tensor, offset=(g * 2 + ty) * 1024, ap=[[1, 128], [256, 4], [1, 128]])
        b.dma_in(hk[:, :].rearrange("p (h q) -> p h q", h=4), src, [bhk], rbufs=(b_vd,))
        mask_state[(g, ty)] = (hk, bhk)

    def mask_B(g, ty):
        hk, bhk = mask_state[(g, ty)]
        ps2, bps2 = b.get("ps")
        b.mm(ps2[:, :], bps2, [(jm[:, :], hk[:, :])], reads=(b_jm, bhk))
        b.copy(masks[g][ty][:, :, :], ps2[:, :].rearrange("p (h q) -> p h q", h=4), reads=(bps2,), writes=(b_masks[g][ty],))
        if ty == 1:
            b.ts(masks[g][2][:, :, :], masks[g][1][:, :, :], kval[:, g:g + 1], None, ALU.mult, None,
                 reads=(b_masks[g][1], b_kval), writes=(b_masks[g][2],))

    if from_xn:
        for k in range(KC):
            b.dma_in(xn[:, k, :], xn_d[:, k, :], [b_xn[k][0], b_xn[k][1]])
        st_mem_rstd(b, M.memT, M.b_mem, M.rstd_m, M.b_rstdm)
        st_memkv(b, M.memT, M.b_mem, M.rstd_m, M.b_rstdm, M.memn, M.b_memn, M.memk, M.b_memk, M.memv, M.b_memv, 16, 33)
    else:
        pss = [b.get("ps"), b.get("ps")]
        for k in range(KC):
            for t in range(2):
                xt, bxt = b.get("xt")
                b.dma_in(xt[:, :], xT_d[:, k, t * 512:(t + 1) * 512], [bxt])
                sq, bsq = b.get("sq")
                b.act(sq[:, :], xt[:, :], AF.Square, reads=(bxt,), writes=(bsq,))

                def f(e, k=k, sq=sq, ps=pss[t][0]):
                    return e.matmul(ps[:, :], b.ones[:, :], sq[:, :], start=(k == 0), stop=(k == KC - 1))
                P.op(P.pe, f, reads=(bsq, b.b_ones), writes=(pss[t][1],))
        for t in range(2):
            b.rstd_from_ss(pss[t][0][:, :], pss[t][1], 1.0 / D, rstd_x[t][:, :], b_rstdx[t])
        st_mem_rstd(b, M.memT, M.b_mem, M.rstd_m, M.b_rstdm)
        st_memkv(b, M.memT, M.b_mem, M.rstd_m, M.b_rstdm, M.memn, M.b_memn, M.memk, M.b_memk, M.memv, M.b_memv, 16, 33)
        for k in range(KC):
            for t in range(2):
                tsl = slice(t * 512, (t + 1) * 512)
                xt, bxt = b.get("xt")
                b.dma_in(xt[:, :], xT_d[:, k, tsl], [bxt])
                b.stt(xn[:, k, tsl], xt[:, :], b.pcol(k), rstd_x[t][:, :], ALU.mult, ALU.mult,
                      reads=(bxt, b_rstdx[t], b.b_par), writes=(b_xn[k][t],))

    qpipe = Pipe()
    for cg in range(8):
        if cg < 6:
            mask_A(cg // 2, cg % 2)
        if 1 <= cg < 7:
            mask_B((cg - 1) // 2, (cg - 1) % 2)
        wt, bw = b.next_w()
        wv = wt[:, :].rearrange("p (k c) -> p k c", k=16)
        for half in range(2):
            hq = cg * 2 + half
            cbt = None
            if hq >= 12:
                cbt, bcb = b.get("cb")
            for t in range(2):
                tsl = slice(t * 512, (t + 1) * 512)
                ps, bps = b.get("ps")
                b.mm(ps[:, :], bps, [(wv[:, k, half * 128:(half + 1) * 128], xn[:, k, tsl]) for k in range(KC)],
                     reads=[bw] + [b_xn[k][t] for k in range(KC)])
                if hq >= 12:
                    after = None
                    if t == 1:
                        after = (lambda hq=hq, cbt=cbt, bcb=bcb: b.dma_out(cat_d[:, 4 + hq - 12, :], cbt[:, :], rbufs=(bcb,)))
                    qpipe.push(mem_attn_stages(b, ps[:, :], bps, M.memk, M.b_memk, M.memv, M.b_memv, hq - 12, 32, cbt[:, tsl], bcb,
                                               after=after))
                else:
                    stq = {}

                    def QA(ps=ps, bps=bps, stq=stq):
                        sq, bsq = b.get("sq")
                        b.act(sq[:, :], ps[:, :], AF.Square, reads=(bps,), writes=(bsq,))
                        ps2, bps2 = b.get("ps")
                        b.mm(ps2[:, :], bps2, [(b.ones[:, :], sq[:, :])], reads=(bsq, b.b_ones))
                        stq.update(ps2=ps2, bps2=bps2)

                    def QB(ps=ps, bps=bps, stq=stq, hq=hq, t=t):
                        g = hq // 4
                        d = DIL[g]
                        rs, brs = b.get("rstd")
                        b.rstd_from_ss(stq["ps2"][:, :], stq["bps2"], 1.0 / 128, rs[:, :], brs)
                        nl = 512 // d
                        dst = qn[:, hq, :].rearrange("p (r l) -> p l r", r=d)[:, t * nl:(t + 1) * nl, :]
                        extra = (M.b_mem,) if hq < 8 else (M.b_memn,)
                        b.stt(dst, ps[:, :].rearrange("p (l r) -> p l r", r=d), b.pcol(34 + g), rs[:, :].rearrange("p (l r) -> p l r", r=d),
                              ALU.mult, ALU.mult, reads=(bps, brs, b.b_par), writes=(b_qn[hq],) + extra)
                    qpipe.push([QA, QB])
    qpipe.flush()

    batches = []
    for h in range(4):
        for g in range(3):
            d, lg, nq = DIL[g], LG[g], NQ[g]
            upc = lg // nq
            nunits = d * upc
            U = 512 // nq
            for u0 in range(0, nunits, U):
                batches.append(dict(h=h, g=g, u0=u0, first=(u0 == 0), last=(g == 2 and u0 + U >= nunits)))
    cur_kv = {}

    def T1(bt):
        h, g, u0 = bt["h"], bt["g"], bt["u0"]
        d, lg, nq, nkc, nbc = DIL[g], LG[g], NQ[g], NKC[g], NBC[g]
        if bt["first"]:
            kt, bkt = b.get("kt")
            vv, bvv = b.get("vv")
            b.dma_in(kt[:, :NK[g]], kt_d[g][:, h, :], [bkt])
            b.dma_in(vv[:, :NB[g], :], vv_d[g][:, h, :, :], [bvv])
            cur_kv[(h, g)] = (kt, bkt, vv, bvv)
        kt, bkt, vv, bvv = cur_kv[(h, g)]
        hq = g * 4 + h
        upc = lg // nq
        U = 512 // nq
        psP, bpsP = b.get("ps")
        psD, bpsD = b.get("ps")
        units = []
        for ui in range(U):
            u = u0 + ui
            r, j = u // upc, u % upc
            qs = qn[:, hq, r * lg + j * nq: r * lg + (j + 1) * nq]
            kp = kt[:, r * nkc + j * nq: r * nkc + j * nq + 128]
            kd = kt[:, r * nkc + 128 + j * nq: r * nkc + 128 + (j + 1) * nq]
            units.append((r, j, qs, kp, kd))

        def fS(e, units=units, psP=psP, psD=psD, nq=nq):
            for ui, (r, j, qs, kp, kd) in enumerate(units):
                e.matmul(psP[:, ui * nq:(ui + 1) * nq], kp, qs, start=True, stop=True)
                ins = e.matmul(psD[:nq, ui * nq:(ui + 1) * nq], kd, qs, start=True, stop=True)
            return ins
        P.op(P.pe, fS, reads=(bkt, b_qn[hq]), writes=(bpsP, bpsD))
        bt.update(units=units, psP=psP, bpsP=bpsP, psD=psD, bpsD=bpsD, vv=vv, bvv=bvv)

    def T2(bt):
        h, g = bt["h"], bt["g"]
        nq = NQ[g]
        eP, beP = b.get("tf")
        eD, beD = b.get("tf")
        b.act(eP[:, :], bt["psP"][:, :], AF.Exp, reads=(bt["bpsP"],), writes=(beP,), scale=SCALE)
        b.act(eD[:nq, :], bt["psD"][:nq, :], AF.Exp, reads=(bt["bpsD"],), writes=(beD,), scale=SCALE)
        pP, bpP = b.get("pp")
        pD, bpD = b.get("pp")
        for ui, (r, j, qs, kp, kd) in enumerate(bt["units"]):
            csl = slice(ui * nq, (ui + 1) * nq)
            mty = 2 if j == 0 else 1
            b.tt(pP[:, csl], eP[:, csl], masks[g][mty][:, h, :nq], ALU.mult, reads=(beP, b_masks[g][mty]), writes=(bpP,))
            b.tt(pD[:nq, csl], eD[:nq, csl], masks[g][0][:nq, h, :nq], ALU.mult, reads=(beD, b_masks[g][0]), writes=(bpD,),
                 eng=P.pool)
        bt.update(pP=pP, bpP=bpP, pD=pD, bpD=bpD)

    def T3(bt):
        g = bt["g"]
        nq, nbc = NQ[g], NBC[g]
        psN, bpsN = b.get("ps")
        psS, bpsS = b.get("ps")
        pP, pD, vv = bt["pP"], bt["pD"], bt["vv"]

        def fV(e, units=bt["units"], psN=psN, psS=psS, nq=nq, pP=pP, pD=pD, vv=vv, nbc=nbc):
            for ui, (r, j, qs, kp, kd) in enumerate(units):
                csl = slice(ui * nq, (ui + 1) * nq)
                e.matmul(psN[:, csl], vv[:, r * nbc + j, :], pP[:, csl], start=True, stop=False)
                e.matmul(psN[:, csl], vv[:nq, r * nbc + j + 1, :], pD[:nq, csl], start=False, stop=True)
            e.matmul(psS[:, :], b.ones[:, :], pP[:, :], start=True, stop=False)
            ins = e.matmul(psS[:, :], b.ones[:nq, :], pD[:nq, :], start=False, stop=True)
            return ins
        P.op(P.pe, fV, reads=(bt["bvv"], bt["bpP"], bt["bpD"], b.b_ones), writes=(bpsN, bpsS))
        bt.update(psN=psN, bpsN=bpsN, psS=psS, bpsS=bpsS)

    def T4(bt):
        h, g, u0 = bt["h"], bt["g"], bt["u0"]
        d, lg, nq = DIL[g], LG[g], NQ[g]
        upc = lg // nq
        U = 512 // nq
        psN, bpsN, psS, bpsS = bt["psN"], bt["bpsN"], bt["psS"], bt["bpsS"]
        r0 = u0 // upc
        if d == 1:
            l0 = u0 * nq
            dN = accN[:, l0:l0 + 512]
            dD = accD[:, l0:l0 + 512]
            sN, sS = psN[:, :], psS[:, :]
        else:
            nr = U // upc
            dN = accN[:, :].rearrange("p (l r) -> p r l", r=d)[:, r0:r0 + nr, :]
            dD = accD[:, :].rearrange("p (l r) -> p r l", r=d)[:, r0:r0 + nr, :]
            sN = psN[:, :].rearrange("p (r l) -> p r l", r=nr)
            sS = psS[:, :].rearrange("p (r l) -> p r l", r=nr)
        if g == 0:
            b.copy(dN, sN, reads=(bpsN,), writes=(b_acc,))
            b.copy(dD, sS, reads=(bpsS,), writes=(b_acc,))
        else:
            b.tt(dN, sN, dN, ALU.add, reads=(bpsN, b_acc), writes=(b_acc,))
            b.tt(dD, sS, dD, ALU.add, reads=(bpsS, b_acc), writes=(b_acc,))
        if bt["last"]:
            cbt, bcb = b.get("cb")
            for t in range(2):
                tsl = slice(t * 512, (t + 1) * 512)
                rd, brd = b.get("tf")
                b.recip_act(rd[:, :], accD[:, tsl], reads=(b_acc,), writes=(brd,))
                b.tt(cbt[:, tsl], accN[:, tsl], rd[:, :], ALU.mult, reads=(b_acc, brd), writes=(bcb,))
            b.dma_out(cat_d[:, h, :], cbt[:, :], rbufs=(bcb,))

    nbt = len(batches)
    for s_ in range(nbt + 3):
        if 0 <= s_ - 2 < nbt:
            T3(batches[s_ - 2])
        if s_ < nbt:
            T1(batches[s_])
        if 0 <= s_ - 1 < nbt:
            T2(batches[s_ - 1])
        if 0 <= s_ - 3 < nbt:
            T4(batches[s_ - 3])
    return b.finish(), b.wtiles


def tiles_B2(l):
    tl = [wt_std("b_w_out", l - 2, 0, 8, [(c, 512)]) for c in range(0, 2048, 512)]
    tl += mlp_tiles(l)
    return tl


def build_B2(l, with_next=True):
    b = Bld(tiles_B2(l), n_ps=6)
    b.pool("rstdp", 4, [128, 512], F32)
    stats = Stats(b)
    nc, P = b.nc, b.P
    xT_d = b.din("xT", [128, KC, T])
    cat_d = b.din("cat", [128, 8, T], BF16)
    yT_d = b.dout("yT", [128, KC, T])
    xs = b.sb("xs", [128, KC, T], F32)
    b_xs = [[Buf() for t in range(2)] for k in range(KC)]
    cat = b.sb("cat", [128, 16, T], BF16)
    b_cat = [[Buf() for t in range(2)] for k in range(16)]
    h1 = b.sb("h1", [128, 16, T], BF16)
    b_h1 = [[Buf() for t in range(2)] for k in range(16)]
    for k in range(8):
        b.dma_in(cat[:, k, :], cat_d[:, k, :], [b_cat[k][0], b_cat[k][1]])
    for k in range(KC):
        b.dma_in(xs[:, k, :], xT_d[:, k, :], [b_xs[k][0], b_xs[k][1]])
    for cg in range(4):
        wt, bw = b.next_w()
        wv = wt[:, :].rearrange("p (k c) -> p k c", k=8)
        for q4 in range(4):
            dc = cg * 4 + q4
            for t in range(2):
                tsl = slice(t * 512, (t + 1) * 512)
                ps, bps = b.get("ps")
                b.mm(ps[:, :], bps, [(wv[:, k, q4 * 128:(q4 + 1) * 128], cat[:, k, tsl]) for k in range(8)],
                     reads=[bw] + [b_cat[k][t] for k in range(8)])
                b.tt(xs[:, dc, tsl], ps[:, :], xs[:, dc, tsl], ALU.add, reads=(bps, b_xs[dc][t]), writes=(b_xs[dc][t],))
                stats.add(xs[:, dc, tsl], b_xs[dc][t], t)
    st_mlp(b, xs, b_xs, cat, b_cat, h1, b_h1, 0, out_d=yT_d, in_rstd=stats.rstd(), out_stats=(stats if with_next else None))
    if with_next:
        xnn_d = b.dout("xnn", [128, KC, T], BF16)
        st_apply_norm(b, xs, b_xs, cat, b_cat, 40, stats.rstd())
        for k in range(KC):
            b.dma_out(xnn_d[:, k, :], cat[:, k, :], rbufs=(b_cat[k][0], b_cat[k][1]))
    return b.finish(), b.wtiles


def _t5_bucket(n):
    n = np.maximum(np.asarray(n, np.int64), 0)
    nf = np.maximum(n, 1).astype(np.float32)
    large = 16 + (np.log(nf / np.float32(16.0)) / np.float32(math.log(2048 / 16)) * np.float32(16.0)).astype(np.int32)
    large = np.minimum(large, 31)
    return np.where(n < 16, n, large)


def _structural():
    oh = np.zeros((33, 6, 256), np.float32)
    for g, d in enumerate(DIL):
        for i in range(256):
            w = i - 127
            if 0 <= w <= 127:
                oh[_t5_bucket(w * d), g * 2 + 0, i] = 1.0
            else:
                oh[32, g * 2 + 0, i] = 1.0
            if -127 <= w <= 0:
                oh[_t5_bucket((w + 128) * d), g * 2 + 1, i] = 1.0
            else:
                oh[32, g * 2 + 1, i] = 1.0
    jm = np.ascontiguousarray(np.eye(128, dtype=np.float32)[::-1])
    return oh, jm


def _kv_layout(kT, vT):
    bf = ml_dtypes.bfloat16
    Kf = np.concatenate([np.asarray(k) for k in kT], axis=2)
    Vf = np.concatenate([np.asarray(v) for v in vT], axis=2)
    outs = [dict() for _ in range(NCORES)]
    for g, d in enumerate(DIL):
        lg, nkc, nbc = LG[g], NKC[g], NBC[g]
        Lf = S // d
        def cm(A):
            A = A[:, g * 4:(g + 1) * 4, :].reshape(128, 4, Lf, d).transpose(0, 1, 3, 2)
            pad = np.zeros((128, 4, d, 128), bf)
            return np.concatenate([pad, A], axis=3)
        Kc, Vc = cm(Kf), cm(Vf)
        for c in range(NCORES):
            ks = Kc[:, :, :, c * lg:c * lg + nkc]
            outs[c][f"kt{g}"] = np.ascontiguousarray(ks.reshape(128, 4, d * nkc))
            vs = Vc[:, :, :, c * lg:c * lg + nkc]
            vp = np.zeros((128, 4, d, nbc * 128), bf)
            vp[:, :, :, :nkc] = vs
            vp = vp.reshape(128, 4, d, nbc, 128).transpose(4, 1, 2, 3, 0)
            outs[c][f"vv{g}"] = np.ascontiguousarray(vp.reshape(128, 4, d * nbc, 128))
            kv = outs[c].setdefault("kval", np.zeros((128, 3), np.float32))
            kv[:, g] = ((c * lg - 128 + np.arange(128)) >= 0).astype(np.float32)
    return outs


def run_B_layer(l, xT, memT, W, kvin, xn_in=None, next_g=None):
    f32 = np.float32
    j = l - 2
    par = np.zeros((128, NPAR), f32)
    par[:, 0:16] = colk(W["norm_mix_g"][l])
    par[:, 16:32] = colk(W["mem_norm_g"][l])
    par[:, 32] = W["mem_q_norm_g"][l]
    par[:, 33] = W["mem_k_norm_g"][l]
    par[:, 34:37] = np.asarray(W["b_q_norm_g"][j], f32).T
    par[:, 255] = EPS
    nb1 = _CACHE.get(("B1", l))
    if nb1 is None:
        nb1 = _CACHE[("B1", l)] = build_B1(l, from_xn=(xn_in is not None))
    wst = pack_weights(nb1[1], W)
    oh, jm = _structural()
    relb = np.concatenate([np.asarray(W["rel_bias"], f32), np.full((1, 12), -30000.0, f32)], axis=0)
    ins = []
    for c in range(NCORES):
        dct = {"memT": memT, "wst": wst, "par": par, "oh": oh, "jm": jm, "relb": relb}
        if xn_in is not None:
            dct["xn"] = xn_in[c]
        else:
            dct["xT"] = xT[c]
        dct.update(kvin[c])
        ins.append(dct)
    r1 = _run(nb1[0], ins, f"B1_{l}")
    par2 = np.zeros((128, NPAR), f32)
    par2[:, 0:16] = colk(W["norm_mlp_g"][l])
    par2[:, 255] = EPS
    if next_g is not None:
        par2[:, 40:56] = colk(next_g)
    nb2 = _CACHE.get(("B2", l))
    if nb2 is None:
        nb2 = _CACHE[("B2", l)] = build_B2(l, with_next=(next_g is not None))
    wst2 = pack_weights(nb2[1], W)
    ins2 = [{"xT": xT[c], "cat": r1[c]["cat"], "wst": wst2, "par": par2} for c in range(NCORES)]
    r2 = _run(nb2[0], ins2, f"B2_{l}")
    xnn = [r2[c]["xnn"] for c in range(NCORES)] if next_g is not None else None
    return [r2[c]["yT"] for c in range(NCORES)], r1, xnn


def kernel(**inputs):
    W = {k: np.asarray(v) for k, v in inputs.items()}
    x = W["x"][0]
    memT = fm(W["mem"][0])
    xT = [fm(x[c * T:(c + 1) * T]) for c in range(NCORES)]
    g = W["norm_mix_g"]
    xT, _, _, _, xnn = run_A_layer(0, xT, memT, W, with_kv=False, next_g=g[1])
    xT, kT, vT, _, xnn = run_A_layer(1, xT, memT, W, with_kv=True, xn_in=xnn, next_g=g[2])
    kvin = _kv_layout(kT, vT)
    xT, _, xnn = run_B_layer(2, xT, memT, W, kvin, xn_in=xnn, next_g=g[3])
    xT, _, xnn = run_B_layer(3, xT, memT, W, kvin, xn_in=xnn, next_g=None)
    out = np.concatenate([unfm(t) for t in xT], axis=0)
    return out.reshape(1, S, D).astype(np.float32)
```

```python
import math
import numpy as np
import ml_dtypes
import concourse.bass as bass
import concourse.mybir as mybir
from concourse.bass_utils import run_bass_kernel_spmd

F32 = mybir.dt.float32
BF16 = mybir.dt.bfloat16
AF = mybir.ActivationFunctionType
ALU = mybir.AluOpType

NCORES = 8
D = 2048
S = 8192
T = S // NCORES
KC = D // 128
DFF = 4 * D
EPS = 1e-6
WT_ELEMS = 4096
NPAR = 256
SCALE = 128 ** -0.5
DIL = (1, 4, 16)


class Buf:
    __slots__ = ("name", "w", "r", "slot")

    def __init__(self, name=""):
        self.name = name
        self.w = None
        self.r = []
        self.slot = None


class Eng:
    def __init__(self, name, sem, is_pe=False):
        self.name = name
        self.sem = sem
        self.count = 0
        self.ops = []
        self.waited = {}
        self.is_pe = is_pe


class Prog:
    N_DMA_SEMS = 12

    def __init__(self, nc):
        self.nc = nc
        self.pe = Eng("pe", nc.alloc_semaphore("s_pe"), is_pe=True)
        self.act = Eng("act", nc.alloc_semaphore("s_act"))
        self.dve = Eng("dve", nc.alloc_semaphore("s_dve"))
        self.pool = Eng("pool", nc.alloc_semaphore("s_pool"))
        self.sp = Eng("sp", None)
        self.engs = [self.pe, self.act, self.dve, self.pool, self.sp]
        self.dma_sems = [nc.alloc_semaphore(f"s_dma{i}") for i in range(self.N_DMA_SEMS)]
        self.dma_cnt = [0] * self.N_DMA_SEMS
        self.dma_last = [None] * self.N_DMA_SEMS
        self.dma_rr = 0

    def _deps(self, reads, writes):
        for b in list(reads) + list(writes):
            if b.slot is not None and b.slot[0] is not b:
                raise RuntimeError(f"stale pooled buffer {b.name}: re-allocated before this access was recorded")
        deps = []
        for b in reads:
            if b.w is not None:
                deps.append(b.w)
        for b in writes:
            if b.w is not None:
                deps.append(b.w)
            deps.extend(b.r)
        return deps

    def _filter(self, eng, deps):
        waits = {}
        for (sem, val) in deps:
            if eng.is_pe and sem is eng.sem:
                continue
            k = id(sem)
            if eng.waited.get(k, 0) >= val:
                continue
            if k not in waits or waits[k][1] < val:
                waits[k] = (sem, val)
        for k, (sem, val) in waits.items():
            eng.waited[k] = val
        return list(waits.values())

    def _update(self, tok, reads, writes):
        for b in writes:
            b.w = tok
            b.r = []
        for b in reads:
            if b not in writes:
                b.r.append(tok)

    def op(self, eng, fn, reads=(), writes=()):
        deps = self._deps(reads, writes)
        waits = self._filter(eng, deps)
        eng.count += 1
        tok = (eng.sem, eng.count)
        eng.ops.append((waits, fn, (eng.sem, 1)))
        self._update(tok, reads, writes)
        return tok

    def dma(self, eng, fn, reads=(), writes=(), n=1):
        k = self.dma_rr
        self.dma_rr = (self.dma_rr + 1) % self.N_DMA_SEMS
        sem = self.dma_sems[k]
        deps = self._deps(reads, writes)
        if self.dma_last[k] is not None:
            deps.append(self.dma_last[k])
        waits = self._filter(eng, deps)
        self.dma_cnt[k] += 16 * n
        tok = (sem, self.dma_cnt[k])
        self.dma_last[k] = tok
        eng.ops.append((waits, (lambda e, fn=fn, sem=sem: fn(e, sem)), None))
        self._update(tok, reads, writes)
        return tok

    def wait_all(self, eng, toks):
        waits = self._filter(eng, list(toks))
        eng.ops.append((waits, None, None))

    def emit(self):
        nc = self.nc

        def run(eng, h):
            for (waits, fn, inc) in eng.ops:
                for (sem, val) in waits:
                    h.wait_ge(sem, val)
                if fn is None:
                    continue
                ins = fn(h)
                if inc is not None:
                    ins.then_inc(inc[0], inc[1])

        with nc.Block() as block:
            @block.tensor
            def _(h):
                run(self.pe, h)

            @block.scalar
            def _(h):
                run(self.act, h)

            @block.vector
            def _(h):
                run(self.dve, h)

            @block.gpsimd
            def _(h):
                run(self.pool, h)

            @block.sync
            def _(h):
                run(self.sp, h)


class Bld:
    def __init__(self, wtiles, n_wslots=4, n_ps=8):
        self.discover = wtiles is None
        if self.discover:
            wtiles = []
        self.nc = bass.Bass("TRN2", target_bir_lowering=False)
        self.P = Prog(self.nc)
        self.pools = {}
        self.rr = {}
        self.out_toks = []
        self.pool("ps", n_ps, [128, 512], F32, psum=True)
        if n_ps < 8:
            self.pool("pstat", 8 - n_ps, [128, 512], F32, psum=True)
        self.wtiles = wtiles
        self.NT = 4096 if self.discover else len(wtiles)
        self.n_wslots = n_wslots
        self.wst_d = self.din("wst", [max(self.NT, 1), 128, WT_ELEMS])
        self.par_d = self.din("par", [128, NPAR])
        self.pool("w", n_wslots, [128, WT_ELEMS], BF16)
        self.par = self.sb("par_sb", [128, NPAR], F32)
        self.b_par = Buf("par")
        self.ones = self.sb("ones", [128, 128], BF16)
        self.b_ones = Buf("ones")
        self.w_next_load = 0
        self.w_cur = 0
        self.pool("tf", 4, [128, 512], F32)
        self.pool("tb", 6, [128, 512], BF16)
        self.pool("sq", 4, [128, 512], BF16)
        self.pool("rstd", 4, [128, 512], F32)
        self.dma_in(self.par[:, :], self.par_d, [self.b_par])
        self.P.op(self.P.pool, lambda e: e.memset(self.ones[:, :], 1.0), writes=(self.b_ones,))
        self._ensure(n_wslots - 1)

    def din(self, name, shape, dt=F32):
        return self.nc.dram_tensor(name, list(shape), dt, kind="ExternalInput").ap()

    def dout(self, name, shape, dt=F32):
        return self.nc.dram_tensor(name, list(shape), dt, kind="ExternalOutput").ap()

    def sb(self, name, shape, dt):
        return self.nc.alloc_sbuf_tensor("sb_" + name, list(shape), dt)

    def pool(self, name, n, shape, dt, psum=False):
        if psum:
            lst = [(self.nc.alloc_psum_tensor(f"pp_{name}{i}", list(shape), dt), Buf(f"{name}{i}")) for i in range(n)]
        else:
            lst = [(self.nc.alloc_sbuf_tensor(f"pl_{name}{i}", list(shape), dt), Buf(f"{name}{i}")) for i in range(n)]
        self.pools[name] = lst
        self.rr[name] = 0

    def get(self, name):
        lst = self.pools[name]
        i = self.rr[name]
        self.rr[name] = (i + 1) % len(lst)
        t, old = lst[i]
        nb = Buf(old.name)
        nb.w, nb.r = old.w, list(old.r)
        cell = old.slot if old.slot is not None else [None]
        nb.slot = cell
        cell[0] = nb
        lst[i] = (t, nb)
        return t, nb

    def dma_in(self, dst, src, wbufs, eng=None, rbufs=()):
        eng = eng or self.P.sp
        return self.P.dma(eng, lambda e, s: e.dma_start(out=dst, in_=src).then_inc(s, 16), reads=rbufs, writes=wbufs)

    def dma_out(self, dst, src, rbufs):
        tok = self.P.dma(self.P.sp, lambda e, s: e.dma_start(out=dst, in_=src).then_inc(s, 16), reads=rbufs)
        self.out_toks.append(tok)
        return tok

    def finish(self):
        if self.discover:
            return None
        self.P.wait_all(self.P.sp, self.out_toks)
        self.P.emit()
        return self.nc

    def _ensure(self, upto):
        while self.w_next_load <= min(upto, self.NT - 1):
            i = self.w_next_load
            wt, bw = self.pools["w"][i % self.n_wslots]
            self.P.dma(self.P.pool, (lambda e, s, i=i, wt=wt: e.dma_start(out=wt[:, :], in_=self.wst_d[i]).then_inc(s, 16)),
                       writes=(bw,))
            self.w_next_load += 1

    def next_w(self, desc=None):
        i = self.w_cur
        if desc is not None:
            if self.discover:
                self.wtiles.append(desc)
            else:
                assert self.wtiles[i] == desc, (i, self.wtiles[i], desc)
        assert i < self.NT, "weight stream exhausted"
        self.w_cur += 1
        self._ensure(i + self.n_wslots - 1)
        return self.pools["w"][i % self.n_wslots]

    def mm(self, out_ap, bps, pairs, reads):
        n = len(pairs)

        def f(e):
            for i, (l, r) in enumerate(pairs):
                ins = e.matmul(out_ap, l, r, start=(i == 0), stop=(i == n - 1))
            return ins
        return self.P.op(self.P.pe, f, reads=reads, writes=(bps,))

    def act(self, out, in_, func, reads, writes, bias=None, scale=None):
        kw = {}
        if bias is not None:
            kw["bias"] = bias
        if scale is not None:
            kw["scale"] = scale
        return self.P.op(self.P.act, lambda e: e.activation(out=out, in_=in_, func=func, **kw), reads=reads, writes=writes)

    def tt(self, out, a, b_, op, reads, writes, eng=None):
        eng = eng or self.P.dve
        return self.P.op(eng, lambda e: e.tensor_tensor(out=out, in0=a, in1=b_, op=op), reads=reads, writes=writes)

    def ts(self, out, a, s1, s2, op0, op1, reads, writes, eng=None):
        eng = eng or self.P.dve
        if s2 is None:
            return self.P.op(eng, lambda e: e.tensor_scalar(out=out, in0=a, scalar1=s1, scalar2=None, op0=op0), reads=reads, writes=writes)
        return self.P.op(eng, lambda e: e.tensor_scalar(out=out, in0=a, scalar1=s1, scalar2=s2, op0=op0, op1=op1), reads=reads, writes=writes)

    def stt(self, out, a, sc, b_, op0, op1, reads, writes, eng=None):
        eng = eng or self.P.dve
        return self.P.op(eng, lambda e: e.scalar_tensor_tensor(out=out, in0=a, scalar=sc, in1=b_, op0=op0, op1=op1), reads=reads, writes=writes)

    def recip(self, out, in_, reads, writes):
        return self.P.op(self.P.dve, lambda e: e.reciprocal(out=out, in_=in_), reads=reads, writes=writes)

    def copy(self, out, in_, reads, writes, eng=None):
        eng = eng or self.P.act
        if eng is self.P.act:
            return self.P.op(eng, lambda e: e.copy(out=out, in_=in_), reads=reads, writes=writes)
        return self.P.op(eng, lambda e: e.tensor_copy(out=out, in_=in_), reads=reads, writes=writes)

    def pcol(self, c, n=1):
        return self.par[:, c:c + n]

    def rstd_from_ss(self, ss_ap, b_ss, inv_count, out_ap, out_b, ncols=512):
        tf, btf = self.get("tf")
        self.act(tf[:, :ncols], ss_ap, AF.Ln, reads=(b_ss, self.b_par), writes=(btf,), bias=self.pcol(255), scale=inv_count)
        self.act(out_ap, tf[:, :ncols], AF.Exp, reads=(btf,), writes=(out_b,), scale=-0.5)

    def recip_act(self, out_ap, in_ap, reads, writes, nrows=128, ncols=512):
        tf, btf = self.get("tf")
        self.act(tf[:nrows, :ncols], in_ap, AF.Ln, reads=reads, writes=(btf,))
        self.act(out_ap, tf[:nrows, :ncols], AF.Exp, reads=(btf,), writes=writes, scale=-1.0)


def wt_std(key, l, row0, nk, cols):
    return ("std", key, l, row0, nk, tuple(cols))


def pack_weights(tiles, W):
    out = np.zeros((max(len(tiles), 1), 128, WT_ELEMS), np.float32)
    for i, tl in enumerate(tiles):
        if tl[0] == "std":
            _, key, l, row0, nk, cols = tl
            w = W[key][l] if l is not None else W[key]
            blk = np.concatenate([w[row0:row0 + nk * 128, c0:c0 + n] for (c0, n) in cols], axis=1)
            ncol = blk.shape[1]
            out[i, :, :nk * ncol] = blk.reshape(nk, 128, ncol).transpose(1, 0, 2).reshape(128, nk * ncol)
        elif tl[0] == "gates":
            _, l = tl
            g = np.concatenate([W["a_gate_r_w"][l], W["a_gate_i_w"][l]], axis=0)
            out[i, :, :24 * 128] = g.transpose(1, 0, 2).reshape(128, 24 * 128)
    return out


def mlp_tiles(l):
    tl = []
    for q in range(4):
        for c in range(0, 2048, 256):
            tl.append(wt_std("mlp_w1", l, 0, 16, [(q * 2048 + c, 256)]))
        for c in range(0, 2048, 256):
            tl.append(wt_std("mlp_w2", l, q * 2048, 16, [(c, 256)]))
    return tl


def memkv_tiles(l):
    return [wt_std("mem_w_kv", l, 0, 16, [(c, 256)]) for c in range(0, 1024, 256)]


class Stats:
    def __init__(self, b):
        self.b = b
        self.ps = [b.pools["pstat"][t] for t in range(2)]
        self.n = [0, 0]
        self.pipe = Pipe()
        b.pool("sqs", 8, [128, 512], BF16)

    def add(self, src_ap, b_src, t):
        b = self.b
        k = self.n[t]
        self.n[t] += 1
        ps, bps = self.ps[t]
        st = {}

        def A():
            sq, bsq = b.get("sqs")
            b.act(sq[:, :], src_ap, AF.Square, reads=(b_src,), writes=(bsq,))
            st.update(sq=sq, bsq=bsq)

        def nop():
            pass

        def B():
            sq, bsq = st["sq"], st["bsq"]

            def f(e, k=k, sq=sq, ps=ps):
                return e.matmul(ps[:, :], b.ones[:, :], sq[:, :], start=(k == 0), stop=(k == KC - 1))
            b.P.op(b.P.pe, f, reads=(bsq, b.b_ones), writes=(bps,))
        self.pipe.push([nop, A, nop, nop, B])

    def rstd(self):
        b = self.b
        assert self.n == [KC, KC]
        self.pipe.flush()
        out = []
        for t in range(2):
            rs, brs = b.get("rstdp")
            b.rstd_from_ss(self.ps[t][0][:, :], self.ps[t][1], 1.0 / D, rs[:, :], brs)
            out.append((rs, brs))
        self.n = [0, 0]
        return out


def st_apply_norm(b, xs, b_xs, xn, b_xn, gcol, rstds):
    for t in range(2):
        tsl = slice(t * 512, (t + 1) * 512)
        rs, brs = rstds[t]
        for k in range(KC):
            b.stt(xn[:, k, tsl], xs[:, k, tsl], b.pcol(gcol + k), rs[:, :], ALU.mult, ALU.mult,
                  reads=(b_xs[k][t], brs, b.b_par), writes=(b_xn[k][t],))


def st_norm_resident(b, xs, b_xs, xn, b_xn, gcol):
    for t in range(2):
        tsl = slice(t * 512, (t + 1) * 512)
        ps, bps = b.get("ps")
        for k in range(KC):
            sq, bsq = b.get("sq")
            b.act(sq[:, :], xs[:, k, tsl], AF.Square, reads=(b_xs[k][t],), writes=(bsq,))

            def f(e, k=k, sq=sq, ps=ps):
                return e.matmul(ps[:, :], b.ones[:, :], sq[:, :], start=(k == 0), stop=(k == KC - 1))
            b.P.op(b.P.pe, f, reads=(bsq, b.b_ones), writes=(bps,))
        rs, brs = b.get("rstd")
        b.rstd_from_ss(ps[:, :], bps, 1.0 / D, rs[:, :], brs)
        for k in range(KC):
            b.stt(xn[:, k, tsl], xs[:, k, tsl], b.pcol(gcol + k), rs[:, :], ALU.mult, ALU.mult,
                  reads=(b_xs[k][t], brs, b.b_par), writes=(b_xn[k][t],))


def st_mlp(b, xs, b_xs, xn, b_xn, h1, b_h1, gcol, out_d=None, in_rstd=None, out_stats=None):
    if in_rstd is not None:
        st_apply_norm(b, xs, b_xs, xn, b_xn, gcol, in_rstd)
    else:
        st_norm_resident(b, xs, b_xs, xn, b_xn, gcol)
    for q in range(4):
        for cg in range(8):
            wt, bw = b.next_w()
            wv = wt[:, :].rearrange("p (k c) -> p k c", k=16)
            for half in range(2):
                fc = cg * 2 + half
                for t in range(2):
                    tsl = slice(t * 512, (t + 1) * 512)
                    ps, bps = b.get("ps")
                    b.mm(ps[:, :], bps, [(wv[:, k, half * 128:(half + 1) * 128], xn[:, k, tsl]) for k in range(KC)],
                         reads=[bw] + [b_xn[k][t] for k in range(KC)])
                    tf, btf = b.get("tf")
                    b.act(tf[:, :], ps[:, :], AF.Relu, reads=(bps,), writes=(btf,))
                    b.tt(h1[:, fc, tsl], tf[:, :], tf[:, :], ALU.mult, reads=(btf,), writes=(b_h1[fc][t],), eng=b.P.pool)
        for cg in range(8):
            wt, bw = b.next_w()
            wv = wt[:, :].rearrange("p (k c) -> p k c", k=16)
            for half in range(2):
                dc = cg * 2 + half
                for t in range(2):
                    tsl = slice(t * 512, (t + 1) * 512)
                    ps, bps = b.get("ps")
                    b.mm(ps[:, :], bps, [(wv[:, k, half * 128:(half + 1) * 128], h1[:, k, tsl]) for k in range(16)],
                         reads=[bw] + [b_h1[k][t] for k in range(16)])
                    b.tt(xs[:, dc, tsl], ps[:, :], xs[:, dc, tsl], ALU.add, reads=(bps, b_xs[dc][t]), writes=(b_xs[dc][t],))
                    if q == 3 and out_stats is not None:
                        out_stats.add(xs[:, dc, tsl], b_xs[dc][t], t)
                if q == 3 and out_d is not None:
                    b.dma_out(out_d[:, dc, :], xs[:, dc, :], rbufs=(b_xs[dc][0], b_xs[dc][1]))


def st_mem_rstd(b, memT, b_mem, rstd_m, b_rstdm):
    ps, bps = b.get("ps")
    for k in range(KC):
        sq, bsq = b.get("sq")
        b.act(sq[:, :256], memT[:, k, :], AF.Square, reads=(b_mem,), writes=(bsq,))

        def f(e, k=k, sq=sq, ps=ps):
            return e.matmul(ps[:, :256], b.ones[:, :], sq[:, :256], start=(k == 0), stop=(k == KC - 1))
        b.P.op(b.P.pe, f, reads=(bsq, b.b_ones), writes=(bps,))
    b.rstd_from_ss(ps[:, :256], bps, 1.0 / D, rstd_m[:, :], b_rstdm, ncols=256)


def st_memkv(b, memT, b_mem, rstd_m, b_rstdm, memn, b_memn, memk, b_memk, memv, b_memv, gcol_mem, col_kg, descs=None):
    for k in range(KC):
        b.stt(memn[:, k, :], memT[:, k, :], b.pcol(gcol_mem + k), rstd_m[:, :], ALU.mult, ALU.mult,
              reads=(b_mem, b_rstdm, b.b_par), writes=(b_memn,))
    for hp in range(2):
        wt, bw = b.next_w(descs[hp] if descs else None)
        wv = wt[:, :].rearrange("p (k c) -> p k c", k=16)
        for hh in range(2):
            h = hp * 2 + hh
            hs = slice(hh * 128, hh * 128 + 128)
            ps, bps = b.get("ps")
            b.mm(ps[:, :256], bps, [(wv[:, k, hs], memn[:, k, :]) for k in range(KC)], reads=(bw, b_memn))
            sq, bsq = b.get("sq")
            b.act(sq[:, :256], ps[:, :256], AF.Square, reads=(bps,), writes=(bsq,))
            ps2, bps2 = b.get("ps")
            b.mm(ps2[:, :256], bps2, [(b.ones[:, :], sq[:, :256])], reads=(bsq, b.b_ones))
            rs, brs = b.get("rstd")
            b.rstd_from_ss(ps2[:, :256], bps2, 1.0 / 128, rs[:, :256], brs, ncols=256)
            b.stt(memk[:, h, :], ps[:, :256], b.pcol(col_kg), rs[:, :256], ALU.mult, ALU.mult,
                  reads=(bps, brs, b.b_par), writes=(b_memk,))
    for vh in range(2):
        wt, bw = b.next_w(descs[2 + vh] if descs else None)
        wv = wt[:, :].rearrange("p (k c) -> p k c", k=16)
        for mc in range(2):
            ps, bps = b.get("ps")
            b.mm(ps[:, :256], bps, [(memn[:, k, mc * 128:(mc + 1) * 128], wv[:, k, :]) for k in range(KC)],
                 reads=(bw, b_memn))
            b.copy(memv[:, mc, vh * 256:(vh + 1) * 256], ps[:, :256], reads=(bps,), writes=(b_memv,))


class Pipe:
    def __init__(self):
        self.items = []

    def _advance(self, skip_new=False):
        for it in reversed(self.items):
            if it:
                it.pop(0)()
        self.items = [it for it in self.items if it]

    def push(self, stages):
        self.items.append(list(stages))
        self._advance()

    def tick(self):
        self._advance()

    def flush(self):
        while self.items:
            self._advance()


def mem_attn_stages(b, mq_ps, b_mq, memk, b_memk, memv, b_memv, h, col_qg, out_ap, b_out, after=None):
    st = {}

    def A():
        sq, bsq = b.get("sq")
        b.act(sq[:, :], mq_ps, AF.Square, reads=(b_mq,), writes=(bsq,))
        ps2, bps2 = b.get("ps")
        b.mm(ps2[:, :], bps2, [(b.ones[:, :], sq[:, :])], reads=(bsq, b.b_ones))
        st.update(ps2=ps2, bps2=bps2)

    def B():
        rs, brs = b.get("rstd")
        b.rstd_from_ss(st["ps2"][:, :], st["bps2"], 1.0 / 128, rs[:, :], brs)
        qn, bqn = b.get("tb")
        b.stt(qn[:, :], mq_ps, b.pcol(col_qg), rs[:, :], ALU.mult, ALU.mult, reads=(b_mq, brs, b.b_par), writes=(bqn,))
        pts = []
        for mc in range(2):
            ps3, bps3 = b.get("ps")
            b.mm(ps3[:, :], bps3, [(memk[:, h, mc * 128:(mc + 1) * 128], qn[:, :])], reads=(b_memk, bqn))
            pt, bpt = b.get("tb")
            b.act(pt[:, :], ps3[:, :], AF.Exp, reads=(bps3,), writes=(bpt,), scale=SCALE)
            pts.append((pt, bpt))
        st.update(pts=pts)

    def C():
        pts = st["pts"]
        pn, bpn = b.get("ps")
        b.mm(pn[:, :], bpn, [(memv[:, mc, h * 128:(h + 1) * 128], pts[mc][0][:, :]) for mc in range(2)],
             reads=(b_memv, pts[0][1], pts[1][1]))
        pd, bpd = b.get("ps")
        b.mm(pd[:, :], bpd, [(b.ones[:, :], pts[mc][0][:, :]) for mc in range(2)], reads=(b.b_ones, pts[0][1], pts[1][1]))
        rd, brd = b.get("tf")
        b.recip_act(rd[:, :], pd[:, :], reads=(bpd,), writes=(brd,))
        b.tt(out_ap, pn[:, :], rd[:, :], ALU.mult, reads=(bpn, brd), writes=(b_out,))
        if after is not None:
            after()
    return [A, B, C]


def mem_attn_stages2(b, mq_ps, b_mq, memk, b_memk, memv, b_memv, h, col_qg, out_ap, b_out, after=None):
    st = {}

    def A():
        mqs, bmqs = b.get("mqs")
        b.copy(mqs[:, :], mq_ps, reads=(b_mq,), writes=(bmqs,))
        sq, bsq = b.get("sq")
        b.act(sq[:, :], mq_ps, AF.Square, reads=(b_mq,), writes=(bsq,))
        ps2, bps2 = b.get("ps")
        b.mm(ps2[:, :], bps2, [(b.ones[:, :], sq[:, :])], reads=(bsq, b.b_ones))
        rs, brs = b.get("rstd")
        b.rstd_from_ss(ps2[:, :], bps2, 1.0 / 128, rs[:, :], brs)
        st.update(mqs=mqs, bmqs=bmqs, rs=rs, brs=brs)

    def B():
        qn, bqn = b.get("tb")
        b.stt(qn[:, :], st["mqs"][:, :], b.pcol(col_qg), st["rs"][:, :], ALU.mult, ALU.mult,
              reads=(st["bmqs"], st["brs"], b.b_par), writes=(bqn,))
        pts = []
        for mc in range(2):
            ps3, bps3 = b.get("ps")
            b.mm(ps3[:, :], bps3, [(memk[:, h, mc * 128:(mc + 1) * 128], qn[:, :])], reads=(b_memk, bqn))
            pt, bpt = b.get("ptm")
            b.act(pt[:, :], ps3[:, :], AF.Exp, reads=(bps3,), writes=(bpt,), scale=SCALE)
            pts.append((pt, bpt))
        st.update(pts=pts)

    def C():
        pts = st["pts"]
        pn, bpn = b.get("ps")
        b.mm(pn[:, :], bpn, [(memv[:, mc, h * 128:(h + 1) * 128], pts[mc][0][:, :]) for mc in range(2)],
             reads=(b_memv, pts[0][1], pts[1][1]))
        pd, bpd = b.get("ps")
        b.mm(pd[:, :], bpd, [(b.ones[:, :], pts[mc][0][:, :]) for mc in range(2)], reads=(b.b_ones, pts[0][1], pts[1][1]))
        rd, brd = b.get("tf")
        b.recip_act(rd[:, :], pd[:, :], reads=(bpd,), writes=(brd,))
        b.tt(out_ap, pn[:, :], rd[:, :], ALU.mult, reads=(bpn, brd), writes=(b_out,))
        if after is not None:
            after()
    return [A, B, C]


class MemState:
    def __init__(self, b, alias=None):
        if alias is None:
            self.memT = b.sb("memT", [128, KC, 256], F32)
            self.memn = b.sb("memn", [128, KC, 256], BF16)
        else:
            self.memT = alias[:, 0:8, :].bitcast(F32).rearrange("p a (b c) -> p (a b) c", c=256)
            self.memn = alias[:, 8:12, :].rearrange("p a (b c) -> p (a b) c", c=256)
        self.b_mem = Buf()
        self.b_memn = Buf()
        self.memk = b.sb("memk", [128, 4, 256], BF16)
        self.b_memk = Buf()
        self.memv = b.sb("memv", [128, 2, 512], BF16)
        self.b_memv = Buf()
        self.rstd_m = b.sb("rstd_m", [128, 256], F32)
        self.b_rstdm = Buf()


def tiles_A1(l):
    tl = list(memkv_tiles(l))
    tl += [wt_std("a_w_in", l, 0, 16, [(3072 + c, 256)]) for c in (0, 256)]
    tl.append(("gates", l))
    for n in range(12):
        tl.append(wt_std("a_w_in", l, 0, 16, [(n * 128, 128), (1536 + n * 128, 128)]))
    return tl


def build_A1(l, from_xn=False):
    _, tiles = _build_A1(l, from_xn, None)
    return _build_A1(l, from_xn, tiles)


def _build_A1(l, from_xn, wtiles):
    b = Bld(wtiles)
    nc, P = b.nc, b.P
    if from_xn:
        xn_d = b.din("xn", [128, KC, T], BF16)
        xnh_d = b.din("xnh", [128, KC, 4], BF16)
    else:
        xT_d = b.din("xT", [128, KC, T])
        xh_d = b.din("xh", [128, KC, 4])
    memT_d = b.din("memT", [128, KC, 256])
    cat_d = b.dout("cat", [128, 16, T], BF16)
    q_d = b.dout("qq", [128, 12, T], BF16)
    car_d = b.dout("carry", [128, 24])

    b.pool("xt", 4, [128, 512], F32)
    b.pool("ub", 3, [128, 4 + T], F32)
    b.pool("gb", 3, [128, T], F32)
    b.pool("xc", 2, [128, T], F32)
    b.pool("rb", 2, [128, T], F32)
    b.pool("ib", 2, [128, T], F32)
    b.pool("s3", 5, [128, T], F32)
    xn = b.sb("xn", [128, KC, T], BF16)
    b_xn = [[Buf() for t in range(2)] for k in range(KC)]
    xnh = b.sb("xnh", [128, KC, 4], BF16)
    b_xnh = Buf()
    xh = b.sb("xh", [128, KC, 4], F32)
    b_xh = Buf()
    b.pool("cb", 4, [128, T], BF16)
    b.pool("cbm", 2, [128, T], BF16)
    b.pool("mqs", 2, [128, 512], F32)
    b.pool("ptm", 4, [128, 512], BF16)
    M = MemState(b, alias=xn)
    rstd_x = [b.sb(f"rstd_x{t}", [128, 512], F32) for t in range(2)]
    b_rstdx = [Buf(), Buf()]
    rstd_h = b.sb("rstd_h", [128, 4], F32)
    b_rstdh = Buf()
    carry = b.sb("carry", [128, 24], F32)
    b_carry = Buf()
    gw = b.sb("gw", [128, 24, 128], BF16)
    b_gw = Buf()
    nsp = b.sb("nsp", [128, 12], F32)
    b_nsp = Buf()
    zeros = b.sb("zeros", [128, T], F32)
    b_zeros = Buf()
    sml = [b.sb(f"sml{i}", [128, 12], F32) for i in range(6)]
    b_sml = [Buf() for i in range(6)]

    b.dma_in(M.memT[:, :, :], memT_d, [M.b_mem])
    if not from_xn:
        b.dma_in(xh[:, :, :], xh_d, [b_xh])
    P.op(P.pool, lambda e: e.memset(zeros[:, :], 0.0), writes=(b_zeros,))

    st_mem_rstd(b, M.memT, M.b_mem, M.rstd_m, M.b_rstdm)
    st_memkv(b, M.memT, M.b_mem, M.rstd_m, M.b_rstdm, M.memn, M.b_memn, M.memk, M.b_memk, M.memv, M.b_memv, 16, 33, descs=memkv_tiles(l))

    if from_xn:
        b.dma_in(xnh[:, :, :], xnh_d, [b_xnh])
        for k in list(range(12, KC)) + list(range(12)):
            extra = (M.b_mem,) if k < 8 else ((M.b_memn,) if k < 12 else ())
            b.dma_in(xn[:, k, :], xn_d[:, k, :], [b_xn[k][0], b_xn[k][1]] + list(extra))
    else:
        pss = [b.get("ps"), b.get("ps")]
        for k in range(KC):
            for t in range(2):
                xt, bxt = b.get("xt")
                b.dma_in(xt[:, :], xT_d[:, k, t * 512:(t + 1) * 512], [bxt])
                sq, bsq = b.get("sq")
                b.act(sq[:, :], xt[:, :], AF.Square, reads=(bxt,), writes=(bsq,))

                def f(e, k=k, sq=sq, ps=pss[t][0]):
                    return e.matmul(ps[:, :], b.ones[:, :], sq[:, :], start=(k == 0), stop=(k == KC - 1))
                P.op(P.pe, f, reads=(bsq, b.b_ones), writes=(pss[t][1],))
        for t in range(2):
            b.rstd_from_ss(pss[t][0][:, :], pss[t][1], 1.0 / D, rstd_x[t][:, :], b_rstdx[t])
        psh, bpsh = b.get("ps")
        for k in range(KC):
            sq, bsq = b.get("sq")
            b.act(sq[:, :4], xh[:, k, :], AF.Square, reads=(b_xh,), writes=(bsq,))

            def f(e, k=k, sq=sq, psh=psh):
                return e.matmul(psh[:, :4], b.ones[:, :], sq[:, :4], start=(k == 0), stop=(k == KC - 1))
            P.op(P.pe, f, reads=(bsq, b.b_ones), writes=(bpsh,))
        b.rstd_from_ss(psh[:, :4], bpsh, 1.0 / D, rstd_h[:, :], b_rstdh, ncols=4)
        for k in range(KC):
            b.stt(xnh[:, k, :], xh[:, k, :], b.pcol(k), rstd_h[:, :], ALU.mult, ALU.mult,
                  reads=(b_xh, b_rstdh, b.b_par), writes=(b_xnh,))
        for k in range(KC):
            for t in range(2):
                tsl = slice(t * 512, (t + 1) * 512)
                xt, bxt = b.get("xt")
                b.dma_in(xt[:, :], xT_d[:, k, tsl], [bxt])
                extra = (M.b_mem,) if k < 8 else ((M.b_memn,) if k < 12 else ())
                b.stt(xn[:, k, tsl], xt[:, :], b.pcol(k), rstd_x[t][:, :], ALU.mult, ALU.mult,
                      reads=(bxt, b_rstdx[t], b.b_par), writes=(b_xn[k][t],) + extra)

    pipe = Pipe()
    mem_cb = {}

    def mem_iter(j):
        h, t = j // 2, j % 2
        wt, bw = b.next_w(wt_std("a_w_in", l, 0, 16, [(3072 + h * 128, 128)]))
        wv = wt[:, :16 * 128].rearrange("p (k c) -> p k c", k=16)
        if t == 0:
            mem_cb[h] = b.get("cbm")
        cbt, bcb = mem_cb[h]
        tsl = slice(t * 512, (t + 1) * 512)
        ps, bps = b.get("ps")
        b.mm(ps[:, :], bps, [(wv[:, k, 0:128], xn[:, k, tsl]) for k in range(KC)],
             reads=[bw] + [b_xn[k][t] for k in range(KC)])
        after = None
        if t == 1:
            after = (lambda h=h, cbt=cbt, bcb=bcb: b.dma_out(cat_d[:, 12 + h, :], cbt[:, :], rbufs=(bcb,)))
        pipe.push(mem_attn_stages2(b, ps[:, :], bps, M.memk, M.b_memk, M.memv, M.b_memv, h, 32, cbt[:, tsl], bcb, after=after))

    wt, bw = b.next_w(("gates", l))
    b.copy(gw[:, :, :], wt[:, :24 * 128].rearrange("p (g d) -> p g d", g=24), reads=(bw,), writes=(b_gw,), eng=P.pool)

    lam = b.par[:, 124:136]
    s0, s1, s2, s3, s4, s5 = sml
    B0, B1, B2, B3, B4, B5 = b_sml
    b.ts(s0[:, :], lam, -1.0, None, ALU.mult, None, reads=(b.b_par,), writes=(B0,))
    b.tt(s0[:, :], s0[:, :], lam, ALU.max, reads=(B0, b.b_par), writes=(B0,))
    b.act(s1[:, :], s0[:, :], AF.Exp, reads=(B0,), writes=(B1,), scale=-1.0)
    b.ts(s2[:, :], s1[:, :], 2.0, None, ALU.add, None, reads=(B1,), writes=(B2,))
    b.recip(s3[:, :], s2[:, :], reads=(B2,), writes=(B3,))
    b.tt(s2[:, :], s1[:, :], s3[:, :], ALU.mult, reads=(B1, B3), writes=(B2,))
    b.tt(s3[:, :], s2[:, :], s2[:, :], ALU.mult, reads=(B2,), writes=(B3,))
    b.ts(s4[:, :], s3[:, :], 1.0 / 11, 1.0 / 9, ALU.mult, ALU.add, reads=(B3,), writes=(B4,))
    for cst in (1.0 / 7, 1.0 / 5, 1.0 / 3, 1.0):
        b.tt(s4[:, :], s4[:, :], s3[:, :], ALU.mult, reads=(B4, B3), writes=(B4,))
        b.ts(s4[:, :], s4[:, :], cst, None, ALU.add, None, reads=(B4,), writes=(B4,))
    b.tt(s4[:, :], s4[:, :], s2[:, :], ALU.mult, reads=(B4, B2), writes=(B4,))
    b.ts(s5[:, :], lam, -1.0, 0.0, ALU.mult, ALU.max, reads=(b.b_par,), writes=(B5,))
    b.stt(s5[:, :], s4[:, :], 2.0, s5[:, :], ALU.mult, ALU.add, reads=(B4, B5), writes=(B5,))
    b.ts(nsp[:, :], s5[:, :], -8.0, None, ALU.mult, None, reads=(B5,), writes=(b_nsp,))

    stt_ = {}

    def S1a(n):
        wt, bw = b.next_w(wt_std("a_w_in", l, 0, 16, [(n * 128, 128), (1536 + n * 128, 128)]))
        wv = wt[:, :].rearrange("p (k c) -> p k c", k=16)
        ub, bub = b.get("ub")
        gb, bgb = b.get("gb")
        psh, bpsh = b.get("ps")
        b.mm(psh[:, :4], bpsh, [(wv[:, k, 0:128], xnh[:, k, :]) for k in range(KC)], reads=(bw, b_xnh))
        b.copy(ub[:, 0:4], psh[:, :4], reads=(bpsh,), writes=(bub,))
        stt_[n] = dict(ub=ub, bub=bub, gb=gb, bgb=bgb, wv=wv, bw=bw)
        S1t(n, 0)

    def S1t(n, t):
        d_ = stt_[n]
        wv, bw, ub, bub, gb, bgb = d_["wv"], d_["bw"], d_["ub"], d_["bub"], d_["gb"], d_["bgb"]
        tsl = slice(t * 512, (t + 1) * 512)
        ps, bps = b.get("ps")
        b.mm(ps[:, :], bps, [(wv[:, k, 0:128], xn[:, k, tsl]) for k in range(KC)],
             reads=[bw] + [b_xn[k][t] for k in range(KC)])
        b.copy(ub[:, 4 + t * 512:4 + (t + 1) * 512], ps[:, :], reads=(bps,), writes=(bub,))
        pg, bpg = b.get("ps")
        b.mm(pg[:, :], bpg, [(wv[:, k, 128:256], xn[:, k, tsl]) for k in range(KC)],
             reads=[bw] + [b_xn[k][t] for k in range(KC)])
        b.act(gb[:, tsl], pg[:, :], AF.Gelu_apprx_tanh, reads=(bpg,), writes=(bgb,))

    def S1b(n):
        S1t(n, 1)

    def S2a(n):
        d_ = stt_[n]
        ub, bub = d_["ub"], d_["bub"]
        xc, bxc = b.get("xc")
        cw = 40 + n * 4
        b.ts(xc[:, :], ub[:, 1:1 + T], b.pcol(cw), b.pcol(88 + n), ALU.mult, ALU.add, reads=(bub, b.b_par), writes=(bxc,), eng=P.pool)
        for j in range(1, 4):
            b.stt(xc[:, :], ub[:, 1 + j:1 + j + T], b.pcol(cw + j), xc[:, :], ALU.mult, ALU.add,
                  reads=(bub, b.b_par, bxc), writes=(bxc,))
        d_.update(xc=xc, bxc=bxc)

    def S2b(n):
        d_ = stt_[n]
        xc, bxc = d_["xc"], d_["bxc"]
        xcb = [b.get("tb"), b.get("tb")]
        for t in range(2):
            b.copy(xcb[t][0][:, :], xc[:, t * 512:(t + 1) * 512], reads=(bxc,), writes=(xcb[t][1],))
        rb, brb = b.get("rb")
        ib, bib = b.get("ib")
        for t in range(2):
            tsl = slice(t * 512, (t + 1) * 512)
            pr, bpr = b.get("ps")
            b.mm(pr[:, :], bpr, [(gw[:, n, :], xcb[t][0][:, :])], reads=(b_gw, xcb[t][1]))
            b.act(rb[:, tsl], pr[:, :], AF.Sigmoid, reads=(bpr, b.b_par), writes=(brb,), bias=b.pcol(100 + n))
            pi_, bpi = b.get("ps")
            b.mm(pi_[:, :], bpi, [(gw[:, 12 + n, :], xcb[t][0][:, :])], reads=(b_gw, xcb[t][1]))
            b.act(ib[:, tsl], pi_[:, :], AF.Sigmoid, reads=(bpi, b.b_par), writes=(bib,), bias=b.pcol(112 + n))
        d_.update(rb=rb, brb=brb, ib=ib, bib=bib)

    def S3a(n):
        d_ = stt_[n]
        xc, bxc, ab, bab, ib, bib = d_["xc"], d_["bxc"], d_["rb"], d_["brb"], d_["ib"], d_["bib"]
        b.act(ab[:, :], ab[:, :], AF.Exp, reads=(bab, b_nsp), writes=(bab,), scale=nsp[:, n:n + 1])
        bb_, bbb = b.get("s3")
        b.act(bb_[:, :], ab[:, :], AF.Square, reads=(bab,), writes=(bbb,))
        b.act(bb_[:, :], bb_[:, :], AF.Sqrt, reads=(bbb, b.b_par), writes=(bbb,), scale=-1.0, bias=b.pcol(254))
        b.tt(ib[:, :], ib[:, :], xc[:, :], ALU.mult, reads=(bib, bxc), writes=(bib,), eng=P.pool)
        d_.update(bb_=bb_, bbb=bbb)

    def S3b(n):
        d_ = stt_.pop(n)
        gb, bgb, ab, bab, ib, bib, bb_, bbb = d_["gb"], d_["bgb"], d_["rb"], d_["brb"], d_["ib"], d_["bib"], d_["bb_"], d_["bbb"]
        b.tt(bb_[:, :], bb_[:, :], ib[:, :], ALU.mult, reads=(bbb, bib), writes=(bbb,))
        hb, bhb = b.get("s3")
        P.op(P.dve, lambda e, hb=hb, ab=ab, bb_=bb_: e.tensor_tensor_scan(out=hb[:, :], data0=ab[:, :], data1=bb_[:, :], initial=0.0,
                                                                            op0=ALU.mult, op1=ALU.add),
             reads=(bab, bbb), writes=(bhb,))
        Ab, bAb = b.get("s3")
        P.op(P.dve, lambda e, Ab=Ab, ab=ab: e.tensor_tensor_scan(out=Ab[:, :], data0=ab[:, :], data1=zeros[:, :], initial=1.0,
                                                                  op0=ALU.mult, op1=ALU.add),
             reads=(bab, b_zeros), writes=(bAb,))
        b.copy(carry[:, n:n + 1], hb[:, T - 1:T], reads=(bhb,), writes=(b_carry,), eng=P.dve)
        b.copy(carry[:, 12 + n:13 + n], Ab[:, T - 1:T], reads=(bAb,), writes=(b_carry,), eng=P.dve)
        cbt, bcb = b.get("cb")
        b.tt(cbt[:, :], hb[:, :], gb[:, :], ALU.mult, reads=(bhb, bgb), writes=(bcb,), eng=P.pool)
        b.dma_out(cat_d[:, n, :], cbt[:, :], rbufs=(bcb,))
        cbq, bcq = b.get("cb")
        b.tt(cbq[:, :], Ab[:, :], gb[:, :], ALU.mult, reads=(bAb, bgb), writes=(bcq,), eng=P.pool)
        b.dma_out(q_d[:, n, :], cbq[:, :], rbufs=(bcq,))

    for s_ in range(12 + 2):
        if s_ < 8:
            mem_iter(s_)
        else:
            pipe.tick()
        if 0 <= s_ - 1 < 12:
            S2a(s_ - 1)
        if 0 <= s_ - 2 < 12:
            S3a(s_ - 2)
        if s_ < 12:
            S1a(s_)
        if 0 <= s_ - 1 < 12:
            S2b(s_ - 1)
        if 0 <= s_ - 2 < 12:
            S3b(s_ - 2)
        if s_ < 12:
            S1b(s_)
    pipe.flush()

    b.dma_out(car_d, carry[:, :], rbufs=(b_carry,))
    return b.finish(), b.wtiles


def tiles_A2(l, with_kv):
    tl = [wt_std("a_w_out", l, 0, 16, [(c, 256)]) for c in range(0, 2048, 256)]
    tl += mlp_tiles(l)
    if with_kv:
        tl += [wt_std("kv_w", None, 0, 16, [(c, 256)]) for c in range(0, 3072, 256)]
    return tl


def build_A2(l, with_kv, with_next=True):
    b = Bld(tiles_A2(l, with_kv), n_ps=6)
    b.pool("rstdp", 4, [128, 512], F32)
    stats = Stats(b)
    nc, P = b.nc, b.P
    xT_d = b.din("xT", [128, KC, T])
    cat_d = b.din("cat", [128, 16, T], BF16)
    q_d = b.din("qq", [128, 12, T], BF16)
    car_d = b.din("carr", [128, 8, 24])
    sel_d = b.din("sel", [128, 8])
    yT_d = b.dout("yT", [128, KC, T])
    xs = b.sb("xs", [128, KC, T], F32)
    b_xs = [[Buf() for t in range(2)] for k in range(KC)]
    cat = b.sb("cat", [128, 16, T], BF16)
    b_cat = [[Buf() for t in range(2)] for k in range(16)]
    h1 = b.sb("h1", [128, 16, T], BF16)
    b_h1 = [[Buf() for t in range(2)] for k in range(16)]
    carr = b.sb("carr", [128, 8, 24], F32)
    b_carr = Buf()
    sel = b.sb("sel", [128, 8], F32)
    b_sel = Buf()
    cst = [b.sb(f"cst{i}", [128, 12], F32) for i in range(2)]
    b_cst = [Buf(), Buf()]
    hin = b.sb("hin", [128, 12], F32)
    b_hin = Buf()
    tmp12 = b.sb("tmp12", [128, 12], F32)
    b_tmp12 = Buf()

    b.dma_in(carr[:, :, :], car_d, [b_carr])
    b.dma_in(sel[:, :], sel_d, [b_sel])
    for k in range(16):
        if k < 12:
            b.dma_in(h1[:, k, :], q_d[:, k, :], [b_h1[k][0], b_h1[k][1]])
        b.dma_in(cat[:, k, :], cat_d[:, k, :], [b_cat[k][0], b_cat[k][1]])
    for k in range(KC):
        b.dma_in(xs[:, k, :], xT_d[:, k, :], [b_xs[k][0], b_xs[k][1]])

    P.op(P.pool, lambda e: e.memset(cst[0][:, :], 0.0), writes=(b_cst[0],))
    P.op(P.pool, lambda e: e.memset(hin[:, :], 0.0), writes=(b_hin,))
    for r in range(8):
        cur, bcur = cst[r % 2], b_cst[r % 2]
        nx, bnx = cst[(r + 1) % 2], b_cst[(r + 1) % 2]
        b.stt(hin[:, :], cur[:, :], sel[:, r:r + 1], hin[:, :], ALU.mult, ALU.add, reads=(bcur, b_sel, b_hin), writes=(b_hin,))
        if r < 7:
            b.tt(tmp12[:, :], carr[:, r, 12:24], cur[:, :], ALU.mult, reads=(b_carr, bcur), writes=(b_tmp12,))
            b.tt(nx[:, :], tmp12[:, :], carr[:, r, 0:12], ALU.add, reads=(b_tmp12, b_carr), writes=(bnx,))
    for n in range(12):
        for t in range(2):
            tsl = slice(t * 512, (t + 1) * 512)
            b.stt(cat[:, n, tsl], h1[:, n, tsl], hin[:, n:n + 1], cat[:, n, tsl], ALU.mult, ALU.add,
                  reads=(b_h1[n][t], b_hin, b_cat[n][t]), writes=(b_cat[n][t],))
    for cg in range(8):
        wt, bw = b.next_w()
        wv = wt[:, :].rearrange("p (k c) -> p k c", k=16)
        for half in range(2):
            dc = cg * 2 + half
            for t in range(2):
                tsl = slice(t * 512, (t + 1) * 512)
                ps, bps = b.get("ps")
                b.mm(ps[:, :], bps, [(wv[:, k, half * 128:(half + 1) * 128], cat[:, k, tsl]) for k in range(16)],
                     reads=[bw] + [b_cat[k][t] for k in range(16)])
                b.tt(xs[:, dc, tsl], ps[:, :], xs[:, dc, tsl], ALU.add, reads=(bps, b_xs[dc][t]), writes=(b_xs[dc][t],))
                stats.add(xs[:, dc, tsl], b_xs[dc][t], t)
    need_out = with_next or with_kv
    st_mlp(b, xs, b_xs, cat, b_cat, h1, b_h1, 0, out_d=yT_d, in_rstd=stats.rstd(), out_stats=(stats if need_out else None))
    if need_out:
        rs_out = stats.rstd()
    if with_next:
        xnn_d = b.dout("xnn", [128, KC, T], BF16)
        st_apply_norm(b, xs, b_xs, cat, b_cat, 40, rs_out)
        for k in range(KC):
            b.dma_out(xnn_d[:, k, :], cat[:, k, :], rbufs=(b_cat[k][0], b_cat[k][1]))
    if with_kv:
        kT_d = b.dout("kT", [128, 12, T], BF16)
        vT_d = b.dout("vT", [128, 12, T], BF16)
        st_apply_norm(b, xs, b_xs, cat, b_cat, 16, rs_out)
        kpipe = Pipe()
        for cg in range(12):
            wt, bw = b.next_w()
            wv = wt[:, :].rearrange("p (k c) -> p k c", k=16)
            for half in range(2):
                hc = cg * 2 + half
                for t in range(2):
                    tsl = slice(t * 512, (t + 1) * 512)
                    ps, bps = b.get("ps")
                    b.mm(ps[:, :], bps, [(wv[:, k, half * 128:(half + 1) * 128], cat[:, k, tsl]) for k in range(16)],
                         reads=[bw] + [b_cat[k][t] for k in range(16)])
                    if hc < 12:
                        stq = {}

                        def KA(ps=ps, bps=bps, stq=stq):
                            sq, bsq = b.get("sq")
                            b.act(sq[:, :], ps[:, :], AF.Square, reads=(bps,), writes=(bsq,))
                            ps2, bps2 = b.get("ps")
                            b.mm(ps2[:, :], bps2, [(b.ones[:, :], sq[:, :])], reads=(bsq, b.b_ones))
                            stq.update(ps2=ps2, bps2=bps2)

                        def KB(ps=ps, bps=bps, stq=stq, hc=hc, t=t, tsl=tsl):
                            rs, brs = b.get("rstd")
                            b.rstd_from_ss(stq["ps2"][:, :], stq["bps2"], 1.0 / 128, rs[:, :], brs)
                            b.stt(h1[:, hc, tsl], ps[:, :], b.pcol(32 + hc // 4), rs[:, :], ALU.mult, ALU.mult,
                                  reads=(bps, brs, b.b_par), writes=(b_h1[hc][t],))
                        kpipe.push([KA, KB])
                    else:
                        b.copy(h1[:, hc - 12, tsl], ps[:, :], reads=(bps,), writes=(b_h1[hc - 12][t],))
            if cg == 5:
                kpipe.flush()
                for k in range(12):
                    b.dma_out(kT_d[:, k, :], h1[:, k, :], rbufs=(b_h1[k][0], b_h1[k][1]))
        for k in range(12):
            b.dma_out(vT_d[:, k, :], h1[:, k, :], rbufs=(b_h1[k][0], b_h1[k][1]))
    return b.finish(), b.wtiles


def fm(x):
    n, f = x.shape
    return np.ascontiguousarray(x.T.reshape(f // 128, 128, n).transpose(1, 0, 2))


def unfm(xT):
    p, k, n = xT.shape
    return np.ascontiguousarray(xT.transpose(1, 0, 2).reshape(k * p, n).T)


def colk(v):
    return np.ascontiguousarray(np.asarray(v, np.float32).reshape(-1, 128).T)


_CACHE = {}
_TIMES = []


def _run(nc, ins, tag=""):
    import os
    if os.environ.get("KTRACE"):
        res = run_bass_kernel_spmd(nc, ins, core_ids=list(range(NCORES)), trace=True)
        _TIMES.append((tag, res.exec_time_ns))
        print("KTRACE", tag, res.exec_time_ns, flush=True)
    else:
        res = run_bass_kernel_spmd(nc, ins, core_ids=list(range(NCORES)))
    return res.results


def _launch(key, builder, in_maps):
    if key not in _CACHE:
        _CACHE[key] = builder()
    nc, tiles = _CACHE[key]
    res = run_bass_kernel_spmd(nc, in_maps, core_ids=list(range(NCORES)))
    return res.results


def run_A_layer(l, xT, memT, W, with_kv, xn_in=None, next_g=None):
    f32 = np.float32
    par = np.zeros((128, NPAR), f32)
    par[:, 0:16] = colk(W["norm_mix_g"][l])
    par[:, 16:32] = colk(W["mem_norm_g"][l])
    par[:, 32] = W["mem_q_norm_g"][l]
    par[:, 33] = W["mem_k_norm_g"][l]
    cw = W["a_conv_w"][l]
    for n in range(12):
        for j in range(4):
            par[:, 40 + n * 4 + j] = cw[j, n * 128:(n + 1) * 128]
    par[:, 88:100] = colk(W["a_conv_b"][l])
    par[:, 100:112] = np.asarray(W["a_gate_r_b"][l], f32).T
    par[:, 112:124] = np.asarray(W["a_gate_i_b"][l], f32).T
    par[:, 124:136] = colk(W["a_lambda"][l])
    par[:, 254] = 1.0
    par[:, 255] = EPS
    nc1 = _CACHE.get(("A1", l))
    if nc1 is None:
        nc1 = _CACHE[("A1", l)] = build_A1(l, from_xn=(xn_in is not None))
    wst = pack_weights(nc1[1], W)
    ins = []
    for c in range(NCORES):
        if xn_in is not None:
            xnh = np.zeros((128, KC, 4), ml_dtypes.bfloat16)
            if c > 0:
                xnh[:, :, :] = np.asarray(xn_in[c - 1])[:, :, T - 4:]
            ins.append({"xn": xn_in[c], "xnh": xnh, "memT": memT, "wst": wst, "par": par})
        else:
            xh = np.zeros((128, KC, 4), f32)
            if c > 0:
                xh[:, :, :] = xT[c - 1][:, :, T - 4:]
            ins.append({"xT": xT[c], "xh": xh, "memT": memT, "wst": wst, "par": par})
    r1 = _run(nc1[0], ins, f"A1_{l}")
    par2 = np.zeros((128, NPAR), f32)
    par2[:, 0:16] = colk(W["norm_mlp_g"][l])
    if with_kv:
        par2[:, 16:32] = colk(W["kv_norm_g"])
        par2[:, 32:35] = np.asarray(W["k_norm_g"], f32).T
    par2[:, 255] = EPS
    if next_g is not None:
        par2[:, 40:56] = colk(next_g)
    nc2 = _CACHE.get(("A2", l))
    if nc2 is None:
        nc2 = _CACHE[("A2", l)] = build_A2(l, with_kv, with_next=(next_g is not None))
    wst2 = pack_weights(nc2[1], W)
    carr = np.ascontiguousarray(np.stack([r1[c]["carry"] for c in range(NCORES)], axis=1))
    ins2 = []
    for c in range(NCORES):
        sel = np.zeros((128, 8), f32)
        sel[:, c] = 1.0
        ins2.append({"xT": xT[c], "cat": r1[c]["cat"], "qq": r1[c]["qq"], "carr": carr, "sel": sel, "wst": wst2, "par": par2})
    r2 = _run(nc2[0], ins2, f"A2_{l}")
    out = [r2[c]["yT"] for c in range(NCORES)]
    xnn = [r2[c]["xnn"] for c in range(NCORES)] if next_g is not None else None
    if with_kv:
        return out, [r2[c]["kT"] for c in range(NCORES)], [r2[c]["vT"] for c in range(NCORES)], r1, xnn
    return out, None, None, r1, xnn


LG = [T // d for d in DIL]
NQ = [min(128, lg) for lg in LG]
NKC = [128 + lg for lg in LG]
NBC = [(n + 127) // 128 for n in NKC]
NK = [d * n for d, n in zip(DIL, NKC)]
NB = [d * n for d, n in zip(DIL, NBC)]


def tiles_B1(l):
    tl = list(memkv_tiles(l))
    tl += [wt_std("b_w_q", l - 2, 0, 16, [(c, 256)]) for c in range(0, 2048, 256)]
    return tl


def build_B1(l, from_xn=True):
    b = Bld(tiles_B1(l))
    nc, P = b.nc, b.P
    if from_xn:
        xn_d = b.din("xn", [128, KC, T], BF16)
    else:
        xT_d = b.din("xT", [128, KC, T])
    memT_d = b.din("memT", [128, KC, 256])
    kt_d = [b.din(f"kt{g}", [128, 4, NK[g]], BF16) for g in range(3)]
    vv_d = [b.din(f"vv{g}", [128, 4, NB[g], 128], BF16) for g in range(3)]
    oh_d = b.din("oh", [33, 6, 256])
    jm_d = b.din("jm", [128, 128])
    relb_d = b.din("relb", [33, 12])
    kval_d = b.din("kval", [128, 3])
    cat_d = b.dout("cat", [128, 8, T], BF16)
    vec_d = nc.dram_tensor("vecd", [6, 4, 256], F32).ap()

    if not from_xn:
        b.pool("xt", 4, [128, 512], F32)
    b.pool("cb", 4, [128, T], BF16)
    b.pool("pp", 6, [128, 512], BF16)
    b.pool("kt", 2, [128, max(NK)], BF16)
    b.pool("vv", 2, [128, max(NB), 128], BF16)
    xn = b.sb("xn", [128, KC, T], BF16)
    b_xn = [[Buf() for t in range(2)] for k in range(KC)]
    qn = b.sb("qn", [128, 12, T], BF16)
    b_qn = [Buf() for k in range(12)]
    M = MemState(b, alias=qn)
    rstd_x = [b.sb(f"rstd_x{t}", [128, 512], F32) for t in range(2)] if not from_xn else None
    b_rstdx = [Buf(), Buf()]
    accN = b.sb("accN", [128, T], F32)
    accD = b.sb("accD", [128, T], F32)
    b_acc = Buf()
    relb = b.sb("relb", [33, 12], F32)
    b_relb = Buf()
    jm = b.sb("jm", [128, 128], F32)
    b_jm = Buf()
    kval = b.sb("kval", [128, 3], F32)
    b_kval = Buf()
    b.pool("vec", 2, [4, 256], F32)
    b.pool("hk", 2, [128, 512], F32)
    masks = [[b.sb(f"mask{g}_{ty}", [128, 4, 128], F32) for ty in range(3)] for g in range(3)]
    b_masks = [[Buf() for ty in range(3)] for g in range(3)]

    b.dma_in(M.memT, memT_d, [M.b_mem])
    b.dma_in(relb[:, :], relb_d, [b_relb])
    b.dma_in(jm[:, :], jm_d, [b_jm])
    b.dma_in(kval[:, :], kval_d, [b_kval])

    mask_state = {}

    def mask_A(g, ty):
        oh, boh = b.get("tf")
        b.dma_in(oh[:33, :256], oh_d[:, g * 2 + ty, :], [boh])
        ps, bps = b.get("ps")
        b.mm(ps[:4, :256], bps, [(relb[:33, g * 4:(g + 1) * 4], oh[:33, :256])], reads=(b_relb, boh))
        vec, b_vec = b.get("vec")
        b.act(vec[:, :], ps[:4, :256], AF.Exp, reads=(bps,), writes=(b_vec,))
        b_vd = Buf()
        P.dma(P.sp, (lambda e, s, g=g, ty=ty, vec=vec: e.dma_start(out=vec_d[g * 2 + ty], in_=vec[:, :]).then_inc(s, 16)),
              reads=(b_vec,), writes=(b_vd,))
        hk, bhk = b.get("hk")
        src = bass.AP(tensor=vec_d.tensor, offset=(g * 2 + ty) * 1024, ap=[[1, 128], [256, 4], [1, 128]])
        b.dma_in(hk[:, :].rearrange("p (h q) -> p h q", h=4), src, [bhk], rbufs=(b_vd,))
        mask_state[(g, ty)] = (hk, bhk)

    def mask_B(g, ty):
        hk, bhk = mask_state[(g, ty)]
        ps2, bps2 = b.get("ps")
        b.mm(ps2[:, :], bps2, [(jm[:, :], hk[:, :])], reads=(b_jm, bhk))
        b.copy(masks[g][ty][:, :, :], ps2[:, :].rearrange("p (h q) -> p h q", h=4), reads=(bps2,), writes=(b_masks[g][ty],))
        if ty == 1:
            b.ts(masks[g][2][:, :, :], masks[g][1][:, :, :], kval[:, g:g + 1], None, ALU.mult, None,
                 reads=(b_masks[g][1], b_kval), writes=(b_masks[g][2],))

    if from_xn:
        for k in range(KC):
            b.dma_in(xn[:, k, :], xn_d[:, k, :], [b_xn[k][0], b_xn[k][1]])
        st_mem_rstd(b, M.memT, M.b_mem, M.rstd_m, M.b_rstdm)
        st_memkv(b, M.memT, M.b_mem, M.rstd_m, M.b_rstdm, M.memn, M.b_memn, M.memk, M.b_memk, M.memv, M.b_memv, 16, 33)
    else:
        pss = [b.get("ps"), b.get("ps")]
        for k in range(KC):
            for t in range(2):
                xt, bxt = b.get("xt")
                b.dma_in(xt[:, :], xT_d[:, k, t * 512:(t + 1) * 512], [bxt])
                sq, bsq = b.get("sq")
                b.act(sq[:, :], xt[:, :], AF.Square, reads=(bxt,), writes=(bsq,))

                def f(e, k=k, sq=sq, ps=pss[t][0]):
                    return e.matmul(ps[:, :], b.ones[:, :], sq[:, :], start=(k == 0), stop=(k == KC - 1))
                P.op(P.pe, f, reads=(bsq, b.b_ones), writes=(pss[t][1],))
        for t in range(2):
            b.rstd_from_ss(pss[t][0][:, :], pss[t][1], 1.0 / D, rstd_x[t][:, :], b_rstdx[t])
        st_mem_rstd(b, M.memT, M.b_mem, M.rstd_m, M.b_rstdm)
        st_memkv(b, M.memT, M.b_mem, M.rstd_m, M.b_rstdm, M.memn, M.b_memn, M.memk, M.b_memk, M.memv, M.b_memv, 16, 33)
        for k in range(KC):
            for t in range(2):
                tsl = slice(t * 512, (t + 1) * 512)
                xt, bxt = b.get("xt")
                b.dma_in(xt[:, :], xT_d[:, k, tsl], [bxt])
                b.stt(xn[:, k, tsl], xt[:, :], b.pcol(k), rstd_x[t][:, :], ALU.mult, ALU.mult,
                      reads=(bxt, b_rstdx[t], b.b_par), writes=(b_xn[k][t],))

    qpipe = Pipe()
    for cg in range(8):
        if cg < 6:
            mask_A(cg // 2, cg % 2)
        if 1 <= cg < 7:
            mask_B((cg - 1) // 2, (cg - 1) % 2)
        wt, bw = b.next_w()
        wv = wt[:, :].rearrange("p (k c) -> p k c", k=16)
        for half in range(2):
            hq = cg * 2 + half
            cbt = None
            if hq >= 12:
                cbt, bcb = b.get("cb")
            for t in range(2):
                tsl = slice(t * 512, (t + 1) * 512)
                ps, bps = b.get("ps")
                b.mm(ps[:, :], bps, [(wv[:, k, half * 128:(half + 1) * 128], xn[:, k, tsl]) for k in range(KC)],
                     reads=[bw] + [b_xn[k][t] for k in range(KC)])
                if hq >= 12:
                    after = None
                    if t == 1:
                        after = (lambda hq=hq, cbt=cbt, bcb=bcb: b.dma_out(cat_d[:, 4 + hq - 12, :], cbt[:, :], rbufs=(bcb,)))
                    qpipe.push(mem_attn_stages(b, ps[:, :], bps, M.memk, M.b_memk, M.memv, M.b_memv, hq - 12, 32, cbt[:, tsl], bcb,
                                               after=after))
                else:
                    stq = {}

                    def QA(ps=ps, bps=bps, stq=stq):
                        sq, bsq = b.get("sq")
                        b.act(sq[:, :], ps[:, :], AF.Square, reads=(bps,), writes=(bsq,))
                        ps2, bps2 = b.get("ps")
                        b.mm(ps2[:, :], bps2, [(b.ones[:, :], sq[:, :])], reads=(bsq, b.b_ones))
                        stq.update(ps2=ps2, bps2=bps2)

                    def QB(ps=ps, bps=bps, stq=stq, hq=hq, t=t):
                        g = hq // 4
                        d = DIL[g]
                        rs, brs = b.get("rstd")
                        b.rstd_from_ss(stq["ps2"][:, :], stq["bps2"], 1.0 / 128, rs[:, :], brs)
                        nl = 512 // d
                        dst = qn[:, hq, :].rearrange("p (r l) -> p l r", r=d)[:, t * nl:(t + 1) * nl, :]
                        extra = (M.b_mem,) if hq < 8 else (M.b_memn,)
                        b.stt(dst, ps[:, :].rearrange("p (l r) -> p l r", r=d), b.pcol(34 + g), rs[:, :].rearrange("p (l r) -> p l r", r=d),
                              ALU.mult, ALU.mult, reads=(bps, brs, b.b_par), writes=(b_qn[hq],) + extra)
                    qpipe.push([QA, QB])
    qpipe.flush()

    batches = []
    for h in range(4):
        for g in range(3):
            d, lg, nq = DIL[g], LG[g], NQ[g]
            upc = lg // nq
            nunits = d * upc
            U = 512 // nq
            for u0 in range(0, nunits, U):
                batches.append(dict(h=h, g=g, u0=u0, first=(u0 == 0), last=(g == 2 and u0 + U >= nunits)))
    cur_kv = {}

    def T1(bt):
        h, g, u0 = bt["h"], bt["g"], bt["u0"]
        d, lg, nq, nkc, nbc = DIL[g], LG[g], NQ[g], NKC[g], NBC[g]
        if bt["first"]:
            kt, bkt = b.get("kt")
            vv, bvv = b.get("vv")
            b.dma_in(kt[:, :NK[g]], kt_d[g][:, h, :], [bkt])
            b.dma_in(vv[:, :NB[g], :], vv_d[g][:, h, :, :], [bvv])
            cur_kv[(h, g)] = (kt, bkt, vv, bvv)
        kt, bkt, vv, bvv = cur_kv[(h, g)]
        hq = g * 4 + h
        upc = lg // nq
        U = 512 // nq
        psP, bpsP = b.get("ps")
        psD, bpsD = b.get("ps")
        units = []
        for ui in range(U):
            u = u0 + ui
            r, j = u // upc, u % upc
            qs = qn[:, hq, r * lg + j * nq: r * lg + (j + 1) * nq]
            kp = kt[:, r * nkc + j * nq: r * nkc + j * nq + 128]
            kd = kt[:, r * nkc + 128 + j * nq: r * nkc + 128 + (j + 1) * nq]
            units.append((r, j, qs, kp, kd))

        def fS(e, units=units, psP=psP, psD=psD, nq=nq):
            for ui, (r, j, qs, kp, kd) in enumerate(units):
                e.matmul(psP[:, ui * nq:(ui + 1) * nq], kp, qs, start=True, stop=True)
                ins = e.matmul(psD[:nq, ui * nq:(ui + 1) * nq], kd, qs, start=True, stop=True)
            return ins
        P.op(P.pe, fS, reads=(bkt, b_qn[hq]), writes=(bpsP, bpsD))
        bt.update(units=units, psP=psP, bpsP=bpsP, psD=psD, bpsD=bpsD, vv=vv, bvv=bvv)

    def T2(bt):
        h, g = bt["h"], bt["g"]
        nq = NQ[g]
        eP, beP = b.get("tf")
        eD, beD = b.get("tf")
        b.act(eP[:, :], bt["psP"][:, :], AF.Exp, reads=(bt["bpsP"],), writes=(beP,), scale=SCALE)
        b.act(eD[:nq, :], bt["psD"][:nq, :], AF.Exp, reads=(bt["bpsD"],), writes=(beD,), scale=SCALE)
        pP, bpP = b.get("pp")
        pD, bpD = b.get("pp")
        for ui, (r, j, qs, kp, kd) in enumerate(bt["units"]):
            csl = slice(ui * nq, (ui + 1) * nq)
            mty = 2 if j == 0 else 1
            b.tt(pP[:, csl], eP[:, csl], masks[g][mty][:, h, :nq], ALU.mult, reads=(beP, b_masks[g][mty]), writes=(bpP,))
            b.tt(pD[:nq, csl], eD[:nq, csl], masks[g][0][:nq, h, :nq], ALU.mult, reads=(beD, b_masks[g][0]), writes=(bpD,),
                 eng=P.pool)
        bt.update(pP=pP, bpP=bpP, pD=pD, bpD=bpD)

    def T3(bt):
        g = bt["g"]
        nq, nbc = NQ[g], NBC[g]
        psN, bpsN = b.get("ps")
        psS, bpsS = b.get("ps")
        pP, pD, vv = bt["pP"], bt["pD"], bt["vv"]

        def fV(e, units=bt["units"], psN=psN, psS=psS, nq=nq, pP=pP, pD=pD, vv=vv, nbc=nbc):
            for ui, (r, j, qs, kp, kd) in enumerate(units):
                csl = slice(ui * nq, (ui + 1) * nq)
                e.matmul(psN[:, csl], vv[:, r * nbc + j, :], pP[:, csl], start=True, stop=False)
                e.matmul(psN[:, csl], vv[:nq, r * nbc + j + 1, :], pD[:nq, csl], start=False, stop=True)
            e.matmul(psS[:, :], b.ones[:, :], pP[:, :], start=True, stop=False)
            ins = e.matmul(psS[:, :], b.ones[:nq, :], pD[:nq, :], start=False, stop=True)
            return ins
        P.op(P.pe, fV, reads=(bt["bvv"], bt["bpP"], bt["bpD"], b.b_ones), writes=(bpsN, bpsS))
        bt.update(psN=psN, bpsN=bpsN, psS=psS, bpsS=bpsS)

    def T4(bt):
        h, g, u0 = bt["h"], bt["g"], bt["u0"]
        d, lg, nq = DIL[g], LG[g], NQ[g]
        upc = lg // nq
        U = 512 // nq
        psN, bpsN, psS, bpsS = bt["psN"], bt["bpsN"], bt["psS"], bt["bpsS"]
        r0 = u0 // upc
        if d == 1:
            l0 = u0 * nq
            dN = accN[:, l0:l0 + 512]
            dD = accD[:, l0:l0 + 512]
            sN, sS = psN[:, :], psS[:, :]
        else:
            nr = U // upc
            dN = accN[:, :].rearrange("p (l r) -> p r l", r=d)[:, r0:r0 + nr, :]
            dD = accD[:, :].rearrange("p (l r) -> p r l", r=d)[:, r0:r0 + nr, :]
            sN = psN[:, :].rearrange("p (r l) -> p r l", r=nr)
            sS = psS[:, :].rearrange("p (r l) -> p r l", r=nr)
        if g == 0:
            b.copy(dN, sN, reads=(bpsN,), writes=(b_acc,))
            b.copy(dD, sS, reads=(bpsS,), writes=(b_acc,))
        else:
            b.tt(dN, sN, dN, ALU.add, reads=(bpsN, b_acc), writes=(b_acc,))
            b.tt(dD, sS, dD, ALU.add, reads=(bpsS, b_acc), writes=(b_acc,))
        if bt["last"]:
            cbt, bcb = b.get("cb")
            for t in range(2):
                tsl = slice(t * 512, (t + 1) * 512)
                rd, brd = b.get("tf")
                b.recip_act(rd[:, :], accD[:, tsl], reads=(b_acc,), writes=(brd,))
                b.tt(cbt[:, tsl], accN[:, tsl], rd[:, :], ALU.mult, reads=(b_acc, brd), writes=(bcb,))
            b.dma_out(cat_d[:, h, :], cbt[:, :], rbufs=(bcb,))

    nbt = len(batches)
    for s_ in range(nbt + 3):
        if 0 <= s_ - 2 < nbt:
            T3(batches[s_ - 2])
        if s_ < nbt:
            T1(batches[s_])
        if 0 <= s_ - 1 < nbt:
            T2(batches[s_ - 1])
        if 0 <= s_ - 3 < nbt:
            T4(batches[s_ - 3])
    return b.finish(), b.wtiles


def tiles_B2(l):
    tl = [wt_std("b_w_out", l - 2, 0, 8, [(c, 512)]) for c in range(0, 2048, 512)]
    tl += mlp_tiles(l)
    return tl


def build_B2(l, with_next=True):
    b = Bld(tiles_B2(l), n_ps=6)
    b.pool("rstdp", 4, [128, 512], F32)
    stats = Stats(b)
    nc, P = b.nc, b.P
    xT_d = b.din("xT", [128, KC, T])
    cat_d = b.din("cat", [128, 8, T], BF16)
    yT_d = b.dout("yT", [128, KC, T])
    xs = b.sb("xs", [128, KC, T], F32)
    b_xs = [[Buf() for t in range(2)] for k in range(KC)]
    cat = b.sb("cat", [128, 16, T], BF16)
    b_cat = [[Buf() for t in range(2)] for k in range(16)]
    h1 = b.sb("h1", [128, 16, T], BF16)
    b_h1 = [[Buf() for t in range(2)] for k in range(16)]
    for k in range(8):
        b.dma_in(cat[:, k, :], cat_d[:, k, :], [b_cat[k][0], b_cat[k][1]])
    for k in range(KC):
        b.dma_in(xs[:, k, :], xT_d[:, k, :], [b_xs[k][0], b_xs[k][1]])
    for cg in range(4):
        wt, bw = b.next_w()
        wv = wt[:, :].rearrange("p (k c) -> p k c", k=8)
        for q4 in range(4):
            dc = cg * 4 + q4
            for t in range(2):
                tsl = slice(t * 512, (t + 1) * 512)
                ps, bps = b.get("ps")
                b.mm(ps[:, :], bps, [(wv[:, k, q4 * 128:(q4 + 1) * 128], cat[:, k, tsl]) for k in range(8)],
                     reads=[bw] + [b_cat[k][t] for k in range(8)])
                b.tt(xs[:, dc, tsl], ps[:, :], xs[:, dc, tsl], ALU.add, reads=(bps, b_xs[dc][t]), writes=(b_xs[dc][t],))
                stats.add(xs[:, dc, tsl], b_xs[dc][t], t)
    st_mlp(b, xs, b_xs, cat, b_cat, h1, b_h1, 0, out_d=yT_d, in_rstd=stats.rstd(), out_stats=(stats if with_next else None))
    if with_next:
        xnn_d = b.dout("xnn", [128, KC, T], BF16)
        st_apply_norm(b, xs, b_xs, cat, b_cat, 40, stats.rstd())
        for k in range(KC):
            b.dma_out(xnn_d[:, k, :], cat[:, k, :], rbufs=(b_cat[k][0], b_cat[k][1]))
    return b.finish(), b.wtiles


def _t5_bucket(n):
    n = np.maximum(np.asarray(n, np.int64), 0)
    nf = np.maximum(n, 1).astype(np.float32)
    large = 16 + (np.log(nf / np.float32(16.0)) / np.float32(math.log(2048 / 16)) * np.float32(16.0)).astype(np.int32)
    large = np.minimum(large, 31)
    return np.where(n < 16, n, large)


def _structural():
    oh = np.zeros((33, 6, 256), np.float32)
    for g, d in enumerate(DIL):
        for i in range(256):
            w = i - 127
            if 0 <= w <= 127:
                oh[_t5_bucket(w * d), g * 2 + 0, i] = 1.0
            else:
                oh[32, g * 2 + 0, i] = 1.0
            if -127 <= w <= 0:
                oh[_t5_bucket((w + 128) * d), g * 2 + 1, i] = 1.0
            else:
                oh[32, g * 2 + 1, i] = 1.0
    jm = np.ascontiguousarray(np.eye(128, dtype=np.float32)[::-1])
    return oh, jm


def _kv_layout(kT, vT):
    bf = ml_dtypes.bfloat16
    Kf = np.concatenate([np.asarray(k) for k in kT], axis=2)
    Vf = np.concatenate([np.asarray(v) for v in vT], axis=2)
    outs = [dict() for _ in range(NCORES)]
    for g, d in enumerate(DIL):
        lg, nkc, nbc = LG[g], NKC[g], NBC[g]
        Lf = S // d
        def cm(A):
            A = A[:, g * 4:(g + 1) * 4, :].reshape(128, 4, Lf, d).transpose(0, 1, 3, 2)
            pad = np.zeros((128, 4, d, 128), bf)
            return np.concatenate([pad, A], axis=3)
        Kc, Vc = cm(Kf), cm(Vf)
        for c in range(NCORES):
            ks = Kc[:, :, :, c * lg:c * lg + nkc]
            outs[c][f"kt{g}"] = np.ascontiguousarray(ks.reshape(128, 4, d * nkc))
            vs = Vc[:, :, :, c * lg:c * lg + nkc]
            vp = np.zeros((128, 4, d, nbc * 128), bf)
            vp[:, :, :, :nkc] = vs
            vp = vp.reshape(128, 4, d, nbc, 128).transpose(4, 1, 2, 3, 0)
            outs[c][f"vv{g}"] = np.ascontiguousarray(vp.reshape(128, 4, d * nbc, 128))
            kv = outs[c].setdefault("kval", np.zeros((128, 3), np.float32))
            kv[:, g] = ((c * lg - 128 + np.arange(128)) >= 0).astype(np.float32)
    return outs


def run_B_layer(l, xT, memT, W, kvin, xn_in=None, next_g=None):
    f32 = np.float32
    j = l - 2
    par = np.zeros((128, NPAR), f32)
    par[:, 0:16] = colk(W["norm_mix_g"][l])
    par[:, 16:32] = colk(W["mem_norm_g"][l])
    par[:, 32] = W["mem_q_norm_g"][l]
    par[:, 33] = W["mem_k_norm_g"][l]
    par[:, 34:37] = np.asarray(W["b_q_norm_g"][j], f32).T
    par[:, 255] = EPS
    nb1 = _CACHE.get(("B1", l))
    if nb1 is None:
        nb1 = _CACHE[("B1", l)] = build_B1(l, from_xn=(xn_in is not None))
    wst = pack_weights(nb1[1], W)
    oh, jm = _structural()
    relb = np.concatenate([np.asarray(W["rel_bias"], f32), np.full((1, 12), -30000.0, f32)], axis=0)
    ins = []
    for c in range(NCORES):
        dct = {"memT": memT, "wst": wst, "par": par, "oh": oh, "jm": jm, "relb": relb}
        if xn_in is not None:
            dct["xn"] = xn_in[c]
        else:
            dct["xT"] = xT[c]
        dct.update(kvin[c])
        ins.append(dct)
    r1 = _run(nb1[0], ins, f"B1_{l}")
    par2 = np.zeros((128, NPAR), f32)
    par2[:, 0:16] = colk(W["norm_mlp_g"][l])
    par2[:, 255] = EPS
    if next_g is not None:
        par2[:, 40:56] = colk(next_g)
    nb2 = _CACHE.get(("B2", l))
    if nb2 is None:
        nb2 = _CACHE[("B2", l)] = build_B2(l, with_next=(next_g is not None))
    wst2 = pack_weights(nb2[1], W)
    ins2 = [{"xT": xT[c], "cat": r1[c]["cat"], "wst": wst2, "par": par2} for c in range(NCORES)]
    r2 = _run(nb2[0], ins2, f"B2_{l}")
    xnn = [r2[c]["xnn"] for c in range(NCORES)] if next_g is not None else None
    return [r2[c]["yT"] for c in range(NCORES)], r1, xnn


def kernel(**inputs):
    W = {k: np.asarray(v) for k, v in inputs.items()}
    x = W["x"][0]
    memT = fm(W["mem"][0])
    xT = [fm(x[c * T:(c + 1) * T]) for c in range(NCORES)]
    g = W["norm_mix_g"]
    xT, _, _, _, xnn = run_A_layer(0, xT, memT, W, with_kv=False, next_g=g[1])
    xT, kT, vT, _, xnn = run_A_layer(1, xT, memT, W, with_kv=True, xn_in=xnn, next_g=g[2])
    kvin = _kv_layout(kT, vT)
    xT, _, xnn = run_B_layer(2, xT, memT, W, kvin, xn_in=xnn, next_g=g[3])
    xT, _, xnn = run_B_layer(3, xT, memT, W, kvin, xn_in=xnn, next_g=None)
    out = np.concatenate([unfm(t) for t in xT], axis=0)
    return out.reshape(1, S, D).astype(np.float32)
```
